# Optimizing a Trainium2 kernel written in Bass

```python
import math
import jax
import jax.numpy as jnp
from jax import lax
import numpy as np


D_MODEL = 2048
BATCH = 1
SEQ = 16384
DEPTH = 2

NSA_HEADS = 8
NSA_KV_HEADS = 2
NSA_HEAD_DIM = 64
NSA_GROUP = NSA_HEADS // NSA_KV_HEADS
NSA_WIDTH = NSA_HEADS * NSA_HEAD_DIM
NSA_KV_WIDTH = NSA_KV_HEADS * NSA_HEAD_DIM
CMP_LEN = 32
CMP_STRIDE = 16
CMP_HIDDEN = 256
SLC_BLOCK = 64
SLC_TOPK = 16
WINDOW = 512
Q_BLOCK = 128
ROPE_THETA = 500000.0
ROPE_DIMS = NSA_HEAD_DIM // 4

S5_WIDTH = 512
S5_GROUP_CH = 16
S5_GROUPS = S5_WIDTH // S5_GROUP_CH
S5_STATE = 64

GDN_HEADS = 8
GDN_HEAD_DIM = 128
GDN_WIDTH = GDN_HEADS * GDN_HEAD_DIM
GDN_CONV = 4
GDN_CHUNK = 64

MIX_WIDTH = NSA_WIDTH + S5_WIDTH + GDN_WIDTH
D_FF = 5504
FFN_CONV = 3
EPS = 1e-6
NEG_BIG = -1e30

IN_SPLITS = (NSA_WIDTH,) + (NSA_KV_WIDTH,) * 6 + (3 * NSA_HEADS, S5_WIDTH, 3 * GDN_WIDTH, GDN_WIDTH, GDN_HEADS, GDN_HEADS)
IN_COLS = sum(IN_SPLITS)

kernel_name = 'hymba_nsa_s5_gdn_convffn'


def rmsnorm(x, g):
    xf = x.astype(jnp.float32)
    y = xf * lax.rsqrt(jnp.mean(xf * xf, axis=-1, keepdims=True) + EPS)
    return (y * g.astype(jnp.float32)).astype(x.dtype)


def l2norm(x):
    xf = x.astype(jnp.float32)
    return xf * lax.rsqrt(jnp.sum(xf * xf, axis=-1, keepdims=True) + EPS)


def rope_partial(x, pos):
    half = ROPE_DIMS // 2
    inv = ROPE_THETA ** (-jnp.arange(half, dtype=jnp.float32) / half)
    ang = pos.astype(jnp.float32)[:, None] * inv[None, :]
    cos = jnp.cos(ang)[:, None, :]
    sin = jnp.sin(ang)[:, None, :]
    xr = x[..., :ROPE_DIMS].astype(jnp.float32)
    x1, x2 = xr[..., :half], xr[..., half:]
    rot = jnp.concatenate([x1 * cos - x2 * sin, x2 * cos + x1 * sin], axis=-1)
    return jnp.concatenate([rot.astype(x.dtype), x[..., ROPE_DIMS:]], axis=-1)


def causal_dwconv(x, w):
    width = w.shape[0]
    L = x.shape[1]
    xp = jnp.pad(x, ((0, 0), (width - 1, 0), (0, 0)))
    return sum(xp[:, j:j + L] * w[j] for j in range(width))


def masked_softmax(s, mask):
    s = jnp.where(mask, s.astype(jnp.float32), NEG_BIG)
    p = jnp.where(mask, jnp.exp(s - jnp.max(s, axis=-1, keepdims=True)), 0.0)
    return p / jnp.maximum(jnp.sum(p, axis=-1, keepdims=True), 1e-30)


def compress_blocks(win, pe, w1, w2):
    h = win.astype(jnp.float32) + pe[:, None, :]
    h = jnp.einsum('bnlhd,ldf->bnhf', h, w1)
    return jnp.einsum('bnhf,fd->bnhd', jax.nn.gelu(h), w2)


def nsa_mixer(q, kc_raw, vc_raw, ks, vs, kw, vw, gate_logits,
              q_norm, kc_norm, ks_norm, kw_norm, cmp_pe, cmp_k_w1, cmp_k_w2, cmp_v_w1, cmp_v_w2):
    B_, L, _ = q.shape
    H, HK, G, DH = NSA_HEADS, NSA_KV_HEADS, NSA_GROUP, NSA_HEAD_DIM
    scale = DH ** -0.5
    pos = jnp.arange(L, dtype=jnp.float32)
    heads = lambda t, h: t.reshape(B_, L, h, DH)
    q = rope_partial(rmsnorm(heads(q, H), q_norm), pos)
    ks = rope_partial(rmsnorm(heads(ks, HK), ks_norm), pos)
    kw = rope_partial(rmsnorm(heads(kw, HK), kw_norm), pos)
    vs = heads(vs, HK)
    vw = heads(vw, HK)

    n_cmp = (L - CMP_LEN) // CMP_STRIDE + 1
    cmp_start = jnp.arange(n_cmp) * CMP_STRIDE
    win_idx = cmp_start[:, None] + jnp.arange(CMP_LEN)[None, :]
    kc = compress_blocks(heads(kc_raw, HK)[:, win_idx], cmp_pe, cmp_k_w1, cmp_k_w2)
    vc = compress_blocks(heads(vc_raw, HK)[:, win_idx], cmp_pe, cmp_v_w1, cmp_v_w2)
    kc = rope_partial(rmsnorm(kc, kc_norm), (cmp_start + CMP_LEN // 2).astype(jnp.float32))
    cmp_end = cmp_start + CMP_LEN - 1

    n_slc = L // SLC_BLOCK
    slc_start = jnp.arange(n_slc) * SLC_BLOCK
    sel_map = ((cmp_start[:, None] <= slc_start[None, :] + SLC_BLOCK - 1)
               & (cmp_end[:, None] >= slc_start[None, :])).astype(jnp.float32)
    n_sel = min(SLC_TOPK, n_slc)
    ks_blocks = ks.reshape(B_, n_slc, SLC_BLOCK, HK, DH).transpose(0, 3, 1, 2, 4)
    vs_blocks = vs.reshape(B_, n_slc, SLC_BLOCK, HK, DH).transpose(0, 3, 1, 2, 4)
    kw_pad = jnp.pad(kw, ((0, 0), (WINDOW, 0), (0, 0), (0, 0)))
    vw_pad = jnp.pad(vw, ((0, 0), (WINDOW, 0), (0, 0), (0, 0)))
    gates = jax.nn.sigmoid(gate_logits.astype(jnp.float32)).reshape(B_, L, HK, G, 3)
    b_idx = jnp.arange(B_)[:, None, None, None]
    h_idx = jnp.arange(HK)[None, None, :, None]
    j_idx = jnp.arange(n_slc)

    def query_block(qb_i):
        s = qb_i * Q_BLOCK
        t = s + jnp.arange(Q_BLOCK)
        qg = lax.dynamic_slice_in_dim(q, s, Q_BLOCK, axis=1).reshape(B_, Q_BLOCK, HK, G, DH)
        gb = lax.dynamic_slice_in_dim(gates, s, Q_BLOCK, axis=1)

        sc = jnp.einsum('bqhgd,bnhd->bhgqn', qg, kc) * scale
        pc = masked_softmax(sc, cmp_end[None, :] <= t[:, None])
        o_c = jnp.einsum('bhgqn,bnhd->bqhgd', pc, vc)

        imp = jnp.einsum('bhgqn,nj->bqhj', pc, sel_map)
        cur = t // SLC_BLOCK
        forced = (j_idx[None, :] == 0) | (j_idx[None, :] == cur[:, None]) | (j_idx[None, :] == cur[:, None] - 1)
        valid = j_idx[None, :] <= cur[:, None]
        imp = jnp.where(valid[None, :, None, :], jnp.where(forced[None, :, None, :], jnp.inf, imp), -jnp.inf)
        _, sel = lax.top_k(imp, n_sel)
        kb = ks_blocks[b_idx, h_idx, sel].reshape(B_, Q_BLOCK, HK, n_sel * SLC_BLOCK, DH)
        vb = vs_blocks[b_idx, h_idx, sel].reshape(B_, Q_BLOCK, HK, n_sel * SLC_BLOCK, DH)
        kpos = (sel[..., None] * SLC_BLOCK + jnp.arange(SLC_BLOCK)).reshape(B_, Q_BLOCK, HK, n_sel * SLC_BLOCK)
        mask_s = (kpos <= t[None, :, None, None]).transpose(0, 2, 1, 3)[:, :, None]
        ss = jnp.einsum('bqhgd,bqhkd->bhgqk', qg, kb) * scale
        o_s = jnp.einsum('bhgqk,bqhkd->bqhgd', masked_softmax(ss, mask_s), vb)

        kwb = lax.dynamic_slice_in_dim(kw_pad, s, Q_BLOCK + WINDOW, axis=1)
        vwb = lax.dynamic_slice_in_dim(vw_pad, s, Q_BLOCK + WINDOW, axis=1)
        kpos_w = s - WINDOW + jnp.arange(Q_BLOCK + WINDOW)
        mask_w = ((kpos_w[None, :] <= t[:, None]) & (kpos_w[None, :] > t[:, None] - WINDOW)
                  & (kpos_w[None, :] >= 0))
        sw = jnp.einsum('bqhgd,bkhd->bhgqk', qg, kwb) * scale
        o_w = jnp.einsum('bhgqk,bkhd->bqhgd', masked_softmax(sw, mask_w), vwb)

        out = gb[..., 0, None] * o_c + gb[..., 1, None] * o_s + gb[..., 2, None] * o_w
        return out.reshape(B_, Q_BLOCK, NSA_WIDTH)

    out = lax.map(query_block, jnp.arange(L // Q_BLOCK))
    return out.transpose(1, 0, 2, 3).reshape(B_, L, NSA_WIDTH).astype(q.dtype)


def s5_mixer(u, lam_re, lam_im, log_dt, b_re, b_im, c_re, c_im, d_skip, w_glu):
    B_, L, _ = u.shape
    f32 = lambda t: t.astype(jnp.float32)
    lam_re, lam_im, b_re, b_im, c_re, c_im = map(f32, (lam_re, lam_im, b_re, b_im, c_re, c_im))
    ug = f32(u).reshape(B_, L, S5_GROUPS, S5_GROUP_CH)
    dt = jnp.exp(f32(log_dt))[:, None]
    mag = jnp.exp(lam_re * dt)
    lb_re = mag * jnp.cos(lam_im * dt)
    lb_im = mag * jnp.sin(lam_im * dt)
    den = lam_re * lam_re + lam_im * lam_im
    nr, ni = lb_re - 1.0, lb_im
    coef_re = (nr * lam_re + ni * lam_im) / den
    coef_im = (ni * lam_re - nr * lam_im) / den
    bb_re = coef_re[..., None] * b_re - coef_im[..., None] * b_im
    bb_im = coef_re[..., None] * b_im + coef_im[..., None] * b_re
    bu_re = jnp.einsum('blgh,gph->blgp', ug, bb_re)
    bu_im = jnp.einsum('blgh,gph->blgp', ug, bb_im)
    a_re = jnp.broadcast_to(lb_re, bu_re.shape)
    a_im = jnp.broadcast_to(lb_im, bu_im.shape)

    def combine(e1, e2):
        a1r, a1i, b1r, b1i = e1
        a2r, a2i, b2r, b2i = e2
        return (a2r * a1r - a2i * a1i, a2r * a1i + a2i * a1r,
                a2r * b1r - a2i * b1i + b2r, a2r * b1i + a2i * b1r + b2i)

    _, _, xr, xi = lax.associative_scan(combine, (a_re, a_im, bu_re, bu_im), axis=1)
    y = jnp.einsum('blgp,ghp->blgh', xr, c_re) - jnp.einsum('blgp,ghp->blgh', xi, c_im)
    y = y + f32(d_skip).reshape(S5_GROUPS, S5_GROUP_CH) * ug
    y = jax.nn.gelu(y.reshape(B_, L, S5_WIDTH))
    y = y * jax.nn.sigmoid(y @ f32(w_glu))
    return y.astype(u.dtype)


def chunk_gated_delta_rule(q, k, v, g, beta):
    B_, L, H, DK = q.shape
    DV = v.shape[-1]
    C = GDN_CHUNK
    N = L // C

    def chunks(t):
        return t.astype(jnp.float32).reshape(B_, N, C, H, -1).transpose(1, 0, 3, 2, 4)

    q, k, v = chunks(q), chunks(k), chunks(v)
    beta = chunks(beta[..., None])[..., 0]
    gc = jnp.cumsum(chunks(g[..., None])[..., 0], axis=-1)
    causal = jnp.tril(jnp.ones((C, C), bool))
    strict = jnp.tril(jnp.ones((C, C), bool), -1)
    diff = gc[..., :, None] - gc[..., None, :]
    decay = jnp.where(causal, jnp.exp(jnp.where(causal, diff, 0.0)), 0.0)
    k_beta = k * beta[..., None]
    a_mat = jnp.where(strict, jnp.einsum('nbhcd,nbhed->nbhce', k_beta, k) * decay, 0.0)
    eye = jnp.broadcast_to(jnp.eye(C, dtype=jnp.float32), a_mat.shape)
    t_mat = lax.linalg.triangular_solve(a_mat, eye, left_side=True, lower=True, unit_diagonal=True)
    value = t_mat @ (v * beta[..., None])
    k_cum = t_mat @ (k_beta * jnp.exp(gc)[..., None])
    qk = jnp.einsum('nbhcd,nbhed->nbhce', q, k) * decay

    def step(state, inp):
        q_i, k_i, val_i, kc_i, qk_i, g_i = inp
        v_new = val_i - kc_i @ state
        o_i = (q_i * jnp.exp(g_i)[..., None]) @ state + qk_i @ v_new
        g_last = g_i[..., -1:]
        state = state * jnp.exp(g_last)[..., None] + jnp.einsum(
            'bhcd,bhce->bhde', k_i * jnp.exp(g_last - g_i)[..., None], v_new)
        return state, o_i

    state0 = jnp.zeros((B_, H, DK, DV), jnp.float32)
    _, o = lax.scan(step, state0, (q, k, value, k_cum, qk, gc))
    return o.transpose(1, 0, 3, 2, 4).reshape(B_, L, H, DV)


def gdn_mixer(qkv, z, a, b, conv_w, a_log, dt_bias, out_norm):
    B_, L, _ = qkv.shape
    qkv = jax.nn.silu(causal_dwconv(qkv, conv_w))
    q, k, v = jnp.split(qkv, 3, axis=-1)
    q = l2norm(q.reshape(B_, L, GDN_HEADS, GDN_HEAD_DIM)) * (GDN_HEAD_DIM ** -0.5)
    k = l2norm(k.reshape(B_, L, GDN_HEADS, GDN_HEAD_DIM))
    v = v.reshape(B_, L, GDN_HEADS, GDN_HEAD_DIM)
    beta = jax.nn.sigmoid(b.astype(jnp.float32))
    g = -jnp.exp(a_log.astype(jnp.float32)) * jax.nn.softplus(a.astype(jnp.float32) + dt_bias.astype(jnp.float32))
    o = chunk_gated_delta_rule(q, k, v, g, beta)
    o = rmsnorm(o, out_norm) * jax.nn.silu(z.astype(jnp.float32).reshape(B_, L, GDN_HEADS, GDN_HEAD_DIM))
    return o.reshape(B_, L, GDN_WIDTH).astype(qkv.dtype)


def conv_ffn(x, w_in, conv_w, conv_b, w_out):
    h = causal_dwconv(x @ w_in, conv_w) + conv_b
    gate, up = jnp.split(h, 2, axis=-1)
    return (jax.nn.silu(gate) * up) @ w_out


def setup_inputs(seed: int = 0) -> dict:
    key = jax.random.key(seed)
    keys = iter(jax.random.split(key, 48))

    def normal(shape, scale):
        return jax.random.normal(next(keys), shape, jnp.float32) * scale

    def gain(shape):
        return 1.0 + 0.02 * jax.random.normal(next(keys), shape, jnp.float32)

    def uniform(shape, lo, hi):
        return jax.random.uniform(next(keys), shape, jnp.float32, lo, hi)

    Ly, dh = DEPTH, NSA_HEAD_DIM
    n_idx = jnp.arange(S5_STATE, dtype=jnp.float32)
    dt_gdn = jnp.exp(uniform((Ly, GDN_HEADS), math.log(1e-3), math.log(1e-1)))
    return {
        'x': normal((BATCH, SEQ, D_MODEL), 1.0),
        'attn_norm': gain((Ly, D_MODEL)),
        'w_in': normal((Ly, D_MODEL, IN_COLS), D_MODEL ** -0.5),
        'nsa_q_norm': gain((Ly, dh)),
        'nsa_kc_norm': gain((Ly, dh)),
        'nsa_ks_norm': gain((Ly, dh)),
        'nsa_kw_norm': gain((Ly, dh)),
        'cmp_pe': normal((Ly, CMP_LEN, dh), 0.5),
        'cmp_k_w1': normal((Ly, CMP_LEN, dh, CMP_HIDDEN), (CMP_LEN * dh) ** -0.5),
        'cmp_k_w2': normal((Ly, CMP_HIDDEN, dh), CMP_HIDDEN ** -0.5),
        'cmp_v_w1': normal((Ly, CMP_LEN, dh, CMP_HIDDEN), (CMP_LEN * dh) ** -0.5),
        'cmp_v_w2': normal((Ly, CMP_HIDDEN, dh), CMP_HIDDEN ** -0.5),
        'nsa_out_norm': gain((Ly, NSA_WIDTH)),
        's5_lam_re': -0.5 + normal((Ly, S5_GROUPS, S5_STATE), 0.01),
        's5_lam_im': jnp.pi * n_idx + normal((Ly, S5_GROUPS, S5_STATE), 0.01),
        's5_log_dt': uniform((Ly, S5_GROUPS), math.log(1e-3), math.log(1e-1)),
        's5_b_re': normal((Ly, S5_GROUPS, S5_STATE, S5_GROUP_CH), (2 * S5_GROUP_CH) ** -0.5),
        's5_b_im': normal((Ly, S5_GROUPS, S5_STATE, S5_GROUP_CH), (2 * S5_GROUP_CH) ** -0.5),
        's5_c_re': normal((Ly, S5_GROUPS, S5_GROUP_CH, S5_STATE), S5_STATE ** -0.5),
        's5_c_im': normal((Ly, S5_GROUPS, S5_GROUP_CH, S5_STATE), S5_STATE ** -0.5),
        's5_d': normal((Ly, S5_WIDTH), 0.5),
        's5_w_glu': normal((Ly, S5_WIDTH, S5_WIDTH), S5_WIDTH ** -0.5),
        's5_out_norm': gain((Ly, S5_WIDTH)),
        'gdn_conv': normal((Ly, GDN_CONV, 3 * GDN_WIDTH), GDN_CONV ** -0.5),
        'gdn_a_log': jnp.log(uniform((Ly, GDN_HEADS), 1.0, 16.0)),
        'gdn_dt_bias': dt_gdn + jnp.log(-jnp.expm1(-dt_gdn)),
        'gdn_norm': gain((Ly, GDN_HEAD_DIM)),
        'w_out': normal((Ly, MIX_WIDTH, D_MODEL), MIX_WIDTH ** -0.5),
        'ffn_norm': gain((Ly, D_MODEL)),
        'ffn_w_in': normal((Ly, D_MODEL, 2 * D_FF), D_MODEL ** -0.5),
        'ffn_conv': normal((Ly, FFN_CONV, 2 * D_FF), FFN_CONV ** -0.5),
        'ffn_conv_b': normal((Ly, 2 * D_FF), 0.02),
        'ffn_w_out': normal((Ly, D_FF, D_MODEL), D_FF ** -0.5),
    }


def reference(x, attn_norm, w_in, nsa_q_norm, nsa_kc_norm, nsa_ks_norm, nsa_kw_norm, cmp_pe,
              cmp_k_w1, cmp_k_w2, cmp_v_w1, cmp_v_w2, nsa_out_norm, s5_lam_re, s5_lam_im, s5_log_dt,
              s5_b_re, s5_b_im, s5_c_re, s5_c_im, s5_d, s5_w_glu, s5_out_norm, gdn_conv, gdn_a_log,
              gdn_dt_bias, gdn_norm, w_out, ffn_norm, ffn_w_in, ffn_conv, ffn_conv_b, ffn_w_out):
    split_points = [int(p) for p in np.cumsum(IN_SPLITS)[:-1]]
    for l in range(DEPTH):
        h = rmsnorm(x, attn_norm[l])
        (q, kc, vc, ks, vs, kw, vw, nsa_gates, u, qkv, z, a, b) = jnp.split(h @ w_in[l], split_points, axis=-1)
        y_a = nsa_mixer(q, kc, vc, ks, vs, kw, vw, nsa_gates,
                        nsa_q_norm[l], nsa_kc_norm[l], nsa_ks_norm[l], nsa_kw_norm[l], cmp_pe[l],
                        cmp_k_w1[l], cmp_k_w2[l], cmp_v_w1[l], cmp_v_w2[l])
        y_b = s5_mixer(u, s5_lam_re[l], s5_lam_im[l], s5_log_dt[l], s5_b_re[l], s5_b_im[l],
                       s5_c_re[l], s5_c_im[l], s5_d[l], s5_w_glu[l])
        y_c = gdn_mixer(qkv, z, a, b, gdn_conv[l], gdn_a_log[l], gdn_dt_bias[l], gdn_norm[l])
        mix = jnp.concatenate([rmsnorm(y_a, nsa_out_norm[l]), rmsnorm(y_b, s5_out_norm[l]), y_c], axis=-1)
        x = x + (mix @ w_out[l]).astype(x.dtype)
        x = x + conv_ffn(rmsnorm(x, ffn_norm[l]), ffn_w_in[l], ffn_conv[l], ffn_conv_b[l], ffn_w_out[l]).astype(x.dtype)
    return x
```

```python
from contextlib import ExitStack
import numpy as np
import concourse.bass as bass
import concourse.mybir as mybir
from concourse.bass_utils import run_bass_kernel_spmd

F32 = mybir.dt.float32
BF16 = mybir.dt.bfloat16
AF = mybir.ActivationFunctionType
ALU = mybir.AluOpType
AX = mybir.AxisListType

NCORES = 8
D_MODEL = 2048
SEQ = 16384
DEPTH = 2
IN_COLS = 5928
D_FF = 5504
EPS = 1e-6
TOK = SEQ // NCORES


class T:
    __slots__ = ("h", "name", "w", "r", "din", "dout", "sin", "sout", "psum")

    def __init__(self, h, name, psum=False):
        self.psum = psum
        self.h = h
        self.name = name
        self.w = None
        self.r = {}
        self.din = 0
        self.dout = 0
        self.sin = None
        self.sout = None

    def __getitem__(self, idx):
        return self.h[idx]


class KB:
    ENGS = ("pe", "act", "dve", "pool", "sp")

    def __init__(self, nc):
        self.nc = nc
        self.es = ExitStack()
        self.eng = {"pe": nc.tensor, "act": nc.scalar, "dve": nc.vector,
                    "pool": nc.gpsimd, "sp": nc.sync}
        self.sem = {k: self.es.enter_context(nc.semaphore("s_" + k)) for k in self.ENGS}
        self.cnt = {k: 0 for k in self.ENGS}
        self.known = {k: {} for k in self.ENGS}
        self.nsem = len(self.ENGS)
        self.ntile = 0
        self.out_waits = []
        self.rr = 0

    def sb(self, shape, dtype=F32, name=None):
        self.ntile += 1
        name = name or "t"
        h = self.es.enter_context(self.nc.sbuf_tensor(f"{name}_{self.ntile}", list(shape), dtype))
        return T(h, name)

    def ps(self, shape, dtype=F32, name=None):
        self.ntile += 1
        name = name or "p"
        h = self.es.enter_context(self.nc.psum_tensor(f"{name}_{self.ntile}", list(shape), dtype))
        return T(h, name, psum=True)

    def newsem(self, name):
        self.nsem += 1
        assert self.nsem < 145, "too many semaphores"
        return self.es.enter_context(self.nc.semaphore(f"{name}_{self.nsem}"))

    def _collect(self, e, reads, writes):
        needs = {}

        def need(sem, val):
            key = id(sem)
            if needs.get(key, (None, 0))[1] < val:
                needs[key] = (sem, val)

        for t in reads:
            if t.w is not None:
                need(self.sem[t.w[0]], t.w[1])
            if t.din:
                need(t.sin, t.din)
            if t.psum:
                for kk, c in t.r.items():
                    if kk != e:
                        need(self.sem[kk], c)
        for t in writes:
            if t.w is not None and not (t.w[0] == e and e == "pe"):
                need(self.sem[t.w[0]], t.w[1])
            for kk, c in t.r.items():
                need(self.sem[kk], c)
            if t.din:
                need(t.sin, t.din)
            if t.dout:
                need(t.sout, t.dout)
        kn = self.known[e]
        for key, (sem, val) in needs.items():
            if kn.get(key, 0) < val:
                self.eng[e].wait_ge(sem, val)
                kn[key] = val

    def op(self, e, fn, reads=(), writes=()):
        reads = [getattr(t, "base", t) for t in reads]
        writes = [getattr(t, "base", t) for t in writes]
        self._collect(e, reads, writes)
        ins = fn(self.eng[e])
        ins.then_inc(self.sem[e], 1)
        self.cnt[e] += 1
        c = self.cnt[e]
        self.known[e][id(self.sem[e])] = max(self.known[e].get(id(self.sem[e]), 0), 0)
        for t in writes:
            t.w = (e, c)
            t.r = {}
        for t in reads:
            if t not in writes:
                t.r[e] = c
        return ins

    def dma(self, q, out, in_, t_out=None, t_in=None, final=False, **kw):
        reads = [t_in] if t_in is not None else []
        writes = [t_out] if t_out is not None else []
        self._collect(q, reads, writes)
        ins = self.eng[q].dma_start(out=out, in_=in_, **kw)
        if t_out is not None:
            if t_out.sin is None:
                t_out.sin = self.newsem("di")
            ins.then_inc(t_out.sin, 16)
            t_out.din += 16
            t_out.w = None
            t_out.r = {}
        if t_in is not None and t_out is None:
            if t_in.sout is None:
                t_in.sout = self.newsem("do")
            ins.then_inc(t_in.sout, 16)
            t_in.dout += 16
            if final and t_in not in self.out_waits:
                self.out_waits.append(t_in)
        return ins

    def finish(self):
        sp = self.eng["sp"]
        for t in self.out_waits:
            sp.wait_ge(t.sout, t.dout)
        for kk in self.ENGS:
            if kk != "sp" and self.cnt[kk]:
                sp.wait_ge(self.sem[kk], self.cnt[kk])
        self.es.close()

    def evac_eng(self):
        self.rr += 1
        return ("act", "dve")[self.rr % 2]

    def copy(self, e, out_t, out_ap, in_t, in_ap):
        if e == "act":
            return self.op("act", lambda g: g.copy(out=out_ap, in_=in_ap), [in_t], [out_t])
        return self.op(e, lambda g: g.tensor_copy(out=out_ap, in_=in_ap), [in_t], [out_t])


def col_chunks(n0, n1):
    out = []
    c = n0
    while c < n1:
        w = min(128, n1 - c)
        out.append((c, w))
        c += w
    return out


def rms_stats(k, ones, src, nchunk, W, ps, sq_tiles, rstd, eps=EPS):
    for c in range(nchunk):
        sq = sq_tiles[c % len(sq_tiles)]
        k.op("act", lambda g, sq=sq, c=c: g.activation(out=sq[:, :W], in_=src[:, c, :W], func=AF.Square),
             [src], [sq])
        k.op("pe", lambda g, sq=sq, c=c: g.matmul(ps[:, :W], lhsT=ones[:], rhs=sq[:, :W],
                                               start=(c == 0), stop=(c == nchunk - 1)), [ones, sq], [ps])
    k.op("dve", lambda g: g.tensor_scalar(out=rstd[:, :W], in0=ps[:, :W], scalar1=eps, scalar2=None,
                                          op0=ALU.add), [ps], [rstd])
    k.op("act", lambda g: g.activation(out=rstd[:, :W], in_=rstd[:, :W], func=AF.Sqrt), [rstd], [rstd])
    k.op("dve", lambda g: g.reciprocal(out=rstd[:, :W], in_=rstd[:, :W]), [rstd], [rstd])


def build_stage_A(ntok=TOK):
    nc = bass.Bass("TRN2", target_bir_lowering=False)
    C = D_MODEL // 128
    TW = 512
    ntile = ntok // TW
    xT = nc.dram_tensor("xT", [D_MODEL, ntok], F32, kind="ExternalInput").ap()
    gain_d = nc.dram_tensor("gain", [128, C], F32, kind="ExternalInput").ap()
    w_d = nc.dram_tensor("w", [D_MODEL, IN_COLS], F32, kind="ExternalInput").ap()
    out_d = nc.dram_tensor("projT", [IN_COLS, ntok], F32, kind="ExternalOutput").ap()
    k = KB(nc)
    ones = k.sb([128, 128], F32, "ones")
    k.op("pool", lambda g: g.memset(ones[:], 1.0 / D_MODEL), [], [ones])
    gain = k.sb([128, C], F32, "gain")
    k.dma("sp", gain[:], gain_d, t_out=gain)
    hT = [k.sb([128, C, TW], BF16, f"hT{i}") for i in range(ntile)]
    xs = [k.sb([128, C, TW], F32, f"xs{i}") for i in range(2)]
    sqs = [k.sb([128, TW], F32, f"sq{i}") for i in range(2)]
    rstd = k.sb([128, TW], F32, "rstd")
    pstat = k.ps([128, TW], F32, "pstat")
    pacc = [k.ps([128, TW], F32, f"pacc{i}") for i in range(6)]
    xv = xT.rearrange("(c p) t -> p c t", p=128)
    wv = w_d.rearrange("(c p) n -> p c n", p=128)
    wb = [k.sb([128, C, 512], BF16, f"wb{i}") for i in range(2)]
    ost = [k.sb([128, TW], F32, f"ost{i}") for i in range(4)]
    groups = [(g0, min(512, IN_COLS - g0)) for g0 in range(0, IN_COLS, 512)]
    k.dma("pool", wb[0][:, :, :groups[0][1]], wv[:, :, 0:groups[0][1]], t_out=wb[0])
    for i in range(ntile):
        xt = xs[i % 2]
        k.dma("sp", xt[:], xv[:, :, i * TW:(i + 1) * TW], t_out=xt)
        rms_stats(k, ones, xt, C, TW, pstat, sqs, rstd)
        for c in range(C):
            k.op("dve", lambda g, c=c: g.scalar_tensor_tensor(
                out=hT[i][:, c, :], in0=xt[:, c, :], scalar=gain[:, c:c + 1], in1=rstd[:],
                op0=ALU.mult, op1=ALU.mult), [xt, gain, rstd], [hT[i]])
    n = 0
    for gi, (g0, gw) in enumerate(groups):
        w = wb[gi % 2]
        if gi + 1 < len(groups):
            g1, gw1 = groups[gi + 1]
            k.dma("pool", wb[(gi + 1) % 2][:, :, :gw1], wv[:, :, g1:g1 + gw1], t_out=wb[(gi + 1) % 2])
        for (c0, cw) in col_chunks(g0, g0 + gw):
            off = c0 - g0
            for i in range(ntile):
                p = pacc[n % len(pacc)]
                o = ost[n % len(ost)]
                n += 1
                for c in range(C):
                    k.op("pe", lambda g, c=c, p=p: g.matmul(p[:cw, :], lhsT=w[:, c, off:off + cw], rhs=hT[i][:, c, :],
                                                        start=(c == 0), stop=(c == C - 1)), [w, hT[i]], [p])
                k.copy(k.evac_eng(), o, o[:cw, :], p, p[:cw, :])
                k.dma("sp", out_d[c0:c0 + cw, i * TW:(i + 1) * TW], o[:cw, :], t_in=o, final=True)
    k.finish()
    return nc


def build_stage_C(ntok=TOK):
    nc = bass.Bass("TRN2", target_bir_lowering=False)
    C = D_MODEL // 128
    HALO = 2
    NT = ntok + HALO
    TW = 512
    NJ = D_FF // 128
    xT = nc.dram_tensor("xT", [D_MODEL, NT], F32, kind="ExternalInput").ap()
    yaT = nc.dram_tensor("yaT", [512, NT], F32, kind="ExternalInput").ap()
    ybT = nc.dram_tensor("ybT", [512, NT], F32, kind="ExternalInput").ap()
    ycT = nc.dram_tensor("ycT", [1024, NT], F32, kind="ExternalInput").ap()
    g_nsa = nc.dram_tensor("g_nsa", [128, 4], F32, kind="ExternalInput").ap()
    g_s5 = nc.dram_tensor("g_s5", [128, 4], F32, kind="ExternalInput").ap()
    g_ffn = nc.dram_tensor("g_ffn", [128, C], F32, kind="ExternalInput").ap()
    cw_d = nc.dram_tensor("ffn_conv", [128, 3, 2 * NJ], F32, kind="ExternalInput").ap()
    cb_d = nc.dram_tensor("ffn_conv_b", [128, 2 * NJ], F32, kind="ExternalInput").ap()
    wglu_d = nc.dram_tensor("w_glu", [512, 512], F32, kind="ExternalInput").ap()
    wout_d = nc.dram_tensor("w_out", [D_MODEL, D_MODEL], F32, kind="ExternalInput").ap()
    wfi_d = nc.dram_tensor("ffn_w_in", [D_MODEL, 2 * D_FF], F32, kind="ExternalInput").ap()
    wfo_d = nc.dram_tensor("ffn_w_out", [D_FF, D_MODEL], F32, kind="ExternalInput").ap()
    out_d = nc.dram_tensor("xoT", [D_MODEL, ntok], F32, kind="ExternalOutput").ap()

    k = KB(nc)
    ones_d = k.sb([128, 128], F32, "ones_d")
    k.op("pool", lambda g: g.memset(ones_d[:], 1.0 / D_MODEL), [], [ones_d])
    ones_4 = k.sb([128, 128], F32, "ones_4")
    k.op("pool", lambda g: g.memset(ones_4[:], 1.0 / 512.0), [], [ones_4])
    gn = k.sb([128, 4], F32, "gn"); k.dma("sp", gn[:], g_nsa, t_out=gn)
    gs = k.sb([128, 4], F32, "gs"); k.dma("sp", gs[:], g_s5, t_out=gs)
    gf = k.sb([128, C], F32, "gf"); k.dma("sp", gf[:], g_ffn, t_out=gf)
    cw = k.sb([128, 3, 2 * NJ], F32, "cw"); k.dma("sp", cw[:], cw_d, t_out=cw)
    cb = k.sb([128, 2 * NJ], F32, "cb"); k.dma("sp", cb[:], cb_d, t_out=cb)
    wglu = k.sb([128, 4, 512], BF16, "wglu")
    k.dma("pool", wglu[:], wglu_d.rearrange("(c p) n -> p c n", p=128), t_out=wglu)

    xt = k.sb([128, C, TW], F32, "xt")
    ain = k.sb([128, C, TW], BF16, "ain")
    act = k.sb([128, NJ, TW], BF16, "act")
    ya = k.sb([128, 4, TW], F32, "ya")
    yb = k.sb([128, 4, TW], F32, "yb")
    ybb = k.sb([128, 4, TW], BF16, "ybb")
    y2 = ya
    sqs = [k.sb([128, TW], F32, f"sq{i}") for i in range(2)]
    rstd = k.sb([128, TW], F32, "rstd")
    sig = k.sb([128, TW], F32, "sig")
    tails = k.sb([128, 2 * NJ, 2], F32, "tails")
    k.op("pool", lambda g: g.memset(tails[:], 0.0), [], [tails])
    ext = [k.sb([128, TW + 2], F32, f"ext{i}") for i in range(4)]
    gt = [k.sb([128, TW], F32, f"gt{i}") for i in range(2)]
    ut = [k.sb([128, TW], F32, f"ut{i}") for i in range(2)]
    ost = [k.sb([128, TW], F32, f"ost{i}") for i in range(2)]
    wb = [k.sb([128, C, 512], BF16, f"wb{i}") for i in range(3)]
    pstat = k.ps([128, TW], F32, "pstat")
    pacc = [k.ps([128, TW], F32, f"pacc{i}") for i in range(7)]
    state = {"wn": 0, "pn": 0}

    def next_w():
        w = wb[state["wn"] % len(wb)]
        state["wn"] += 1
        return w

    def next_p():
        p = pacc[state["pn"] % len(pacc)]
        state["pn"] += 1
        return p

    woutv = wout_d.rearrange("(c p) n -> p c n", p=128)
    wfiv = wfi_d.rearrange("(c p) n -> p c n", p=128)
    wfov = wfo_d.rearrange("(c p) n -> p c n", p=128)

    tiles = [(0, HALO)] + [(HALO + i * TW, TW) for i in range(ntok // TW)]
    for ti, (t0, W) in enumerate(tiles):
        halo = (ti == 0)
        k.dma("sp", xt[:, :, :W], xT.rearrange("(c p) t -> p c t", p=128)[:, :, t0:t0 + W], t_out=xt)
        k.dma("sp", ya[:, :, :W], yaT.rearrange("(c p) t -> p c t", p=128)[:, :, t0:t0 + W], t_out=ya)
        k.dma("sp", yb[:, :, :W], ybT.rearrange("(c p) t -> p c t", p=128)[:, :, t0:t0 + W], t_out=yb)
        rms_stats(k, ones_4, ya, 4, W, pstat, sqs, rstd)
        for c in range(4):
            k.op("dve", lambda g, c=c: g.scalar_tensor_tensor(
                out=ain[:, c, :W], in0=ya[:, c, :W], scalar=gn[:, c:c + 1], in1=rstd[:, :W],
                op0=ALU.mult, op1=ALU.mult), [ya, gn, rstd], [ain])
        k.op("pool", lambda g: g.tensor_copy(out=ybb[:, :, :W], in_=yb[:, :, :W]), [yb], [ybb])
        for m in range(4):
            p = next_p()
            for c in range(4):
                k.op("pe", lambda g, c=c, p=p, m=m: g.matmul(p[:, :W], lhsT=wglu[:, c, m * 128:(m + 1) * 128],
                                                         rhs=ybb[:, c, :W], start=(c == 0), stop=(c == 3)),
                     [wglu, ybb], [p])
            k.op("act", lambda g, p=p: g.activation(out=sig[:, :W], in_=p[:, :W], func=AF.Sigmoid), [p], [sig])
            k.op("dve", lambda g, m=m: g.tensor_tensor(out=y2[:, m, :W], in0=yb[:, m, :W], in1=sig[:, :W],
                                                   op=ALU.mult), [yb, sig], [y2])
        rms_stats(k, ones_4, y2, 4, W, pstat, sqs, rstd)
        for c in range(4):
            k.op("dve", lambda g, c=c: g.scalar_tensor_tensor(
                out=ain[:, 4 + c, :W], in0=y2[:, c, :W], scalar=gs[:, c:c + 1], in1=rstd[:, :W],
                op0=ALU.mult, op1=ALU.mult), [y2, gs, rstd], [ain])
        k.dma("pool", ain[:, 8:16, :W], ycT.rearrange("(c p) t -> p c t", p=128)[:, :, t0:t0 + W], t_out=ain)
        for mg in range(4):
            w = next_w()
            k.dma("pool", w[:], woutv[:, :, mg * 512:(mg + 1) * 512], t_out=w)
            for mm in range(4):
                m = mg * 4 + mm
                p = next_p()
                for c in range(C):
                    k.op("pe", lambda g, c=c, p=p, mm=mm, w=w: g.matmul(
                        p[:, :W], lhsT=w[:, c, mm * 128:(mm + 1) * 128], rhs=ain[:, c, :W],
                        start=(c == 0), stop=(c == C - 1)), [w, ain], [p])
                k.op("dve", lambda g, m=m, p=p: g.tensor_tensor(out=xt[:, m, :W], in0=xt[:, m, :W], in1=p[:, :W],
                                                            op=ALU.add), [xt, p], [xt])
        rms_stats(k, ones_d, xt, C, W, pstat, sqs, rstd)
        for c in range(C):
            k.op("dve", lambda g, c=c: g.scalar_tensor_tensor(
                out=ain[:, c, :W], in0=xt[:, c, :W], scalar=gf[:, c:c + 1], in1=rstd[:, :W],
                op0=ALU.mult, op1=ALU.mult), [xt, gf, rstd], [ain])
        ngrp = (NJ + 3) // 4
        en = 0
        for jg in range(ngrp):
            j0 = jg * 4
            nj = min(4, NJ - j0)
            wg = next_w()
            k.dma("pool", wg[:, :, :nj * 128], wfiv[:, :, j0 * 128:(j0 + nj) * 128], t_out=wg)
            wu = next_w()
            k.dma("pool", wu[:, :, :nj * 128], wfiv[:, :, D_FF + j0 * 128:D_FF + (j0 + nj) * 128], t_out=wu)
            for jj in range(nj):
                j = j0 + jj
                res = []
                for which, w in ((0, wg), (1, wu)):
                    idx = which * NJ + j
                    p = next_p()
                    for c in range(C):
                        k.op("pe", lambda g, c=c, p=p, jj=jj, w=w: g.matmul(
                            p[:, :W], lhsT=w[:, c, jj * 128:(jj + 1) * 128], rhs=ain[:, c, :W],
                            start=(c == 0), stop=(c == C - 1)), [w, ain], [p])
                    e = ext[en % len(ext)]
                    en += 1
                    k.op("pool", lambda g, e=e, idx=idx: g.tensor_copy(out=e[:, 0:2], in_=tails[:, idx, :]),
                         [tails], [e])
                    k.op("act", lambda g, e=e, p=p: g.copy(out=e[:, 2:2 + W], in_=p[:, :W]), [p], [e])
                    k.op("pool", lambda g, e=e, idx=idx: g.tensor_copy(out=tails[:, idx, :], in_=e[:, W:W + 2]),
                         [e], [tails])
                    if halo:
                        continue
                    dst = (gt if which == 0 else ut)[j % 2]
                    k.op("dve", lambda g, e=e, idx=idx, dst=dst: g.tensor_scalar(
                        out=dst[:, :W], in0=e[:, 2:2 + W], scalar1=cw[:, 2, idx:idx + 1], scalar2=cb[:, idx:idx + 1],
                        op0=ALU.mult, op1=ALU.add), [e, cw, cb], [dst])
                    k.op("dve", lambda g, e=e, idx=idx, dst=dst: g.scalar_tensor_tensor(
                        out=dst[:, :W], in0=e[:, 1:1 + W], scalar=cw[:, 1, idx:idx + 1], in1=dst[:, :W],
                        op0=ALU.mult, op1=ALU.add), [e, cw, dst], [dst])
                    k.op("dve", lambda g, e=e, idx=idx, dst=dst: g.scalar_tensor_tensor(
                        out=dst[:, :W], in0=e[:, 0:W], scalar=cw[:, 0, idx:idx + 1], in1=dst[:, :W],
                        op0=ALU.mult, op1=ALU.add), [e, cw, dst], [dst])
                    res.append(dst)
                if halo:
                    continue
                gtile, utile = res
                k.op("act", lambda g, gtile=gtile: g.activation(out=gtile[:, :W], in_=gtile[:, :W], func=AF.Silu),
                     [gtile], [gtile])
                k.op("pool", lambda g, gtile=gtile, utile=utile, j=j: g.tensor_tensor(
                    out=act[:, j, :W], in0=gtile[:, :W], in1=utile[:, :W], op=ALU.mult), [gtile, utile], [act])
        if halo:
            continue
        jgroups = [(0, 16), (16, 16), (32, NJ - 32)]
        for mg in range(4):
            ps4 = [next_p() for _ in range(4)]
            for gi, (ja, jn) in enumerate(jgroups):
                w = next_w()
                k.dma("pool", w[:, :jn, :], wfov[:, ja:ja + jn, mg * 512:(mg + 1) * 512], t_out=w)
                for mm in range(4):
                    p = ps4[mm]
                    for jj in range(jn):
                        j = ja + jj
                        k.op("pe", lambda g, p=p, jj=jj, j=j, mm=mm, w=w: g.matmul(
                            p[:, :W], lhsT=w[:, jj, mm * 128:(mm + 1) * 128], rhs=act[:, j, :W],
                            start=(j == 0), stop=(j == NJ - 1)), [w, act], [p])
            for mm in range(4):
                m = mg * 4 + mm
                o = ost[m % 2]
                k.op("dve", lambda g, m=m, o=o, p=ps4[mm]: g.tensor_tensor(
                    out=o[:, :W], in0=xt[:, m, :W], in1=p[:, :W], op=ALU.add), [xt, ps4[mm]], [o])
                k.dma("sp", out_d[m * 128:(m + 1) * 128, t0 - HALO:t0 - HALO + W], o[:, :W], t_in=o, final=True)
    k.finish()
    return nc


def lay128(v):
    v = np.asarray(v, np.float32)
    return np.ascontiguousarray(v.reshape(-1, 128).T)


def stage_C_inputs(inp, l, xT, yaT, ybT, ycT):
    NJ = D_FF // 128
    return {
        "xT": xT, "yaT": yaT, "ybT": ybT, "ycT": ycT,
        "g_nsa": lay128(inp["nsa_out_norm"][l]), "g_s5": lay128(inp["s5_out_norm"][l]),
        "g_ffn": lay128(inp["ffn_norm"][l]),
        "ffn_conv": np.ascontiguousarray(np.asarray(inp["ffn_conv"][l], np.float32).reshape(3, 2 * NJ, 128).transpose(2, 0, 1)),
        "ffn_conv_b": lay128(inp["ffn_conv_b"][l]),
        "w_glu": np.asarray(inp["s5_w_glu"][l], np.float32), "w_out": np.asarray(inp["w_out"][l], np.float32),
        "ffn_w_in": np.asarray(inp["ffn_w_in"][l], np.float32), "ffn_w_out": np.asarray(inp["ffn_w_out"][l], np.float32),
    }


GDN_C = 64
GDN_DEBUG = 3
GDN_PAR_STOP = 0
NEG = -1.0e6


class View:
    __slots__ = ("base", "ap")

    def __init__(self, base, ap):
        self.base = base
        self.ap = ap

    def __getitem__(self, idx):
        return self.ap[idx]


class Slots:
    def __init__(self, k, nbanks, width, name):
        banks = [k.ps([128, 512], F32, f"{name}{b}") for b in range(nbanks)]
        self.slots = []
        for j in range(512 // width):
            for bank in banks:
                self.slots.append(View(bank, bank.h[:, j * width:(j + 1) * width]))
        self.n = 0

    def get(self):
        s = self.slots[self.n % len(self.slots)]
        self.n += 1
        return s


def gdn_consts():
    c = {}
    c["ident"] = np.eye(128, dtype=np.float32)
    ii = np.arange(64)
    c["maskc"] = np.where(ii[:, None] >= ii[None, :], 0.0, NEG).astype(np.float32)
    c["maskcT"] = np.ascontiguousarray(c["maskc"].T)
    c["strict01"] = (ii[:, None] > ii[None, :]).astype(np.float32)
    cm = np.zeros((33, 4), np.float32)
    cm[0, 0] = 1.0; cm[32, 1] = 1.0; cm[32, 2] = -1.0; cm[0, 3] = 1.0
    c["cm"] = cm
    sel = np.zeros((33, 2), np.float32); sel[0, 0] = 1.0; sel[32, 1] = 1.0
    c["sel"] = sel
    bsel = np.zeros((33, 128), np.float32); bsel[0, :] = 1.0
    c["bsel"] = bsel
    rm = np.ones((33, 512), np.float32); rm[:, ::64] = 0.0
    c["resetmask"] = rm
    return c


def build_gdn(L=SEQ):
    nc = bass.Bass("TRN2", target_bir_lowering=False)
    ST = 512
    NST = L // ST
    CH = GDN_C
    NCH = ST // CH
    din = {}

    def inp(name, shape):
        din[name] = nc.dram_tensor(name, list(shape), F32, kind="ExternalInput").ap()
        return din[name]

    qT_d = inp("g_qT", [128, L]); kT_d = inp("g_kT", [128, L]); vT_d = inp("g_vT", [128, L])
    z_d = inp("g_z", [L, 128]); a_d = inp("g_a33", [33, L]); b_d = inp("g_b33", [33, L])
    cw_d = inp("g_cw", [128, 3, 4]); prm_d = inp("g_prm33", [33, 2]); gain_d = inp("g_gain", [64, 128])
    cd = {n: inp("gc_" + n, v.shape) for n, v in gdn_consts().items()}
    out_d = nc.dram_tensor("g_out", [L, 128], F32, kind="ExternalOutput").ap()
    k = KB(nc)
    emit_gdn(k, L, qT_d, kT_d, vT_d, z_d, a_d, b_d, cw_d, prm_d, gain_d, cd, out_d)
    k.finish()
    return nc


def emit_gdn(k, L, qT_d, kT_d, vT_d, z_d, a_d, b_d, cw_d, prm_d, gain_d, cd, out_d):
    ST = 512
    NST = L // ST
    CH = GDN_C
    NCH = ST // CH

    def const(name, shape):
        t = k.sb(shape, F32, "c_" + name)
        k.dma("sp", t[:], cd[name], t_out=t)
        return t

    ident = const("ident", [128, 128]); maskc = const("maskc", [64, 64]); maskcT = const("maskcT", [64, 64])
    strict01 = const("strict01", [64, 64]); cm = const("cm", [33, 4]); sel = const("sel", [33, 2])
    bsel = const("bsel", [33, 128]); rmask = const("resetmask", [33, 512])
    cw = k.sb([128, 3, 4], F32, "cw"); k.dma("sp", cw[:], cw_d, t_out=cw)
    prm = k.sb([33, 2], F32, "prm"); k.dma("sp", prm[:], prm_d, t_out=prm)
    gain = k.sb([64, 128], F32, "gain"); k.dma("sp", gain[:], gain_d, t_out=gain)
    ones = k.sb([128, 128], F32, "ones")
    k.op("pool", lambda g: g.memset(ones[:], 1.0), [], [ones])
    nA = k.sb([33, 1], F32, "nA")
    k.op("act", lambda g: g.activation(out=nA[:], in_=prm[:, 0:1], func=AF.Exp), [prm], [nA])
    k.op("dve", lambda g: g.tensor_scalar(out=nA[:], in0=nA[:], scalar1=-1.0, scalar2=None, op0=ALU.mult), [nA], [nA])
    S = k.sb([128, 128], F32, "S")
    k.op("pool", lambda g: g.memset(S[:], 0.0), [], [S])

    ext = [[k.sb([128, ST + 3], F32, f"ext{w}{i}") for i in range(2)] for w in range(3)]
    for w in range(3):
        k.op("pool", lambda g, w=w: g.memset(ext[w][1][:, ST:ST + 3], 0.0), [], [ext[w][1]])
    cv = [[k.sb([128, ST], F32, f"cv{w}{i}") for i in range(2)] for w in range(3)]
    qn = [k.sb([128, ST], F32, f"qn{i}") for i in range(2)]
    kn = [k.sb([128, ST], F32, f"kn{i}") for i in range(2)]
    sq = k.sb([128, ST], F32, "sq")
    rs = k.sb([128, ST], F32, "rs")
    a33 = [k.sb([33, ST], F32, f"a33{i}") for i in range(2)]
    b33 = [k.sb([33, ST], F32, f"b33{i}") for i in range(2)]
    gc33 = k.sb([33, ST], F32, "gc33")
    U33 = [k.sb([33, ST], F32, f"U33{i}") for i in range(2)]
    V33 = [k.sb([33, ST], F32, f"V33{i}") for i in range(2)]
    RC = [k.sb([33, ST], F32, f"RC{i}") for i in range(2)]
    GB = [k.sb([128, ST], F32, f"GB{i}") for i in range(2)]
    EG = [k.sb([128, ST], F32, f"EG{i}") for i in range(2)]
    zt = [k.sb([64, NCH, 128], F32, f"zt{i}") for i in range(2)]
    yo = [k.sb([64, NCH, 128], F32, f"yo{i}") for i in range(2)]
    pbig = k.ps([128, 512], F32, "pbig")
    pstat = k.ps([128, 512], F32, "pstat")
    small = Slots(k, 2, 64, "ps")
    wide = Slots(k, 4, 128, "pw")
    NB = 3
    mk = lambda shape, nm: [k.sb(shape, F32, f"{nm}{i}") for i in range(NB)]
    Dm = mk([64, 64], "Dm"); DTm = mk([64, 64], "DTm"); Ds = mk([64, 64], "Ds")
    Xa = mk([64, 64], "Xa"); Xb = mk([64, 64], "Xb"); Ya = mk([64, 64], "Ya"); Yb = mk([64, 64], "Yb")
    Pm = mk([64, 64], "Pm"); tmp64 = mk([64, 64], "tmp64"); tmp64b = mk([64, 64], "tmp64b")
    cols = mk([64, 8], "cols")
    kb = mk([64, 128], "kb"); kd = mk([64, 128], "kd"); vb = mk([64, 128], "vb")
    val = mk([64, 128], "val"); kcumT = mk([128, 64], "kcumT"); qkT = mk([64, 64], "qkT")
    qgT = mk([128, 64], "qgT"); vnew = mk([64, 128], "vnew"); junk = mk([64, 128], "junk")
    ycur = mk([64, 128], "ycur")

    qv = [qT_d, kT_d, vT_d]
    zv = z_d.rearrange("(n c) d -> c n d", c=CH)
    ov = out_d.rearrange("(n c) d -> c n d", c=CH)
    pending = [None]
    gi = [0]

    def supertile(s):
        b = s % 2
        t0 = s * ST
        for w in range(3):
            e = ext[w][b]
            k.dma("sp", e[:, 3:3 + ST], qv[w][:, t0:t0 + ST], t_out=e)
        k.dma("sp", a33[b][:], a_d[:, t0:t0 + ST], t_out=a33[b])
        k.dma("sp", b33[b][:], b_d[:, t0:t0 + ST], t_out=b33[b])
        k.dma("sp", zt[b][:], zv[:, s * NCH:(s + 1) * NCH, :], t_out=zt[b])
        for w in range(3):
            e = ext[w][b]; eo = ext[w][1 - b]; o = cv[w][b]
            k.op("pool", lambda g, e=e, eo=eo: g.tensor_copy(out=e[:, 0:3], in_=eo[:, ST:ST + 3]), [eo], [e])
            k.op("dve", lambda g, e=e, o=o, w=w: g.tensor_scalar(out=o[:], in0=e[:, 3:3 + ST], scalar1=cw[:, w, 3:4],
                                                              scalar2=None, op0=ALU.mult), [e, cw], [o])
            for j in range(3):
                k.op("dve", lambda g, e=e, o=o, w=w, j=j: g.scalar_tensor_tensor(
                    out=o[:], in0=e[:, j:j + ST], scalar=cw[:, w, j:j + 1], in1=o[:], op0=ALU.mult, op1=ALU.add),
                    [e, cw, o], [o])
            k.op("act", lambda g, o=o: g.activation(out=o[:], in_=o[:], func=AF.Silu), [o], [o])
        for w, dst, scl in ((0, qn[b], 128.0 ** -0.5), (1, kn[b], 1.0)):
            src = cv[w][b]
            k.op("act", lambda g, src=src: g.activation(out=sq[:], in_=src[:], func=AF.Square), [src], [sq])
            k.op("pe", lambda g: g.matmul(pstat[:], lhsT=ones[:], rhs=sq[:], start=True, stop=True), [ones, sq], [pstat])
            k.op("dve", lambda g: g.tensor_scalar(out=rs[:], in0=pstat[:], scalar1=EPS, scalar2=None, op0=ALU.add),
                 [pstat], [rs])
            k.op("act", lambda g: g.activation(out=rs[:], in_=rs[:], func=AF.Sqrt), [rs], [rs])
            k.op("dve", lambda g: g.reciprocal(out=rs[:], in_=rs[:]), [rs], [rs])
            k.op("dve", lambda g, src=src, dst=dst, scl=scl: g.scalar_tensor_tensor(
                out=dst[:], in0=src[:], scalar=scl, in1=rs[:], op0=ALU.mult, op1=ALU.mult), [src, rs], [dst])
        A = a33[b]; B = b33[b]
        k.op("act", lambda g: g.activation(out=A[:], in_=A[:], func=AF.Exp, bias=prm[:, 1:2], scale=1.0), [A, prm], [A])
        k.op("dve", lambda g: g.tensor_scalar(out=A[:], in0=A[:], scalar1=1.0, scalar2=None, op0=ALU.add), [A], [A])
        k.op("act", lambda g: g.activation(out=A[:], in_=A[:], func=AF.Ln), [A], [A])
        k.op("dve", lambda g: g.tensor_scalar(out=A[:], in0=A[:], scalar1=nA[:, 0:1], scalar2=None, op0=ALU.mult),
             [A, nA], [A])
        k.op("dve", lambda g: g.tensor_tensor_scan(out=gc33[:], data0=rmask[:], data1=A[:], initial=0.0,
                                                   op0=ALU.mult, op1=ALU.add), [rmask, A], [gc33])
        k.op("act", lambda g: g.activation(out=B[:], in_=B[:], func=AF.Sigmoid), [B], [B])
        k.op("dve", lambda g: g.tensor_scalar(out=U33[b][:], in0=gc33[:], scalar1=cm[:, 0:1], scalar2=cm[:, 1:2],
                                              op0=ALU.mult, op1=ALU.add), [gc33, cm], [U33[b]])
        k.op("dve", lambda g: g.tensor_scalar(out=V33[b][:], in0=gc33[:], scalar1=cm[:, 2:3], scalar2=cm[:, 3:4],
                                              op0=ALU.mult, op1=ALU.add), [gc33, cm], [V33[b]])
        k.op("dve", lambda g: g.tensor_scalar(out=RC[b][:], in0=gc33[:], scalar1=cm[:, 0:1], scalar2=None,
                                              op0=ALU.mult), [gc33, cm], [RC[b]])
        k.op("dve", lambda g: g.scalar_tensor_tensor(out=RC[b][:], in0=B[:], scalar=cm[:, 1:2], in1=RC[b][:],
                                                     op0=ALU.mult, op1=ALU.add), [B, cm, RC[b]], [RC[b]])
        k.op("pe", lambda g: g.matmul(pbig[:], lhsT=bsel[:], rhs=gc33[:], start=True, stop=True), [bsel, gc33], [pbig])
        k.op("dve", lambda g: g.tensor_copy(out=GB[b][:], in_=pbig[:]), [pbig], [GB[b]])
        k.op("act", lambda g: g.activation(out=EG[b][:], in_=pbig[:], func=AF.Exp), [pbig], [EG[b]])
        k.op("act", lambda g: g.activation(out=zt[b][:], in_=zt[b][:], func=AF.Silu), [zt[b]], [zt[b]])

    def par(s, ci):
        b = s % 2
        i = gi[0] % NB
        gi[0] += 1
        c0 = ci * CH
        cs = slice(c0, c0 + CH)
        kT = kn[b]; qT = qn[b]; vT = cv[2][b]
        U = U33[b]; V = V33[b]
        pc = small.get()
        k.op("pe", lambda g: g.matmul(pc[0:64, 0:2], lhsT=RC[b][:, cs], rhs=sel[:], start=True, stop=True),
             [RC[b], sel], [pc])
        cl = cols[i]
        k.op("dve", lambda g: g.tensor_copy(out=cl[:, 0:2], in_=pc[0:64, 0:2]), [pc], [cl])
        k.op("act", lambda g: g.activation(out=cl[:, 2:3], in_=cl[:, 0:1], func=AF.Exp), [cl], [cl])
        k.op("dve", lambda g: g.tensor_tensor(out=cl[:, 3:4], in0=cl[:, 2:3], in1=cl[:, 1:2], op=ALU.mult), [cl], [cl])
        k.op("dve", lambda g: g.tensor_scalar(out=cl[:, 4:5], in0=cl[:, 1:2], scalar1=-1.0, scalar2=None, op0=ALU.mult),
             [cl], [cl])
        last = c0 + CH - 1
        k.op("act", lambda g: g.activation(out=cl[:, 5:6], in_=cl[:, 0:1], func=AF.Exp, scale=-1.0,
                                           bias=GB[b][0:64, last:last + 1]), [cl, GB[b]], [cl])
        if GDN_PAR_STOP == 1:
            return None
        pd = small.get(); pdT = small.get()
        k.op("pe", lambda g: g.matmul(pd[0:64, :], lhsT=U[:, cs], rhs=V[:, cs], start=True, stop=True), [U, V], [pd])
        k.op("pe", lambda g: g.matmul(pdT[0:64, :], lhsT=V[:, cs], rhs=U[:, cs], start=True, stop=True), [U, V], [pdT])
        k.op("dve", lambda g: g.tensor_tensor(out=Dm[i][:], in0=pd[0:64, :], in1=maskc[:], op=ALU.add), [pd, maskc], [Dm[i]])
        k.op("act", lambda g: g.activation(out=Dm[i][:], in_=Dm[i][:], func=AF.Exp), [Dm[i]], [Dm[i]])
        k.op("dve", lambda g: g.tensor_tensor(out=DTm[i][:], in0=pdT[0:64, :], in1=maskcT[:], op=ALU.add),
             [pdT, maskcT], [DTm[i]])
        k.op("act", lambda g: g.activation(out=DTm[i][:], in_=DTm[i][:], func=AF.Exp), [DTm[i]], [DTm[i]])
        k.op("pool", lambda g: g.tensor_tensor(out=Ds[i][:], in0=Dm[i][:], in1=strict01[:], op=ALU.mult),
             [Dm[i], strict01], [Ds[i]])
        if GDN_PAR_STOP == 2:
            return None
        pkk = small.get()
        k.op("pe", lambda g: g.matmul(pkk[0:64, :], lhsT=kT[:, cs], rhs=kT[:, cs], start=True, stop=True), [kT], [pkk])
        Y = Ya[i]; X = Xa[i]; Y2 = Yb[i]; X2 = Xb[i]
        k.op("dve", lambda g: g.scalar_tensor_tensor(out=Y[:], in0=pkk[0:64, :], scalar=cl[:, 4:5], in1=Ds[i][:],
                                                     op0=ALU.mult, op1=ALU.mult), [pkk, cl, Ds[i]], [Y])
        if GDN_PAR_STOP == 3:
            return None
        px = small.get()
        k.op("pe", lambda g: g.matmul(px[0:64, :], lhsT=Y[:], rhs=ident[0:64, 0:64], start=True, stop=True), [Y, ident], [px])
        P = Pm[i]
        if GDN_PAR_STOP == 41:
            return None
        k.op("act", lambda g: g.copy(out=X[:], in_=px[0:64, :]), [px], [X])
        if GDN_PAR_STOP == 42:
            return None
        k.op("dve", lambda g: g.tensor_tensor(out=P[:], in0=X[:], in1=ident[0:64, 0:64], op=ALU.add),
             [X, ident], [P])
        if GDN_PAR_STOP == 4:
            return None
        for lvl in range(1, 6):
            lastl = (lvl == 5)
            py = small.get()
            k.op("pe", lambda g, X=X, Y=Y, py=py: g.matmul(py[0:64, :], lhsT=X[:], rhs=Y[:], start=True, stop=True),
                 [X, Y], [py])
            if not lastl:
                pxx = small.get()
                k.op("pe", lambda g, X=X, Y=Y, pxx=pxx: g.matmul(pxx[0:64, :], lhsT=Y[:], rhs=X[:], start=True, stop=True),
                     [X, Y], [pxx])
            k.op("act", lambda g, Y2=Y2, py=py: g.copy(out=Y2[:], in_=py[0:64, :]), [py], [Y2])
            if not lastl:
                k.op("dve", lambda g, X2=X2, pxx=pxx: g.tensor_copy(out=X2[:], in_=pxx[0:64, :]), [pxx], [X2])
            pp = small.get()
            k.op("pe", lambda g, Y2=Y2, P=P, pp=pp: g.matmul(pp[0:64, :], lhsT=Y2[:], rhs=P[:], start=True, stop=True),
                 [Y2, P], [pp])
            k.op("dve", lambda g, P=P, pp=pp: g.tensor_tensor(out=P[:], in0=P[:], in1=pp[0:64, :], op=ALU.add),
                 [P, pp], [P])
            X, X2 = X2, X
            Y, Y2 = Y2, Y
        if GDN_PAR_STOP == 5:
            return None
        pk = wide.get()
        k.op("pe", lambda g: g.matmul(pk[0:64, :], lhsT=kT[:, cs], rhs=ident[:], start=True, stop=True), [kT, ident], [pk])
        k.op("dve", lambda g: g.tensor_scalar(out=kb[i][:], in0=pk[0:64, :], scalar1=cl[:, 3:4], scalar2=None,
                                              op0=ALU.mult), [pk, cl], [kb[i]])
        k.op("dve", lambda g: g.tensor_scalar(out=kd[i][:], in0=pk[0:64, :], scalar1=cl[:, 5:6], scalar2=None,
                                              op0=ALU.mult), [pk, cl], [kd[i]])
        pvv = wide.get()
        k.op("pe", lambda g: g.matmul(pvv[0:64, :], lhsT=vT[:, cs], rhs=ident[:], start=True, stop=True), [vT, ident], [pvv])
        k.op("dve", lambda g: g.tensor_scalar(out=vb[i][:], in0=pvv[0:64, :], scalar1=cl[:, 1:2], scalar2=None,
                                              op0=ALU.mult), [pvv, cl], [vb[i]])
        if GDN_PAR_STOP == 6:
            return None
        pval = wide.get()
        k.op("pe", lambda g: g.matmul(pval[0:64, :], lhsT=P[:], rhs=vb[i][:], start=True, stop=True), [P, vb[i]], [pval])
        k.op("act", lambda g: g.copy(out=val[i][:], in_=pval[0:64, :]), [pval], [val[i]])
        pkc = small.get()
        k.op("pe", lambda g: g.matmul(pkc[:, :], lhsT=kb[i][:], rhs=P[:], start=True, stop=True), [kb[i], P], [pkc])
        k.op("act", lambda g: g.copy(out=kcumT[i][:], in_=pkc[:, :]), [pkc], [kcumT[i]])
        pqk = small.get()
        k.op("pe", lambda g: g.matmul(pqk[0:64, :], lhsT=kT[:, cs], rhs=qT[:, cs], start=True, stop=True), [kT, qT], [pqk])
        k.op("dve", lambda g: g.tensor_tensor(out=qkT[i][:], in0=pqk[0:64, :], in1=DTm[i][:], op=ALU.mult),
             [pqk, DTm[i]], [qkT[i]])
        k.op("pool", lambda g: g.tensor_tensor(out=qgT[i][:], in0=qT[:, cs], in1=EG[b][:, cs], op=ALU.mult),
             [qT, EG[b]], [qgT[i]])
        return dict(i=i, b=b, ci=ci, s=s, last=last)

    def seq(d):
        i = d["i"]; b = d["b"]; ci = d["ci"]; last = d["last"]
        pv = wide.get()
        k.op("pe", lambda g: g.matmul(pv[0:64, :], lhsT=kcumT[i][:], rhs=S[:], start=True, stop=True), [kcumT[i], S], [pv])
        k.op("dve", lambda g: g.tensor_tensor(out=vnew[i][:], in0=val[i][:], in1=pv[0:64, :], op=ALU.subtract),
             [val[i], pv], [vnew[i]])
        po = wide.get()
        k.op("pe", lambda g: g.matmul(po[0:64, :], lhsT=qgT[i][:], rhs=S[:], start=True, stop=False), [qgT[i], S], [po])
        k.op("pe", lambda g: g.matmul(po[0:64, :], lhsT=qkT[i][:], rhs=vnew[i][:], start=False, stop=True),
             [qkT[i], vnew[i]], [po])
        pS = wide.get()
        k.op("pe", lambda g: g.matmul(pS[:, :], lhsT=kd[i][:], rhs=vnew[i][:], start=True, stop=True), [kd[i], vnew[i]], [pS])
        k.op("dve", lambda g: g.scalar_tensor_tensor(out=S[:], in0=S[:], scalar=EG[b][:, last:last + 1], in1=pS[:, :],
                                                     op0=ALU.mult, op1=ALU.add), [S, EG[b], pS], [S])
        cl = cols[i]
        k.op("act", lambda g: g.activation(out=junk[i][:], in_=po[0:64, :], func=AF.Square, accum_out=cl[:, 6:7]),
             [po], [junk[i], cl])
        k.op("dve", lambda g: g.tensor_scalar(out=cl[:, 6:7], in0=cl[:, 6:7], scalar1=1.0 / 128.0, scalar2=EPS,
                                              op0=ALU.mult, op1=ALU.add), [cl], [cl])
        k.op("act", lambda g: g.activation(out=cl[:, 6:7], in_=cl[:, 6:7], func=AF.Sqrt), [cl], [cl])
        k.op("dve", lambda g: g.reciprocal(out=cl[:, 6:7], in_=cl[:, 6:7]), [cl], [cl])
        k.op("dve", lambda g: g.scalar_tensor_tensor(out=ycur[i][:], in0=po[0:64, :], scalar=cl[:, 6:7], in1=gain[:],
                                                     op0=ALU.mult, op1=ALU.mult), [po, cl, gain], [ycur[i]])
        k.op("pool", lambda g: g.tensor_tensor(out=yo[b][:, ci, :], in0=ycur[i][:], in1=zt[b][:, ci, :], op=ALU.mult),
             [ycur[i], zt[b]], [yo[b]])
        if ci == NCH - 1:
            s = d["s"]
            k.dma("sp", ov[:, s * NCH:(s + 1) * NCH, :], yo[b][:], t_in=yo[b], final=True)

    for s in range(NST):
        supertile(s)
        if GDN_DEBUG < 2:
            continue
        for ci in range(NCH):
            d = par(s, ci)
            if GDN_DEBUG < 3:
                continue
            if pending[0] is not None:
                seq(pending[0])
            pending[0] = d
    if GDN_DEBUG >= 3:
        seq(pending[0])


def gdn_inputs(projT, inp, l, h, L=SEQ):
    r = lambda a: np.ascontiguousarray(a, dtype=np.float32)
    q0, k0, v0, z0 = 1816, 2840, 3864, 4888
    a = projT[5912 + h, :L]
    b = projT[5920 + h, :L]
    a33 = np.zeros((33, L), np.float32); a33[0] = a; a33[32] = a
    b33 = np.zeros((33, L), np.float32); b33[0] = b; b33[32] = b
    conv = np.asarray(inp["gdn_conv"][l], np.float32)
    cw = np.stack([conv[:, w * 1024 + h * 128: w * 1024 + (h + 1) * 128].T for w in range(3)], axis=1)
    prm = np.zeros((33, 2), np.float32)
    prm[:, 0] = inp["gdn_a_log"][l][h]; prm[:, 1] = inp["gdn_dt_bias"][l][h]
    d = {
        "g_qT": r(projT[q0 + h * 128:q0 + (h + 1) * 128, :L]), "g_kT": r(projT[k0 + h * 128:k0 + (h + 1) * 128, :L]),
        "g_vT": r(projT[v0 + h * 128:v0 + (h + 1) * 128, :L]), "g_z": r(projT[z0 + h * 128:z0 + (h + 1) * 128, :L].T),
        "g_a33": a33, "g_b33": b33, "g_cw": r(cw), "g_prm33": prm,
        "g_gain": r(np.tile(np.asarray(inp["gdn_norm"][l], np.float32)[None, :], (64, 1))),
    }
    for n, v in gdn_consts().items():
        d["gc_" + n] = v
    return d


import math
I32 = mybir.dt.int32
S5_W = 512


def build_s5(L=SEQ):
    nc = bass.Bass("TRN2", target_bir_lowering=False)
    W = S5_W
    d = {}

    def inp(name, shape):
        d[name] = nc.dram_tensor(name, list(shape), F32, kind="ExternalInput").ap()
        return d[name]

    inp("s_uT", [64, L]); inp("s_lamre", [128, 2]); inp("s_lamim", [128, 2]); inp("s_logdt", [128, 2])
    inp("s_bre", [2, 64, 128]); inp("s_bim", [2, 64, 128]); inp("s_cre", [2, 128, 64]); inp("s_cim", [2, 128, 64])
    inp("s_d", [64, 1]); inp("s_trow", [128, W + 1])
    out_d = nc.dram_tensor("s_out", [64, L], F32, kind="ExternalOutput").ap()
    k = KB(nc)
    emit_s5(k, L, d, out_d)
    k.finish()
    return nc


def emit_s5(k, L, d, out_d):
    W = S5_W
    NST = L // W
    TWO_PI = 2.0 * math.pi

    def load(name, shape):
        t = k.sb(shape, F32, name)
        k.dma("sp", t[:], d[name], t_out=t)
        return t

    lamre = load("s_lamre", [128, 2]); lamim = load("s_lamim", [128, 2]); logdt = load("s_logdt", [128, 2])
    trow = load("s_trow", [128, W + 1]); dsk = load("s_d", [64, 1])
    Bre = []; Bim = []; Cre = []; Cim = []
    for j in range(2):
        for lst, nm, shp in ((Bre, "s_bre", [64, 128]), (Bim, "s_bim", [64, 128]), (Cre, "s_cre", [128, 64]),
                             (Cim, "s_cim", [128, 64])):
            t = k.sb(shp, F32, f"{nm}{j}")
            k.dma("sp", t[:], d[nm][j], t_out=t)
            lst.append(t)
    ki = k.sb([128, W + 1], I32, "ki")
    kf = k.sb([128, W + 1], F32, "kf")
    ang = k.sb([128, W + 1], F32, "ang")
    sc = k.sb([128, 16], F32, "sc")
    Ct = []; St = []; nSt = []; Trt = []; Tit = []; rr = []; eW = []

    def reduce_inplace(t, n):
        k.op("dve", lambda g: g.tensor_scalar(out=ki[:, :n], in0=t[:, :n], scalar1=1.0 / TWO_PI, scalar2=None,
                                              op0=ALU.mult), [t], [ki])
        k.op("dve", lambda g: g.tensor_copy(out=kf[:, :n], in_=ki[:, :n]), [ki], [kf])
        k.op("dve", lambda g: g.scalar_tensor_tensor(out=t[:, :n], in0=kf[:, :n], scalar=-TWO_PI, in1=t[:, :n],
                                                     op0=ALU.mult, op1=ALU.add), [kf, t], [t])

    for j in range(2):
        C = k.sb([128, W + 1], F32, f"C{j}"); S = k.sb([128, W + 1], F32, f"S{j}"); nS = k.sb([128, W + 1], F32, f"nS{j}")
        Tr = k.sb([128, W], F32, f"Tr{j}"); Ti = k.sb([128, W], F32, f"Ti{j}")
        pr = k.sb([128, 16], F32, f"pr{j}")
        k.op("act", lambda g: g.activation(out=pr[:, 0:1], in_=logdt[:, j:j + 1], func=AF.Exp), [logdt], [pr])
        k.op("dve", lambda g: g.tensor_tensor(out=pr[:, 1:2], in0=lamim[:, j:j + 1], in1=pr[:, 0:1], op=ALU.mult),
             [lamim, pr], [pr])
        k.op("dve", lambda g: g.tensor_tensor(out=pr[:, 2:3], in0=lamre[:, j:j + 1], in1=pr[:, 0:1], op=ALU.mult),
             [lamre, pr], [pr])
        k.op("act", lambda g: g.activation(out=pr[:, 2:3], in_=pr[:, 2:3], func=AF.Exp), [pr], [pr])
        k.op("dve", lambda g: g.tensor_copy(out=ang[:, 0:1], in_=pr[:, 1:2]), [pr], [ang])
        reduce_inplace(ang, 1)
        k.op("dve", lambda g: g.tensor_copy(out=pr[:, 3:4], in_=ang[:, 0:1]), [ang], [pr])
        k.op("dve", lambda g: g.tensor_scalar(out=ang[:], in0=trow[:], scalar1=pr[:, 3:4], scalar2=None, op0=ALU.mult),
             [trow, pr], [ang])
        reduce_inplace(ang, W + 1)
        k.op("act", lambda g: g.activation(out=S[:], in_=ang[:], func=AF.Sin), [ang], [S])
        k.op("dve", lambda g: g.tensor_scalar(out=ang[:], in0=ang[:], scalar1=0.5 * math.pi, scalar2=None, op0=ALU.add),
             [ang], [ang])
        reduce_inplace(ang, W + 1)
        k.op("act", lambda g: g.activation(out=C[:], in_=ang[:], func=AF.Sin), [ang], [C])
        k.op("pool", lambda g: g.tensor_scalar(out=nS[:], in0=S[:], scalar1=-1.0, scalar2=None, op0=ALU.mult), [S], [nS])
        k.op("dve", lambda g: g.tensor_tensor(out=pr[:, 4:5], in0=pr[:, 2:3], in1=C[:, 1:2], op=ALU.mult), [pr, C], [pr])
        k.op("dve", lambda g: g.tensor_scalar(out=pr[:, 4:5], in0=pr[:, 4:5], scalar1=-1.0, scalar2=None, op0=ALU.add),
             [pr], [pr])
        k.op("dve", lambda g: g.tensor_tensor(out=pr[:, 5:6], in0=pr[:, 2:3], in1=S[:, 1:2], op=ALU.mult), [pr, S], [pr])
        k.op("dve", lambda g: g.tensor_tensor(out=pr[:, 6:7], in0=lamre[:, j:j + 1], in1=lamre[:, j:j + 1], op=ALU.mult),
             [lamre], [pr])
        k.op("dve", lambda g: g.scalar_tensor_tensor(out=pr[:, 6:7], in0=lamim[:, j:j + 1], scalar=lamim[:, j:j + 1],
                                                     in1=pr[:, 6:7], op0=ALU.mult, op1=ALU.add), [lamim, pr], [pr])
        k.op("dve", lambda g: g.reciprocal(out=pr[:, 6:7], in_=pr[:, 6:7]), [pr], [pr])
        k.op("dve", lambda g: g.tensor_tensor(out=pr[:, 7:8], in0=pr[:, 4:5], in1=lamre[:, j:j + 1], op=ALU.mult),
             [pr, lamre], [pr])
        k.op("dve", lambda g: g.scalar_tensor_tensor(out=pr[:, 7:8], in0=pr[:, 5:6], scalar=lamim[:, j:j + 1],
                                                     in1=pr[:, 7:8], op0=ALU.mult, op1=ALU.add), [pr, lamim], [pr])
        k.op("dve", lambda g: g.tensor_tensor(out=pr[:, 7:8], in0=pr[:, 7:8], in1=pr[:, 6:7], op=ALU.mult), [pr], [pr])
        k.op("dve", lambda g: g.tensor_tensor(out=pr[:, 9:10], in0=pr[:, 4:5], in1=lamim[:, j:j + 1], op=ALU.mult),
             [pr, lamim], [pr])
        k.op("dve", lambda g: g.scalar_tensor_tensor(out=pr[:, 8:9], in0=pr[:, 5:6], scalar=lamre[:, j:j + 1],
                                                     in1=pr[:, 9:10], op0=ALU.mult, op1=ALU.subtract), [pr, lamre], [pr])
        k.op("dve", lambda g: g.tensor_tensor(out=pr[:, 8:9], in0=pr[:, 8:9], in1=pr[:, 6:7], op=ALU.mult), [pr], [pr])
        k.op("dve", lambda g: g.tensor_scalar(out=Tr[:], in0=C[:, 0:W], scalar1=pr[:, 7:8], scalar2=None, op0=ALU.mult),
             [C, pr], [Tr])
        k.op("dve", lambda g: g.scalar_tensor_tensor(out=Tr[:], in0=S[:, 0:W], scalar=pr[:, 8:9], in1=Tr[:],
                                                     op0=ALU.mult, op1=ALU.add), [S, pr, Tr], [Tr])
        k.op("dve", lambda g: g.tensor_scalar(out=Ti[:], in0=C[:, 0:W], scalar1=pr[:, 8:9], scalar2=None, op0=ALU.mult),
             [C, pr], [Ti])
        k.op("dve", lambda g: g.scalar_tensor_tensor(out=Ti[:], in0=nS[:, 0:W], scalar=pr[:, 7:8], in1=Ti[:],
                                                     op0=ALU.mult, op1=ALU.add), [nS, pr, Ti], [Ti])
        Ct.append(C); St.append(S); nSt.append(nS); Trt.append(Tr); Tit.append(Ti); rr.append(pr)

    ub = [k.sb([64, W], F32, f"ub{i}") for i in range(2)]
    pP = [k.ps([128, W], F32, f"pP{i}") for i in range(4)]
    pY = [k.ps([128, W], F32, f"pY{i}") for i in range(2)]
    mkt = lambda nm, n=2: [k.sb([128, W], F32, f"{nm}{i}") for i in range(n)]
    prs = mkt("prs"); pis = mkt("pis"); m1 = mkt("m1"); m2 = mkt("m2"); m3 = mkt("m3"); m4 = mkt("m4")
    cr = mkt("cr"); ci = mkt("ci")
    vr = [mkt("vr0"), mkt("vr1")]; vi = [mkt("vi0"), mkt("vi1")]
    xr = mkt("xr"); xi = mkt("xi")
    init = [k.sb([128, 4], F32, f"init{j}") for j in range(2)]
    yb = [k.sb([64, W], F32, f"yb{i}") for i in range(2)]
    g1 = k.sb([64, W], F32, "g1"); g2 = k.sb([64, W], F32, "g2")
    pn = [0]
    for s in range(NST):
        u = ub[s % 2]
        k.dma("sp", u[:], d["s_uT"][:, s * W:(s + 1) * W], t_out=u)
        py = pY[s % 2]
        for j in range(2):
            b = (2 * s + j) % 2
            Pr = pP[pn[0] % 4]; Pi = pP[(pn[0] + 1) % 4]; pn[0] += 2
            k.op("pe", lambda g, Pr=Pr: g.matmul(Pr[:], lhsT=Bre[j][:], rhs=u[:], start=True, stop=True), [Bre[j], u], [Pr])
            k.op("pe", lambda g, Pi=Pi: g.matmul(Pi[:], lhsT=Bim[j][:], rhs=u[:], start=True, stop=True), [Bim[j], u], [Pi])
            k.op("act", lambda g, Pr=Pr: g.copy(out=prs[b][:], in_=Pr[:]), [Pr], [prs[b]])
            k.op("act", lambda g, Pi=Pi: g.copy(out=pis[b][:], in_=Pi[:]), [Pi], [pis[b]])
            Tr = Trt[j]; Ti = Tit[j]; C = Ct[j]; nS = nSt[j]; pr = rr[j]
            k.op("pool", lambda g: g.tensor_tensor(out=m1[b][:], in0=Tr[:], in1=prs[b][:], op=ALU.mult), [Tr, prs[b]], [m1[b]])
            k.op("pool", lambda g: g.tensor_tensor(out=m2[b][:], in0=Ti[:], in1=pis[b][:], op=ALU.mult), [Ti, pis[b]], [m2[b]])
            k.op("pool", lambda g: g.tensor_tensor(out=cr[b][:], in0=m1[b][:], in1=m2[b][:], op=ALU.subtract),
                 [m1[b], m2[b]], [cr[b]])
            k.op("dve", lambda g: g.tensor_tensor(out=m3[b][:], in0=Tr[:], in1=pis[b][:], op=ALU.mult), [Tr, pis[b]], [m3[b]])
            k.op("dve", lambda g: g.tensor_tensor(out=m4[b][:], in0=Ti[:], in1=prs[b][:], op=ALU.mult), [Ti, prs[b]], [m4[b]])
            k.op("dve", lambda g: g.tensor_tensor(out=ci[b][:], in0=m3[b][:], in1=m4[b][:], op=ALU.add),
                 [m3[b], m4[b]], [ci[b]])
            VR = vr[j][s % 2]; VI = vi[j][s % 2]
            rb = pr[:, 2:3].to_broadcast([128, W])
            if s == 0:
                ir = 0.0; ii_ = 0.0; extra = []
            else:
                ir = init[j][:, 0:1]; ii_ = init[j][:, 1:2]; extra = [init[j]]
            k.op("dve", lambda g, ir=ir: g.tensor_tensor_scan(out=VR[:], data0=rb, data1=cr[b][:], initial=ir,
                                                           op0=ALU.mult, op1=ALU.add), [pr, cr[b]] + extra, [VR])
            k.op("dve", lambda g, ii_=ii_: g.tensor_tensor_scan(out=VI[:], data0=rb, data1=ci[b][:], initial=ii_,
                                                             op0=ALU.mult, op1=ALU.add), [pr, ci[b]] + extra, [VI])
            it = init[j]
            k.op("dve", lambda g: g.tensor_tensor(out=it[:, 2:3], in0=VI[:, W - 1:W], in1=St[j][:, W:W + 1], op=ALU.mult),
                 [VI, St[j]], [it])
            k.op("dve", lambda g: g.scalar_tensor_tensor(out=it[:, 0:1], in0=VR[:, W - 1:W], scalar=C[:, W:W + 1],
                                                         in1=it[:, 2:3], op0=ALU.mult, op1=ALU.subtract), [VR, C, it], [it])
            k.op("dve", lambda g: g.tensor_tensor(out=it[:, 3:4], in0=VI[:, W - 1:W], in1=C[:, W:W + 1], op=ALU.mult),
                 [VI, C], [it])
            k.op("dve", lambda g: g.scalar_tensor_tensor(out=it[:, 1:2], in0=VR[:, W - 1:W], scalar=St[j][:, W:W + 1],
                                                         in1=it[:, 3:4], op0=ALU.mult, op1=ALU.add), [VR, St[j], it], [it])
            k.op("pool", lambda g: g.tensor_tensor(out=m1[b][:], in0=VR[:], in1=C[:, 0:W], op=ALU.mult), [VR, C], [m1[b]])
            k.op("pool", lambda g: g.tensor_tensor(out=m2[b][:], in0=VI[:], in1=nS[:, 0:W], op=ALU.mult), [VI, nS], [m2[b]])
            k.op("pool", lambda g: g.tensor_tensor(out=xr[b][:], in0=m1[b][:], in1=m2[b][:], op=ALU.add),
                 [m1[b], m2[b]], [xr[b]])
            k.op("dve", lambda g: g.tensor_tensor(out=m3[b][:], in0=VR[:], in1=nS[:, 0:W], op=ALU.mult), [VR, nS], [m3[b]])
            k.op("dve", lambda g: g.tensor_tensor(out=m4[b][:], in0=VI[:], in1=C[:, 0:W], op=ALU.mult), [VI, C], [m4[b]])
            k.op("dve", lambda g: g.tensor_tensor(out=xi[b][:], in0=m3[b][:], in1=m4[b][:], op=ALU.subtract),
                 [m3[b], m4[b]], [xi[b]])
            k.op("pe", lambda g: g.matmul(py[0:64, :], lhsT=Cre[j][:], rhs=xr[b][:], start=(j == 0), stop=False),
                 [Cre[j], xr[b]], [py])
            k.op("pe", lambda g: g.matmul(py[0:64, :], lhsT=Cim[j][:], rhs=xi[b][:], start=False, stop=(j == 1)),
                 [Cim[j], xi[b]], [py])
        y = yb[s % 2]
        k.op("dve", lambda g: g.scalar_tensor_tensor(out=g1[:], in0=u[:], scalar=dsk[:, 0:1], in1=py[0:64, :],
                                                     op0=ALU.mult, op1=ALU.add), [u, dsk, py], [g1])
        k.op("pool", lambda g: g.tensor_tensor(out=g2[:], in0=g1[:], in1=g1[:], op=ALU.mult), [g1], [g2])
        k.op("pool", lambda g: g.tensor_scalar(out=g2[:], in0=g2[:], scalar1=0.044715, scalar2=1.0, op0=ALU.mult, op1=ALU.add),
             [g2], [g2])
        k.op("pool", lambda g: g.tensor_tensor(out=g2[:], in0=g2[:], in1=g1[:], op=ALU.mult), [g2, g1], [g2])
        k.op("act", lambda g: g.activation(out=g2[:], in_=g2[:], func=AF.Tanh, scale=math.sqrt(2.0 / math.pi)), [g2], [g2])
        k.op("pool", lambda g: g.tensor_scalar(out=g2[:], in0=g2[:], scalar1=1.0, scalar2=0.5, op0=ALU.add, op1=ALU.mult),
             [g2], [g2])
        k.op("pool", lambda g: g.tensor_tensor(out=y[:], in0=g2[:], in1=g1[:], op=ALU.mult), [g2, g1], [y])
        k.dma("sp", out_d[:, s * W:(s + 1) * W], y[:], t_in=y, final=True)


def s5_inputs(projT, inp, l, core, L=SEQ):
    r = lambda a: np.ascontiguousarray(a, dtype=np.float32)
    g0 = core * 4
    f = lambda nm: np.asarray(inp[nm][l], np.float32)
    lamre = f("s5_lam_re")[g0:g0 + 4].reshape(2, 128).T
    lamim = f("s5_lam_im")[g0:g0 + 4].reshape(2, 128).T
    logdt = np.repeat(f("s5_log_dt")[g0:g0 + 4], 64).reshape(2, 128).T
    bre = np.zeros((2, 64, 128), np.float32); bim = np.zeros((2, 64, 128), np.float32)
    cre = np.zeros((2, 128, 64), np.float32); cim = np.zeros((2, 128, 64), np.float32)
    for j in range(2):
        for gl in range(2):
            g = g0 + 2 * j + gl
            ch = slice((2 * j + gl) * 16, (2 * j + gl + 1) * 16)
            st = slice(gl * 64, (gl + 1) * 64)
            bre[j, ch, st] = f("s5_b_re")[g].T
            bim[j, ch, st] = f("s5_b_im")[g].T
            cre[j, st, ch] = f("s5_c_re")[g].T
            cim[j, st, ch] = f("s5_c_im")[g].T
    return {
        "s_uT": r(projT[1304 + core * 64:1304 + (core + 1) * 64, :L]),
        "s_lamre": r(lamre), "s_lamim": r(lamim), "s_logdt": r(logdt),
        "s_bre": bre, "s_bim": bim, "s_cre": cre, "s_cim": cim,
        "s_d": r(f("s5_d")[core * 64:(core + 1) * 64].reshape(64, 1)),
        "s_trow": r(np.tile(np.arange(S5_W + 1, dtype=np.float32)[None], (128, 1))),
    }


ROPE_THETA = 500000.0


def rope_tables(pos):
    half = 8
    inv = (ROPE_THETA ** (-np.arange(half, dtype=np.float32) / half)).astype(np.float32)
    ang = pos.astype(np.float32)[None, :] * inv[:, None]
    c = np.ones((64, len(pos)), np.float32); s = np.zeros((64, len(pos)), np.float32)
    c[0:8] = np.cos(ang); c[8:16] = np.cos(ang)
    s[0:8] = np.sin(ang); s[8:16] = np.sin(ang)
    return c, s


def rope_perm():
    pm = np.zeros((64, 64), np.float32)
    for dd in range(8):
        pm[dd + 8, dd] = -1.0
        pm[dd, dd + 8] = 1.0
    return pm


def build_prep(ntok=TOK):
    nc = bass.Bass("TRN2", target_bir_lowering=False)
    d = {}

    def inp(name, shape):
        d[name] = nc.dram_tensor(name, list(shape), F32, kind="ExternalInput").ap()

    NB = ntok // 16
    inp("p_q", [8, 64, ntok]); inp("p_ks", [2, 64, ntok]); inp("p_kw", [2, 64, ntok])
    inp("p_kc", [2, 64, ntok + 16]); inp("p_vc", [2, 64, ntok + 16])
    inp("p_gains", [64, 4]); inp("p_cos", [64, ntok]); inp("p_sin", [64, ntok])
    inp("p_cosc", [64, NB]); inp("p_sinc", [64, NB]); inp("p_pm", [64, 64])
    inp("p_w1k", [64, 32, 256]); inp("p_w1v", [64, 32, 256]); inp("p_w2k", [128, 2, 64]); inp("p_w2v", [128, 2, 64])
    inp("p_peT", [64, 32])
    o = {}
    for name, shape in (("o_q", [8, 64, ntok]), ("o_ks", [2, 64, ntok]), ("o_kw", [2, 64, ntok]),
                        ("o_kc", [2, 64, NB]), ("o_vc", [NB, 2, 64])):
        o[name] = nc.dram_tensor(name, shape, F32, kind="ExternalOutput").ap()
    k = KB(nc)
    W = 512

    def load(name, shape):
        t = k.sb(shape, F32, name)
        k.dma("sp", t[:], d[name], t_out=t)
        return t

    gains = load("p_gains", [64, 4]); pm = load("p_pm", [64, 64])
    cosT = load("p_cos", [64, ntok]); sinT = load("p_sin", [64, ntok])
    cosc = load("p_cosc", [64, NB]); sinc = load("p_sinc", [64, NB])
    ones = k.sb([64, 64], F32, "ones64")
    k.op("pool", lambda g: g.memset(ones[:], 1.0), [], [ones])
    xin = [k.sb([64, W], F32, f"xin{i}") for i in range(3)]
    sq = k.sb([64, W], F32, "sq"); rs = k.sb([64, W], F32, "rs"); xn = [k.sb([64, W], F32, f"xn{i}") for i in range(2)]
    t1 = [k.sb([64, W], F32, f"t1{i}") for i in range(2)]
    ob = [k.sb([64, W], F32, f"ob{i}") for i in range(3)]
    pst = [k.ps([128, W], F32, f"pst{i}") for i in range(2)]
    prt = [k.ps([128, W], F32, f"prt{i}") for i in range(2)]
    cnt = [0]

    def norm_rope(src_t, src_ap, n, gcol, scale_eps, ones_val_scale, cos_ap, sin_ap, dst_t, dst_ap):
        i = cnt[0]; cnt[0] += 1
        ps = pst[i % 2]; pr = prt[i % 2]; x = xn[i % 2]; tt = t1[i % 2]
        k.op("act", lambda g: g.activation(out=sq[:, :n], in_=src_ap, func=AF.Square), [src_t], [sq])
        k.op("pe", lambda g: g.matmul(ps[0:64, :n], lhsT=ones[:], rhs=sq[:, :n], start=True, stop=True), [ones, sq], [ps])
        k.op("dve", lambda g: g.tensor_scalar(out=rs[:, :n], in0=ps[0:64, :n], scalar1=ones_val_scale, scalar2=scale_eps,
                                              op0=ALU.mult, op1=ALU.add), [ps], [rs])
        k.op("act", lambda g: g.activation(out=rs[:, :n], in_=rs[:, :n], func=AF.Sqrt), [rs], [rs])
        k.op("dve", lambda g: g.reciprocal(out=rs[:, :n], in_=rs[:, :n]), [rs], [rs])
        k.op("dve", lambda g: g.scalar_tensor_tensor(out=x[:, :n], in0=src_ap, scalar=gains[:, gcol:gcol + 1], in1=rs[:, :n],
                                                     op0=ALU.mult, op1=ALU.mult), [src_t, gains, rs], [x])
        k.op("pe", lambda g: g.matmul(pr[0:64, :n], lhsT=pm[:], rhs=x[:, :n], start=True, stop=True), [pm, x], [pr])
        k.op("dve", lambda g: g.tensor_tensor(out=tt[:, :n], in0=pr[0:64, :n], in1=sin_ap, op=ALU.mult), [pr, sinT, sinc], [tt])
        k.op("pool", lambda g: g.tensor_tensor(out=dst_ap, in0=x[:, :n], in1=cos_ap, op=ALU.mult), [x, cosT, cosc], [dst_t])
        k.op("pool", lambda g: g.tensor_tensor(out=dst_ap, in0=dst_ap, in1=tt[:, :n], op=ALU.add), [dst_t, tt], [dst_t])

    n_it = 0
    for name, oname, nh, gcol, isq in (("p_q", "o_q", 8, 0, True), ("p_ks", "o_ks", 2, 2, False), ("p_kw", "o_kw", 2, 3, False)):
        for h in range(nh):
            for s in range(ntok // W):
                xi = xin[n_it % 3]; oo = ob[n_it % 3]; n_it += 1
                k.dma("sp", xi[:], d[name][h, :, s * W:(s + 1) * W], t_out=xi)
                if isq:
                    a, bb = 1.0, 64.0 * EPS
                else:
                    a, bb = 1.0 / 64.0, EPS
                norm_rope(xi, xi[:], W, gcol, bb, a, cosT[:, s * W:(s + 1) * W], sinT[:, s * W:(s + 1) * W], oo, oo[:])
                k.dma("sp", o[oname][h, :, s * W:(s + 1) * W], oo[:], t_in=oo, final=True)

    peT = load("p_peT", [64, 32])
    w2k = load("p_w2k", [128, 2, 64]); w2v = load("p_w2v", [128, 2, 64])
    w1 = k.sb([64, 32, 256], F32, "w1")
    raw = [k.sb([64, ntok + 16], F32, f"raw{i}") for i in range(2)]
    gel = [k.sb([128, NB], F32, f"gel{i}") for i in range(2)]
    ga = k.sb([128, NB], F32, "ga"); gb = k.sb([128, NB], F32, "gb")
    bias = k.sb([128, 2], F32, "bias")
    kcn = k.sb([64, NB], F32, "kcn"); kco = k.sb([64, NB], F32, "kco")
    vco = k.sb([NB, 2, 64], F32, "vco")
    ph = [k.ps([128, 512], F32, f"ph{i}") for i in range(2)]
    pb = k.ps([128, 512], F32, "pb")
    po = k.ps([128, 512], F32, "po")
    for which, (rname, wname, w2) in enumerate((("p_kc", "p_w1k", w2k), ("p_vc", "p_w1v", w2v))):
        k.dma("sp", w1[:], d[wname], t_out=w1)
        for ft in range(2):
            for l in range(32):
                k.op("pe", lambda g, l=l, ft=ft: g.matmul(pb[:, ft:ft + 1], lhsT=w1[:, l, ft * 128:(ft + 1) * 128],
                                                        rhs=peT[:, l:l + 1], start=(l == 0), stop=(l == 31)), [w1, peT], [pb])
        k.op("dve", lambda g: g.tensor_copy(out=bias[:], in_=pb[:, 0:2]), [pb], [bias])
        for hk in range(2):
            r = raw[hk]
            k.dma("sp", r[:], d[rname][hk], t_out=r)
            for ft in range(2):
                p = ph[ft]
                for l in range(32):
                    k.op("pe", lambda g, l=l, ft=ft, p=p, r=r: g.matmul(
                        p[:, :NB], lhsT=w1[:, l, ft * 128:(ft + 1) * 128], rhs=r[:, l:l + 16 * (NB - 1) + 1:16],
                        start=(l == 0), stop=(l == 31)), [w1, r], [p])
                k.op("act", lambda g, p=p, ft=ft: g.activation(out=ga[:], in_=p[:, :NB], func=AF.Identity,
                                                               bias=bias[:, ft:ft + 1], scale=1.0), [p, bias], [ga])
                k.op("pool", lambda g: g.tensor_tensor(out=gb[:], in0=ga[:], in1=ga[:], op=ALU.mult), [ga], [gb])
                k.op("pool", lambda g: g.tensor_scalar(out=gb[:], in0=gb[:], scalar1=0.044715, scalar2=1.0, op0=ALU.mult,
                                                       op1=ALU.add), [gb], [gb])
                k.op("pool", lambda g: g.tensor_tensor(out=gb[:], in0=gb[:], in1=ga[:], op=ALU.mult), [gb, ga], [gb])
                k.op("act", lambda g: g.activation(out=gb[:], in_=gb[:], func=AF.Tanh, scale=math.sqrt(2.0 / math.pi)),
                     [gb], [gb])
                k.op("pool", lambda g: g.tensor_scalar(out=gb[:], in0=gb[:], scalar1=1.0, scalar2=0.5, op0=ALU.add,
                                                       op1=ALU.mult), [gb], [gb])
                k.op("pool", lambda g, ft=ft: g.tensor_tensor(out=gel[ft][:], in0=gb[:], in1=ga[:], op=ALU.mult),
                     [gb, ga], [gel[ft]])
            if which == 0:
                for ft in range(2):
                    k.op("pe", lambda g, ft=ft: g.matmul(po[0:64, :NB], lhsT=w2[:, ft, :], rhs=gel[ft][:],
                                                        start=(ft == 0), stop=(ft == 1)), [w2, gel[ft]], [po])
                k.op("dve", lambda g: g.tensor_copy(out=kcn[:], in_=po[0:64, :NB]), [po], [kcn])
                norm_rope(kcn, kcn[:], NB, 1, EPS, 1.0 / 64.0, cosc[:], sinc[:], kco, kco[:])
                k.dma("sp", o["o_kc"][hk], kco[:], t_in=kco, final=True)
            else:
                for ft in range(2):
                    k.op("pe", lambda g, ft=ft: g.matmul(po[0:NB, 0:64], lhsT=gel[ft][:], rhs=w2[:, ft, :],
                                                        start=(ft == 0), stop=(ft == 1)), [w2, gel[ft]], [po])
                k.op("dve", lambda g, hk=hk: g.tensor_copy(out=vco[:, hk, :], in_=po[0:NB, 0:64]), [po], [vco])
    k.dma("sp", o["o_vc"], vco[:], t_in=vco, final=True)
    k.finish()
    return nc


def prep_inputs(projT, inp, l, c, ntok=TOK):
    r = lambda a: np.ascontiguousarray(a, dtype=np.float32)
    t0 = c * ntok
    L = projT.shape[1]

    def halo(rows):
        a = np.zeros((rows.shape[0], ntok + 16), np.float32)
        n = min(ntok + 16, L - t0)
        a[:, :n] = rows[:, t0:t0 + n]
        return a.reshape(2, 64, ntok + 16)

    pos = np.arange(t0, t0 + ntok, dtype=np.float32)
    cos, sin = rope_tables(pos)
    posc = (np.arange(t0 // 16, t0 // 16 + ntok // 16) * 16 + 16).astype(np.float32)
    cosc, sinc = rope_tables(posc)
    f = lambda nm: np.asarray(inp[nm][l], np.float32)
    gains = np.stack([f("nsa_q_norm"), f("nsa_kc_norm"), f("nsa_ks_norm"), f("nsa_kw_norm")], axis=1)
    return {
        "p_q": r(projT[0:512, t0:t0 + ntok].reshape(8, 64, ntok)),
        "p_ks": r(projT[768:896, t0:t0 + ntok].reshape(2, 64, ntok)),
        "p_kw": r(projT[1024:1152, t0:t0 + ntok].reshape(2, 64, ntok)),
        "p_kc": halo(projT[512:640]), "p_vc": halo(projT[640:768]),
        "p_gains": r(gains), "p_cos": cos, "p_sin": sin, "p_cosc": cosc, "p_sinc": sinc, "p_pm": rope_perm(),
        "p_w1k": r(f("cmp_k_w1").transpose(1, 0, 2)), "p_w1v": r(f("cmp_v_w1").transpose(1, 0, 2)),
        "p_w2k": r(f("cmp_k_w2").reshape(2, 128, 64).transpose(1, 0, 2)),
        "p_w2v": r(f("cmp_v_w2").reshape(2, 128, 64).transpose(1, 0, 2)),
        "p_peT": r(f("cmp_pe").T),
    }


NSLOT = 16
MASKV = -30000.0
BIGNEG = -1.0e30


def build_nsa(nslots=NSLOT, L=SEQ):
    nc = bass.Bass("TRN2", target_bir_lowering=False)
    d = {}

    def inp(name, shape):
        d[name] = nc.dram_tensor(name, list(shape), F32, kind="ExternalInput").ap()

    NT = L // 128
    inp("n_ksT", [2, 64, L]); inp("n_vs", [L, 2, 65]); inp("n_kcT", [2, 64, 1024]); inp("n_vc", [1024, 2, 65])
    inp("n_qT", [NSLOT, 64, 8, 128]); inp("n_kw", [NSLOT, 2, 64, 640]); inp("n_vw", [NSLOT, 640, 2, 65])
    inp("n_gates", [NSLOT, 128, 24])
    inp("m_diag", [8, 128, 512]); inp("m_win", [NSLOT, 5, 128, 512]); inp("m_cmpT", [NSLOT, 2, 128, 512])
    inp("m_cmpqn", [NSLOT, 128, 1024]); inp("r_tab", [NSLOT, 128, 2, 256])
    inp("c_ident", [128, 128]); inp("c_expand", [128, 64 * 128])
    out_d = nc.dram_tensor("n_out", [NSLOT, 128, 512], F32, kind="ExternalOutput").ap()
    k = KB(nc)
    emit_nsa(k, d, out_d, nslots, L)
    k.finish()
    return nc


def emit_nsa(k, d, out_d, nslots, L):
    NT = L // 128
    ksT = k.sb([64, 2, L], BF16, "ksT")
    for hk in range(2):
        for c4 in range(4):
            sl = slice(c4 * (L // 4), (c4 + 1) * (L // 4))
            k.dma("pool", ksT[:, hk, sl], d["n_ksT"][hk][:, sl], t_out=ksT)
    vs = k.sb([128, NT, 2, 65], BF16, "vs")
    vsv = d["n_vs"].rearrange("(t p) h e -> p t h e", p=128)
    for c4 in range(8):
        sl = slice(c4 * (NT // 8), (c4 + 1) * (NT // 8))
        k.dma("pool", vs[:, sl], vsv[:, sl], t_out=vs)
    kcT = k.sb([64, 2, 1024], BF16, "kcT")
    for hk in range(2):
        k.dma("pool", kcT[:, hk, :], d["n_kcT"][hk], t_out=kcT)
    vc = k.sb([128, 8, 2, 65], BF16, "vc")
    k.dma("pool", vc[:], d["n_vc"].rearrange("(t p) h e -> p t h e", p=128), t_out=vc)
    mdiag = k.sb([128, 8, 512], BF16, "mdiag")
    k.dma("pool", mdiag[:], d["m_diag"].rearrange("r p f -> p r f"), t_out=mdiag)
    expand = k.sb([128, 64 * 128], BF16, "expand")
    k.dma("pool", expand[:], d["c_expand"], t_out=expand)
    ident = k.sb([128, 128], F32, "ident")
    k.dma("sp", ident[:], d["c_ident"], t_out=ident)

    two = lambda shape, dt, nm: [k.sb(shape, dt, f"{nm}{i}") for i in range(2)]
    qT = two([64, 8, 128], BF16, "qT"); kw = two([64, 2, 640], BF16, "kw"); vw = two([128, 5, 2, 65], BF16, "vw")
    mwin = two([128, 5, 512], BF16, "mwin"); mcT = two([128, 2, 512], BF16, "mcT"); mqn = two([128, 1024], BF16, "mqn")
    rtab = two([128, 2, 256], F32, "rtab"); gates = two([128, 24], F32, "gates")
    ya = two([128, 512], F32, "ya")
    negT4 = two([128, 2, 4, 128], BF16, "negT4")
    E = [k.sb([128, 512], BF16, f"E{i}") for i in range(3)]
    Ecmp = [k.sb([128, 1024], F32, f"Ecmp{i}") for i in range(2)]
    Em = k.sb([128, 1024], F32, "Em")
    pcs = k.sb([128, 1032], F32, "pcs")
    imp = k.sb([128, 256], F32, "imp"); ieff = k.sb([128, 256], F32, "ieff")
    zap1 = k.sb([128, 256], F32, "zap1"); zap2 = k.sb([128, 256], F32, "zap2"); negm = k.sb([128, 256], F32, "negm")
    mx8 = k.sb([128, 8], F32, "mx8")
    sm = k.sb([128, 16], F32, "sm")
    cf = k.sb([128, 8], F32, "cf")
    pS = [k.ps([128, 512], F32, f"pS{i}") for i in range(2)]
    pO = [k.ps([128, 512], F32, f"pO{i}") for i in range(4)]
    pC = [k.ps([128, 512], F32, f"pC{i}") for i in range(2)]
    pT = pC[0]
    cnt = {"s": 0, "e": 0, "o": 0, "ec": 0}

    def slot_loads(i):
        b = i % 2
        k.dma("pool", qT[b][:], d["n_qT"][i], t_out=qT[b])
        k.dma("pool", kw[b][:], d["n_kw"][i].rearrange("h d n -> d h n"), t_out=kw[b])
        k.dma("pool", vw[b][:], d["n_vw"][i].rearrange("(t p) h e -> p t h e", p=128), t_out=vw[b])
        k.dma("pool", mwin[b][:], d["m_win"][i].rearrange("w p f -> p w f"), t_out=mwin[b])
        k.dma("pool", mcT[b][:], d["m_cmpT"][i].rearrange("w p f -> p w f"), t_out=mcT[b])
        k.dma("pool", mqn[b][:], d["m_cmpqn"][i], t_out=mqn[b])
        k.dma("sp", rtab[b][:], d["r_tab"][i], t_out=rtab[b])
        k.dma("sp", gates[b][:], d["n_gates"][i], t_out=gates[b])
        k.op("act", lambda g: g.activation(out=gates[b][:], in_=gates[b][:], func=AF.Sigmoid), [gates[b]], [gates[b]])

    def stage1(i, hk):
        b = i % 2
        x = (2 * i + hk) % 2
        NV = 64 * i + 64
        NJ = 16 * i + 16
        k.op("pool", lambda g: g.memset(pcs[:], 0.0), [], [pcs])
        k.op("pool", lambda g: g.memset(imp[:], 0.0), [], [imp])
        for gq in range(4):
            h = hk * 4 + gq
            ec = Ecmp[cnt["ec"] % 2]; cnt["ec"] += 1
            for c0 in range(0, NV, 512):
                cw = min(512, NV - c0)
                p = pC[(c0 // 512) % 2]
                k.op("pe", lambda g, p=p, c0=c0, cw=cw, h=h: g.matmul(p[:, :cw], lhsT=qT[b][:, h, :], rhs=kcT[:, hk, c0:c0 + cw],
                                                                   start=True, stop=True), [qT[b], kcT], [p])
                k.op("act", lambda g, p=p, c0=c0, cw=cw, ec=ec: g.activation(out=ec[:, c0:c0 + cw], in_=p[:, :cw], func=AF.Exp),
                     [p], [ec])
            k.op("dve", lambda g, ec=ec, gq=gq: g.scalar_tensor_tensor(
                out=Em[:, :NV], in0=ec[:, :NV], scalar=1.0, in1=mqn[b][:, :NV], op0=ALU.mult, op1=ALU.mult,
                accum_out=sm[:, gq:gq + 1]), [ec, mqn[b]], [Em, sm])
            k.op("dve", lambda g, gq=gq: g.tensor_scalar(out=sm[:, 4 + gq:5 + gq], in0=sm[:, gq:gq + 1], scalar1=1e-30,
                                                         scalar2=None, op0=ALU.max), [sm], [sm])
            k.op("dve", lambda g, gq=gq: g.reciprocal(out=sm[:, 4 + gq:5 + gq], in_=sm[:, 4 + gq:5 + gq]), [sm], [sm])
            if gq == 0:
                k.op("dve", lambda g, gq=gq: g.tensor_scalar(out=pcs[:, 1:1 + NV], in0=Em[:, :NV], scalar1=sm[:, 4 + gq:5 + gq],
                                                             scalar2=None, op0=ALU.mult), [Em, sm], [pcs])
            else:
                k.op("dve", lambda g, gq=gq: g.scalar_tensor_tensor(out=pcs[:, 1:1 + NV], in0=Em[:, :NV],
                                                                    scalar=sm[:, 4 + gq:5 + gq], in1=pcs[:, 1:1 + NV],
                                                                    op0=ALU.mult, op1=ALU.add), [Em, sm, pcs], [pcs])
        vw_ = lambda r: pcs[:, r:r + 4 * (NJ - 1) + 1:4]
        k.op("dve", lambda g: g.tensor_tensor(out=imp[:, :NJ], in0=vw_(0), in1=vw_(1), op=ALU.add), [pcs], [imp])
        for r in (2, 3, 4):
            k.op("dve", lambda g, r=r: g.tensor_tensor(out=imp[:, :NJ], in0=imp[:, :NJ], in1=vw_(r), op=ALU.add), [pcs, imp], [imp])
        k.op("dve", lambda g: g.tensor_tensor(out=ieff[:], in0=imp[:], in1=rtab[b][:, 0, :], op=ALU.add), [imp, rtab[b]], [ieff])
        k.op("dve", lambda g: g.tensor_tensor(out=ieff[:], in0=ieff[:], in1=rtab[b][:, 1, :], op=ALU.max), [ieff, rtab[b]], [ieff])
        k.op("dve", lambda g: g.max(out=mx8[:], in_=ieff[:]), [ieff], [mx8])
        k.op("dve", lambda g: g.match_replace(out=zap1[:], in_to_replace=mx8[:], in_values=ieff[:], imm_value=BIGNEG),
             [mx8, ieff], [zap1])
        k.op("dve", lambda g: g.max(out=mx8[:], in_=zap1[:]), [zap1], [mx8])
        k.op("dve", lambda g: g.match_replace(out=zap2[:], in_to_replace=mx8[:], in_values=zap1[:], imm_value=BIGNEG),
             [mx8, zap1], [zap2])
        k.op("dve", lambda g: g.tensor_tensor(out=negm[:], in0=ieff[:], in1=zap2[:], op=ALU.subtract), [ieff, zap2], [negm])
        k.op("dve", lambda g: g.tensor_scalar(out=negm[:], in0=negm[:], scalar1=1.0, scalar2=None, op0=ALU.min), [negm], [negm])
        k.op("dve", lambda g: g.tensor_scalar(out=negm[:], in0=negm[:], scalar1=-MASKV, scalar2=MASKV, op0=ALU.mult,
                                              op1=ALU.add), [negm], [negm])
        for half in range(2):
            k.op("pe", lambda g, half=half: g.matmul(pT[:, half * 128:(half + 1) * 128], lhsT=negm[:, half * 128:(half + 1) * 128],
                                                    rhs=ident[:], start=True, stop=True), [negm, ident], [pT])
        n4 = negT4[x]
        for half in range(2):
            for gq in range(4):
                e = "act" if (gq % 2 == 0) else "dve"
                k.copy(e, n4, n4[:, half, gq, :], pT, pT[:, half * 128:(half + 1) * 128])
        return x

    def attend(tiles, q4, q4_t, po):
        nt = len(tiles)
        for idx, tl in enumerate(tiles):
            p = pS[cnt["s"] % 2]; cnt["s"] += 1
            e = E[cnt["e"] % 3]; cnt["e"] += 1
            add = tl.get("add")
            k.op("pe", lambda g, p=p, tl=tl: g.matmul(p[:], lhsT=tl["K"], rhs=q4, start=True, stop=(tl.get("add") is None)),
                 [tl["Kt"], q4_t], [p])
            if add is not None:
                k.op("pe", lambda g, p=p, add=add: g.matmul(p[:], lhsT=add[0], rhs=add[1], start=False, stop=True),
                     [expand, add[2]], [p])
            k.op("act", lambda g, p=p, e=e: g.activation(out=e[:], in_=p[:], func=AF.Exp), [p], [e])
            mul = tl.get("mul")
            if mul is not None:
                k.op("pool", lambda g, e=e, mul=mul: g.tensor_tensor(out=e[:], in0=e[:], in1=mul[0], op=ALU.mult), [e, mul[1]], [e])
            for gq in range(4):
                k.op("pe", lambda g, e=e, gq=gq, tl=tl: g.matmul(po[gq][:, 0:65], lhsT=e[:, gq * 128:(gq + 1) * 128], rhs=tl["V"],
                                                              start=(idx == 0), stop=(idx == nt - 1)), [e, tl["Vt"]], [po[gq]])

    def combine(i, hk, br, po):
        b = i % 2
        y = ya[b]
        for gq in range(4):
            h = hk * 4 + gq
            k.op("dve", lambda g, gq=gq: g.tensor_scalar(out=cf[:, 0:1], in0=po[gq][:, 64:65], scalar1=1e-30, scalar2=None,
                                                         op0=ALU.max), [po[gq]], [cf])
            k.op("dve", lambda g: g.reciprocal(out=cf[:, 0:1], in_=cf[:, 0:1]), [cf], [cf])
            k.op("dve", lambda g, h=h: g.tensor_tensor(out=cf[:, 1:2], in0=cf[:, 0:1], in1=gates[b][:, h * 3 + br:h * 3 + br + 1],
                                                       op=ALU.mult), [cf, gates[b]], [cf])
            if br == 0:
                k.op("dve", lambda g, gq=gq, h=h: g.tensor_scalar(out=y[:, h * 64:(h + 1) * 64], in0=po[gq][:, 0:64],
                                                                  scalar1=cf[:, 1:2], scalar2=None, op0=ALU.mult), [po[gq], cf], [y])
            else:
                k.op("dve", lambda g, gq=gq, h=h: g.scalar_tensor_tensor(out=y[:, h * 64:(h + 1) * 64], in0=po[gq][:, 0:64],
                                                                         scalar=cf[:, 1:2], in1=y[:, h * 64:(h + 1) * 64],
                                                                         op0=ALU.mult, op1=ALU.add), [po[gq], cf, y], [y])

    def stage2(i, hk, x):
        b = i % 2
        q4 = qT[b][:, hk * 4:(hk + 1) * 4, :]
        ntc = i // 2 + 1
        tiles = []
        for nt_ in range(ntc):
            tl = {"K": kcT[:, hk, nt_ * 128:(nt_ + 1) * 128], "Kt": kcT, "V": vc[:, nt_, hk, :], "Vt": vc}
            m = nt_ - (ntc - 2)
            if m >= 0:
                tl["mul"] = (mcT[b][:, m, :], mcT[b])
            tiles.append(tl)
        po = pO
        attend(tiles, q4, qT[b], po)
        combine(i, hk, 0, po)
        tiles = []
        for w in range(5):
            tl = {"K": kw[b][:, hk, w * 128:(w + 1) * 128], "Kt": kw[b], "V": vw[b][:, w, hk, :], "Vt": vw[b]}
            if i == 0 or w in (0, 4):
                tl["mul"] = (mwin[b][:, w, :], mwin[b])
            tiles.append(tl)
        po = pO
        attend(tiles, q4, qT[b], po)
        combine(i, hk, 2, po)
        tiles = []
        for kt in range(8 * i + 8):
            tl = {"K": ksT[:, hk, kt * 128:(kt + 1) * 128], "Kt": ksT, "V": vs[:, kt, hk, :], "Vt": vs,
                  "add": (expand[:, (kt % 64) * 128:(kt % 64 + 1) * 128], negT4[x][:, kt // 64, :, :], negT4[x])}
            if kt >= 8 * i:
                tl["mul"] = (mdiag[:, kt - 8 * i, :], mdiag)
            tiles.append(tl)
        po = pO
        attend(tiles, q4, qT[b], po)
        combine(i, hk, 1, po)
        if hk == 1:
            k.dma("sp", out_d[i], ya[b][:], t_in=ya[b], final=True)

    units = [(i, hk) for i in range(nslots) for hk in range(2)]
    slot_loads(0)
    xs = {}
    xs[units[0]] = stage1(*units[0])
    for ui, (i, hk) in enumerate(units):
        if ui + 1 < len(units):
            ni, nhk = units[ui + 1]
            if nhk == 0:
                slot_loads(ni)
            xs[(ni, nhk)] = stage1(ni, nhk)
        stage2(i, hk, xs[(i, hk)])


def nsa_consts():
    ident = np.eye(128, dtype=np.float32)
    ex = np.zeros((128, 64, 128), np.float32)
    for kt in range(64):
        ex[2 * kt, kt, 0:64] = 1.0
        ex[2 * kt + 1, kt, 64:128] = 1.0
    return ident, ex.reshape(128, 64 * 128)


def nsa_masks(c, L=SEQ):
    q = np.arange(128)
    key = np.arange(128)
    m_diag = np.zeros((8, 128, 512), np.float32)
    for r in range(8):
        if r < c:
            m_diag[r] = 1.0
        elif r == c:
            m_diag[r] = np.tile((key[:, None] <= q[None, :]).astype(np.float32), (1, 4))
    m_win = np.zeros((NSLOT, 5, 128, 512), np.float32)
    m_cmpT = np.zeros((NSLOT, 2, 128, 512), np.float32)
    m_cmpqn = np.zeros((NSLOT, 128, 1024), np.float32)
    r_tab = np.zeros((NSLOT, 128, 2, 256), np.float32)
    n_all = np.arange(1024)
    j = np.arange(256)
    for i in range(NSLOT):
        qb = 8 * i + c
        s = 128 * qb
        t = s + q
        for w in range(5):
            kpos = s - 512 + 128 * w + key
            ok = (kpos[:, None] <= t[None, :]) & (kpos[:, None] > t[None, :] - 512) & (kpos[:, None] >= 0)
            m_win[i, w] = np.tile(ok.astype(np.float32), (1, 4))
        ntc = i // 2 + 1
        for m in range(2):
            nt_ = ntc - 2 + m
            if nt_ < 0:
                continue
            n = 128 * nt_ + key
            ok = (16 * n[:, None] + 31 <= t[None, :]) & (n[:, None] <= 1022)
            m_cmpT[i, m] = np.tile(ok.astype(np.float32), (1, 4))
        m_cmpqn[i] = ((16 * n_all[None, :] + 31 <= t[:, None]) & (n_all[None, :] <= 1022)).astype(np.float32)
        cur = t // 64
        r_tab[i, :, 0, :] = np.where(j[None, :] <= cur[:, None], 0.0, BIGNEG)
        force = np.full((128, 256), 2 * BIGNEG, np.float32)
        force[:, 0] = 8.0
        force[q, cur] = 16.0
        prev = cur - 1
        okp = prev >= 0
        force[q[okp], prev[okp]] = 32.0
        r_tab[i, :, 1, :] = force
    return {"m_diag": m_diag, "m_win": m_win, "m_cmpT": m_cmpT, "m_cmpqn": m_cmpqn, "r_tab": r_tab}


def nsa_inputs(projT, prep, c, L=SEQ):
    r = lambda a: np.ascontiguousarray(a, dtype=np.float32)
    ones = lambda shp: np.ones(shp, np.float32)
    vs = projT[896:1024, :L].T.reshape(L, 2, 64)
    vs_aug = np.concatenate([vs, ones((L, 2, 1))], axis=2)
    kcT = np.zeros((2, 64, 1024), np.float32); kcT[:, :, :1023] = prep["kc"][:, :, :1023]
    vc_aug = np.zeros((1024, 2, 65), np.float32); vc_aug[:1023, :, :64] = prep["vc"][:1023]; vc_aug[:1023, :, 64] = 1.0
    kw_pad = np.concatenate([np.zeros((2, 64, 512), np.float32), prep["kw"]], axis=2)
    vw = projT[1152:1280, :L].T.reshape(L, 2, 64)
    vw_aug = np.concatenate([vw, ones((L, 2, 1))], axis=2)
    vw_pad = np.concatenate([np.zeros((512, 2, 65), np.float32), vw_aug], axis=0)
    gl = projT[1280:1304, :L].T
    qT = np.zeros((NSLOT, 64, 8, 128), np.float32); kws = np.zeros((NSLOT, 2, 64, 640), np.float32)
    vws = np.zeros((NSLOT, 640, 2, 65), np.float32); gts = np.zeros((NSLOT, 128, 24), np.float32)
    for i in range(NSLOT):
        qb = 8 * i + c
        s = 128 * qb
        qT[i] = prep["q"][:, :, s:s + 128].transpose(1, 0, 2)
        kws[i] = kw_pad[:, :, s:s + 640]
        vws[i] = vw_pad[s:s + 640]
        gts[i] = gl[s:s + 128]
    ident, ex = nsa_consts()
    dd = {"n_ksT": r(prep["ks"]), "n_vs": r(vs_aug), "n_kcT": kcT, "n_vc": vc_aug, "n_qT": qT, "n_kw": kws, "n_vw": vws,
          "n_gates": gts, "c_ident": ident, "c_expand": ex}
    dd.update(nsa_masks(c, L))
    return dd


_PROGS = {}


def _prog(name, builder):
    if name not in _PROGS:
        _PROGS[name] = builder()
    return _PROGS[name]


def _run(nc, maps):
    res = run_bass_kernel_spmd(nc, maps, core_ids=list(range(NCORES)))
    return res.results


def kernel(**inputs):
    inp = {k_: np.asarray(v) for k_, v in inputs.items()}
    L = SEQ
    xT = np.ascontiguousarray(inp["x"][0].T.astype(np.float32))
    for l in range(DEPTH):
        ncA = _prog("A", build_stage_A)
        gainA = lay128(inp["attn_norm"][l])
        wA = np.asarray(inp["w_in"][l], np.float32)
        res = _run(ncA, [{"xT": np.ascontiguousarray(xT[:, c * TOK:(c + 1) * TOK]), "gain": gainA, "w": wA}
                         for c in range(NCORES)])
        projT = np.concatenate([r["projT"] for r in res], axis=1)
        del res
        ncP = _prog("P", build_prep)
        res = _run(ncP, [prep_inputs(projT, inp, l, c) for c in range(NCORES)])
        prep = {
            "q": np.concatenate([r["o_q"] for r in res], axis=2),
            "ks": np.concatenate([r["o_ks"] for r in res], axis=2),
            "kw": np.concatenate([r["o_kw"] for r in res], axis=2),
            "kc": np.concatenate([r["o_kc"] for r in res], axis=2),
            "vc": np.concatenate([r["o_vc"] for r in res], axis=0),
        }
        del res
        ncN = _prog("N", build_nsa)
        res = _run(ncN, [nsa_inputs(projT, prep, c) for c in range(NCORES)])
        ya = np.zeros((L, 512), np.float32)
        for c in range(NCORES):
            o = res[c]["n_out"]
            for i in range(NSLOT):
                qb = 8 * i + c
                ya[qb * 128:(qb + 1) * 128] = o[i]
        del res, prep
        ncG = _prog("G", build_gdn)
        res = _run(ncG, [gdn_inputs(projT, inp, l, h) for h in range(NCORES)])
        ycT = np.concatenate([np.ascontiguousarray(r["g_out"].T) for r in res], axis=0)
        del res
        ncS = _prog("S", build_s5)
        res = _run(ncS, [s5_inputs(projT, inp, l, c) for c in range(NCORES)])
        ybT = np.concatenate([r["s_out"] for r in res], axis=0)
        del res, projT
        ncC = _prog("C", build_stage_C)

        def halo(a, c):
            out = np.zeros((a.shape[0], TOK + 2), np.float32)
            lo = c * TOK - 2
            if lo < 0:
                out[:, 2:] = a[:, 0:TOK]
            else:
                out[:] = a[:, lo:lo + TOK + 2]
            return out

        yaT = np.ascontiguousarray(ya.T)
        maps = [stage_C_inputs(inp, l, halo(xT, c), halo(yaT, c), halo(ybT, c), halo(ycT, c)) for c in range(NCORES)]
        res = _run(ncC, maps)
        xT = np.concatenate([r["xoT"] for r in res], axis=1)
        del res, maps
    return np.ascontiguousarray(xT.T)[None].astype(np.float32)
```

```python
from contextlib import ExitStack
import numpy as np
import concourse.bass as bass
import concourse.mybir as mybir
from concourse.bass_utils import run_bass_kernel_spmd

F32 = mybir.dt.float32
BF16 = mybir.dt.bfloat16
AF = mybir.ActivationFunctionType
ALU = mybir.AluOpType
AX = mybir.AxisListType

NCORES = 8
D_MODEL = 2048
SEQ = 16384
DEPTH = 2
IN_COLS = 5928
D_FF = 5504
EPS = 1e-6
TOK = SEQ // NCORES


class T:
    __slots__ = ("h", "name", "w", "r", "din", "dout", "sin", "sout", "psum")

    def __init__(self, h, name, psum=False):
        self.psum = psum
        self.h = h
        self.name = name
        self.w = None
        self.r = {}
        self.din = 0
        self.dout = 0
        self.sin = None
        self.sout = None

    def __getitem__(self, idx):
        return self.h[idx]


class KB:
    ENGS = ("pe", "act", "dve", "pool", "sp")

    def __init__(self, nc):
        self.nc = nc
        self.es = ExitStack()
        self.eng = {"pe": nc.tensor, "act": nc.scalar, "dve": nc.vector,
                    "pool": nc.gpsimd, "sp": nc.sync}
        self.sem = {k: self.es.enter_context(nc.semaphore("s_" + k)) for k in self.ENGS}
        self.cnt = {k: 0 for k in self.ENGS}
        self.known = {k: {} for k in self.ENGS}
        self.nsem = len(self.ENGS)
        self.ntile = 0
        self.out_waits = []
        self.rr = 0
        self.rec = None

    def sb(self, shape, dtype=F32, name=None):
        self.ntile += 1
        name = name or "t"
        h = self.es.enter_context(self.nc.sbuf_tensor(f"{name}_{self.ntile}", list(shape), dtype))
        return T(h, name)

    def ps(self, shape, dtype=F32, name=None):
        self.ntile += 1
        name = name or "p"
        h = self.es.enter_context(self.nc.psum_tensor(f"{name}_{self.ntile}", list(shape), dtype))
        return T(h, name, psum=True)

    def newsem(self, name):
        self.nsem += 1
        assert self.nsem < 145, "too many semaphores"
        return self.es.enter_context(self.nc.semaphore(f"{name}_{self.nsem}"))

    def _collect(self, e, reads, writes):
        needs = {}

        def need(sem, val):
            key = id(sem)
            if needs.get(key, (None, 0))[1] < val:
                needs[key] = (sem, val)

        for t in reads:
            if t.w is not None:
                need(self.sem[t.w[0]], t.w[1])
            if t.din:
                need(t.sin, t.din)
            if t.psum:
                for kk, c in t.r.items():
                    if kk != e:
                        need(self.sem[kk], c)
        for t in writes:
            if t.w is not None and not (t.w[0] == e and e == "pe"):
                need(self.sem[t.w[0]], t.w[1])
            for kk, c in t.r.items():
                need(self.sem[kk], c)
            if t.din:
                need(t.sin, t.din)
            if t.dout:
                need(t.sout, t.dout)
        kn = self.known[e]
        for key, (sem, val) in needs.items():
            if kn.get(key, 0) < val:
                self.eng[e].wait_ge(sem, val)
                kn[key] = val

    def op(self, e, fn, reads=(), writes=()):
        if self.rec is not None:
            self.rec.append((e, fn, list(reads), list(writes)))
            return None
        reads = [getattr(t, "base", t) for t in reads]
        writes = [getattr(t, "base", t) for t in writes]
        self._collect(e, reads, writes)
        ins = fn(self.eng[e])
        ins.then_inc(self.sem[e], 1)
        self.cnt[e] += 1
        c = self.cnt[e]
        self.known[e][id(self.sem[e])] = max(self.known[e].get(id(self.sem[e]), 0), 0)
        for t in writes:
            t.w = (e, c)
            t.r = {}
        for t in reads:
            if t not in writes:
                t.r[e] = c
        return ins

    def dma(self, q, out, in_, t_out=None, t_in=None, final=False, **kw):
        reads = [t_in] if t_in is not None else []
        writes = [t_out] if t_out is not None else []
        self._collect(q, reads, writes)
        ins = self.eng[q].dma_start(out=out, in_=in_, **kw)
        if t_out is not None:
            if t_out.sin is None:
                t_out.sin = self.newsem("di")
            ins.then_inc(t_out.sin, 16)
            t_out.din += 16
            t_out.w = None
            t_out.r = {}
        if t_in is not None and t_out is None:
            if t_in.sout is None:
                t_in.sout = self.newsem("do")
            ins.then_inc(t_in.sout, 16)
            t_in.dout += 16
            if final and t_in not in self.out_waits:
                self.out_waits.append(t_in)
        return ins

    def finish(self):
        sp = self.eng["sp"]
        for t in self.out_waits:
            sp.wait_ge(t.sout, t.dout)
        for kk in self.ENGS:
            if kk != "sp" and self.cnt[kk]:
                sp.wait_ge(self.sem[kk], self.cnt[kk])
        self.es.close()

    def record(self, fn, *a, **kw):
        assert self.rec is None
        self.rec = []
        try:
            r = fn(*a, **kw)
        finally:
            lst, self.rec = self.rec, None
        return r, lst

    def emit_interleaved(self, lists):
        n = max(len(l) for l in lists)
        for j in range(n):
            for l in lists:
                if j < len(l):
                    self.op(*l[j])

    def evac_eng(self):
        self.rr += 1
        return ("act", "dve")[self.rr % 2]

    def copy(self, e, out_t, out_ap, in_t, in_ap):
        if e == "act":
            return self.op("act", lambda g: g.copy(out=out_ap, in_=in_ap), [in_t], [out_t])
        return self.op(e, lambda g: g.tensor_copy(out=out_ap, in_=in_ap), [in_t], [out_t])


def col_chunks(n0, n1):
    out = []
    c = n0
    while c < n1:
        w = min(128, n1 - c)
        out.append((c, w))
        c += w
    return out


def rms_stats(k, ones, src, nchunk, W, ps, sq_tiles, rstd, eps=EPS):
    for c in range(nchunk):
        sq = sq_tiles[c % len(sq_tiles)]
        k.op("act", lambda g, sq=sq, c=c: g.activation(out=sq[:, :W], in_=src[:, c, :W], func=AF.Square),
             [src], [sq])
        k.op("pe", lambda g, sq=sq, c=c: g.matmul(ps[:, :W], lhsT=ones[:], rhs=sq[:, :W],
                                               start=(c == 0), stop=(c == nchunk - 1)), [ones, sq], [ps])
    k.op("dve", lambda g: g.tensor_scalar(out=rstd[:, :W], in0=ps[:, :W], scalar1=eps, scalar2=None,
                                          op0=ALU.add), [ps], [rstd])
    k.op("act", lambda g: g.activation(out=rstd[:, :W], in_=rstd[:, :W], func=AF.Sqrt), [rstd], [rstd])
    k.op("dve", lambda g: g.reciprocal(out=rstd[:, :W], in_=rstd[:, :W]), [rstd], [rstd])


def build_stage_A(ntok=TOK):
    nc = bass.Bass("TRN2", target_bir_lowering=False)
    C = D_MODEL // 128
    TW = 512
    ntile = ntok // TW
    xT = nc.dram_tensor("xT", [D_MODEL, ntok], F32, kind="ExternalInput").ap()
    gain_d = nc.dram_tensor("gain", [128, C], F32, kind="ExternalInput").ap()
    w_d = nc.dram_tensor("w", [D_MODEL, IN_COLS], F32, kind="ExternalInput").ap()
    out_d = nc.dram_tensor("projT", [IN_COLS, ntok], F32, kind="ExternalOutput").ap()
    k = KB(nc)
    ones = k.sb([128, 128], F32, "ones")
    k.op("pool", lambda g: g.memset(ones[:], 1.0 / D_MODEL), [], [ones])
    gain = k.sb([128, C], F32, "gain")
    k.dma("sp", gain[:], gain_d, t_out=gain)
    hT = [k.sb([128, C, TW], BF16, f"hT{i}") for i in range(ntile)]
    xs = [k.sb([128, C, TW], F32, f"xs{i}") for i in range(2)]
    sqs = [k.sb([128, TW], F32, f"sq{i}") for i in range(2)]
    rstd = k.sb([128, TW], F32, "rstd")
    pstat = k.ps([128, TW], F32, "pstat")
    pacc = [k.ps([128, TW], F32, f"pacc{i}") for i in range(6)]
    xv = xT.rearrange("(c p) t -> p c t", p=128)
    wv = w_d.rearrange("(c p) n -> p c n", p=128)
    wb = [k.sb([128, C, 512], BF16, f"wb{i}") for i in range(2)]
    ost = [k.sb([128, TW], F32, f"ost{i}") for i in range(4)]
    groups = [(g0, min(512, IN_COLS - g0)) for g0 in range(0, IN_COLS, 512)]
    k.dma("pool", wb[0][:, :, :groups[0][1]], wv[:, :, 0:groups[0][1]], t_out=wb[0])
    for i in range(ntile):
        xt = xs[i % 2]
        k.dma("sp", xt[:], xv[:, :, i * TW:(i + 1) * TW], t_out=xt)
        rms_stats(k, ones, xt, C, TW, pstat, sqs, rstd)
        for c in range(C):
            k.op("dve", lambda g, c=c: g.scalar_tensor_tensor(
                out=hT[i][:, c, :], in0=xt[:, c, :], scalar=gain[:, c:c + 1], in1=rstd[:],
                op0=ALU.mult, op1=ALU.mult), [xt, gain, rstd], [hT[i]])
    n = 0
    for gi, (g0, gw) in enumerate(groups):
        w = wb[gi % 2]
        if gi + 1 < len(groups):
            g1, gw1 = groups[gi + 1]
            k.dma("pool", wb[(gi + 1) % 2][:, :, :gw1], wv[:, :, g1:g1 + gw1], t_out=wb[(gi + 1) % 2])
        for (c0, cw) in col_chunks(g0, g0 + gw):
            off = c0 - g0
            for i in range(ntile):
                p = pacc[n % len(pacc)]
                o = ost[n % len(ost)]
                n += 1
                for c in range(C):
                    k.op("pe", lambda g, c=c, p=p: g.matmul(p[:cw, :], lhsT=w[:, c, off:off + cw], rhs=hT[i][:, c, :],
                                                        start=(c == 0), stop=(c == C - 1)), [w, hT[i]], [p])
                k.copy(k.evac_eng(), o, o[:cw, :], p, p[:cw, :])
                k.dma("sp", out_d[c0:c0 + cw, i * TW:(i + 1) * TW], o[:cw, :], t_in=o, final=True)
    k.finish()
    return nc


def build_stage_C(ntok=TOK):
    nc = bass.Bass("TRN2", target_bir_lowering=False)
    C = D_MODEL // 128
    HALO = 2
    NT = ntok + HALO
    TW = 512
    NJ = D_FF // 128
    xT = nc.dram_tensor("xT", [D_MODEL, NT], F32, kind="ExternalInput").ap()
    yaT = nc.dram_tensor("yaT", [512, NT], F32, kind="ExternalInput").ap()
    ybT = nc.dram_tensor("ybT", [512, NT], F32, kind="ExternalInput").ap()
    ycT = nc.dram_tensor("ycT", [1024, NT], F32, kind="ExternalInput").ap()
    g_nsa = nc.dram_tensor("g_nsa", [128, 4], F32, kind="ExternalInput").ap()
    g_s5 = nc.dram_tensor("g_s5", [128, 4], F32, kind="ExternalInput").ap()
    g_ffn = nc.dram_tensor("g_ffn", [128, C], F32, kind="ExternalInput").ap()
    cw_d = nc.dram_tensor("ffn_conv", [128, 3, 2 * NJ], F32, kind="ExternalInput").ap()
    cb_d = nc.dram_tensor("ffn_conv_b", [128, 2 * NJ], F32, kind="ExternalInput").ap()
    wglu_d = nc.dram_tensor("w_glu", [512, 512], F32, kind="ExternalInput").ap()
    wout_d = nc.dram_tensor("w_out", [D_MODEL, D_MODEL], F32, kind="ExternalInput").ap()
    wfi_d = nc.dram_tensor("ffn_w_in", [D_MODEL, 2 * D_FF], F32, kind="ExternalInput").ap()
    wfo_d = nc.dram_tensor("ffn_w_out", [D_FF, D_MODEL], F32, kind="ExternalInput").ap()
    out_d = nc.dram_tensor("xoT", [D_MODEL, ntok], F32, kind="ExternalOutput").ap()

    k = KB(nc)
    ones_d = k.sb([128, 128], F32, "ones_d")
    k.op("pool", lambda g: g.memset(ones_d[:], 1.0 / D_MODEL), [], [ones_d])
    ones_4 = k.sb([128, 128], F32, "ones_4")
    k.op("pool", lambda g: g.memset(ones_4[:], 1.0 / 512.0), [], [ones_4])
    gn = k.sb([128, 4], F32, "gn"); k.dma("sp", gn[:], g_nsa, t_out=gn)
    gs = k.sb([128, 4], F32, "gs"); k.dma("sp", gs[:], g_s5, t_out=gs)
    gf = k.sb([128, C], F32, "gf"); k.dma("sp", gf[:], g_ffn, t_out=gf)
    cw = k.sb([128, 3, 2 * NJ], F32, "cw"); k.dma("sp", cw[:], cw_d, t_out=cw)
    cb = k.sb([128, 2 * NJ], F32, "cb"); k.dma("sp", cb[:], cb_d, t_out=cb)
    wglu = k.sb([128, 4, 512], BF16, "wglu")
    k.dma("pool", wglu[:], wglu_d.rearrange("(c p) n -> p c n", p=128), t_out=wglu)

    xt = k.sb([128, C, TW], F32, "xt")
    ain = k.sb([128, C, TW], BF16, "ain")
    act = k.sb([128, NJ, TW], BF16, "act")
    ya = k.sb([128, 4, TW], F32, "ya")
    yb = k.sb([128, 4, TW], F32, "yb")
    ybb = k.sb([128, 4, TW], BF16, "ybb")
    y2 = ya
    sqs = [k.sb([128, TW], F32, f"sq{i}") for i in range(2)]
    rstd = k.sb([128, TW], F32, "rstd")
    sig = k.sb([128, TW], F32, "sig")
    tails = k.sb([128, 2 * NJ, 2], F32, "tails")
    k.op("pool", lambda g: g.memset(tails[:], 0.0), [], [tails])
    ext = [k.sb([128, TW + 2], F32, f"ext{i}") for i in range(4)]
    gt = [k.sb([128, TW], F32, f"gt{i}") for i in range(2)]
    ut = [k.sb([128, TW], F32, f"ut{i}") for i in range(2)]
    ost = [k.sb([128, TW], F32, f"ost{i}") for i in range(2)]
    wb = [k.sb([128, C, 512], BF16, f"wb{i}") for i in range(3)]
    pstat = k.ps([128, TW], F32, "pstat")
    pacc = [k.ps([128, TW], F32, f"pacc{i}") for i in range(7)]
    state = {"wn": 0, "pn": 0}

    def next_w():
        w = wb[state["wn"] % len(wb)]
        state["wn"] += 1
        return w

    def next_p():
        p = pacc[state["pn"] % len(pacc)]
        state["pn"] += 1
        return p

    woutv = wout_d.rearrange("(c p) n -> p c n", p=128)
    wfiv = wfi_d.rearrange("(c p) n -> p c n", p=128)
    wfov = wfo_d.rearrange("(c p) n -> p c n", p=128)

    tiles = [(0, HALO)] + [(HALO + i * TW, TW) for i in range(ntok // TW)]
    for ti, (t0, W) in enumerate(tiles):
        halo = (ti == 0)
        k.dma("sp", xt[:, :, :W], xT.rearrange("(c p) t -> p c t", p=128)[:, :, t0:t0 + W], t_out=xt)
        k.dma("sp", ya[:, :, :W], yaT.rearrange("(c p) t -> p c t", p=128)[:, :, t0:t0 + W], t_out=ya)
        k.dma("sp", yb[:, :, :W], ybT.rearrange("(c p) t -> p c t", p=128)[:, :, t0:t0 + W], t_out=yb)
        rms_stats(k, ones_4, ya, 4, W, pstat, sqs, rstd)
        for c in range(4):
            k.op("dve", lambda g, c=c: g.scalar_tensor_tensor(
                out=ain[:, c, :W], in0=ya[:, c, :W], scalar=gn[:, c:c + 1], in1=rstd[:, :W],
                op0=ALU.mult, op1=ALU.mult), [ya, gn, rstd], [ain])
        k.op("pool", lambda g: g.tensor_copy(out=ybb[:, :, :W], in_=yb[:, :, :W]), [yb], [ybb])
        for m in range(4):
            p = next_p()
            for c in range(4):
                k.op("pe", lambda g, c=c, p=p, m=m: g.matmul(p[:, :W], lhsT=wglu[:, c, m * 128:(m + 1) * 128],
                                                         rhs=ybb[:, c, :W], start=(c == 0), stop=(c == 3)),
                     [wglu, ybb], [p])
            k.op("act", lambda g, p=p: g.activation(out=sig[:, :W], in_=p[:, :W], func=AF.Sigmoid), [p], [sig])
            k.op("dve", lambda g, m=m: g.tensor_tensor(out=y2[:, m, :W], in0=yb[:, m, :W], in1=sig[:, :W],
                                                   op=ALU.mult), [yb, sig], [y2])
        rms_stats(k, ones_4, y2, 4, W, pstat, sqs, rstd)
        for c in range(4):
            k.op("dve", lambda g, c=c: g.scalar_tensor_tensor(
                out=ain[:, 4 + c, :W], in0=y2[:, c, :W], scalar=gs[:, c:c + 1], in1=rstd[:, :W],
                op0=ALU.mult, op1=ALU.mult), [y2, gs, rstd], [ain])
        k.dma("pool", ain[:, 8:16, :W], ycT.rearrange("(c p) t -> p c t", p=128)[:, :, t0:t0 + W], t_out=ain)
        for mg in range(4):
            w = next_w()
            k.dma("pool", w[:], woutv[:, :, mg * 512:(mg + 1) * 512], t_out=w)
            for mm in range(4):
                m = mg * 4 + mm
                p = next_p()
                for c in range(C):
                    k.op("pe", lambda g, c=c, p=p, mm=mm, w=w: g.matmul(
                        p[:, :W], lhsT=w[:, c, mm * 128:(mm + 1) * 128], rhs=ain[:, c, :W],
                        start=(c == 0), stop=(c == C - 1)), [w, ain], [p])
                k.op("dve", lambda g, m=m, p=p: g.tensor_tensor(out=xt[:, m, :W], in0=xt[:, m, :W], in1=p[:, :W],
                                                            op=ALU.add), [xt, p], [xt])
        rms_stats(k, ones_d, xt, C, W, pstat, sqs, rstd)
        for c in range(C):
            k.op("dve", lambda g, c=c: g.scalar_tensor_tensor(
                out=ain[:, c, :W], in0=xt[:, c, :W], scalar=gf[:, c:c + 1], in1=rstd[:, :W],
                op0=ALU.mult, op1=ALU.mult), [xt, gf, rstd], [ain])
        ngrp = (NJ + 3) // 4
        en = 0
        for jg in range(ngrp):
            j0 = jg * 4
            nj = min(4, NJ - j0)
            wg = next_w()
            k.dma("pool", wg[:, :, :nj * 128], wfiv[:, :, j0 * 128:(j0 + nj) * 128], t_out=wg)
            wu = next_w()
            k.dma("pool", wu[:, :, :nj * 128], wfiv[:, :, D_FF + j0 * 128:D_FF + (j0 + nj) * 128], t_out=wu)
            for jj in range(nj):
                j = j0 + jj
                res = []
                for which, w in ((0, wg), (1, wu)):
                    idx = which * NJ + j
                    p = next_p()
                    for c in range(C):
                        k.op("pe", lambda g, c=c, p=p, jj=jj, w=w: g.matmul(
                            p[:, :W], lhsT=w[:, c, jj * 128:(jj + 1) * 128], rhs=ain[:, c, :W],
                            start=(c == 0), stop=(c == C - 1)), [w, ain], [p])
                    e = ext[en % len(ext)]
                    en += 1
                    k.op("pool", lambda g, e=e, idx=idx: g.tensor_copy(out=e[:, 0:2], in_=tails[:, idx, :]),
                         [tails], [e])
                    k.op("act", lambda g, e=e, p=p: g.copy(out=e[:, 2:2 + W], in_=p[:, :W]), [p], [e])
                    k.op("pool", lambda g, e=e, idx=idx: g.tensor_copy(out=tails[:, idx, :], in_=e[:, W:W + 2]),
                         [e], [tails])
                    if halo:
                        continue
                    dst = (gt if which == 0 else ut)[j % 2]
                    k.op("dve", lambda g, e=e, idx=idx, dst=dst: g.tensor_scalar(
                        out=dst[:, :W], in0=e[:, 2:2 + W], scalar1=cw[:, 2, idx:idx + 1], scalar2=cb[:, idx:idx + 1],
                        op0=ALU.mult, op1=ALU.add), [e, cw, cb], [dst])
                    k.op("dve", lambda g, e=e, idx=idx, dst=dst: g.scalar_tensor_tensor(
                        out=dst[:, :W], in0=e[:, 1:1 + W], scalar=cw[:, 1, idx:idx + 1], in1=dst[:, :W],
                        op0=ALU.mult, op1=ALU.add), [e, cw, dst], [dst])
                    k.op("dve", lambda g, e=e, idx=idx, dst=dst: g.scalar_tensor_tensor(
                        out=dst[:, :W], in0=e[:, 0:W], scalar=cw[:, 0, idx:idx + 1], in1=dst[:, :W],
                        op0=ALU.mult, op1=ALU.add), [e, cw, dst], [dst])
                    res.append(dst)
                if halo:
                    continue
                gtile, utile = res
                k.op("act", lambda g, gtile=gtile: g.activation(out=gtile[:, :W], in_=gtile[:, :W], func=AF.Silu),
                     [gtile], [gtile])
                k.op("pool", lambda g, gtile=gtile, utile=utile, j=j: g.tensor_tensor(
                    out=act[:, j, :W], in0=gtile[:, :W], in1=utile[:, :W], op=ALU.mult), [gtile, utile], [act])
        if halo:
            continue
        jgroups = [(0, 16), (16, 16), (32, NJ - 32)]
        for mg in range(4):
            ps4 = [next_p() for _ in range(4)]
            for gi, (ja, jn) in enumerate(jgroups):
                w = next_w()
                k.dma("pool", w[:, :jn, :], wfov[:, ja:ja + jn, mg * 512:(mg + 1) * 512], t_out=w)
                for mm in range(4):
                    p = ps4[mm]
                    for jj in range(jn):
                        j = ja + jj
                        k.op("pe", lambda g, p=p, jj=jj, j=j, mm=mm, w=w: g.matmul(
                            p[:, :W], lhsT=w[:, jj, mm * 128:(mm + 1) * 128], rhs=act[:, j, :W],
                            start=(j == 0), stop=(j == NJ - 1)), [w, act], [p])
            for mm in range(4):
                m = mg * 4 + mm
                o = ost[m % 2]
                k.op("dve", lambda g, m=m, o=o, p=ps4[mm]: g.tensor_tensor(
                    out=o[:, :W], in0=xt[:, m, :W], in1=p[:, :W], op=ALU.add), [xt, ps4[mm]], [o])
                k.dma("sp", out_d[m * 128:(m + 1) * 128, t0 - HALO:t0 - HALO + W], o[:, :W], t_in=o, final=True)
    k.finish()
    return nc


def lay128(v):
    v = np.asarray(v, np.float32)
    return np.ascontiguousarray(v.reshape(-1, 128).T)


def stage_C_inputs(inp, l, xT, yaT, ybT, ycT):
    NJ = D_FF // 128
    return {
        "xT": xT, "yaT": yaT, "ybT": ybT, "ycT": ycT,
        "g_nsa": lay128(inp["nsa_out_norm"][l]), "g_s5": lay128(inp["s5_out_norm"][l]),
        "g_ffn": lay128(inp["ffn_norm"][l]),
        "ffn_conv": np.ascontiguousarray(np.asarray(inp["ffn_conv"][l], np.float32).reshape(3, 2 * NJ, 128).transpose(2, 0, 1)),
        "ffn_conv_b": lay128(inp["ffn_conv_b"][l]),
        "w_glu": np.asarray(inp["s5_w_glu"][l], np.float32), "w_out": np.asarray(inp["w_out"][l], np.float32),
        "ffn_w_in": np.asarray(inp["ffn_w_in"][l], np.float32), "ffn_w_out": np.asarray(inp["ffn_w_out"][l], np.float32),
    }


GDN_C = 64
GDN_DEBUG = 3
GDN_IL = 4
GDN_PAR_STOP = 0
NEG = -1.0e6


class View:
    __slots__ = ("base", "ap")

    def __init__(self, base, ap):
        self.base = base
        self.ap = ap

    def __getitem__(self, idx):
        return self.ap[idx]


class Slots:
    def __init__(self, k, nbanks, width, name):
        banks = [k.ps([128, 512], F32, f"{name}{b}") for b in range(nbanks)]
        self.slots = []
        for j in range(512 // width):
            for bank in banks:
                self.slots.append(View(bank, bank.h[:, j * width:(j + 1) * width]))
        self.n = {}

    def get(self, part=0, nparts=1):
        sub = self.slots[part::nparts]
        c = self.n.get((part, nparts), 0)
        self.n[(part, nparts)] = c + 1
        return sub[c % len(sub)]


def gdn_consts():
    c = {}
    c["ident"] = np.eye(128, dtype=np.float32)
    ii = np.arange(64)
    c["maskc"] = np.where(ii[:, None] >= ii[None, :], 0.0, NEG).astype(np.float32)
    c["maskcT"] = np.ascontiguousarray(c["maskc"].T)
    c["strict01"] = (ii[:, None] > ii[None, :]).astype(np.float32)
    cm = np.zeros((33, 4), np.float32)
    cm[0, 0] = 1.0; cm[32, 1] = 1.0; cm[32, 2] = -1.0; cm[0, 3] = 1.0
    c["cm"] = cm
    sel = np.zeros((33, 2), np.float32); sel[0, 0] = 1.0; sel[32, 1] = 1.0
    c["sel"] = sel
    bsel = np.zeros((33, 128), np.float32); bsel[0, :] = 1.0
    c["bsel"] = bsel
    rm = np.ones((33, 512), np.float32); rm[:, ::64] = 0.0
    c["resetmask"] = rm
    return c


def build_gdn(L=SEQ):
    nc = bass.Bass("TRN2", target_bir_lowering=False)
    ST = 512
    NST = L // ST
    CH = GDN_C
    NCH = ST // CH
    din = {}

    def inp(name, shape):
        din[name] = nc.dram_tensor(name, list(shape), F32, kind="ExternalInput").ap()
        return din[name]

    qT_d = inp("g_qT", [128, L]); kT_d = inp("g_kT", [128, L]); vT_d = inp("g_vT", [128, L])
    z_d = inp("g_z", [L, 128]); a_d = inp("g_a33", [33, L]); b_d = inp("g_b33", [33, L])
    cw_d = inp("g_cw", [128, 3, 4]); prm_d = inp("g_prm33", [33, 2]); gain_d = inp("g_gain", [64, 128])
    cd = {n: inp("gc_" + n, v.shape) for n, v in gdn_consts().items()}
    out_d = nc.dram_tensor("g_out", [L, 128], F32, kind="ExternalOutput").ap()
    k = KB(nc)
    emit_gdn(k, L, qT_d, kT_d, vT_d, z_d, a_d, b_d, cw_d, prm_d, gain_d, cd, out_d)
    k.finish()
    return nc


def emit_gdn(k, L, qT_d, kT_d, vT_d, z_d, a_d, b_d, cw_d, prm_d, gain_d, cd, out_d):
    ST = 512
    NST = L // ST
    CH = GDN_C
    NCH = ST // CH

    def const(name, shape):
        t = k.sb(shape, F32, "c_" + name)
        k.dma("sp", t[:], cd[name], t_out=t)
        return t

    ident = const("ident", [128, 128]); maskc = const("maskc", [64, 64]); maskcT = const("maskcT", [64, 64])
    strict01 = const("strict01", [64, 64]); cm = const("cm", [33, 4]); sel = const("sel", [33, 2])
    bsel = const("bsel", [33, 128]); rmask = const("resetmask", [33, 512])
    cw = k.sb([128, 3, 4], F32, "cw"); k.dma("sp", cw[:], cw_d, t_out=cw)
    prm = k.sb([33, 2], F32, "prm"); k.dma("sp", prm[:], prm_d, t_out=prm)
    gain = k.sb([64, 128], F32, "gain"); k.dma("sp", gain[:], gain_d, t_out=gain)
    ones = k.sb([128, 128], F32, "ones")
    k.op("pool", lambda g: g.memset(ones[:], 1.0), [], [ones])
    nA = k.sb([33, 1], F32, "nA")
    k.op("act", lambda g: g.activation(out=nA[:], in_=prm[:, 0:1], func=AF.Exp), [prm], [nA])
    k.op("dve", lambda g: g.tensor_scalar(out=nA[:], in0=nA[:], scalar1=-1.0, scalar2=None, op0=ALU.mult), [nA], [nA])
    S = k.sb([128, 128], F32, "S")
    k.op("pool", lambda g: g.memset(S[:], 0.0), [], [S])

    ext = [[k.sb([128, ST + 3], F32, f"ext{w}{i}") for i in range(2)] for w in range(3)]
    for w in range(3):
        k.op("pool", lambda g, w=w: g.memset(ext[w][1][:, ST:ST + 3], 0.0), [], [ext[w][1]])
    cv = [[k.sb([128, ST], F32, f"cv{w}{i}") for i in range(2)] for w in range(3)]
    qn = [k.sb([128, ST], F32, f"qn{i}") for i in range(2)]
    kn = [k.sb([128, ST], F32, f"kn{i}") for i in range(2)]
    sq = k.sb([128, ST], F32, "sq")
    rs = k.sb([128, ST], F32, "rs")
    a33 = [k.sb([33, ST], F32, f"a33{i}") for i in range(2)]
    b33 = [k.sb([33, ST], F32, f"b33{i}") for i in range(2)]
    gc33 = k.sb([33, ST], F32, "gc33")
    U33 = [k.sb([33, ST], F32, f"U33{i}") for i in range(2)]
    V33 = [k.sb([33, ST], F32, f"V33{i}") for i in range(2)]
    RC = [k.sb([33, ST], F32, f"RC{i}") for i in range(2)]
    GB = [k.sb([128, ST], F32, f"GB{i}") for i in range(2)]
    EG = [k.sb([128, ST], F32, f"EG{i}") for i in range(2)]
    zt = [k.sb([64, NCH, 128], F32, f"zt{i}") for i in range(2)]
    yo = [k.sb([64, NCH, 128], F32, f"yo{i}") for i in range(2)]
    pbig = k.ps([128, 512], F32, "pbig")
    pstat = k.ps([128, 512], F32, "pstat")
    small = Slots(k, 2, 64, "ps")
    wide = Slots(k, 4, 128, "pw")
    NB = 2 * GDN_IL + 1
    mk = lambda shape, nm: [k.sb(shape, F32, f"{nm}{i}") for i in range(NB)]
    Dm = mk([64, 64], "Dm"); DTm = mk([64, 64], "DTm"); Ds = mk([64, 64], "Ds")
    Xa = mk([64, 64], "Xa"); Xb = mk([64, 64], "Xb"); Ya = mk([64, 64], "Ya"); Yb = mk([64, 64], "Yb")
    Pm = mk([64, 64], "Pm"); tmp64 = mk([64, 64], "tmp64"); tmp64b = mk([64, 64], "tmp64b")
    cols = mk([64, 8], "cols")
    kb = mk([64, 128], "kb"); kd = mk([64, 128], "kd"); vb = mk([64, 128], "vb")
    val = mk([64, 128], "val"); kcumT = mk([128, 64], "kcumT"); qkT = mk([64, 64], "qkT")
    qgT = mk([128, 64], "qgT"); vnew = mk([64, 128], "vnew"); junk = mk([64, 128], "junk")
    ycur = mk([64, 128], "ycur")

    qv = [qT_d, kT_d, vT_d]
    zv = z_d.rearrange("(n c) d -> c n d", c=CH)
    ov = out_d.rearrange("(n c) d -> c n d", c=CH)
    pending = []
    gi = [0]

    def supertile(s):
        b = s % 2
        t0 = s * ST
        for w in range(3):
            e = ext[w][b]
            k.dma("sp", e[:, 3:3 + ST], qv[w][:, t0:t0 + ST], t_out=e)
        k.dma("sp", a33[b][:], a_d[:, t0:t0 + ST], t_out=a33[b])
        k.dma("sp", b33[b][:], b_d[:, t0:t0 + ST], t_out=b33[b])
        k.dma("sp", zt[b][:], zv[:, s * NCH:(s + 1) * NCH, :], t_out=zt[b])
        for w in range(3):
            e = ext[w][b]; eo = ext[w][1 - b]; o = cv[w][b]
            k.op("pool", lambda g, e=e, eo=eo: g.tensor_copy(out=e[:, 0:3], in_=eo[:, ST:ST + 3]), [eo], [e])
            k.op("dve", lambda g, e=e, o=o, w=w: g.tensor_scalar(out=o[:], in0=e[:, 3:3 + ST], scalar1=cw[:, w, 3:4],
                                                              scalar2=None, op0=ALU.mult), [e, cw], [o])
            for j in range(3):
                k.op("dve", lambda g, e=e, o=o, w=w, j=j: g.scalar_tensor_tensor(
                    out=o[:], in0=e[:, j:j + ST], scalar=cw[:, w, j:j + 1], in1=o[:], op0=ALU.mult, op1=ALU.add),
                    [e, cw, o], [o])
            k.op("act", lambda g, o=o: g.activation(out=o[:], in_=o[:], func=AF.Silu), [o], [o])
        for w, dst, scl in ((0, qn[b], 128.0 ** -0.5), (1, kn[b], 1.0)):
            src = cv[w][b]
            k.op("act", lambda g, src=src: g.activation(out=sq[:], in_=src[:], func=AF.Square), [src], [sq])
            k.op("pe", lambda g: g.matmul(pstat[:], lhsT=ones[:], rhs=sq[:], start=True, stop=True), [ones, sq], [pstat])
            k.op("dve", lambda g: g.tensor_scalar(out=rs[:], in0=pstat[:], scalar1=EPS, scalar2=None, op0=ALU.add),
                 [pstat], [rs])
            k.op("act", lambda g: g.activation(out=rs[:], in_=rs[:], func=AF.Sqrt), [rs], [rs])
            k.op("dve", lambda g: g.reciprocal(out=rs[:], in_=rs[:]), [rs], [rs])
            k.op("dve", lambda g, src=src, dst=dst, scl=scl: g.scalar_tensor_tensor(
                out=dst[:], in0=src[:], scalar=scl, in1=rs[:], op0=ALU.mult, op1=ALU.mult), [src, rs], [dst])
        A = a33[b]; B = b33[b]
        k.op("act", lambda g: g.activation(out=A[:], in_=A[:], func=AF.Exp, bias=prm[:, 1:2], scale=1.0), [A, prm], [A])
        k.op("dve", lambda g: g.tensor_scalar(out=A[:], in0=A[:], scalar1=1.0, scalar2=None, op0=ALU.add), [A], [A])
        k.op("act", lambda g: g.activation(out=A[:], in_=A[:], func=AF.Ln), [A], [A])
        k.op("dve", lambda g: g.tensor_scalar(out=A[:], in0=A[:], scalar1=nA[:, 0:1], scalar2=None, op0=ALU.mult),
             [A, nA], [A])
        k.op("dve", lambda g: g.tensor_tensor_scan(out=gc33[:], data0=rmask[:], data1=A[:], initial=0.0,
                                                   op0=ALU.mult, op1=ALU.add), [rmask, A], [gc33])
        k.op("act", lambda g: g.activation(out=B[:], in_=B[:], func=AF.Sigmoid), [B], [B])
        k.op("dve", lambda g: g.tensor_scalar(out=U33[b][:], in0=gc33[:], scalar1=cm[:, 0:1], scalar2=cm[:, 1:2],
                                              op0=ALU.mult, op1=ALU.add), [gc33, cm], [U33[b]])
        k.op("dve", lambda g: g.tensor_scalar(out=V33[b][:], in0=gc33[:], scalar1=cm[:, 2:3], scalar2=cm[:, 3:4],
                                              op0=ALU.mult, op1=ALU.add), [gc33, cm], [V33[b]])
        k.op("dve", lambda g: g.tensor_scalar(out=RC[b][:], in0=gc33[:], scalar1=cm[:, 0:1], scalar2=None,
                                              op0=ALU.mult), [gc33, cm], [RC[b]])
        k.op("dve", lambda g: g.scalar_tensor_tensor(out=RC[b][:], in0=B[:], scalar=cm[:, 1:2], in1=RC[b][:],
                                                     op0=ALU.mult, op1=ALU.add), [B, cm, RC[b]], [RC[b]])
        k.op("pe", lambda g: g.matmul(pbig[:], lhsT=bsel[:], rhs=gc33[:], start=True, stop=True), [bsel, gc33], [pbig])
        k.op("dve", lambda g: g.tensor_copy(out=GB[b][:], in_=pbig[:]), [pbig], [GB[b]])
        k.op("act", lambda g: g.activation(out=EG[b][:], in_=pbig[:], func=AF.Exp), [pbig], [EG[b]])
        k.op("act", lambda g: g.activation(out=zt[b][:], in_=zt[b][:], func=AF.Silu), [zt[b]], [zt[b]])

    def par(s, ci, u=0):
        b = s % 2
        i = gi[0] % NB
        gi[0] += 1
        c0 = ci * CH
        cs = slice(c0, c0 + CH)
        kT = kn[b]; qT = qn[b]; vT = cv[2][b]
        U = U33[b]; V = V33[b]
        pc = small.get(u, GDN_IL)
        k.op("pe", lambda g: g.matmul(pc[0:64, 0:2], lhsT=RC[b][:, cs], rhs=sel[:], start=True, stop=True),
             [RC[b], sel], [pc])
        cl = cols[i]
        k.op("dve", lambda g: g.tensor_copy(out=cl[:, 0:2], in_=pc[0:64, 0:2]), [pc], [cl])
        k.op("act", lambda g: g.activation(out=cl[:, 2:3], in_=cl[:, 0:1], func=AF.Exp), [cl], [cl])
        k.op("dve", lambda g: g.tensor_tensor(out=cl[:, 3:4], in0=cl[:, 2:3], in1=cl[:, 1:2], op=ALU.mult), [cl], [cl])
        k.op("dve", lambda g: g.tensor_scalar(out=cl[:, 4:5], in0=cl[:, 1:2], scalar1=-1.0, scalar2=None, op0=ALU.mult),
             [cl], [cl])
        last = c0 + CH - 1
        k.op("act", lambda g: g.activation(out=cl[:, 5:6], in_=cl[:, 0:1], func=AF.Exp, scale=-1.0,
                                           bias=GB[b][0:64, last:last + 1]), [cl, GB[b]], [cl])
        if GDN_PAR_STOP == 1:
            return None
        pd = small.get(u, GDN_IL); pdT = small.get(u, GDN_IL)
        k.op("pe", lambda g: g.matmul(pd[0:64, :], lhsT=U[:, cs], rhs=V[:, cs], start=True, stop=True), [U, V], [pd])
        k.op("pe", lambda g: g.matmul(pdT[0:64, :], lhsT=V[:, cs], rhs=U[:, cs], start=True, stop=True), [U, V], [pdT])
        k.op("dve", lambda g: g.tensor_tensor(out=Dm[i][:], in0=pd[0:64, :], in1=maskc[:], op=ALU.add), [pd, maskc], [Dm[i]])
        k.op("act", lambda g: g.activation(out=Dm[i][:], in_=Dm[i][:], func=AF.Exp), [Dm[i]], [Dm[i]])
        k.op("dve", lambda g: g.tensor_tensor(out=DTm[i][:], in0=pdT[0:64, :], in1=maskcT[:], op=ALU.add),
             [pdT, maskcT], [DTm[i]])
        k.op("act", lambda g: g.activation(out=DTm[i][:], in_=DTm[i][:], func=AF.Exp), [DTm[i]], [DTm[i]])
        k.op("pool", lambda g: g.tensor_tensor(out=Ds[i][:], in0=Dm[i][:], in1=strict01[:], op=ALU.mult),
             [Dm[i], strict01], [Ds[i]])
        if GDN_PAR_STOP == 2:
            return None
        pkk = small.get(u, GDN_IL)
        k.op("pe", lambda g: g.matmul(pkk[0:64, :], lhsT=kT[:, cs], rhs=kT[:, cs], start=True, stop=True), [kT], [pkk])
        Y = Ya[i]; X = Xa[i]; Y2 = Yb[i]; X2 = Xb[i]
        k.op("dve", lambda g, Y=Y: g.scalar_tensor_tensor(out=Y[:], in0=pkk[0:64, :], scalar=cl[:, 4:5], in1=Ds[i][:],
                                                     op0=ALU.mult, op1=ALU.mult), [pkk, cl, Ds[i]], [Y])
        if GDN_PAR_STOP == 3:
            return None
        px = small.get(u, GDN_IL)
        k.op("pe", lambda g, Y=Y: g.matmul(px[0:64, :], lhsT=Y[:], rhs=ident[0:64, 0:64], start=True, stop=True), [Y, ident], [px])
        P = Pm[i]
        if GDN_PAR_STOP == 41:
            return None
        k.op("act", lambda g, X=X: g.copy(out=X[:], in_=px[0:64, :]), [px], [X])
        if GDN_PAR_STOP == 42:
            return None
        k.op("dve", lambda g, X=X: g.tensor_tensor(out=P[:], in0=X[:], in1=ident[0:64, 0:64], op=ALU.add),
             [X, ident], [P])
        if GDN_PAR_STOP == 4:
            return None
        for lvl in range(1, 6):
            lastl = (lvl == 5)
            py = small.get(u, GDN_IL)
            k.op("pe", lambda g, X=X, Y=Y, py=py: g.matmul(py[0:64, :], lhsT=X[:], rhs=Y[:], start=True, stop=True),
                 [X, Y], [py])
            if not lastl:
                pxx = small.get(u, GDN_IL)
                k.op("pe", lambda g, X=X, Y=Y, pxx=pxx: g.matmul(pxx[0:64, :], lhsT=Y[:], rhs=X[:], start=True, stop=True),
                     [X, Y], [pxx])
            k.op("act", lambda g, Y2=Y2, py=py: g.copy(out=Y2[:], in_=py[0:64, :]), [py], [Y2])
            if not lastl:
                k.op("dve", lambda g, X2=X2, pxx=pxx: g.tensor_copy(out=X2[:], in_=pxx[0:64, :]), [pxx], [X2])
            pp = small.get(u, GDN_IL)
            k.op("pe", lambda g, Y2=Y2, P=P, pp=pp: g.matmul(pp[0:64, :], lhsT=Y2[:], rhs=P[:], start=True, stop=True),
                 [Y2, P], [pp])
            k.op("dve", lambda g, P=P, pp=pp: g.tensor_tensor(out=P[:], in0=P[:], in1=pp[0:64, :], op=ALU.add),
                 [P, pp], [P])
            X, X2 = X2, X
            Y, Y2 = Y2, Y
        if GDN_PAR_STOP == 5:
            return None
        pk = wide.get(u, GDN_IL + 1)
        k.op("pe", lambda g: g.matmul(pk[0:64, :], lhsT=kT[:, cs], rhs=ident[:], start=True, stop=True), [kT, ident], [pk])
        k.op("dve", lambda g: g.tensor_scalar(out=kb[i][:], in0=pk[0:64, :], scalar1=cl[:, 3:4], scalar2=None,
                                              op0=ALU.mult), [pk, cl], [kb[i]])
        k.op("dve", lambda g: g.tensor_scalar(out=kd[i][:], in0=pk[0:64, :], scalar1=cl[:, 5:6], scalar2=None,
                                              op0=ALU.mult), [pk, cl], [kd[i]])
        pvv = wide.get(u, GDN_IL + 1)
        k.op("pe", lambda g: g.matmul(pvv[0:64, :], lhsT=vT[:, cs], rhs=ident[:], start=True, stop=True), [vT, ident], [pvv])
        k.op("dve", lambda g: g.tensor_scalar(out=vb[i][:], in0=pvv[0:64, :], scalar1=cl[:, 1:2], scalar2=None,
                                              op0=ALU.mult), [pvv, cl], [vb[i]])
        if GDN_PAR_STOP == 6:
            return None
        pval = wide.get(u, GDN_IL + 1)
        k.op("pe", lambda g: g.matmul(pval[0:64, :], lhsT=P[:], rhs=vb[i][:], start=True, stop=True), [P, vb[i]], [pval])
        k.op("act", lambda g: g.copy(out=val[i][:], in_=pval[0:64, :]), [pval], [val[i]])
        pkc = small.get(u, GDN_IL)
        k.op("pe", lambda g: g.matmul(pkc[:, :], lhsT=kb[i][:], rhs=P[:], start=True, stop=True), [kb[i], P], [pkc])
        k.op("act", lambda g: g.copy(out=kcumT[i][:], in_=pkc[:, :]), [pkc], [kcumT[i]])
        pqk = small.get(u, GDN_IL)
        k.op("pe", lambda g: g.matmul(pqk[0:64, :], lhsT=kT[:, cs], rhs=qT[:, cs], start=True, stop=True), [kT, qT], [pqk])
        k.op("dve", lambda g: g.tensor_tensor(out=qkT[i][:], in0=pqk[0:64, :], in1=DTm[i][:], op=ALU.mult),
             [pqk, DTm[i]], [qkT[i]])
        k.op("pool", lambda g: g.tensor_tensor(out=qgT[i][:], in0=qT[:, cs], in1=EG[b][:, cs], op=ALU.mult),
             [qT, EG[b]], [qgT[i]])
        return dict(i=i, b=b, ci=ci, s=s, last=last)

    def seq(d):
        i = d["i"]; b = d["b"]; ci = d["ci"]; last = d["last"]
        pv = wide.get(GDN_IL, GDN_IL + 1)
        k.op("pe", lambda g: g.matmul(pv[0:64, :], lhsT=kcumT[i][:], rhs=S[:], start=True, stop=True), [kcumT[i], S], [pv])
        k.op("dve", lambda g: g.tensor_tensor(out=vnew[i][:], in0=val[i][:], in1=pv[0:64, :], op=ALU.subtract),
             [val[i], pv], [vnew[i]])
        po = wide.get(GDN_IL, GDN_IL + 1)
        k.op("pe", lambda g: g.matmul(po[0:64, :], lhsT=qgT[i][:], rhs=S[:], start=True, stop=False), [qgT[i], S], [po])
        k.op("pe", lambda g: g.matmul(po[0:64, :], lhsT=qkT[i][:], rhs=vnew[i][:], start=False, stop=True),
             [qkT[i], vnew[i]], [po])
        pS = wide.get(GDN_IL, GDN_IL + 1)
        k.op("pe", lambda g: g.matmul(pS[:, :], lhsT=kd[i][:], rhs=vnew[i][:], start=True, stop=True), [kd[i], vnew[i]], [pS])
        k.op("dve", lambda g: g.scalar_tensor_tensor(out=S[:], in0=S[:], scalar=EG[b][:, last:last + 1], in1=pS[:, :],
                                                     op0=ALU.mult, op1=ALU.add), [S, EG[b], pS], [S])
        cl = cols[i]
        k.op("act", lambda g: g.activation(out=junk[i][:], in_=po[0:64, :], func=AF.Square, accum_out=cl[:, 6:7]),
             [po], [junk[i], cl])
        k.op("dve", lambda g: g.tensor_scalar(out=cl[:, 6:7], in0=cl[:, 6:7], scalar1=1.0 / 128.0, scalar2=EPS,
                                              op0=ALU.mult, op1=ALU.add), [cl], [cl])
        k.op("act", lambda g: g.activation(out=cl[:, 6:7], in_=cl[:, 6:7], func=AF.Sqrt), [cl], [cl])
        k.op("dve", lambda g: g.reciprocal(out=cl[:, 6:7], in_=cl[:, 6:7]), [cl], [cl])
        k.op("dve", lambda g: g.scalar_tensor_tensor(out=ycur[i][:], in0=po[0:64, :], scalar=cl[:, 6:7], in1=gain[:],
                                                     op0=ALU.mult, op1=ALU.mult), [po, cl, gain], [ycur[i]])
        k.op("pool", lambda g: g.tensor_tensor(out=yo[b][:, ci, :], in0=ycur[i][:], in1=zt[b][:, ci, :], op=ALU.mult),
             [ycur[i], zt[b]], [yo[b]])
        if ci == NCH - 1:
            s = d["s"]
            k.dma("sp", ov[:, s * NCH:(s + 1) * NCH, :], yo[b][:], t_in=yo[b], final=True)

    for s in range(NST):
        supertile(s)
        if GDN_DEBUG < 2:
            continue
        for ci in range(0, NCH, GDN_IL):
            recs = [k.record(par, s, ci + u, u) for u in range(GDN_IL)]
            k.emit_interleaved([r[1] for r in recs])
            if GDN_DEBUG < 3:
                continue
            for dd in pending:
                seq(dd)
            pending[:] = [r[0] for r in recs]
    if GDN_DEBUG >= 3:
        for dd in pending:
            seq(dd)


def gdn_inputs(projT, inp, l, h, L=SEQ):
    r = lambda a: np.ascontiguousarray(a, dtype=np.float32)
    q0, k0, v0, z0 = 1816, 2840, 3864, 4888
    a = projT[5912 + h, :L]
    b = projT[5920 + h, :L]
    a33 = np.zeros((33, L), np.float32); a33[0] = a; a33[32] = a
    b33 = np.zeros((33, L), np.float32); b33[0] = b; b33[32] = b
    conv = np.asarray(inp["gdn_conv"][l], np.float32)
    cw = np.stack([conv[:, w * 1024 + h * 128: w * 1024 + (h + 1) * 128].T for w in range(3)], axis=1)
    prm = np.zeros((33, 2), np.float32)
    prm[:, 0] = inp["gdn_a_log"][l][h]; prm[:, 1] = inp["gdn_dt_bias"][l][h]
    d = {
        "g_qT": r(projT[q0 + h * 128:q0 + (h + 1) * 128, :L]), "g_kT": r(projT[k0 + h * 128:k0 + (h + 1) * 128, :L]),
        "g_vT": r(projT[v0 + h * 128:v0 + (h + 1) * 128, :L]), "g_z": r(projT[z0 + h * 128:z0 + (h + 1) * 128, :L].T),
        "g_a33": a33, "g_b33": b33, "g_cw": r(cw), "g_prm33": prm,
        "g_gain": r(np.tile(np.asarray(inp["gdn_norm"][l], np.float32)[None, :], (64, 1))),
    }
    for n, v in gdn_consts().items():
        d["gc_" + n] = v
    return d


import math
I32 = mybir.dt.int32
S5_W = 512


def build_s5(L=SEQ):
    nc = bass.Bass("TRN2", target_bir_lowering=False)
    W = S5_W
    d = {}

    def inp(name, shape):
        d[name] = nc.dram_tensor(name, list(shape), F32, kind="ExternalInput").ap()
        return d[name]

    inp("s_uT", [64, L]); inp("s_lamre", [128, 2]); inp("s_lamim", [128, 2]); inp("s_logdt", [128, 2])
    inp("s_bre", [2, 64, 128]); inp("s_bim", [2, 64, 128]); inp("s_cre", [2, 128, 64]); inp("s_cim", [2, 128, 64])
    inp("s_d", [64, 1]); inp("s_trow", [128, W + 1])
    out_d = nc.dram_tensor("s_out", [64, L], F32, kind="ExternalOutput").ap()
    k = KB(nc)
    emit_s5(k, L, d, out_d)
    k.finish()
    return nc


def emit_s5(k, L, d, out_d):
    W = S5_W
    NST = L // W
    TWO_PI = 2.0 * math.pi

    def load(name, shape):
        t = k.sb(shape, F32, name)
        k.dma("sp", t[:], d[name], t_out=t)
        return t

    lamre = load("s_lamre", [128, 2]); lamim = load("s_lamim", [128, 2]); logdt = load("s_logdt", [128, 2])
    trow = load("s_trow", [128, W + 1]); dsk = load("s_d", [64, 1])
    Bre = []; Bim = []; Cre = []; Cim = []
    for j in range(2):
        for lst, nm, shp in ((Bre, "s_bre", [64, 128]), (Bim, "s_bim", [64, 128]), (Cre, "s_cre", [128, 64]),
                             (Cim, "s_cim", [128, 64])):
            t = k.sb(shp, F32, f"{nm}{j}")
            k.dma("sp", t[:], d[nm][j], t_out=t)
            lst.append(t)
    ki = k.sb([128, W + 1], I32, "ki")
    kf = k.sb([128, W + 1], F32, "kf")
    ang = k.sb([128, W + 1], F32, "ang")
    sc = k.sb([128, 16], F32, "sc")
    Ct = []; St = []; nSt = []; Trt = []; Tit = []; rr = []; eW = []

    def reduce_inplace(t, n):
        k.op("dve", lambda g: g.tensor_scalar(out=ki[:, :n], in0=t[:, :n], scalar1=1.0 / TWO_PI, scalar2=None,
                                              op0=ALU.mult), [t], [ki])
        k.op("dve", lambda g: g.tensor_copy(out=kf[:, :n], in_=ki[:, :n]), [ki], [kf])
        k.op("dve", lambda g: g.scalar_tensor_tensor(out=t[:, :n], in0=kf[:, :n], scalar=-TWO_PI, in1=t[:, :n],
                                                     op0=ALU.mult, op1=ALU.add), [kf, t], [t])

    for j in range(2):
        C = k.sb([128, W + 1], F32, f"C{j}"); S = k.sb([128, W + 1], F32, f"S{j}"); nS = k.sb([128, W + 1], F32, f"nS{j}")
        Tr = k.sb([128, W], F32, f"Tr{j}"); Ti = k.sb([128, W], F32, f"Ti{j}")
        pr = k.sb([128, 16], F32, f"pr{j}")
        k.op("act", lambda g: g.activation(out=pr[:, 0:1], in_=logdt[:, j:j + 1], func=AF.Exp), [logdt], [pr])
        k.op("dve", lambda g: g.tensor_tensor(out=pr[:, 1:2], in0=lamim[:, j:j + 1], in1=pr[:, 0:1], op=ALU.mult),
             [lamim, pr], [pr])
        k.op("dve", lambda g: g.tensor_tensor(out=pr[:, 2:3], in0=lamre[:, j:j + 1], in1=pr[:, 0:1], op=ALU.mult),
             [lamre, pr], [pr])
        k.op("act", lambda g: g.activation(out=pr[:, 2:3], in_=pr[:, 2:3], func=AF.Exp), [pr], [pr])
        k.op("dve", lambda g: g.tensor_copy(out=ang[:, 0:1], in_=pr[:, 1:2]), [pr], [ang])
        reduce_inplace(ang, 1)
        k.op("dve", lambda g: g.tensor_copy(out=pr[:, 3:4], in_=ang[:, 0:1]), [ang], [pr])
        k.op("dve", lambda g: g.tensor_scalar(out=ang[:], in0=trow[:], scalar1=pr[:, 3:4], scalar2=None, op0=ALU.mult),
             [trow, pr], [ang])
        reduce_inplace(ang, W + 1)
        k.op("act", lambda g: g.activation(out=S[:], in_=ang[:], func=AF.Sin), [ang], [S])
        k.op("dve", lambda g: g.tensor_scalar(out=ang[:], in0=ang[:], scalar1=0.5 * math.pi, scalar2=None, op0=ALU.add),
             [ang], [ang])
        reduce_inplace(ang, W + 1)
        k.op("act", lambda g: g.activation(out=C[:], in_=ang[:], func=AF.Sin), [ang], [C])
        k.op("pool", lambda g: g.tensor_scalar(out=nS[:], in0=S[:], scalar1=-1.0, scalar2=None, op0=ALU.mult), [S], [nS])
        k.op("dve", lambda g: g.tensor_tensor(out=pr[:, 4:5], in0=pr[:, 2:3], in1=C[:, 1:2], op=ALU.mult), [pr, C], [pr])
        k.op("dve", lambda g: g.tensor_scalar(out=pr[:, 4:5], in0=pr[:, 4:5], scalar1=-1.0, scalar2=None, op0=ALU.add),
             [pr], [pr])
        k.op("dve", lambda g: g.tensor_tensor(out=pr[:, 5:6], in0=pr[:, 2:3], in1=S[:, 1:2], op=ALU.mult), [pr, S], [pr])
        k.op("dve", lambda g: g.tensor_tensor(out=pr[:, 6:7], in0=lamre[:, j:j + 1], in1=lamre[:, j:j + 1], op=ALU.mult),
             [lamre], [pr])
        k.op("dve", lambda g: g.scalar_tensor_tensor(out=pr[:, 6:7], in0=lamim[:, j:j + 1], scalar=lamim[:, j:j + 1],
                                                     in1=pr[:, 6:7], op0=ALU.mult, op1=ALU.add), [lamim, pr], [pr])
        k.op("dve", lambda g: g.reciprocal(out=pr[:, 6:7], in_=pr[:, 6:7]), [pr], [pr])
        k.op("dve", lambda g: g.tensor_tensor(out=pr[:, 7:8], in0=pr[:, 4:5], in1=lamre[:, j:j + 1], op=ALU.mult),
             [pr, lamre], [pr])
        k.op("dve", lambda g: g.scalar_tensor_tensor(out=pr[:, 7:8], in0=pr[:, 5:6], scalar=lamim[:, j:j + 1],
                                                     in1=pr[:, 7:8], op0=ALU.mult, op1=ALU.add), [pr, lamim], [pr])
        k.op("dve", lambda g: g.tensor_tensor(out=pr[:, 7:8], in0=pr[:, 7:8], in1=pr[:, 6:7], op=ALU.mult), [pr], [pr])
        k.op("dve", lambda g: g.tensor_tensor(out=pr[:, 9:10], in0=pr[:, 4:5], in1=lamim[:, j:j + 1], op=ALU.mult),
             [pr, lamim], [pr])
        k.op("dve", lambda g: g.scalar_tensor_tensor(out=pr[:, 8:9], in0=pr[:, 5:6], scalar=lamre[:, j:j + 1],
                                                     in1=pr[:, 9:10], op0=ALU.mult, op1=ALU.subtract), [pr, lamre], [pr])
        k.op("dve", lambda g: g.tensor_tensor(out=pr[:, 8:9], in0=pr[:, 8:9], in1=pr[:, 6:7], op=ALU.mult), [pr], [pr])
        k.op("dve", lambda g: g.tensor_scalar(out=Tr[:], in0=C[:, 0:W], scalar1=pr[:, 7:8], scalar2=None, op0=ALU.mult),
             [C, pr], [Tr])
        k.op("dve", lambda g: g.scalar_tensor_tensor(out=Tr[:], in0=S[:, 0:W], scalar=pr[:, 8:9], in1=Tr[:],
                                                     op0=ALU.mult, op1=ALU.add), [S, pr, Tr], [Tr])
        k.op("dve", lambda g: g.tensor_scalar(out=Ti[:], in0=C[:, 0:W], scalar1=pr[:, 8:9], scalar2=None, op0=ALU.mult),
             [C, pr], [Ti])
        k.op("dve", lambda g: g.scalar_tensor_tensor(out=Ti[:], in0=nS[:, 0:W], scalar=pr[:, 7:8], in1=Ti[:],
                                                     op0=ALU.mult, op1=ALU.add), [nS, pr, Ti], [Ti])
        Ct.append(C); St.append(S); nSt.append(nS); Trt.append(Tr); Tit.append(Ti); rr.append(pr)

    ub = [k.sb([64, W], F32, f"ub{i}") for i in range(2)]
    pP = [k.ps([128, W], F32, f"pP{i}") for i in range(4)]
    pY = [k.ps([128, W], F32, f"pY{i}") for i in range(2)]
    mkt = lambda nm, n=2: [k.sb([128, W], F32, f"{nm}{i}") for i in range(n)]
    prs = mkt("prs"); pis = mkt("pis"); m1 = mkt("m1"); m2 = mkt("m2"); m3 = mkt("m3"); m4 = mkt("m4")
    cr = mkt("cr"); ci = mkt("ci")
    vr = [mkt("vr0"), mkt("vr1")]; vi = [mkt("vi0"), mkt("vi1")]
    xr = mkt("xr"); xi = mkt("xi")
    init = [k.sb([128, 4], F32, f"init{j}") for j in range(2)]
    yb = [k.sb([64, W], F32, f"yb{i}") for i in range(2)]
    g1 = k.sb([64, W], F32, "g1"); g2 = k.sb([64, W], F32, "g2")
    pn = [0]
    for s in range(NST):
        u = ub[s % 2]
        k.dma("sp", u[:], d["s_uT"][:, s * W:(s + 1) * W], t_out=u)
        py = pY[s % 2]
        for j in range(2):
            b = (2 * s + j) % 2
            Pr = pP[pn[0] % 4]; Pi = pP[(pn[0] + 1) % 4]; pn[0] += 2
            k.op("pe", lambda g, Pr=Pr: g.matmul(Pr[:], lhsT=Bre[j][:], rhs=u[:], start=True, stop=True), [Bre[j], u], [Pr])
            k.op("pe", lambda g, Pi=Pi: g.matmul(Pi[:], lhsT=Bim[j][:], rhs=u[:], start=True, stop=True), [Bim[j], u], [Pi])
            k.op("act", lambda g, Pr=Pr: g.copy(out=prs[b][:], in_=Pr[:]), [Pr], [prs[b]])
            k.op("act", lambda g, Pi=Pi: g.copy(out=pis[b][:], in_=Pi[:]), [Pi], [pis[b]])
            Tr = Trt[j]; Ti = Tit[j]; C = Ct[j]; nS = nSt[j]; pr = rr[j]
            k.op("pool", lambda g: g.tensor_tensor(out=m1[b][:], in0=Tr[:], in1=prs[b][:], op=ALU.mult), [Tr, prs[b]], [m1[b]])
            k.op("pool", lambda g: g.tensor_tensor(out=m2[b][:], in0=Ti[:], in1=pis[b][:], op=ALU.mult), [Ti, pis[b]], [m2[b]])
            k.op("pool", lambda g: g.tensor_tensor(out=cr[b][:], in0=m1[b][:], in1=m2[b][:], op=ALU.subtract),
                 [m1[b], m2[b]], [cr[b]])
            k.op("dve", lambda g: g.tensor_tensor(out=m3[b][:], in0=Tr[:], in1=pis[b][:], op=ALU.mult), [Tr, pis[b]], [m3[b]])
            k.op("dve", lambda g: g.tensor_tensor(out=m4[b][:], in0=Ti[:], in1=prs[b][:], op=ALU.mult), [Ti, prs[b]], [m4[b]])
            k.op("dve", lambda g: g.tensor_tensor(out=ci[b][:], in0=m3[b][:], in1=m4[b][:], op=ALU.add),
                 [m3[b], m4[b]], [ci[b]])
            VR = vr[j][s % 2]; VI = vi[j][s % 2]
            rb = pr[:, 2:3].to_broadcast([128, W])
            if s == 0:
                ir = 0.0; ii_ = 0.0; extra = []
            else:
                ir = init[j][:, 0:1]; ii_ = init[j][:, 1:2]; extra = [init[j]]
            k.op("dve", lambda g, ir=ir: g.tensor_tensor_scan(out=VR[:], data0=rb, data1=cr[b][:], initial=ir,
                                                           op0=ALU.mult, op1=ALU.add), [pr, cr[b]] + extra, [VR])
            k.op("dve", lambda g, ii_=ii_: g.tensor_tensor_scan(out=VI[:], data0=rb, data1=ci[b][:], initial=ii_,
                                                             op0=ALU.mult, op1=ALU.add), [pr, ci[b]] + extra, [VI])
            it = init[j]
            k.op("dve", lambda g: g.tensor_tensor(out=it[:, 2:3], in0=VI[:, W - 1:W], in1=St[j][:, W:W + 1], op=ALU.mult),
                 [VI, St[j]], [it])
            k.op("dve", lambda g: g.scalar_tensor_tensor(out=it[:, 0:1], in0=VR[:, W - 1:W], scalar=C[:, W:W + 1],
                                                         in1=it[:, 2:3], op0=ALU.mult, op1=ALU.subtract), [VR, C, it], [it])
            k.op("dve", lambda g: g.tensor_tensor(out=it[:, 3:4], in0=VI[:, W - 1:W], in1=C[:, W:W + 1], op=ALU.mult),
                 [VI, C], [it])
            k.op("dve", lambda g: g.scalar_tensor_tensor(out=it[:, 1:2], in0=VR[:, W - 1:W], scalar=St[j][:, W:W + 1],
                                                         in1=it[:, 3:4], op0=ALU.mult, op1=ALU.add), [VR, St[j], it], [it])
            k.op("pool", lambda g: g.tensor_tensor(out=m1[b][:], in0=VR[:], in1=C[:, 0:W], op=ALU.mult), [VR, C], [m1[b]])
            k.op("pool", lambda g: g.tensor_tensor(out=m2[b][:], in0=VI[:], in1=nS[:, 0:W], op=ALU.mult), [VI, nS], [m2[b]])
            k.op("pool", lambda g: g.tensor_tensor(out=xr[b][:], in0=m1[b][:], in1=m2[b][:], op=ALU.add),
                 [m1[b], m2[b]], [xr[b]])
            k.op("dve", lambda g: g.tensor_tensor(out=m3[b][:], in0=VR[:], in1=nS[:, 0:W], op=ALU.mult), [VR, nS], [m3[b]])
            k.op("dve", lambda g: g.tensor_tensor(out=m4[b][:], in0=VI[:], in1=C[:, 0:W], op=ALU.mult), [VI, C], [m4[b]])
            k.op("dve", lambda g: g.tensor_tensor(out=xi[b][:], in0=m3[b][:], in1=m4[b][:], op=ALU.subtract),
                 [m3[b], m4[b]], [xi[b]])
            k.op("pe", lambda g: g.matmul(py[0:64, :], lhsT=Cre[j][:], rhs=xr[b][:], start=(j == 0), stop=False),
                 [Cre[j], xr[b]], [py])
            k.op("pe", lambda g: g.matmul(py[0:64, :], lhsT=Cim[j][:], rhs=xi[b][:], start=False, stop=(j == 1)),
                 [Cim[j], xi[b]], [py])
        y = yb[s % 2]
        k.op("dve", lambda g: g.scalar_tensor_tensor(out=g1[:], in0=u[:], scalar=dsk[:, 0:1], in1=py[0:64, :],
                                                     op0=ALU.mult, op1=ALU.add), [u, dsk, py], [g1])
        k.op("pool", lambda g: g.tensor_tensor(out=g2[:], in0=g1[:], in1=g1[:], op=ALU.mult), [g1], [g2])
        k.op("pool", lambda g: g.tensor_scalar(out=g2[:], in0=g2[:], scalar1=0.044715, scalar2=1.0, op0=ALU.mult, op1=ALU.add),
             [g2], [g2])
        k.op("pool", lambda g: g.tensor_tensor(out=g2[:], in0=g2[:], in1=g1[:], op=ALU.mult), [g2, g1], [g2])
        k.op("act", lambda g: g.activation(out=g2[:], in_=g2[:], func=AF.Tanh, scale=math.sqrt(2.0 / math.pi)), [g2], [g2])
        k.op("pool", lambda g: g.tensor_scalar(out=g2[:], in0=g2[:], scalar1=1.0, scalar2=0.5, op0=ALU.add, op1=ALU.mult),
             [g2], [g2])
        k.op("pool", lambda g: g.tensor_tensor(out=y[:], in0=g2[:], in1=g1[:], op=ALU.mult), [g2, g1], [y])
        k.dma("sp", out_d[:, s * W:(s + 1) * W], y[:], t_in=y, final=True)


def s5_inputs(projT, inp, l, core, L=SEQ):
    r = lambda a: np.ascontiguousarray(a, dtype=np.float32)
    g0 = core * 4
    f = lambda nm: np.asarray(inp[nm][l], np.float32)
    lamre = f("s5_lam_re")[g0:g0 + 4].reshape(2, 128).T
    lamim = f("s5_lam_im")[g0:g0 + 4].reshape(2, 128).T
    logdt = np.repeat(f("s5_log_dt")[g0:g0 + 4], 64).reshape(2, 128).T
    bre = np.zeros((2, 64, 128), np.float32); bim = np.zeros((2, 64, 128), np.float32)
    cre = np.zeros((2, 128, 64), np.float32); cim = np.zeros((2, 128, 64), np.float32)
    for j in range(2):
        for gl in range(2):
            g = g0 + 2 * j + gl
            ch = slice((2 * j + gl) * 16, (2 * j + gl + 1) * 16)
            st = slice(gl * 64, (gl + 1) * 64)
            bre[j, ch, st] = f("s5_b_re")[g].T
            bim[j, ch, st] = f("s5_b_im")[g].T
            cre[j, st, ch] = f("s5_c_re")[g].T
            cim[j, st, ch] = f("s5_c_im")[g].T
    return {
        "s_uT": r(projT[1304 + core * 64:1304 + (core + 1) * 64, :L]),
        "s_lamre": r(lamre), "s_lamim": r(lamim), "s_logdt": r(logdt),
        "s_bre": bre, "s_bim": bim, "s_cre": cre, "s_cim": cim,
        "s_d": r(f("s5_d")[core * 64:(core + 1) * 64].reshape(64, 1)),
        "s_trow": r(np.tile(np.arange(S5_W + 1, dtype=np.float32)[None], (128, 1))),
    }


ROPE_THETA = 500000.0


def rope_tables(pos):
    half = 8
    inv = (ROPE_THETA ** (-np.arange(half, dtype=np.float32) / half)).astype(np.float32)
    ang = pos.astype(np.float32)[None, :] * inv[:, None]
    c = np.ones((64, len(pos)), np.float32); s = np.zeros((64, len(pos)), np.float32)
    c[0:8] = np.cos(ang); c[8:16] = np.cos(ang)
    s[0:8] = np.sin(ang); s[8:16] = np.sin(ang)
    return c, s


def rope_perm():
    pm = np.zeros((64, 64), np.float32)
    for dd in range(8):
        pm[dd + 8, dd] = -1.0
        pm[dd, dd + 8] = 1.0
    return pm


def build_prep(ntok=TOK):
    nc = bass.Bass("TRN2", target_bir_lowering=False)
    d = {}

    def inp(name, shape):
        d[name] = nc.dram_tensor(name, list(shape), F32, kind="ExternalInput").ap()

    NB = ntok // 16
    inp("p_q", [8, 64, ntok]); inp("p_ks", [2, 64, ntok]); inp("p_kw", [2, 64, ntok])
    inp("p_kc", [2, 64, ntok + 16]); inp("p_vc", [2, 64, ntok + 16])
    inp("p_gains", [64, 4]); inp("p_cos", [64, ntok]); inp("p_sin", [64, ntok])
    inp("p_cosc", [64, NB]); inp("p_sinc", [64, NB]); inp("p_pm", [64, 64])
    inp("p_w1k", [64, 32, 256]); inp("p_w1v", [64, 32, 256]); inp("p_w2k", [128, 2, 64]); inp("p_w2v", [128, 2, 64])
    inp("p_peT", [64, 32])
    o = {}
    for name, shape in (("o_q", [8, 64, ntok]), ("o_ks", [2, 64, ntok]), ("o_kw", [2, 64, ntok]),
                        ("o_kc", [2, 64, NB]), ("o_vc", [NB, 2, 64])):
        o[name] = nc.dram_tensor(name, shape, F32, kind="ExternalOutput").ap()
    k = KB(nc)
    W = 512

    def load(name, shape):
        t = k.sb(shape, F32, name)
        k.dma("sp", t[:], d[name], t_out=t)
        return t

    gains = load("p_gains", [64, 4]); pm = load("p_pm", [64, 64])
    cosT = load("p_cos", [64, ntok]); sinT = load("p_sin", [64, ntok])
    cosc = load("p_cosc", [64, NB]); sinc = load("p_sinc", [64, NB])
    ones = k.sb([64, 64], F32, "ones64")
    k.op("pool", lambda g: g.memset(ones[:], 1.0), [], [ones])
    xin = [k.sb([64, W], F32, f"xin{i}") for i in range(3)]
    sq = k.sb([64, W], F32, "sq"); rs = k.sb([64, W], F32, "rs"); xn = [k.sb([64, W], F32, f"xn{i}") for i in range(2)]
    t1 = [k.sb([64, W], F32, f"t1{i}") for i in range(2)]
    ob = [k.sb([64, W], F32, f"ob{i}") for i in range(3)]
    pst = [k.ps([128, W], F32, f"pst{i}") for i in range(2)]
    prt = [k.ps([128, W], F32, f"prt{i}") for i in range(2)]
    cnt = [0]

    def norm_rope(src_t, src_ap, n, gcol, scale_eps, ones_val_scale, cos_ap, sin_ap, dst_t, dst_ap):
        i = cnt[0]; cnt[0] += 1
        ps = pst[i % 2]; pr = prt[i % 2]; x = xn[i % 2]; tt = t1[i % 2]
        k.op("act", lambda g: g.activation(out=sq[:, :n], in_=src_ap, func=AF.Square), [src_t], [sq])
        k.op("pe", lambda g: g.matmul(ps[0:64, :n], lhsT=ones[:], rhs=sq[:, :n], start=True, stop=True), [ones, sq], [ps])
        k.op("dve", lambda g: g.tensor_scalar(out=rs[:, :n], in0=ps[0:64, :n], scalar1=ones_val_scale, scalar2=scale_eps,
                                              op0=ALU.mult, op1=ALU.add), [ps], [rs])
        k.op("act", lambda g: g.activation(out=rs[:, :n], in_=rs[:, :n], func=AF.Sqrt), [rs], [rs])
        k.op("dve", lambda g: g.reciprocal(out=rs[:, :n], in_=rs[:, :n]), [rs], [rs])
        k.op("dve", lambda g: g.scalar_tensor_tensor(out=x[:, :n], in0=src_ap, scalar=gains[:, gcol:gcol + 1], in1=rs[:, :n],
                                                     op0=ALU.mult, op1=ALU.mult), [src_t, gains, rs], [x])
        k.op("pe", lambda g: g.matmul(pr[0:64, :n], lhsT=pm[:], rhs=x[:, :n], start=True, stop=True), [pm, x], [pr])
        k.op("dve", lambda g: g.tensor_tensor(out=tt[:, :n], in0=pr[0:64, :n], in1=sin_ap, op=ALU.mult), [pr, sinT, sinc], [tt])
        k.op("pool", lambda g: g.tensor_tensor(out=dst_ap, in0=x[:, :n], in1=cos_ap, op=ALU.mult), [x, cosT, cosc], [dst_t])
        k.op("pool", lambda g: g.tensor_tensor(out=dst_ap, in0=dst_ap, in1=tt[:, :n], op=ALU.add), [dst_t, tt], [dst_t])

    n_it = 0
    for name, oname, nh, gcol, isq in (("p_q", "o_q", 8, 0, True), ("p_ks", "o_ks", 2, 2, False), ("p_kw", "o_kw", 2, 3, False)):
        for h in range(nh):
            for s in range(ntok // W):
                xi = xin[n_it % 3]; oo = ob[n_it % 3]; n_it += 1
                k.dma("sp", xi[:], d[name][h, :, s * W:(s + 1) * W], t_out=xi)
                if isq:
                    a, bb = 1.0, 64.0 * EPS
                else:
                    a, bb = 1.0 / 64.0, EPS
                norm_rope(xi, xi[:], W, gcol, bb, a, cosT[:, s * W:(s + 1) * W], sinT[:, s * W:(s + 1) * W], oo, oo[:])
                k.dma("sp", o[oname][h, :, s * W:(s + 1) * W], oo[:], t_in=oo, final=True)

    peT = load("p_peT", [64, 32])
    w2k = load("p_w2k", [128, 2, 64]); w2v = load("p_w2v", [128, 2, 64])
    w1 = k.sb([64, 32, 256], F32, "w1")
    raw = [k.sb([64, ntok + 16], F32, f"raw{i}") for i in range(2)]
    gel = [k.sb([128, NB], F32, f"gel{i}") for i in range(2)]
    ga = k.sb([128, NB], F32, "ga"); gb = k.sb([128, NB], F32, "gb")
    bias = k.sb([128, 2], F32, "bias")
    kcn = k.sb([64, NB], F32, "kcn"); kco = k.sb([64, NB], F32, "kco")
    vco = k.sb([NB, 2, 64], F32, "vco")
    ph = [k.ps([128, 512], F32, f"ph{i}") for i in range(2)]
    pb = k.ps([128, 512], F32, "pb")
    po = k.ps([128, 512], F32, "po")
    for which, (rname, wname, w2) in enumerate((("p_kc", "p_w1k", w2k), ("p_vc", "p_w1v", w2v))):
        k.dma("sp", w1[:], d[wname], t_out=w1)
        for ft in range(2):
            for l in range(32):
                k.op("pe", lambda g, l=l, ft=ft: g.matmul(pb[:, ft:ft + 1], lhsT=w1[:, l, ft * 128:(ft + 1) * 128],
                                                        rhs=peT[:, l:l + 1], start=(l == 0), stop=(l == 31)), [w1, peT], [pb])
        k.op("dve", lambda g: g.tensor_copy(out=bias[:], in_=pb[:, 0:2]), [pb], [bias])
        for hk in range(2):
            r = raw[hk]
            k.dma("sp", r[:], d[rname][hk], t_out=r)
            for ft in range(2):
                p = ph[ft]
                for l in range(32):
                    k.op("pe", lambda g, l=l, ft=ft, p=p, r=r: g.matmul(
                        p[:, :NB], lhsT=w1[:, l, ft * 128:(ft + 1) * 128], rhs=r[:, l:l + 16 * (NB - 1) + 1:16],
                        start=(l == 0), stop=(l == 31)), [w1, r], [p])
                k.op("act", lambda g, p=p, ft=ft: g.activation(out=ga[:], in_=p[:, :NB], func=AF.Identity,
                                                               bias=bias[:, ft:ft + 1], scale=1.0), [p, bias], [ga])
                k.op("pool", lambda g: g.tensor_tensor(out=gb[:], in0=ga[:], in1=ga[:], op=ALU.mult), [ga], [gb])
                k.op("pool", lambda g: g.tensor_scalar(out=gb[:], in0=gb[:], scalar1=0.044715, scalar2=1.0, op0=ALU.mult,
                                                       op1=ALU.add), [gb], [gb])
                k.op("pool", lambda g: g.tensor_tensor(out=gb[:], in0=gb[:], in1=ga[:], op=ALU.mult), [gb, ga], [gb])
                k.op("act", lambda g: g.activation(out=gb[:], in_=gb[:], func=AF.Tanh, scale=math.sqrt(2.0 / math.pi)),
                     [gb], [gb])
                k.op("pool", lambda g: g.tensor_scalar(out=gb[:], in0=gb[:], scalar1=1.0, scalar2=0.5, op0=ALU.add,
                                                       op1=ALU.mult), [gb], [gb])
                k.op("pool", lambda g, ft=ft: g.tensor_tensor(out=gel[ft][:], in0=gb[:], in1=ga[:], op=ALU.mult),
                     [gb, ga], [gel[ft]])
            if which == 0:
                for ft in range(2):
                    k.op("pe", lambda g, ft=ft: g.matmul(po[0:64, :NB], lhsT=w2[:, ft, :], rhs=gel[ft][:],
                                                        start=(ft == 0), stop=(ft == 1)), [w2, gel[ft]], [po])
                k.op("dve", lambda g: g.tensor_copy(out=kcn[:], in_=po[0:64, :NB]), [po], [kcn])
                norm_rope(kcn, kcn[:], NB, 1, EPS, 1.0 / 64.0, cosc[:], sinc[:], kco, kco[:])
                k.dma("sp", o["o_kc"][hk], kco[:], t_in=kco, final=True)
            else:
                for ft in range(2):
                    k.op("pe", lambda g, ft=ft: g.matmul(po[0:NB, 0:64], lhsT=gel[ft][:], rhs=w2[:, ft, :],
                                                        start=(ft == 0), stop=(ft == 1)), [w2, gel[ft]], [po])
                k.op("dve", lambda g, hk=hk: g.tensor_copy(out=vco[:, hk, :], in_=po[0:NB, 0:64]), [po], [vco])
    k.dma("sp", o["o_vc"], vco[:], t_in=vco, final=True)
    k.finish()
    return nc


def prep_inputs(projT, inp, l, c, ntok=TOK):
    r = lambda a: np.ascontiguousarray(a, dtype=np.float32)
    t0 = c * ntok
    L = projT.shape[1]

    def halo(rows):
        a = np.zeros((rows.shape[0], ntok + 16), np.float32)
        n = min(ntok + 16, L - t0)
        a[:, :n] = rows[:, t0:t0 + n]
        return a.reshape(2, 64, ntok + 16)

    pos = np.arange(t0, t0 + ntok, dtype=np.float32)
    cos, sin = rope_tables(pos)
    posc = (np.arange(t0 // 16, t0 // 16 + ntok // 16) * 16 + 16).astype(np.float32)
    cosc, sinc = rope_tables(posc)
    f = lambda nm: np.asarray(inp[nm][l], np.float32)
    gains = np.stack([f("nsa_q_norm"), f("nsa_kc_norm"), f("nsa_ks_norm"), f("nsa_kw_norm")], axis=1)
    return {
        "p_q": r(projT[0:512, t0:t0 + ntok].reshape(8, 64, ntok)),
        "p_ks": r(projT[768:896, t0:t0 + ntok].reshape(2, 64, ntok)),
        "p_kw": r(projT[1024:1152, t0:t0 + ntok].reshape(2, 64, ntok)),
        "p_kc": halo(projT[512:640]), "p_vc": halo(projT[640:768]),
        "p_gains": r(gains), "p_cos": cos, "p_sin": sin, "p_cosc": cosc, "p_sinc": sinc, "p_pm": rope_perm(),
        "p_w1k": r(f("cmp_k_w1").transpose(1, 0, 2)), "p_w1v": r(f("cmp_v_w1").transpose(1, 0, 2)),
        "p_w2k": r(f("cmp_k_w2").reshape(2, 128, 64).transpose(1, 0, 2)),
        "p_w2v": r(f("cmp_v_w2").reshape(2, 128, 64).transpose(1, 0, 2)),
        "p_peT": r(f("cmp_pe").T),
    }


NSLOT = 16
MASKV = -30000.0
BIGNEG = -1.0e30


def build_nsa(nslots=NSLOT, L=SEQ):
    nc = bass.Bass("TRN2", target_bir_lowering=False)
    d = {}

    def inp(name, shape):
        d[name] = nc.dram_tensor(name, list(shape), F32, kind="ExternalInput").ap()

    NT = L // 128
    inp("n_ksT", [2, 64, L]); inp("n_vs", [L, 2, 65]); inp("n_kcT", [2, 64, 1024]); inp("n_vc", [1024, 2, 65])
    inp("n_qT", [NSLOT, 64, 8, 128]); inp("n_kw", [NSLOT, 2, 64, 640]); inp("n_vw", [NSLOT, 640, 2, 65])
    inp("n_gates", [NSLOT, 128, 24])
    inp("m_diag", [8, 128, 512]); inp("m_win", [NSLOT, 5, 128, 512]); inp("m_cmpT", [NSLOT, 2, 128, 512])
    inp("m_cmpqn", [NSLOT, 128, 1024]); inp("r_tab", [NSLOT, 128, 2, 256])
    inp("c_ident", [128, 128]); inp("c_expand", [128, 64 * 128])
    out_d = nc.dram_tensor("n_out", [NSLOT, 128, 512], F32, kind="ExternalOutput").ap()
    k = KB(nc)
    emit_nsa(k, d, out_d, nslots, L)
    k.finish()
    return nc


def emit_nsa(k, d, out_d, nslots, L):
    NT = L // 128
    ksT = k.sb([64, 2, L], BF16, "ksT")
    for hk in range(2):
        for c4 in range(4):
            sl = slice(c4 * (L // 4), (c4 + 1) * (L // 4))
            k.dma("pool", ksT[:, hk, sl], d["n_ksT"][hk][:, sl], t_out=ksT)
    vs = k.sb([128, NT, 2, 65], BF16, "vs")
    vsv = d["n_vs"].rearrange("(t p) h e -> p t h e", p=128)
    for c4 in range(8):
        sl = slice(c4 * (NT // 8), (c4 + 1) * (NT // 8))
        k.dma("pool", vs[:, sl], vsv[:, sl], t_out=vs)
    kcT = k.sb([64, 2, 1024], BF16, "kcT")
    for hk in range(2):
        k.dma("pool", kcT[:, hk, :], d["n_kcT"][hk], t_out=kcT)
    vc = k.sb([128, 8, 2, 65], BF16, "vc")
    k.dma("pool", vc[:], d["n_vc"].rearrange("(t p) h e -> p t h e", p=128), t_out=vc)
    mdiag = k.sb([128, 8, 512], BF16, "mdiag")
    k.dma("pool", mdiag[:], d["m_diag"].rearrange("r p f -> p r f"), t_out=mdiag)
    expand = k.sb([128, 64 * 128], BF16, "expand")
    k.dma("pool", expand[:], d["c_expand"], t_out=expand)
    ident = k.sb([128, 128], F32, "ident")
    k.dma("sp", ident[:], d["c_ident"], t_out=ident)

    two = lambda shape, dt, nm: [k.sb(shape, dt, f"{nm}{i}") for i in range(2)]
    qT = two([64, 8, 128], BF16, "qT"); kw = two([64, 2, 640], BF16, "kw"); vw = two([128, 5, 2, 65], BF16, "vw")
    mwin = two([128, 5, 512], BF16, "mwin"); mcT = two([128, 2, 512], BF16, "mcT"); mqn = two([128, 1024], BF16, "mqn")
    rtab = two([128, 2, 256], F32, "rtab"); gates = two([128, 24], F32, "gates")
    ya = two([128, 512], F32, "ya")
    negT4 = two([128, 2, 4, 128], BF16, "negT4")
    E = [k.sb([128, 512], BF16, f"E{i}") for i in range(3)]
    Ecmp = [k.sb([128, 1024], F32, f"Ecmp{i}") for i in range(2)]
    Em = k.sb([128, 1024], F32, "Em")
    pcs = k.sb([128, 1032], F32, "pcs")
    imp = k.sb([128, 256], F32, "imp"); ieff = k.sb([128, 256], F32, "ieff")
    zap1 = k.sb([128, 256], F32, "zap1"); zap2 = k.sb([128, 256], F32, "zap2"); negm = k.sb([128, 256], F32, "negm")
    mx8 = k.sb([128, 8], F32, "mx8")
    sm = k.sb([128, 16], F32, "sm")
    cf = k.sb([128, 8], F32, "cf")
    pS = [k.ps([128, 512], F32, f"pS{i}") for i in range(2)]
    pO = [k.ps([128, 512], F32, f"pO{i}") for i in range(4)]
    pC = [k.ps([128, 512], F32, f"pC{i}") for i in range(2)]
    pT = pC[0]
    cnt = {"s": 0, "e": 0, "o": 0, "ec": 0}

    def slot_loads(i):
        b = i % 2
        k.dma("pool", qT[b][:], d["n_qT"][i], t_out=qT[b])
        k.dma("pool", kw[b][:], d["n_kw"][i].rearrange("h d n -> d h n"), t_out=kw[b])
        k.dma("pool", vw[b][:], d["n_vw"][i].rearrange("(t p) h e -> p t h e", p=128), t_out=vw[b])
        k.dma("pool", mwin[b][:], d["m_win"][i].rearrange("w p f -> p w f"), t_out=mwin[b])
        k.dma("pool", mcT[b][:], d["m_cmpT"][i].rearrange("w p f -> p w f"), t_out=mcT[b])
        k.dma("pool", mqn[b][:], d["m_cmpqn"][i], t_out=mqn[b])
        k.dma("sp", rtab[b][:], d["r_tab"][i], t_out=rtab[b])
        k.dma("sp", gates[b][:], d["n_gates"][i], t_out=gates[b])
        k.op("act", lambda g: g.activation(out=gates[b][:], in_=gates[b][:], func=AF.Sigmoid), [gates[b]], [gates[b]])

    def stage1(i, hk):
        b = i % 2
        x = (2 * i + hk) % 2
        NV = 64 * i + 64
        NJ = 16 * i + 16
        k.op("pool", lambda g: g.memset(pcs[:], 0.0), [], [pcs])
        k.op("pool", lambda g: g.memset(imp[:], 0.0), [], [imp])
        for gq in range(4):
            h = hk * 4 + gq
            ec = Ecmp[cnt["ec"] % 2]; cnt["ec"] += 1
            for c0 in range(0, NV, 512):
                cw = min(512, NV - c0)
                p = pC[(c0 // 512) % 2]
                k.op("pe", lambda g, p=p, c0=c0, cw=cw, h=h: g.matmul(p[:, :cw], lhsT=qT[b][:, h, :], rhs=kcT[:, hk, c0:c0 + cw],
                                                                   start=True, stop=True), [qT[b], kcT], [p])
                k.op("act", lambda g, p=p, c0=c0, cw=cw, ec=ec: g.activation(out=ec[:, c0:c0 + cw], in_=p[:, :cw], func=AF.Exp),
                     [p], [ec])
            k.op("dve", lambda g, ec=ec, gq=gq: g.scalar_tensor_tensor(
                out=Em[:, :NV], in0=ec[:, :NV], scalar=1.0, in1=mqn[b][:, :NV], op0=ALU.mult, op1=ALU.mult,
                accum_out=sm[:, gq:gq + 1]), [ec, mqn[b]], [Em, sm])
            k.op("dve", lambda g, gq=gq: g.tensor_scalar(out=sm[:, 4 + gq:5 + gq], in0=sm[:, gq:gq + 1], scalar1=1e-30,
                                                         scalar2=None, op0=ALU.max), [sm], [sm])
            k.op("dve", lambda g, gq=gq: g.reciprocal(out=sm[:, 4 + gq:5 + gq], in_=sm[:, 4 + gq:5 + gq]), [sm], [sm])
            if gq == 0:
                k.op("dve", lambda g, gq=gq: g.tensor_scalar(out=pcs[:, 1:1 + NV], in0=Em[:, :NV], scalar1=sm[:, 4 + gq:5 + gq],
                                                             scalar2=None, op0=ALU.mult), [Em, sm], [pcs])
            else:
                k.op("dve", lambda g, gq=gq: g.scalar_tensor_tensor(out=pcs[:, 1:1 + NV], in0=Em[:, :NV],
                                                                    scalar=sm[:, 4 + gq:5 + gq], in1=pcs[:, 1:1 + NV],
                                                                    op0=ALU.mult, op1=ALU.add), [Em, sm, pcs], [pcs])
        vw_ = lambda r: pcs[:, r:r + 4 * (NJ - 1) + 1:4]
        k.op("dve", lambda g: g.tensor_tensor(out=imp[:, :NJ], in0=vw_(0), in1=vw_(1), op=ALU.add), [pcs], [imp])
        for r in (2, 3, 4):
            k.op("dve", lambda g, r=r: g.tensor_tensor(out=imp[:, :NJ], in0=imp[:, :NJ], in1=vw_(r), op=ALU.add), [pcs, imp], [imp])
        k.op("dve", lambda g: g.tensor_tensor(out=ieff[:], in0=imp[:], in1=rtab[b][:, 0, :], op=ALU.add), [imp, rtab[b]], [ieff])
        k.op("dve", lambda g: g.tensor_tensor(out=ieff[:], in0=ieff[:], in1=rtab[b][:, 1, :], op=ALU.max), [ieff, rtab[b]], [ieff])
        k.op("dve", lambda g: g.max(out=mx8[:], in_=ieff[:]), [ieff], [mx8])
        k.op("dve", lambda g: g.match_replace(out=zap1[:], in_to_replace=mx8[:], in_values=ieff[:], imm_value=BIGNEG),
             [mx8, ieff], [zap1])
        k.op("dve", lambda g: g.max(out=mx8[:], in_=zap1[:]), [zap1], [mx8])
        k.op("dve", lambda g: g.match_replace(out=zap2[:], in_to_replace=mx8[:], in_values=zap1[:], imm_value=BIGNEG),
             [mx8, zap1], [zap2])
        k.op("dve", lambda g: g.tensor_tensor(out=negm[:], in0=ieff[:], in1=zap2[:], op=ALU.subtract), [ieff, zap2], [negm])
        k.op("dve", lambda g: g.tensor_scalar(out=negm[:], in0=negm[:], scalar1=1.0, scalar2=None, op0=ALU.min), [negm], [negm])
        k.op("dve", lambda g: g.tensor_scalar(out=negm[:], in0=negm[:], scalar1=-MASKV, scalar2=MASKV, op0=ALU.mult,
                                              op1=ALU.add), [negm], [negm])
        for half in range(2):
            k.op("pe", lambda g, half=half: g.matmul(pT[:, half * 128:(half + 1) * 128], lhsT=negm[:, half * 128:(half + 1) * 128],
                                                    rhs=ident[:], start=True, stop=True), [negm, ident], [pT])
        n4 = negT4[x]
        for half in range(2):
            for gq in range(4):
                e = "act" if (gq % 2 == 0) else "dve"
                k.copy(e, n4, n4[:, half, gq, :], pT, pT[:, half * 128:(half + 1) * 128])
        return x

    def attend(tiles, q4, q4_t, po):
        nt = len(tiles)
        es = []

        def pv(idx):
            tl = tiles[idx]; e = es[idx]
            for gq in range(4):
                k.op("pe", lambda g, e=e, gq=gq, tl=tl: g.matmul(po[gq][:, 0:65], lhsT=e[:, gq * 128:(gq + 1) * 128], rhs=tl["V"],
                                                              start=(idx == 0), stop=(idx == nt - 1)), [e, tl["Vt"]], [po[gq]])

        for idx, tl in enumerate(tiles):
            p = pS[cnt["s"] % 2]; cnt["s"] += 1
            e = E[cnt["e"] % 3]; cnt["e"] += 1
            es.append(e)
            add = tl.get("add")
            k.op("pe", lambda g, p=p, tl=tl: g.matmul(p[:], lhsT=tl["K"], rhs=q4, start=True, stop=(tl.get("add") is None)),
                 [tl["Kt"], q4_t], [p])
            if add is not None:
                k.op("pe", lambda g, p=p, add=add: g.matmul(p[:], lhsT=add[0], rhs=add[1], start=False, stop=True),
                     [expand, add[2]], [p])
            k.op("act", lambda g, p=p, e=e: g.activation(out=e[:], in_=p[:], func=AF.Exp), [p], [e])
            mul = tl.get("mul")
            if mul is not None:
                k.op("pool", lambda g, e=e, mul=mul: g.tensor_tensor(out=e[:], in0=e[:], in1=mul[0], op=ALU.mult), [e, mul[1]], [e])
            if idx >= 1:
                pv(idx - 1)
        pv(nt - 1)

    def combine(i, hk, br, po):
        b = i % 2
        y = ya[b]
        for gq in range(4):
            h = hk * 4 + gq
            k.op("dve", lambda g, gq=gq: g.tensor_scalar(out=cf[:, 0:1], in0=po[gq][:, 64:65], scalar1=1e-30, scalar2=None,
                                                         op0=ALU.max), [po[gq]], [cf])
            k.op("dve", lambda g: g.reciprocal(out=cf[:, 0:1], in_=cf[:, 0:1]), [cf], [cf])
            k.op("dve", lambda g, h=h: g.tensor_tensor(out=cf[:, 1:2], in0=cf[:, 0:1], in1=gates[b][:, h * 3 + br:h * 3 + br + 1],
                                                       op=ALU.mult), [cf, gates[b]], [cf])
            if br == 0:
                k.op("dve", lambda g, gq=gq, h=h: g.tensor_scalar(out=y[:, h * 64:(h + 1) * 64], in0=po[gq][:, 0:64],
                                                                  scalar1=cf[:, 1:2], scalar2=None, op0=ALU.mult), [po[gq], cf], [y])
            else:
                k.op("dve", lambda g, gq=gq, h=h: g.scalar_tensor_tensor(out=y[:, h * 64:(h + 1) * 64], in0=po[gq][:, 0:64],
                                                                         scalar=cf[:, 1:2], in1=y[:, h * 64:(h + 1) * 64],
                                                                         op0=ALU.mult, op1=ALU.add), [po[gq], cf, y], [y])

    def stage2(i, hk, x):
        b = i % 2
        q4 = qT[b][:, hk * 4:(hk + 1) * 4, :]
        ntc = i // 2 + 1
        tiles = []
        for nt_ in range(ntc):
            tl = {"K": kcT[:, hk, nt_ * 128:(nt_ + 1) * 128], "Kt": kcT, "V": vc[:, nt_, hk, :], "Vt": vc}
            m = nt_ - (ntc - 2)
            if m >= 0:
                tl["mul"] = (mcT[b][:, m, :], mcT[b])
            tiles.append(tl)
        po = pO
        attend(tiles, q4, qT[b], po)
        combine(i, hk, 0, po)
        tiles = []
        for w in range(5):
            tl = {"K": kw[b][:, hk, w * 128:(w + 1) * 128], "Kt": kw[b], "V": vw[b][:, w, hk, :], "Vt": vw[b]}
            if i == 0 or w in (0, 4):
                tl["mul"] = (mwin[b][:, w, :], mwin[b])
            tiles.append(tl)
        po = pO
        attend(tiles, q4, qT[b], po)
        combine(i, hk, 2, po)
        tiles = []
        for kt in range(8 * i + 8):
            tl = {"K": ksT[:, hk, kt * 128:(kt + 1) * 128], "Kt": ksT, "V": vs[:, kt, hk, :], "Vt": vs,
                  "add": (expand[:, (kt % 64) * 128:(kt % 64 + 1) * 128], negT4[x][:, kt // 64, :, :], negT4[x])}
            if kt >= 8 * i:
                tl["mul"] = (mdiag[:, kt - 8 * i, :], mdiag)
            tiles.append(tl)
        po = pO
        attend(tiles, q4, qT[b], po)
        combine(i, hk, 1, po)
        if hk == 1:
            k.dma("sp", out_d[i], ya[b][:], t_in=ya[b], final=True)

    units = [(i, hk) for i in range(nslots) for hk in range(2)]
    slot_loads(0)
    xs = {}
    xs[units[0]] = stage1(*units[0])
    for ui, (i, hk) in enumerate(units):
        if ui + 1 < len(units):
            ni, nhk = units[ui + 1]
            if nhk == 0:
                slot_loads(ni)
            xs[(ni, nhk)] = stage1(ni, nhk)
        stage2(i, hk, xs[(i, hk)])


def nsa_consts():
    ident = np.eye(128, dtype=np.float32)
    ex = np.zeros((128, 64, 128), np.float32)
    for kt in range(64):
        ex[2 * kt, kt, 0:64] = 1.0
        ex[2 * kt + 1, kt, 64:128] = 1.0
    return ident, ex.reshape(128, 64 * 128)


def nsa_masks(c, L=SEQ):
    q = np.arange(128)
    key = np.arange(128)
    m_diag = np.zeros((8, 128, 512), np.float32)
    for r in range(8):
        if r < c:
            m_diag[r] = 1.0
        elif r == c:
            m_diag[r] = np.tile((key[:, None] <= q[None, :]).astype(np.float32), (1, 4))
    m_win = np.zeros((NSLOT, 5, 128, 512), np.float32)
    m_cmpT = np.zeros((NSLOT, 2, 128, 512), np.float32)
    m_cmpqn = np.zeros((NSLOT, 128, 1024), np.float32)
    r_tab = np.zeros((NSLOT, 128, 2, 256), np.float32)
    n_all = np.arange(1024)
    j = np.arange(256)
    for i in range(NSLOT):
        qb = 8 * i + c
        s = 128 * qb
        t = s + q
        for w in range(5):
            kpos = s - 512 + 128 * w + key
            ok = (kpos[:, None] <= t[None, :]) & (kpos[:, None] > t[None, :] - 512) & (kpos[:, None] >= 0)
            m_win[i, w] = np.tile(ok.astype(np.float32), (1, 4))
        ntc = i // 2 + 1
        for m in range(2):
            nt_ = ntc - 2 + m
            if nt_ < 0:
                continue
            n = 128 * nt_ + key
            ok = (16 * n[:, None] + 31 <= t[None, :]) & (n[:, None] <= 1022)
            m_cmpT[i, m] = np.tile(ok.astype(np.float32), (1, 4))
        m_cmpqn[i] = ((16 * n_all[None, :] + 31 <= t[:, None]) & (n_all[None, :] <= 1022)).astype(np.float32)
        cur = t // 64
        r_tab[i, :, 0, :] = np.where(j[None, :] <= cur[:, None], 0.0, BIGNEG)
        force = np.full((128, 256), 2 * BIGNEG, np.float32)
        force[:, 0] = 8.0
        force[q, cur] = 16.0
        prev = cur - 1
        okp = prev >= 0
        force[q[okp], prev[okp]] = 32.0
        r_tab[i, :, 1, :] = force
    return {"m_diag": m_diag, "m_win": m_win, "m_cmpT": m_cmpT, "m_cmpqn": m_cmpqn, "r_tab": r_tab}


def nsa_inputs(projT, prep, c, L=SEQ):
    r = lambda a: np.ascontiguousarray(a, dtype=np.float32)
    ones = lambda shp: np.ones(shp, np.float32)
    vs = projT[896:1024, :L].T.reshape(L, 2, 64)
    vs_aug = np.concatenate([vs, ones((L, 2, 1))], axis=2)
    kcT = np.zeros((2, 64, 1024), np.float32); kcT[:, :, :1023] = prep["kc"][:, :, :1023]
    vc_aug = np.zeros((1024, 2, 65), np.float32); vc_aug[:1023, :, :64] = prep["vc"][:1023]; vc_aug[:1023, :, 64] = 1.0
    kw_pad = np.concatenate([np.zeros((2, 64, 512), np.float32), prep["kw"]], axis=2)
    vw = projT[1152:1280, :L].T.reshape(L, 2, 64)
    vw_aug = np.concatenate([vw, ones((L, 2, 1))], axis=2)
    vw_pad = np.concatenate([np.zeros((512, 2, 65), np.float32), vw_aug], axis=0)
    gl = projT[1280:1304, :L].T
    qT = np.zeros((NSLOT, 64, 8, 128), np.float32); kws = np.zeros((NSLOT, 2, 64, 640), np.float32)
    vws = np.zeros((NSLOT, 640, 2, 65), np.float32); gts = np.zeros((NSLOT, 128, 24), np.float32)
    for i in range(NSLOT):
        qb = 8 * i + c
        s = 128 * qb
        qT[i] = prep["q"][:, :, s:s + 128].transpose(1, 0, 2)
        kws[i] = kw_pad[:, :, s:s + 640]
        vws[i] = vw_pad[s:s + 640]
        gts[i] = gl[s:s + 128]
    ident, ex = nsa_consts()
    dd = {"n_ksT": r(prep["ks"]), "n_vs": r(vs_aug), "n_kcT": kcT, "n_vc": vc_aug, "n_qT": qT, "n_kw": kws, "n_vw": vws,
          "n_gates": gts, "c_ident": ident, "c_expand": ex}
    dd.update(nsa_masks(c, L))
    return dd


_PROGS = {}


def _prog(name, builder):
    if name not in _PROGS:
        _PROGS[name] = builder()
    return _PROGS[name]


def _run(nc, maps):
    res = run_bass_kernel_spmd(nc, maps, core_ids=list(range(NCORES)))
    return res.results


def kernel(**inputs):
    inp = {k_: np.asarray(v) for k_, v in inputs.items()}
    L = SEQ
    xT = np.ascontiguousarray(inp["x"][0].T.astype(np.float32))
    for l in range(DEPTH):
        ncA = _prog("A", build_stage_A)
        gainA = lay128(inp["attn_norm"][l])
        wA = np.asarray(inp["w_in"][l], np.float32)
        res = _run(ncA, [{"xT": np.ascontiguousarray(xT[:, c * TOK:(c + 1) * TOK]), "gain": gainA, "w": wA}
                         for c in range(NCORES)])
        projT = np.concatenate([r["projT"] for r in res], axis=1)
        del res
        ncP = _prog("P", build_prep)
        res = _run(ncP, [prep_inputs(projT, inp, l, c) for c in range(NCORES)])
        prep = {
            "q": np.concatenate([r["o_q"] for r in res], axis=2),
            "ks": np.concatenate([r["o_ks"] for r in res], axis=2),
            "kw": np.concatenate([r["o_kw"] for r in res], axis=2),
            "kc": np.concatenate([r["o_kc"] for r in res], axis=2),
            "vc": np.concatenate([r["o_vc"] for r in res], axis=0),
        }
        del res
        ncN = _prog("N", build_nsa)
        res = _run(ncN, [nsa_inputs(projT, prep, c) for c in range(NCORES)])
        ya = np.zeros((L, 512), np.float32)
        for c in range(NCORES):
            o = res[c]["n_out"]
            for i in range(NSLOT):
                qb = 8 * i + c
                ya[qb * 128:(qb + 1) * 128] = o[i]
        del res, prep
        ncG = _prog("G", build_gdn)
        res = _run(ncG, [gdn_inputs(projT, inp, l, h) for h in range(NCORES)])
        ycT = np.concatenate([np.ascontiguousarray(r["g_out"].T) for r in res], axis=0)
        del res
        ncS = _prog("S", build_s5)
        res = _run(ncS, [s5_inputs(projT, inp, l, c) for c in range(NCORES)])
        ybT = np.concatenate([r["s_out"] for r in res], axis=0)
        del res, projT
        ncC = _prog("C", build_stage_C)

        def halo(a, c):
            out = np.zeros((a.shape[0], TOK + 2), np.float32)
            lo = c * TOK - 2
            if lo < 0:
                out[:, 2:] = a[:, 0:TOK]
            else:
                out[:] = a[:, lo:lo + TOK + 2]
            return out

        yaT = np.ascontiguousarray(ya.T)
        maps = [stage_C_inputs(inp, l, halo(xT, c), halo(yaT, c), halo(ybT, c), halo(ycT, c)) for c in range(NCORES)]
        res = _run(ncC, maps)
        xT = np.concatenate([r["xoT"] for r in res], axis=1)
        del res, maps
    return np.ascontiguousarray(xT.T)[None].astype(np.float32)
```

```python
from contextlib import ExitStack
import numpy as np
import concourse.bass as bass
import concourse.mybir as mybir
from concourse.bass_utils import run_bass_kernel_spmd

F32 = mybir.dt.float32
BF16 = mybir.dt.bfloat16
AF = mybir.ActivationFunctionType
ALU = mybir.AluOpType
AX = mybir.AxisListType

NCORES = 8
D_MODEL = 2048
SEQ = 16384
DEPTH = 2
IN_COLS = 5928
D_FF = 5504
EPS = 1e-6
TOK = SEQ // NCORES


class T:
    __slots__ = ("h", "name", "w", "r", "din", "dout", "sin", "sout", "psum")

    def __init__(self, h, name, psum=False):
        self.psum = psum
        self.h = h
        self.name = name
        self.w = None
        self.r = {}
        self.din = 0
        self.dout = 0
        self.sin = None
        self.sout = None

    def __getitem__(self, idx):
        return self.h[idx]


class KB:
    ENGS = ("pe", "act", "dve", "pool", "sp")

    def __init__(self, nc):
        self.nc = nc
        self.es = ExitStack()
        self.eng = {"pe": nc.tensor, "act": nc.scalar, "dve": nc.vector,
                    "pool": nc.gpsimd, "sp": nc.sync}
        self.sem = {k: self.es.enter_context(nc.semaphore("s_" + k)) for k in self.ENGS}
        self.cnt = {k: 0 for k in self.ENGS}
        self.known = {k: {} for k in self.ENGS}
        self.nsem = len(self.ENGS)
        self.ntile = 0
        self.out_waits = []
        self.rr = 0
        self.rec = None

    def sb(self, shape, dtype=F32, name=None):
        self.ntile += 1
        name = name or "t"
        h = self.es.enter_context(self.nc.sbuf_tensor(f"{name}_{self.ntile}", list(shape), dtype))
        return T(h, name)

    def ps(self, shape, dtype=F32, name=None):
        self.ntile += 1
        name = name or "p"
        h = self.es.enter_context(self.nc.psum_tensor(f"{name}_{self.ntile}", list(shape), dtype))
        return T(h, name, psum=True)

    def newsem(self, name):
        self.nsem += 1
        assert self.nsem < 145, "too many semaphores"
        return self.es.enter_context(self.nc.semaphore(f"{name}_{self.nsem}"))

    def _collect(self, e, reads, writes):
        needs = {}

        def need(sem, val):
            key = id(sem)
            if needs.get(key, (None, 0))[1] < val:
                needs[key] = (sem, val)

        for t in reads:
            if t.w is not None:
                need(self.sem[t.w[0]], t.w[1])
            if t.din:
                need(t.sin, t.din)
            if t.psum:
                for kk, c in t.r.items():
                    if kk != e:
                        need(self.sem[kk], c)
        for t in writes:
            if t.w is not None and not (t.w[0] == e and e == "pe"):
                need(self.sem[t.w[0]], t.w[1])
            for kk, c in t.r.items():
                need(self.sem[kk], c)
            if t.din:
                need(t.sin, t.din)
            if t.dout:
                need(t.sout, t.dout)
        kn = self.known[e]
        for key, (sem, val) in needs.items():
            if kn.get(key, 0) < val:
                self.eng[e].wait_ge(sem, val)
                kn[key] = val

    def op(self, e, fn, reads=(), writes=()):
        if self.rec is not None:
            self.rec.append((e, fn, list(reads), list(writes)))
            return None
        reads = [getattr(t, "base", t) for t in reads]
        writes = [getattr(t, "base", t) for t in writes]
        self._collect(e, reads, writes)
        ins = fn(self.eng[e])
        ins.then_inc(self.sem[e], 1)
        self.cnt[e] += 1
        c = self.cnt[e]
        self.known[e][id(self.sem[e])] = max(self.known[e].get(id(self.sem[e]), 0), 0)
        for t in writes:
            t.w = (e, c)
            t.r = {}
        for t in reads:
            if t not in writes:
                t.r[e] = c
        return ins

    def dma(self, q, out, in_, t_out=None, t_in=None, final=False, **kw):
        if self.rec is not None:
            self.rec.append(("dma", q, out, in_, dict(t_out=t_out, t_in=t_in, final=final, **kw)))
            return None
        reads = [t_in] if t_in is not None else []
        writes = [t_out] if t_out is not None else []
        self._collect(q, reads, writes)
        ins = self.eng[q].dma_start(out=out, in_=in_, **kw)
        if t_out is not None:
            if t_out.sin is None:
                t_out.sin = self.newsem("di")
            ins.then_inc(t_out.sin, 16)
            t_out.din += 16
            t_out.w = None
            t_out.r = {}
        if t_in is not None and t_out is None:
            if t_in.sout is None:
                t_in.sout = self.newsem("do")
            ins.then_inc(t_in.sout, 16)
            t_in.dout += 16
            if final and t_in not in self.out_waits:
                self.out_waits.append(t_in)
        return ins

    def finish(self):
        sp = self.eng["sp"]
        for t in self.out_waits:
            sp.wait_ge(t.sout, t.dout)
        for kk in self.ENGS:
            if kk != "sp" and self.cnt[kk]:
                sp.wait_ge(self.sem[kk], self.cnt[kk])
        self.es.close()

    def record(self, fn, *a, **kw):
        assert self.rec is None
        self.rec = []
        try:
            r = fn(*a, **kw)
        finally:
            lst, self.rec = self.rec, None
        return r, lst

    def emit_interleaved(self, lists):
        n = max(len(l) for l in lists)
        for j in range(n):
            for l in lists:
                if j < len(l):
                    it = l[j]
                    if it[0] == "dma":
                        self.dma(it[1], it[2], it[3], **it[4])
                    else:
                        self.op(*it)

    def evac_eng(self):
        self.rr += 1
        return ("act", "dve")[self.rr % 2]

    def copy(self, e, out_t, out_ap, in_t, in_ap):
        if e == "act":
            return self.op("act", lambda g: g.copy(out=out_ap, in_=in_ap), [in_t], [out_t])
        return self.op(e, lambda g: g.tensor_copy(out=out_ap, in_=in_ap), [in_t], [out_t])


def col_chunks(n0, n1):
    out = []
    c = n0
    while c < n1:
        w = min(128, n1 - c)
        out.append((c, w))
        c += w
    return out


def rms_stats(k, ones, src, nchunk, W, ps, sq_tiles, rstd, eps=EPS):
    for c in range(nchunk):
        sq = sq_tiles[c % len(sq_tiles)]
        k.op("act", lambda g, sq=sq, c=c: g.activation(out=sq[:, :W], in_=src[:, c, :W], func=AF.Square),
             [src], [sq])
        k.op("pe", lambda g, sq=sq, c=c: g.matmul(ps[:, :W], lhsT=ones[:], rhs=sq[:, :W],
                                               start=(c == 0), stop=(c == nchunk - 1)), [ones, sq], [ps])
    k.op("dve", lambda g: g.tensor_scalar(out=rstd[:, :W], in0=ps[:, :W], scalar1=eps, scalar2=None,
                                          op0=ALU.add), [ps], [rstd])
    k.op("act", lambda g: g.activation(out=rstd[:, :W], in_=rstd[:, :W], func=AF.Sqrt), [rstd], [rstd])
    k.op("dve", lambda g: g.reciprocal(out=rstd[:, :W], in_=rstd[:, :W]), [rstd], [rstd])


def build_stage_A(ntok=TOK):
    nc = bass.Bass("TRN2", target_bir_lowering=False)
    C = D_MODEL // 128
    TW = 512
    ntile = ntok // TW
    xT = nc.dram_tensor("xT", [D_MODEL, ntok], F32, kind="ExternalInput").ap()
    gain_d = nc.dram_tensor("gain", [128, C], F32, kind="ExternalInput").ap()
    w_d = nc.dram_tensor("w", [D_MODEL, IN_COLS], F32, kind="ExternalInput").ap()
    out_d = nc.dram_tensor("projT", [IN_COLS, ntok], F32, kind="ExternalOutput").ap()
    k = KB(nc)
    ones = k.sb([128, 128], F32, "ones")
    k.op("pool", lambda g: g.memset(ones[:], 1.0 / D_MODEL), [], [ones])
    gain = k.sb([128, C], F32, "gain")
    k.dma("sp", gain[:], gain_d, t_out=gain)
    hT = [k.sb([128, C, TW], BF16, f"hT{i}") for i in range(ntile)]
    xs = [k.sb([128, C, TW], F32, f"xs{i}") for i in range(2)]
    sqs = [k.sb([128, TW], F32, f"sq{i}") for i in range(2)]
    rstd = k.sb([128, TW], F32, "rstd")
    pstat = k.ps([128, TW], F32, "pstat")
    pacc = [k.ps([128, TW], F32, f"pacc{i}") for i in range(6)]
    xv = xT.rearrange("(c p) t -> p c t", p=128)
    wv = w_d.rearrange("(c p) n -> p c n", p=128)
    wb = [k.sb([128, C, 512], BF16, f"wb{i}") for i in range(2)]
    ost = [k.sb([128, TW], F32, f"ost{i}") for i in range(4)]
    groups = [(g0, min(512, IN_COLS - g0)) for g0 in range(0, IN_COLS, 512)]
    k.dma("pool", wb[0][:, :, :groups[0][1]], wv[:, :, 0:groups[0][1]], t_out=wb[0])
    for i in range(ntile):
        xt = xs[i % 2]
        k.dma("sp", xt[:], xv[:, :, i * TW:(i + 1) * TW], t_out=xt)
        rms_stats(k, ones, xt, C, TW, pstat, sqs, rstd)
        for c in range(C):
            k.op("dve", lambda g, c=c: g.scalar_tensor_tensor(
                out=hT[i][:, c, :], in0=xt[:, c, :], scalar=gain[:, c:c + 1], in1=rstd[:],
                op0=ALU.mult, op1=ALU.mult), [xt, gain, rstd], [hT[i]])
    n = 0
    for gi, (g0, gw) in enumerate(groups):
        w = wb[gi % 2]
        if gi + 1 < len(groups):
            g1, gw1 = groups[gi + 1]
            k.dma("pool", wb[(gi + 1) % 2][:, :, :gw1], wv[:, :, g1:g1 + gw1], t_out=wb[(gi + 1) % 2])
        for (c0, cw) in col_chunks(g0, g0 + gw):
            off = c0 - g0
            for i in range(ntile):
                p = pacc[n % len(pacc)]
                o = ost[n % len(ost)]
                n += 1
                for c in range(C):
                    k.op("pe", lambda g, c=c, p=p: g.matmul(p[:cw, :], lhsT=w[:, c, off:off + cw], rhs=hT[i][:, c, :],
                                                        start=(c == 0), stop=(c == C - 1)), [w, hT[i]], [p])
                k.copy(k.evac_eng(), o, o[:cw, :], p, p[:cw, :])
                k.dma("sp", out_d[c0:c0 + cw, i * TW:(i + 1) * TW], o[:cw, :], t_in=o, final=True)
    k.finish()
    return nc


def build_stage_C(ntok=TOK):
    nc = bass.Bass("TRN2", target_bir_lowering=False)
    C = D_MODEL // 128
    HALO = 2
    NT = ntok + HALO
    TW = 512
    NJ = D_FF // 128
    xT = nc.dram_tensor("xT", [D_MODEL, NT], F32, kind="ExternalInput").ap()
    yaT = nc.dram_tensor("yaT", [512, NT], F32, kind="ExternalInput").ap()
    ybT = nc.dram_tensor("ybT", [512, NT], F32, kind="ExternalInput").ap()
    ycT = nc.dram_tensor("ycT", [1024, NT], F32, kind="ExternalInput").ap()
    g_nsa = nc.dram_tensor("g_nsa", [128, 4], F32, kind="ExternalInput").ap()
    g_s5 = nc.dram_tensor("g_s5", [128, 4], F32, kind="ExternalInput").ap()
    g_ffn = nc.dram_tensor("g_ffn", [128, C], F32, kind="ExternalInput").ap()
    cw_d = nc.dram_tensor("ffn_conv", [128, 3, 2 * NJ], F32, kind="ExternalInput").ap()
    cb_d = nc.dram_tensor("ffn_conv_b", [128, 2 * NJ], F32, kind="ExternalInput").ap()
    wglu_d = nc.dram_tensor("w_glu", [512, 512], F32, kind="ExternalInput").ap()
    wout_d = nc.dram_tensor("w_out", [D_MODEL, D_MODEL], F32, kind="ExternalInput").ap()
    wfi_d = nc.dram_tensor("ffn_w_in", [D_MODEL, 2 * D_FF], F32, kind="ExternalInput").ap()
    wfo_d = nc.dram_tensor("ffn_w_out", [D_FF, D_MODEL], F32, kind="ExternalInput").ap()
    out_d = nc.dram_tensor("xoT", [D_MODEL, ntok], F32, kind="ExternalOutput").ap()

    k = KB(nc)
    ones_d = k.sb([128, 128], F32, "ones_d")
    k.op("pool", lambda g: g.memset(ones_d[:], 1.0 / D_MODEL), [], [ones_d])
    ones_4 = k.sb([128, 128], F32, "ones_4")
    k.op("pool", lambda g: g.memset(ones_4[:], 1.0 / 512.0), [], [ones_4])
    gn = k.sb([128, 4], F32, "gn"); k.dma("sp", gn[:], g_nsa, t_out=gn)
    gs = k.sb([128, 4], F32, "gs"); k.dma("sp", gs[:], g_s5, t_out=gs)
    gf = k.sb([128, C], F32, "gf"); k.dma("sp", gf[:], g_ffn, t_out=gf)
    cw = k.sb([128, 3, 2 * NJ], F32, "cw"); k.dma("sp", cw[:], cw_d, t_out=cw)
    cb = k.sb([128, 2 * NJ], F32, "cb"); k.dma("sp", cb[:], cb_d, t_out=cb)
    wglu = k.sb([128, 4, 512], BF16, "wglu")
    k.dma("pool", wglu[:], wglu_d.rearrange("(c p) n -> p c n", p=128), t_out=wglu)

    xt = k.sb([128, C, TW], F32, "xt")
    ain = k.sb([128, C, TW], BF16, "ain")
    act = k.sb([128, NJ, TW], BF16, "act")
    ya = k.sb([128, 4, TW], F32, "ya")
    yb = k.sb([128, 4, TW], F32, "yb")
    ybb = k.sb([128, 4, TW], BF16, "ybb")
    y2 = ya
    sqs = [k.sb([128, TW], F32, f"sq{i}") for i in range(2)]
    rstd = k.sb([128, TW], F32, "rstd")
    sig = k.sb([128, TW], F32, "sig")
    tails = k.sb([128, 2 * NJ, 2], F32, "tails")
    k.op("pool", lambda g: g.memset(tails[:], 0.0), [], [tails])
    ext = [k.sb([128, TW + 2], F32, f"ext{i}") for i in range(4)]
    gt = [k.sb([128, TW], F32, f"gt{i}") for i in range(2)]
    ut = [k.sb([128, TW], F32, f"ut{i}") for i in range(2)]
    ost = [k.sb([128, TW], F32, f"ost{i}") for i in range(2)]
    wb = [k.sb([128, C, 512], BF16, f"wb{i}") for i in range(3)]
    pstat = k.ps([128, TW], F32, "pstat")
    pacc = [k.ps([128, TW], F32, f"pacc{i}") for i in range(7)]
    state = {"wn": 0, "pn": 0}

    def next_w():
        w = wb[state["wn"] % len(wb)]
        state["wn"] += 1
        return w

    def next_p():
        p = pacc[state["pn"] % len(pacc)]
        state["pn"] += 1
        return p

    woutv = wout_d.rearrange("(c p) n -> p c n", p=128)
    wfiv = wfi_d.rearrange("(c p) n -> p c n", p=128)
    wfov = wfo_d.rearrange("(c p) n -> p c n", p=128)

    tiles = [(0, HALO)] + [(HALO + i * TW, TW) for i in range(ntok // TW)]
    for ti, (t0, W) in enumerate(tiles):
        halo = (ti == 0)
        k.dma("sp", xt[:, :, :W], xT.rearrange("(c p) t -> p c t", p=128)[:, :, t0:t0 + W], t_out=xt)
        k.dma("sp", ya[:, :, :W], yaT.rearrange("(c p) t -> p c t", p=128)[:, :, t0:t0 + W], t_out=ya)
        k.dma("sp", yb[:, :, :W], ybT.rearrange("(c p) t -> p c t", p=128)[:, :, t0:t0 + W], t_out=yb)
        rms_stats(k, ones_4, ya, 4, W, pstat, sqs, rstd)
        for c in range(4):
            k.op("dve", lambda g, c=c: g.scalar_tensor_tensor(
                out=ain[:, c, :W], in0=ya[:, c, :W], scalar=gn[:, c:c + 1], in1=rstd[:, :W],
                op0=ALU.mult, op1=ALU.mult), [ya, gn, rstd], [ain])
        k.op("pool", lambda g: g.tensor_copy(out=ybb[:, :, :W], in_=yb[:, :, :W]), [yb], [ybb])
        for m in range(4):
            p = next_p()
            for c in range(4):
                k.op("pe", lambda g, c=c, p=p, m=m: g.matmul(p[:, :W], lhsT=wglu[:, c, m * 128:(m + 1) * 128],
                                                         rhs=ybb[:, c, :W], start=(c == 0), stop=(c == 3)),
                     [wglu, ybb], [p])
            k.op("act", lambda g, p=p: g.activation(out=sig[:, :W], in_=p[:, :W], func=AF.Sigmoid), [p], [sig])
            k.op("dve", lambda g, m=m: g.tensor_tensor(out=y2[:, m, :W], in0=yb[:, m, :W], in1=sig[:, :W],
                                                   op=ALU.mult), [yb, sig], [y2])
        rms_stats(k, ones_4, y2, 4, W, pstat, sqs, rstd)
        for c in range(4):
            k.op("dve", lambda g, c=c: g.scalar_tensor_tensor(
                out=ain[:, 4 + c, :W], in0=y2[:, c, :W], scalar=gs[:, c:c + 1], in1=rstd[:, :W],
                op0=ALU.mult, op1=ALU.mult), [y2, gs, rstd], [ain])
        k.dma("pool", ain[:, 8:16, :W], ycT.rearrange("(c p) t -> p c t", p=128)[:, :, t0:t0 + W], t_out=ain)
        for mg in range(4):
            w = next_w()
            k.dma("pool", w[:], woutv[:, :, mg * 512:(mg + 1) * 512], t_out=w)
            for mm in range(4):
                m = mg * 4 + mm
                p = next_p()
                for c in range(C):
                    k.op("pe", lambda g, c=c, p=p, mm=mm, w=w: g.matmul(
                        p[:, :W], lhsT=w[:, c, mm * 128:(mm + 1) * 128], rhs=ain[:, c, :W],
                        start=(c == 0), stop=(c == C - 1)), [w, ain], [p])
                k.op("dve", lambda g, m=m, p=p: g.tensor_tensor(out=xt[:, m, :W], in0=xt[:, m, :W], in1=p[:, :W],
                                                            op=ALU.add), [xt, p], [xt])
        rms_stats(k, ones_d, xt, C, W, pstat, sqs, rstd)
        for c in range(C):
            k.op("dve", lambda g, c=c: g.scalar_tensor_tensor(
                out=ain[:, c, :W], in0=xt[:, c, :W], scalar=gf[:, c:c + 1], in1=rstd[:, :W],
                op0=ALU.mult, op1=ALU.mult), [xt, gf, rstd], [ain])
        ngrp = (NJ + 3) // 4
        en = 0
        for jg in range(ngrp):
            j0 = jg * 4
            nj = min(4, NJ - j0)
            wg = next_w()
            k.dma("pool", wg[:, :, :nj * 128], wfiv[:, :, j0 * 128:(j0 + nj) * 128], t_out=wg)
            wu = next_w()
            k.dma("pool", wu[:, :, :nj * 128], wfiv[:, :, D_FF + j0 * 128:D_FF + (j0 + nj) * 128], t_out=wu)
            for jj in range(nj):
                j = j0 + jj
                res = []
                for which, w in ((0, wg), (1, wu)):
                    idx = which * NJ + j
                    p = next_p()
                    for c in range(C):
                        k.op("pe", lambda g, c=c, p=p, jj=jj, w=w: g.matmul(
                            p[:, :W], lhsT=w[:, c, jj * 128:(jj + 1) * 128], rhs=ain[:, c, :W],
                            start=(c == 0), stop=(c == C - 1)), [w, ain], [p])
                    e = ext[en % len(ext)]
                    en += 1
                    k.op("pool", lambda g, e=e, idx=idx: g.tensor_copy(out=e[:, 0:2], in_=tails[:, idx, :]),
                         [tails], [e])
                    k.op("act", lambda g, e=e, p=p: g.copy(out=e[:, 2:2 + W], in_=p[:, :W]), [p], [e])
                    k.op("pool", lambda g, e=e, idx=idx: g.tensor_copy(out=tails[:, idx, :], in_=e[:, W:W + 2]),
                         [e], [tails])
                    if halo:
                        continue
                    dst = (gt if which == 0 else ut)[j % 2]
                    k.op("dve", lambda g, e=e, idx=idx, dst=dst: g.tensor_scalar(
                        out=dst[:, :W], in0=e[:, 2:2 + W], scalar1=cw[:, 2, idx:idx + 1], scalar2=cb[:, idx:idx + 1],
                        op0=ALU.mult, op1=ALU.add), [e, cw, cb], [dst])
                    k.op("dve", lambda g, e=e, idx=idx, dst=dst: g.scalar_tensor_tensor(
                        out=dst[:, :W], in0=e[:, 1:1 + W], scalar=cw[:, 1, idx:idx + 1], in1=dst[:, :W],
                        op0=ALU.mult, op1=ALU.add), [e, cw, dst], [dst])
                    k.op("dve", lambda g, e=e, idx=idx, dst=dst: g.scalar_tensor_tensor(
                        out=dst[:, :W], in0=e[:, 0:W], scalar=cw[:, 0, idx:idx + 1], in1=dst[:, :W],
                        op0=ALU.mult, op1=ALU.add), [e, cw, dst], [dst])
                    res.append(dst)
                if halo:
                    continue
                gtile, utile = res
                k.op("act", lambda g, gtile=gtile: g.activation(out=gtile[:, :W], in_=gtile[:, :W], func=AF.Silu),
                     [gtile], [gtile])
                k.op("pool", lambda g, gtile=gtile, utile=utile, j=j: g.tensor_tensor(
                    out=act[:, j, :W], in0=gtile[:, :W], in1=utile[:, :W], op=ALU.mult), [gtile, utile], [act])
        if halo:
            continue
        jgroups = [(0, 16), (16, 16), (32, NJ - 32)]
        for mg in range(4):
            ps4 = [next_p() for _ in range(4)]
            for gi, (ja, jn) in enumerate(jgroups):
                w = next_w()
                k.dma("pool", w[:, :jn, :], wfov[:, ja:ja + jn, mg * 512:(mg + 1) * 512], t_out=w)
                for mm in range(4):
                    p = ps4[mm]
                    for jj in range(jn):
                        j = ja + jj
                        k.op("pe", lambda g, p=p, jj=jj, j=j, mm=mm, w=w: g.matmul(
                            p[:, :W], lhsT=w[:, jj, mm * 128:(mm + 1) * 128], rhs=act[:, j, :W],
                            start=(j == 0), stop=(j == NJ - 1)), [w, act], [p])
            for mm in range(4):
                m = mg * 4 + mm
                o = ost[m % 2]
                k.op("dve", lambda g, m=m, o=o, p=ps4[mm]: g.tensor_tensor(
                    out=o[:, :W], in0=xt[:, m, :W], in1=p[:, :W], op=ALU.add), [xt, ps4[mm]], [o])
                k.dma("sp", out_d[m * 128:(m + 1) * 128, t0 - HALO:t0 - HALO + W], o[:, :W], t_in=o, final=True)
    k.finish()
    return nc


def lay128(v):
    v = np.asarray(v, np.float32)
    return np.ascontiguousarray(v.reshape(-1, 128).T)


def stage_C_inputs(inp, l, xT, yaT, ybT, ycT):
    NJ = D_FF // 128
    return {
        "xT": xT, "yaT": yaT, "ybT": ybT, "ycT": ycT,
        "g_nsa": lay128(inp["nsa_out_norm"][l]), "g_s5": lay128(inp["s5_out_norm"][l]),
        "g_ffn": lay128(inp["ffn_norm"][l]),
        "ffn_conv": np.ascontiguousarray(np.asarray(inp["ffn_conv"][l], np.float32).reshape(3, 2 * NJ, 128).transpose(2, 0, 1)),
        "ffn_conv_b": lay128(inp["ffn_conv_b"][l]),
        "w_glu": np.asarray(inp["s5_w_glu"][l], np.float32), "w_out": np.asarray(inp["w_out"][l], np.float32),
        "ffn_w_in": np.asarray(inp["ffn_w_in"][l], np.float32), "ffn_w_out": np.asarray(inp["ffn_w_out"][l], np.float32),
    }


GDN_C = 64
GDN_DEBUG = 3
GDN_IL = 4
GDN_PAR_STOP = 0
NEG = -1.0e6


class View:
    __slots__ = ("base", "ap")

    def __init__(self, base, ap):
        self.base = base
        self.ap = ap

    def __getitem__(self, idx):
        return self.ap[idx]


class Slots:
    def __init__(self, k, nbanks, width, name):
        banks = [k.ps([128, 512], F32, f"{name}{b}") for b in range(nbanks)]
        self.slots = []
        for j in range(512 // width):
            for bank in banks:
                self.slots.append(View(bank, bank.h[:, j * width:(j + 1) * width]))
        self.n = {}

    def get(self, part=0, nparts=1):
        sub = self.slots[part::nparts]
        c = self.n.get((part, nparts), 0)
        self.n[(part, nparts)] = c + 1
        return sub[c % len(sub)]


def gdn_consts():
    c = {}
    c["ident"] = np.eye(128, dtype=np.float32)
    ii = np.arange(64)
    c["maskc"] = np.where(ii[:, None] >= ii[None, :], 0.0, NEG).astype(np.float32)
    c["maskcT"] = np.ascontiguousarray(c["maskc"].T)
    c["strict01"] = (ii[:, None] > ii[None, :]).astype(np.float32)
    cm = np.zeros((33, 4), np.float32)
    cm[0, 0] = 1.0; cm[32, 1] = 1.0; cm[32, 2] = -1.0; cm[0, 3] = 1.0
    c["cm"] = cm
    sel = np.zeros((33, 2), np.float32); sel[0, 0] = 1.0; sel[32, 1] = 1.0
    c["sel"] = sel
    bsel = np.zeros((33, 128), np.float32); bsel[0, :] = 1.0
    c["bsel"] = bsel
    rm = np.ones((33, 512), np.float32); rm[:, ::64] = 0.0
    c["resetmask"] = rm
    return c


def build_gdn(L=SEQ):
    nc = bass.Bass("TRN2", target_bir_lowering=False)
    ST = 512
    NST = L // ST
    CH = GDN_C
    NCH = ST // CH
    din = {}

    def inp(name, shape):
        din[name] = nc.dram_tensor(name, list(shape), F32, kind="ExternalInput").ap()
        return din[name]

    qT_d = inp("g_qT", [128, L]); kT_d = inp("g_kT", [128, L]); vT_d = inp("g_vT", [128, L])
    z_d = inp("g_z", [L, 128]); a_d = inp("g_a33", [33, L]); b_d = inp("g_b33", [33, L])
    cw_d = inp("g_cw", [128, 3, 4]); prm_d = inp("g_prm33", [33, 2]); gain_d = inp("g_gain", [64, 128])
    cd = {n: inp("gc_" + n, v.shape) for n, v in gdn_consts().items()}
    out_d = nc.dram_tensor("g_out", [L, 128], F32, kind="ExternalOutput").ap()
    k = KB(nc)
    emit_gdn(k, L, qT_d, kT_d, vT_d, z_d, a_d, b_d, cw_d, prm_d, gain_d, cd, out_d)
    k.finish()
    return nc


def emit_gdn(k, L, qT_d, kT_d, vT_d, z_d, a_d, b_d, cw_d, prm_d, gain_d, cd, out_d, merged=False, extra=None):
    ST = 512
    NST = L // ST
    CH = GDN_C
    NCH = ST // CH

    def const(name, shape):
        t = k.sb(shape, F32, "c_" + name)
        k.dma("sp", t[:], cd[name], t_out=t)
        return t

    ident = const("ident", [128, 128]); maskc = const("maskc", [64, 64]); maskcT = const("maskcT", [64, 64])
    strict01 = const("strict01", [64, 64]); cm = const("cm", [33, 4]); sel = const("sel", [33, 2])
    bsel = const("bsel", [33, 128]); rmask = const("resetmask", [33, 512])
    cw = k.sb([128, 3, 4], F32, "cw"); k.dma("sp", cw[:], cw_d, t_out=cw)
    prm = k.sb([33, 2], F32, "prm"); k.dma("sp", prm[:], prm_d, t_out=prm)
    gain = k.sb([64, 128], F32, "gain"); k.dma("sp", gain[:], gain_d, t_out=gain)
    ones = k.sb([128, 128], F32, "ones")
    k.op("pool", lambda g: g.memset(ones[:], 1.0), [], [ones])
    nA = k.sb([33, 1], F32, "nA")
    k.op("act", lambda g: g.activation(out=nA[:], in_=prm[:, 0:1], func=AF.Exp), [prm], [nA])
    k.op("dve", lambda g: g.tensor_scalar(out=nA[:], in0=nA[:], scalar1=-1.0, scalar2=None, op0=ALU.mult), [nA], [nA])
    S = k.sb([128, 128], F32, "S")
    k.op("pool", lambda g: g.memset(S[:], 0.0), [], [S])

    ext = [[k.sb([128, ST + 3], F32, f"ext{w}{i}") for i in range(2)] for w in range(3)]
    for w in range(3):
        k.op("pool", lambda g, w=w: g.memset(ext[w][1][:, ST:ST + 3], 0.0), [], [ext[w][1]])
    cv = [[k.sb([128, ST], F32, f"cv{w}{i}") for i in range(2)] for w in range(3)]
    qn = [k.sb([128, ST], F32, f"qn{i}") for i in range(2)]
    kn = [k.sb([128, ST], F32, f"kn{i}") for i in range(2)]
    sq = k.sb([128, ST], F32, "sq")
    rs = k.sb([128, ST], F32, "rs")
    a33 = [k.sb([33, ST], F32, f"a33{i}") for i in range(2)]
    b33 = [k.sb([33, ST], F32, f"b33{i}") for i in range(2)]
    gc33 = k.sb([33, ST], F32, "gc33")
    U33 = [k.sb([33, ST], F32, f"U33{i}") for i in range(2)]
    V33 = [k.sb([33, ST], F32, f"V33{i}") for i in range(2)]
    RC = [k.sb([33, ST], F32, f"RC{i}") for i in range(2)]
    GB = [k.sb([128, ST], F32, f"GB{i}") for i in range(2)]
    EG = [k.sb([128, ST], F32, f"EG{i}") for i in range(2)]
    zt = [k.sb([64, NCH, 128], F32, f"zt{i}") for i in range(2)]
    yo = [k.sb([64, NCH, 128], F32, f"yo{i}") for i in range(2)]
    pbig = k.ps([128, 512], F32, "pbig")
    pstat = pbig if merged else k.ps([128, 512], F32, "pstat")
    small = Slots(k, 2, 64, "ps")
    wide = Slots(k, 3 if merged else 4, 128, "pw")
    NB = 2 * GDN_IL
    mk = lambda shape, nm: [k.sb(shape, F32, f"{nm}{i}") for i in range(NB)]
    Dm = mk([64, 64], "Dm"); DTm = mk([64, 64], "DTm"); Ds = mk([64, 64], "Ds")
    Xa = mk([64, 64], "Xa"); Xb = mk([64, 64], "Xb"); Ya = mk([64, 64], "Ya"); Yb = mk([64, 64], "Yb")
    Pm = mk([64, 64], "Pm")
    cols = mk([64, 8], "cols")
    kb = mk([64, 128], "kb"); kd = mk([64, 128], "kd"); vb = mk([64, 128], "vb")
    val = mk([64, 128], "val"); kcumT = mk([128, 64], "kcumT"); qkT = mk([64, 64], "qkT")
    qgT = mk([128, 64], "qgT"); vnew = mk([64, 128], "vnew"); junk = [k.sb([64, 128], F32, "junk")] * NB
    ycur = mk([64, 128], "ycur")

    qv = [qT_d, kT_d, vT_d]
    zv = z_d.rearrange("(n c) d -> c n d", c=CH)
    ov = out_d.rearrange("(n c) d -> c n d", c=CH)
    pending = []
    gi = [0]

    def supertile(s):
        b = s % 2
        t0 = s * ST
        for w in range(3):
            e = ext[w][b]
            k.dma("sp", e[:, 3:3 + ST], qv[w][:, t0:t0 + ST], t_out=e)
        k.dma("sp", a33[b][:], a_d[:, t0:t0 + ST], t_out=a33[b])
        k.dma("sp", b33[b][:], b_d[:, t0:t0 + ST], t_out=b33[b])
        k.dma("sp", zt[b][:], zv[:, s * NCH:(s + 1) * NCH, :], t_out=zt[b])
        for w in range(3):
            e = ext[w][b]; eo = ext[w][1 - b]; o = cv[w][b]
            k.op("pool", lambda g, e=e, eo=eo: g.tensor_copy(out=e[:, 0:3], in_=eo[:, ST:ST + 3]), [eo], [e])
            k.op("dve", lambda g, e=e, o=o, w=w: g.tensor_scalar(out=o[:], in0=e[:, 3:3 + ST], scalar1=cw[:, w, 3:4],
                                                              scalar2=None, op0=ALU.mult), [e, cw], [o])
            for j in range(3):
                k.op("dve", lambda g, e=e, o=o, w=w, j=j: g.scalar_tensor_tensor(
                    out=o[:], in0=e[:, j:j + ST], scalar=cw[:, w, j:j + 1], in1=o[:], op0=ALU.mult, op1=ALU.add),
                    [e, cw, o], [o])
            k.op("act", lambda g, o=o: g.activation(out=o[:], in_=o[:], func=AF.Silu), [o], [o])
        for w, dst, scl in ((0, qn[b], 128.0 ** -0.5), (1, kn[b], 1.0)):
            src = cv[w][b]
            k.op("act", lambda g, src=src: g.activation(out=sq[:], in_=src[:], func=AF.Square), [src], [sq])
            k.op("pe", lambda g: g.matmul(pstat[:], lhsT=ones[:], rhs=sq[:], start=True, stop=True), [ones, sq], [pstat])
            k.op("dve", lambda g: g.tensor_scalar(out=rs[:], in0=pstat[:], scalar1=EPS, scalar2=None, op0=ALU.add),
                 [pstat], [rs])
            k.op("act", lambda g: g.activation(out=rs[:], in_=rs[:], func=AF.Sqrt), [rs], [rs])
            k.op("dve", lambda g: g.reciprocal(out=rs[:], in_=rs[:]), [rs], [rs])
            k.op("dve", lambda g, src=src, dst=dst, scl=scl: g.scalar_tensor_tensor(
                out=dst[:], in0=src[:], scalar=scl, in1=rs[:], op0=ALU.mult, op1=ALU.mult), [src, rs], [dst])
        A = a33[b]; B = b33[b]
        k.op("act", lambda g: g.activation(out=A[:], in_=A[:], func=AF.Exp, bias=prm[:, 1:2], scale=1.0), [A, prm], [A])
        k.op("dve", lambda g: g.tensor_scalar(out=A[:], in0=A[:], scalar1=1.0, scalar2=None, op0=ALU.add), [A], [A])
        k.op("act", lambda g: g.activation(out=A[:], in_=A[:], func=AF.Ln), [A], [A])
        k.op("dve", lambda g: g.tensor_scalar(out=A[:], in0=A[:], scalar1=nA[:, 0:1], scalar2=None, op0=ALU.mult),
             [A, nA], [A])
        k.op("dve", lambda g: g.tensor_tensor_scan(out=gc33[:], data0=rmask[:], data1=A[:], initial=0.0,
                                                   op0=ALU.mult, op1=ALU.add), [rmask, A], [gc33])
        k.op("act", lambda g: g.activation(out=B[:], in_=B[:], func=AF.Sigmoid), [B], [B])
        k.op("dve", lambda g: g.tensor_scalar(out=U33[b][:], in0=gc33[:], scalar1=cm[:, 0:1], scalar2=cm[:, 1:2],
                                              op0=ALU.mult, op1=ALU.add), [gc33, cm], [U33[b]])
        k.op("dve", lambda g: g.tensor_scalar(out=V33[b][:], in0=gc33[:], scalar1=cm[:, 2:3], scalar2=cm[:, 3:4],
                                              op0=ALU.mult, op1=ALU.add), [gc33, cm], [V33[b]])
        k.op("dve", lambda g: g.tensor_scalar(out=RC[b][:], in0=gc33[:], scalar1=cm[:, 0:1], scalar2=None,
                                              op0=ALU.mult), [gc33, cm], [RC[b]])
        k.op("dve", lambda g: g.scalar_tensor_tensor(out=RC[b][:], in0=B[:], scalar=cm[:, 1:2], in1=RC[b][:],
                                                     op0=ALU.mult, op1=ALU.add), [B, cm, RC[b]], [RC[b]])
        k.op("pe", lambda g: g.matmul(pbig[:], lhsT=bsel[:], rhs=gc33[:], start=True, stop=True), [bsel, gc33], [pbig])
        k.op("dve", lambda g: g.tensor_copy(out=GB[b][:], in_=pbig[:]), [pbig], [GB[b]])
        k.op("act", lambda g: g.activation(out=EG[b][:], in_=pbig[:], func=AF.Exp), [pbig], [EG[b]])
        k.op("act", lambda g: g.activation(out=zt[b][:], in_=zt[b][:], func=AF.Silu), [zt[b]], [zt[b]])

    def par(s, ci, u=0):
        b = s % 2
        i = gi[0] % NB
        gi[0] += 1
        c0 = ci * CH
        cs = slice(c0, c0 + CH)
        kT = kn[b]; qT = qn[b]; vT = cv[2][b]
        U = U33[b]; V = V33[b]
        pc = small.get(u, GDN_IL)
        k.op("pe", lambda g: g.matmul(pc[0:64, 0:2], lhsT=RC[b][:, cs], rhs=sel[:], start=True, stop=True),
             [RC[b], sel], [pc])
        cl = cols[i]
        k.op("dve", lambda g: g.tensor_copy(out=cl[:, 0:2], in_=pc[0:64, 0:2]), [pc], [cl])
        k.op("act", lambda g: g.activation(out=cl[:, 2:3], in_=cl[:, 0:1], func=AF.Exp), [cl], [cl])
        k.op("dve", lambda g: g.tensor_tensor(out=cl[:, 3:4], in0=cl[:, 2:3], in1=cl[:, 1:2], op=ALU.mult), [cl], [cl])
        k.op("dve", lambda g: g.tensor_scalar(out=cl[:, 4:5], in0=cl[:, 1:2], scalar1=-1.0, scalar2=None, op0=ALU.mult),
             [cl], [cl])
        last = c0 + CH - 1
        k.op("act", lambda g: g.activation(out=cl[:, 5:6], in_=cl[:, 0:1], func=AF.Exp, scale=-1.0,
                                           bias=GB[b][0:64, last:last + 1]), [cl, GB[b]], [cl])
        if GDN_PAR_STOP == 1:
            return None
        pd = small.get(u, GDN_IL); pdT = small.get(u, GDN_IL)
        k.op("pe", lambda g: g.matmul(pd[0:64, :], lhsT=U[:, cs], rhs=V[:, cs], start=True, stop=True), [U, V], [pd])
        k.op("pe", lambda g: g.matmul(pdT[0:64, :], lhsT=V[:, cs], rhs=U[:, cs], start=True, stop=True), [U, V], [pdT])
        k.op("dve", lambda g: g.tensor_tensor(out=Dm[i][:], in0=pd[0:64, :], in1=maskc[:], op=ALU.add), [pd, maskc], [Dm[i]])
        k.op("act", lambda g: g.activation(out=Dm[i][:], in_=Dm[i][:], func=AF.Exp), [Dm[i]], [Dm[i]])
        k.op("dve", lambda g: g.tensor_tensor(out=DTm[i][:], in0=pdT[0:64, :], in1=maskcT[:], op=ALU.add),
             [pdT, maskcT], [DTm[i]])
        k.op("act", lambda g: g.activation(out=DTm[i][:], in_=DTm[i][:], func=AF.Exp), [DTm[i]], [DTm[i]])
        k.op("pool", lambda g: g.tensor_tensor(out=Ds[i][:], in0=Dm[i][:], in1=strict01[:], op=ALU.mult),
             [Dm[i], strict01], [Ds[i]])
        if GDN_PAR_STOP == 2:
            return None
        pkk = small.get(u, GDN_IL)
        k.op("pe", lambda g: g.matmul(pkk[0:64, :], lhsT=kT[:, cs], rhs=kT[:, cs], start=True, stop=True), [kT], [pkk])
        Y = Ya[i]; X = Xa[i]; Y2 = Yb[i]; X2 = Xb[i]
        k.op("dve", lambda g, Y=Y: g.scalar_tensor_tensor(out=Y[:], in0=pkk[0:64, :], scalar=cl[:, 4:5], in1=Ds[i][:],
                                                     op0=ALU.mult, op1=ALU.mult), [pkk, cl, Ds[i]], [Y])
        if GDN_PAR_STOP == 3:
            return None
        px = small.get(u, GDN_IL)
        k.op("pe", lambda g, Y=Y: g.matmul(px[0:64, :], lhsT=Y[:], rhs=ident[0:64, 0:64], start=True, stop=True), [Y, ident], [px])
        P = Pm[i]
        if GDN_PAR_STOP == 41:
            return None
        k.op("act", lambda g, X=X: g.copy(out=X[:], in_=px[0:64, :]), [px], [X])
        if GDN_PAR_STOP == 42:
            return None
        k.op("dve", lambda g, X=X: g.tensor_tensor(out=P[:], in0=X[:], in1=ident[0:64, 0:64], op=ALU.add),
             [X, ident], [P])
        if GDN_PAR_STOP == 4:
            return None
        for lvl in range(1, 6):
            lastl = (lvl == 5)
            py = small.get(u, GDN_IL)
            k.op("pe", lambda g, X=X, Y=Y, py=py: g.matmul(py[0:64, :], lhsT=X[:], rhs=Y[:], start=True, stop=True),
                 [X, Y], [py])
            if not lastl:
                pxx = small.get(u, GDN_IL)
                k.op("pe", lambda g, X=X, Y=Y, pxx=pxx: g.matmul(pxx[0:64, :], lhsT=Y[:], rhs=X[:], start=True, stop=True),
                     [X, Y], [pxx])
            k.op("act", lambda g, Y2=Y2, py=py: g.copy(out=Y2[:], in_=py[0:64, :]), [py], [Y2])
            if not lastl:
                k.op("dve", lambda g, X2=X2, pxx=pxx: g.tensor_copy(out=X2[:], in_=pxx[0:64, :]), [pxx], [X2])
            pp = small.get(u, GDN_IL)
            k.op("pe", lambda g, Y2=Y2, P=P, pp=pp: g.matmul(pp[0:64, :], lhsT=Y2[:], rhs=P[:], start=True, stop=True),
                 [Y2, P], [pp])
            k.op("dve", lambda g, P=P, pp=pp: g.tensor_tensor(out=P[:], in0=P[:], in1=pp[0:64, :], op=ALU.add),
                 [P, pp], [P])
            X, X2 = X2, X
            Y, Y2 = Y2, Y
        if GDN_PAR_STOP == 5:
            return None
        pk = wide.get(u, GDN_IL + 1)
        k.op("pe", lambda g: g.matmul(pk[0:64, :], lhsT=kT[:, cs], rhs=ident[:], start=True, stop=True), [kT, ident], [pk])
        k.op("dve", lambda g: g.tensor_scalar(out=kb[i][:], in0=pk[0:64, :], scalar1=cl[:, 3:4], scalar2=None,
                                              op0=ALU.mult), [pk, cl], [kb[i]])
        k.op("dve", lambda g: g.tensor_scalar(out=kd[i][:], in0=pk[0:64, :], scalar1=cl[:, 5:6], scalar2=None,
                                              op0=ALU.mult), [pk, cl], [kd[i]])
        pvv = wide.get(u, GDN_IL + 1)
        k.op("pe", lambda g: g.matmul(pvv[0:64, :], lhsT=vT[:, cs], rhs=ident[:], start=True, stop=True), [vT, ident], [pvv])
        k.op("dve", lambda g: g.tensor_scalar(out=vb[i][:], in0=pvv[0:64, :], scalar1=cl[:, 1:2], scalar2=None,
                                              op0=ALU.mult), [pvv, cl], [vb[i]])
        if GDN_PAR_STOP == 6:
            return None
        pval = wide.get(u, GDN_IL + 1)
        k.op("pe", lambda g: g.matmul(pval[0:64, :], lhsT=P[:], rhs=vb[i][:], start=True, stop=True), [P, vb[i]], [pval])
        k.op("act", lambda g: g.copy(out=val[i][:], in_=pval[0:64, :]), [pval], [val[i]])
        pkc = small.get(u, GDN_IL)
        k.op("pe", lambda g: g.matmul(pkc[:, :], lhsT=kb[i][:], rhs=P[:], start=True, stop=True), [kb[i], P], [pkc])
        k.op("act", lambda g: g.copy(out=kcumT[i][:], in_=pkc[:, :]), [pkc], [kcumT[i]])
        pqk = small.get(u, GDN_IL)
        k.op("pe", lambda g: g.matmul(pqk[0:64, :], lhsT=kT[:, cs], rhs=qT[:, cs], start=True, stop=True), [kT, qT], [pqk])
        k.op("dve", lambda g: g.tensor_tensor(out=qkT[i][:], in0=pqk[0:64, :], in1=DTm[i][:], op=ALU.mult),
             [pqk, DTm[i]], [qkT[i]])
        k.op("pool", lambda g: g.tensor_tensor(out=qgT[i][:], in0=qT[:, cs], in1=EG[b][:, cs], op=ALU.mult),
             [qT, EG[b]], [qgT[i]])
        return dict(i=i, b=b, ci=ci, s=s, last=last)

    def seq(d):
        i = d["i"]; b = d["b"]; ci = d["ci"]; last = d["last"]
        pv = wide.get(GDN_IL, GDN_IL + 1)
        k.op("pe", lambda g: g.matmul(pv[0:64, :], lhsT=kcumT[i][:], rhs=S[:], start=True, stop=True), [kcumT[i], S], [pv])
        k.op("dve", lambda g: g.tensor_tensor(out=vnew[i][:], in0=val[i][:], in1=pv[0:64, :], op=ALU.subtract),
             [val[i], pv], [vnew[i]])
        po = wide.get(GDN_IL, GDN_IL + 1)
        k.op("pe", lambda g: g.matmul(po[0:64, :], lhsT=qgT[i][:], rhs=S[:], start=True, stop=False), [qgT[i], S], [po])
        k.op("pe", lambda g: g.matmul(po[0:64, :], lhsT=qkT[i][:], rhs=vnew[i][:], start=False, stop=True),
             [qkT[i], vnew[i]], [po])
        pS = wide.get(GDN_IL, GDN_IL + 1)
        k.op("pe", lambda g: g.matmul(pS[:, :], lhsT=kd[i][:], rhs=vnew[i][:], start=True, stop=True), [kd[i], vnew[i]], [pS])
        k.op("dve", lambda g: g.scalar_tensor_tensor(out=S[:], in0=S[:], scalar=EG[b][:, last:last + 1], in1=pS[:, :],
                                                     op0=ALU.mult, op1=ALU.add), [S, EG[b], pS], [S])
        cl = cols[i]
        k.op("act", lambda g: g.activation(out=junk[i][:], in_=po[0:64, :], func=AF.Square, accum_out=cl[:, 6:7]),
             [po], [junk[i], cl])
        k.op("dve", lambda g: g.tensor_scalar(out=cl[:, 6:7], in0=cl[:, 6:7], scalar1=1.0 / 128.0, scalar2=EPS,
                                              op0=ALU.mult, op1=ALU.add), [cl], [cl])
        k.op("act", lambda g: g.activation(out=cl[:, 6:7], in_=cl[:, 6:7], func=AF.Sqrt), [cl], [cl])
        k.op("dve", lambda g: g.reciprocal(out=cl[:, 6:7], in_=cl[:, 6:7]), [cl], [cl])
        k.op("dve", lambda g: g.scalar_tensor_tensor(out=ycur[i][:], in0=po[0:64, :], scalar=cl[:, 6:7], in1=gain[:],
                                                     op0=ALU.mult, op1=ALU.mult), [po, cl, gain], [ycur[i]])
        k.op("pool", lambda g: g.tensor_tensor(out=yo[b][:, ci, :], in0=ycur[i][:], in1=zt[b][:, ci, :], op=ALU.mult),
             [ycur[i], zt[b]], [yo[b]])
        if ci == NCH - 1:
            s = d["s"]
            k.dma("sp", ov[:, s * NCH:(s + 1) * NCH, :], yo[b][:], t_in=yo[b], final=True)

    for s in range(NST):
        supertile(s)
        if GDN_DEBUG < 2:
            continue
        xl = []
        if extra is not None:
            xr = k.record(extra, s)[1]
            ngrp = NCH // GDN_IL
            per = (len(xr) + ngrp - 1) // ngrp
            xl = [xr[gg * per:(gg + 1) * per] for gg in range(ngrp)]
        for ci in range(0, NCH, GDN_IL):
            recs = [k.record(par, s, ci + u, u) for u in range(GDN_IL)]
            chains = [r[1] for r in recs]
            if xl:
                chains.append(xl[ci // GDN_IL])
            k.emit_interleaved(chains)
            if GDN_DEBUG < 3:
                continue
            for dd in pending:
                seq(dd)
            pending[:] = [r[0] for r in recs]
    if GDN_DEBUG >= 3:
        for dd in pending:
            seq(dd)


def gdn_inputs(projT, inp, l, h, L=SEQ):
    r = lambda a: np.ascontiguousarray(a, dtype=np.float32)
    q0, k0, v0, z0 = 1816, 2840, 3864, 4888
    a = projT[5912 + h, :L]
    b = projT[5920 + h, :L]
    a33 = np.zeros((33, L), np.float32); a33[0] = a; a33[32] = a
    b33 = np.zeros((33, L), np.float32); b33[0] = b; b33[32] = b
    conv = np.asarray(inp["gdn_conv"][l], np.float32)
    cw = np.stack([conv[:, w * 1024 + h * 128: w * 1024 + (h + 1) * 128].T for w in range(3)], axis=1)
    prm = np.zeros((33, 2), np.float32)
    prm[:, 0] = inp["gdn_a_log"][l][h]; prm[:, 1] = inp["gdn_dt_bias"][l][h]
    d = {
        "g_qT": r(projT[q0 + h * 128:q0 + (h + 1) * 128, :L]), "g_kT": r(projT[k0 + h * 128:k0 + (h + 1) * 128, :L]),
        "g_vT": r(projT[v0 + h * 128:v0 + (h + 1) * 128, :L]), "g_z": r(projT[z0 + h * 128:z0 + (h + 1) * 128, :L].T),
        "g_a33": a33, "g_b33": b33, "g_cw": r(cw), "g_prm33": prm,
        "g_gain": r(np.tile(np.asarray(inp["gdn_norm"][l], np.float32)[None, :], (64, 1))),
    }
    for n, v in gdn_consts().items():
        d["gc_" + n] = v
    return d


import math
I32 = mybir.dt.int32
S5_W = 512


def build_s5(L=SEQ):
    nc = bass.Bass("TRN2", target_bir_lowering=False)
    W = S5_W
    d = {}

    def inp(name, shape):
        d[name] = nc.dram_tensor(name, list(shape), F32, kind="ExternalInput").ap()
        return d[name]

    inp("s_uT", [64, L]); inp("s_lamre", [128, 2]); inp("s_lamim", [128, 2]); inp("s_logdt", [128, 2])
    inp("s_bre", [2, 64, 128]); inp("s_bim", [2, 64, 128]); inp("s_cre", [2, 128, 64]); inp("s_cim", [2, 128, 64])
    inp("s_d", [64, 1]); inp("s_trow", [128, W + 1])
    out_d = nc.dram_tensor("s_out", [64, L], F32, kind="ExternalOutput").ap()
    k = KB(nc)
    emit_s5(k, L, d, out_d)
    k.finish()
    return nc


def emit_s5(k, L, d, out_d, stepper=False, npp=4, npy=2):
    W = S5_W
    NST = L // W
    TWO_PI = 2.0 * math.pi

    def load(name, shape):
        t = k.sb(shape, F32, name)
        k.dma("sp", t[:], d[name], t_out=t)
        return t

    lamre = load("s_lamre", [128, 2]); lamim = load("s_lamim", [128, 2]); logdt = load("s_logdt", [128, 2])
    trow = load("s_trow", [128, W + 1]); dsk = load("s_d", [64, 1])
    Bre = []; Bim = []; Cre = []; Cim = []
    for j in range(2):
        for lst, nm, shp in ((Bre, "s_bre", [64, 128]), (Bim, "s_bim", [64, 128]), (Cre, "s_cre", [128, 64]),
                             (Cim, "s_cim", [128, 64])):
            t = k.sb(shp, F32, f"{nm}{j}")
            k.dma("sp", t[:], d[nm][j], t_out=t)
            lst.append(t)
    ki = k.sb([128, W + 1], I32, "ki")
    kf = k.sb([128, W + 1], F32, "kf")
    ang = k.sb([128, W + 1], F32, "ang")
    sc = k.sb([128, 16], F32, "sc")
    Ct = []; St = []; nSt = []; Trt = []; Tit = []; rr = []; eW = []

    def reduce_inplace(t, n):
        k.op("dve", lambda g: g.tensor_scalar(out=ki[:, :n], in0=t[:, :n], scalar1=1.0 / TWO_PI, scalar2=None,
                                              op0=ALU.mult), [t], [ki])
        k.op("dve", lambda g: g.tensor_copy(out=kf[:, :n], in_=ki[:, :n]), [ki], [kf])
        k.op("dve", lambda g: g.scalar_tensor_tensor(out=t[:, :n], in0=kf[:, :n], scalar=-TWO_PI, in1=t[:, :n],
                                                     op0=ALU.mult, op1=ALU.add), [kf, t], [t])

    for j in range(2):
        C = k.sb([128, W + 1], F32, f"C{j}"); S = k.sb([128, W + 1], F32, f"S{j}"); nS = k.sb([128, W + 1], F32, f"nS{j}")
        Tr = k.sb([128, W], F32, f"Tr{j}"); Ti = k.sb([128, W], F32, f"Ti{j}")
        pr = k.sb([128, 16], F32, f"pr{j}")
        k.op("act", lambda g: g.activation(out=pr[:, 0:1], in_=logdt[:, j:j + 1], func=AF.Exp), [logdt], [pr])
        k.op("dve", lambda g: g.tensor_tensor(out=pr[:, 1:2], in0=lamim[:, j:j + 1], in1=pr[:, 0:1], op=ALU.mult),
             [lamim, pr], [pr])
        k.op("dve", lambda g: g.tensor_tensor(out=pr[:, 2:3], in0=lamre[:, j:j + 1], in1=pr[:, 0:1], op=ALU.mult),
             [lamre, pr], [pr])
        k.op("act", lambda g: g.activation(out=pr[:, 2:3], in_=pr[:, 2:3], func=AF.Exp), [pr], [pr])
        k.op("dve", lambda g: g.tensor_copy(out=ang[:, 0:1], in_=pr[:, 1:2]), [pr], [ang])
        reduce_inplace(ang, 1)
        k.op("dve", lambda g: g.tensor_copy(out=pr[:, 3:4], in_=ang[:, 0:1]), [ang], [pr])
        k.op("dve", lambda g: g.tensor_scalar(out=ang[:], in0=trow[:], scalar1=pr[:, 3:4], scalar2=None, op0=ALU.mult),
             [trow, pr], [ang])
        reduce_inplace(ang, W + 1)
        k.op("act", lambda g: g.activation(out=S[:], in_=ang[:], func=AF.Sin), [ang], [S])
        k.op("dve", lambda g: g.tensor_scalar(out=ang[:], in0=ang[:], scalar1=0.5 * math.pi, scalar2=None, op0=ALU.add),
             [ang], [ang])
        reduce_inplace(ang, W + 1)
        k.op("act", lambda g: g.activation(out=C[:], in_=ang[:], func=AF.Sin), [ang], [C])
        k.op("pool", lambda g: g.tensor_scalar(out=nS[:], in0=S[:], scalar1=-1.0, scalar2=None, op0=ALU.mult), [S], [nS])
        k.op("dve", lambda g: g.tensor_tensor(out=pr[:, 4:5], in0=pr[:, 2:3], in1=C[:, 1:2], op=ALU.mult), [pr, C], [pr])
        k.op("dve", lambda g: g.tensor_scalar(out=pr[:, 4:5], in0=pr[:, 4:5], scalar1=-1.0, scalar2=None, op0=ALU.add),
             [pr], [pr])
        k.op("dve", lambda g: g.tensor_tensor(out=pr[:, 5:6], in0=pr[:, 2:3], in1=S[:, 1:2], op=ALU.mult), [pr, S], [pr])
        k.op("dve", lambda g: g.tensor_tensor(out=pr[:, 6:7], in0=lamre[:, j:j + 1], in1=lamre[:, j:j + 1], op=ALU.mult),
             [lamre], [pr])
        k.op("dve", lambda g: g.scalar_tensor_tensor(out=pr[:, 6:7], in0=lamim[:, j:j + 1], scalar=lamim[:, j:j + 1],
                                                     in1=pr[:, 6:7], op0=ALU.mult, op1=ALU.add), [lamim, pr], [pr])
        k.op("dve", lambda g: g.reciprocal(out=pr[:, 6:7], in_=pr[:, 6:7]), [pr], [pr])
        k.op("dve", lambda g: g.tensor_tensor(out=pr[:, 7:8], in0=pr[:, 4:5], in1=lamre[:, j:j + 1], op=ALU.mult),
             [pr, lamre], [pr])
        k.op("dve", lambda g: g.scalar_tensor_tensor(out=pr[:, 7:8], in0=pr[:, 5:6], scalar=lamim[:, j:j + 1],
                                                     in1=pr[:, 7:8], op0=ALU.mult, op1=ALU.add), [pr, lamim], [pr])
        k.op("dve", lambda g: g.tensor_tensor(out=pr[:, 7:8], in0=pr[:, 7:8], in1=pr[:, 6:7], op=ALU.mult), [pr], [pr])
        k.op("dve", lambda g: g.tensor_tensor(out=pr[:, 9:10], in0=pr[:, 4:5], in1=lamim[:, j:j + 1], op=ALU.mult),
             [pr, lamim], [pr])
        k.op("dve", lambda g: g.scalar_tensor_tensor(out=pr[:, 8:9], in0=pr[:, 5:6], scalar=lamre[:, j:j + 1],
                                                     in1=pr[:, 9:10], op0=ALU.mult, op1=ALU.subtract), [pr, lamre], [pr])
        k.op("dve", lambda g: g.tensor_tensor(out=pr[:, 8:9], in0=pr[:, 8:9], in1=pr[:, 6:7], op=ALU.mult), [pr], [pr])
        k.op("dve", lambda g: g.tensor_scalar(out=Tr[:], in0=C[:, 0:W], scalar1=pr[:, 7:8], scalar2=None, op0=ALU.mult),
             [C, pr], [Tr])
        k.op("dve", lambda g: g.scalar_tensor_tensor(out=Tr[:], in0=S[:, 0:W], scalar=pr[:, 8:9], in1=Tr[:],
                                                     op0=ALU.mult, op1=ALU.add), [S, pr, Tr], [Tr])
        k.op("dve", lambda g: g.tensor_scalar(out=Ti[:], in0=C[:, 0:W], scalar1=pr[:, 8:9], scalar2=None, op0=ALU.mult),
             [C, pr], [Ti])
        k.op("dve", lambda g: g.scalar_tensor_tensor(out=Ti[:], in0=nS[:, 0:W], scalar=pr[:, 7:8], in1=Ti[:],
                                                     op0=ALU.mult, op1=ALU.add), [nS, pr, Ti], [Ti])
        Ct.append(C); St.append(S); nSt.append(nS); Trt.append(Tr); Tit.append(Ti); rr.append(pr)

    ub = [k.sb([64, W], F32, f"ub{i}") for i in range(2)]
    pP = [k.ps([128, W], F32, f"pP{i}") for i in range(npp)]
    pY = [k.ps([128, W], F32, f"pY{i}") for i in range(npy)]
    mkt = lambda nm, n=2: [k.sb([128, W], F32, f"{nm}{i}") for i in range(n)]
    prs = mkt("prs", 1) * 2; pis = mkt("pis", 1) * 2; m1 = mkt("m1", 1) * 2; m2 = mkt("m2", 1) * 2; m3 = mkt("m3", 1) * 2; m4 = mkt("m4", 1) * 2
    cr = mkt("cr", 1) * 2; ci = mkt("ci", 1) * 2
    vr = [mkt("vr0", 1) * 2, mkt("vr1", 1) * 2]; vi = [mkt("vi0", 1) * 2, mkt("vi1", 1) * 2]
    xr = mkt("xr", 1) * 2; xi = mkt("xi", 1) * 2
    init = [k.sb([128, 4], F32, f"init{j}") for j in range(2)]
    yb = [k.sb([64, W], F32, f"yb{i}") for i in range(2)]
    g1 = k.sb([64, W], F32, "g1"); g2 = k.sb([64, W], F32, "g2")
    pn = [0]

    def step(s):
        u = ub[s % 2]
        k.dma("sp", u[:], d["s_uT"][:, s * W:(s + 1) * W], t_out=u)
        py = pY[s % len(pY)]
        def tile_body(j):
            b = (2 * s + j) % 2
            Pr = pP[pn[0] % len(pP)]; Pi = pP[(pn[0] + 1) % len(pP)]; pn[0] += 2
            k.op("pe", lambda g, Pr=Pr: g.matmul(Pr[:], lhsT=Bre[j][:], rhs=u[:], start=True, stop=True), [Bre[j], u], [Pr])
            k.op("act", lambda g, Pr=Pr: g.copy(out=prs[b][:], in_=Pr[:]), [Pr], [prs[b]])
            k.op("pe", lambda g, Pi=Pi: g.matmul(Pi[:], lhsT=Bim[j][:], rhs=u[:], start=True, stop=True), [Bim[j], u], [Pi])
            k.op("act", lambda g, Pi=Pi: g.copy(out=pis[b][:], in_=Pi[:]), [Pi], [pis[b]])
            Tr = Trt[j]; Ti = Tit[j]; C = Ct[j]; nS = nSt[j]; pr = rr[j]
            k.op("pool", lambda g: g.tensor_tensor(out=m1[b][:], in0=Tr[:], in1=prs[b][:], op=ALU.mult), [Tr, prs[b]], [m1[b]])
            k.op("pool", lambda g: g.tensor_tensor(out=m2[b][:], in0=Ti[:], in1=pis[b][:], op=ALU.mult), [Ti, pis[b]], [m2[b]])
            k.op("pool", lambda g: g.tensor_tensor(out=cr[b][:], in0=m1[b][:], in1=m2[b][:], op=ALU.subtract),
                 [m1[b], m2[b]], [cr[b]])
            k.op("dve", lambda g: g.tensor_tensor(out=m3[b][:], in0=Tr[:], in1=pis[b][:], op=ALU.mult), [Tr, pis[b]], [m3[b]])
            k.op("dve", lambda g: g.tensor_tensor(out=m4[b][:], in0=Ti[:], in1=prs[b][:], op=ALU.mult), [Ti, prs[b]], [m4[b]])
            k.op("dve", lambda g: g.tensor_tensor(out=ci[b][:], in0=m3[b][:], in1=m4[b][:], op=ALU.add),
                 [m3[b], m4[b]], [ci[b]])
            VR = vr[j][s % 2]; VI = vi[j][s % 2]
            rb = pr[:, 2:3].to_broadcast([128, W])
            if s == 0:
                ir = 0.0; ii_ = 0.0; extra = []
            else:
                ir = init[j][:, 0:1]; ii_ = init[j][:, 1:2]; extra = [init[j]]
            k.op("dve", lambda g, ir=ir: g.tensor_tensor_scan(out=VR[:], data0=rb, data1=cr[b][:], initial=ir,
                                                           op0=ALU.mult, op1=ALU.add), [pr, cr[b]] + extra, [VR])
            k.op("dve", lambda g, ii_=ii_: g.tensor_tensor_scan(out=VI[:], data0=rb, data1=ci[b][:], initial=ii_,
                                                             op0=ALU.mult, op1=ALU.add), [pr, ci[b]] + extra, [VI])
            it = init[j]
            k.op("dve", lambda g: g.tensor_tensor(out=it[:, 2:3], in0=VI[:, W - 1:W], in1=St[j][:, W:W + 1], op=ALU.mult),
                 [VI, St[j]], [it])
            k.op("dve", lambda g: g.scalar_tensor_tensor(out=it[:, 0:1], in0=VR[:, W - 1:W], scalar=C[:, W:W + 1],
                                                         in1=it[:, 2:3], op0=ALU.mult, op1=ALU.subtract), [VR, C, it], [it])
            k.op("dve", lambda g: g.tensor_tensor(out=it[:, 3:4], in0=VI[:, W - 1:W], in1=C[:, W:W + 1], op=ALU.mult),
                 [VI, C], [it])
            k.op("dve", lambda g: g.scalar_tensor_tensor(out=it[:, 1:2], in0=VR[:, W - 1:W], scalar=St[j][:, W:W + 1],
                                                         in1=it[:, 3:4], op0=ALU.mult, op1=ALU.add), [VR, St[j], it], [it])
            k.op("pool", lambda g: g.tensor_tensor(out=m1[b][:], in0=VR[:], in1=C[:, 0:W], op=ALU.mult), [VR, C], [m1[b]])
            k.op("pool", lambda g: g.tensor_tensor(out=m2[b][:], in0=VI[:], in1=nS[:, 0:W], op=ALU.mult), [VI, nS], [m2[b]])
            k.op("pool", lambda g: g.tensor_tensor(out=xr[b][:], in0=m1[b][:], in1=m2[b][:], op=ALU.add),
                 [m1[b], m2[b]], [xr[b]])
            k.op("dve", lambda g: g.tensor_tensor(out=m3[b][:], in0=VR[:], in1=nS[:, 0:W], op=ALU.mult), [VR, nS], [m3[b]])
            k.op("dve", lambda g: g.tensor_tensor(out=m4[b][:], in0=VI[:], in1=C[:, 0:W], op=ALU.mult), [VI, C], [m4[b]])
            k.op("dve", lambda g: g.tensor_tensor(out=xi[b][:], in0=m3[b][:], in1=m4[b][:], op=ALU.subtract),
                 [m3[b], m4[b]], [xi[b]])
            k.op("pe", lambda g: g.matmul(py[0:64, :], lhsT=Cre[j][:], rhs=xr[b][:], start=(j == 0), stop=False),
                 [Cre[j], xr[b]], [py])
            k.op("pe", lambda g: g.matmul(py[0:64, :], lhsT=Cim[j][:], rhs=xi[b][:], start=False, stop=(j == 1)),
                 [Cim[j], xi[b]], [py])
        for j in range(2):
            tile_body(j)
        y = yb[s % 2]
        k.op("dve", lambda g: g.scalar_tensor_tensor(out=g1[:], in0=u[:], scalar=dsk[:, 0:1], in1=py[0:64, :],
                                                     op0=ALU.mult, op1=ALU.add), [u, dsk, py], [g1])
        k.op("pool", lambda g: g.tensor_tensor(out=g2[:], in0=g1[:], in1=g1[:], op=ALU.mult), [g1], [g2])
        k.op("pool", lambda g: g.tensor_scalar(out=g2[:], in0=g2[:], scalar1=0.044715, scalar2=1.0, op0=ALU.mult, op1=ALU.add),
             [g2], [g2])
        k.op("pool", lambda g: g.tensor_tensor(out=g2[:], in0=g2[:], in1=g1[:], op=ALU.mult), [g2, g1], [g2])
        k.op("act", lambda g: g.activation(out=g2[:], in_=g2[:], func=AF.Tanh, scale=math.sqrt(2.0 / math.pi)), [g2], [g2])
        k.op("pool", lambda g: g.tensor_scalar(out=g2[:], in0=g2[:], scalar1=1.0, scalar2=0.5, op0=ALU.add, op1=ALU.mult),
             [g2], [g2])
        k.op("pool", lambda g: g.tensor_tensor(out=y[:], in0=g2[:], in1=g1[:], op=ALU.mult), [g2, g1], [y])
        k.dma("sp", out_d[:, s * W:(s + 1) * W], y[:], t_in=y, final=True)

    if stepper:
        return step
    for s in range(NST):
        step(s)


def s5_inputs(projT, inp, l, core, L=SEQ):
    r = lambda a: np.ascontiguousarray(a, dtype=np.float32)
    g0 = core * 4
    f = lambda nm: np.asarray(inp[nm][l], np.float32)
    lamre = f("s5_lam_re")[g0:g0 + 4].reshape(2, 128).T
    lamim = f("s5_lam_im")[g0:g0 + 4].reshape(2, 128).T
    logdt = np.repeat(f("s5_log_dt")[g0:g0 + 4], 64).reshape(2, 128).T
    bre = np.zeros((2, 64, 128), np.float32); bim = np.zeros((2, 64, 128), np.float32)
    cre = np.zeros((2, 128, 64), np.float32); cim = np.zeros((2, 128, 64), np.float32)
    for j in range(2):
        for gl in range(2):
            g = g0 + 2 * j + gl
            ch = slice((2 * j + gl) * 16, (2 * j + gl + 1) * 16)
            st = slice(gl * 64, (gl + 1) * 64)
            bre[j, ch, st] = f("s5_b_re")[g].T
            bim[j, ch, st] = f("s5_b_im")[g].T
            cre[j, st, ch] = f("s5_c_re")[g].T
            cim[j, st, ch] = f("s5_c_im")[g].T
    return {
        "s_uT": r(projT[1304 + core * 64:1304 + (core + 1) * 64, :L]),
        "s_lamre": r(lamre), "s_lamim": r(lamim), "s_logdt": r(logdt),
        "s_bre": bre, "s_bim": bim, "s_cre": cre, "s_cim": cim,
        "s_d": r(f("s5_d")[core * 64:(core + 1) * 64].reshape(64, 1)),
        "s_trow": r(np.tile(np.arange(S5_W + 1, dtype=np.float32)[None], (128, 1))),
    }


ROPE_THETA = 500000.0


def rope_tables(pos):
    half = 8
    inv = (ROPE_THETA ** (-np.arange(half, dtype=np.float32) / half)).astype(np.float32)
    ang = pos.astype(np.float32)[None, :] * inv[:, None]
    c = np.ones((64, len(pos)), np.float32); s = np.zeros((64, len(pos)), np.float32)
    c[0:8] = np.cos(ang); c[8:16] = np.cos(ang)
    s[0:8] = np.sin(ang); s[8:16] = np.sin(ang)
    return c, s


def rope_perm():
    pm = np.zeros((64, 64), np.float32)
    for dd in range(8):
        pm[dd + 8, dd] = -1.0
        pm[dd, dd + 8] = 1.0
    return pm


def build_prep(ntok=TOK):
    nc = bass.Bass("TRN2", target_bir_lowering=False)
    d = {}

    def inp(name, shape):
        d[name] = nc.dram_tensor(name, list(shape), F32, kind="ExternalInput").ap()

    NB = ntok // 16
    inp("p_q", [8, 64, ntok]); inp("p_ks", [2, 64, ntok]); inp("p_kw", [2, 64, ntok])
    inp("p_kc", [2, 64, ntok + 16]); inp("p_vc", [2, 64, ntok + 16])
    inp("p_gains", [64, 4]); inp("p_cos", [64, ntok]); inp("p_sin", [64, ntok])
    inp("p_cosc", [64, NB]); inp("p_sinc", [64, NB]); inp("p_pm", [64, 64])
    inp("p_w1k", [64, 32, 256]); inp("p_w1v", [64, 32, 256]); inp("p_w2k", [128, 2, 64]); inp("p_w2v", [128, 2, 64])
    inp("p_peT", [64, 32])
    o = {}
    for name, shape in (("o_q", [8, 64, ntok]), ("o_ks", [2, 64, ntok]), ("o_kw", [2, 64, ntok]),
                        ("o_kc", [2, 64, NB]), ("o_vc", [NB, 2, 64])):
        o[name] = nc.dram_tensor(name, shape, F32, kind="ExternalOutput").ap()
    k = KB(nc)
    W = 512

    def load(name, shape):
        t = k.sb(shape, F32, name)
        k.dma("sp", t[:], d[name], t_out=t)
        return t

    gains = load("p_gains", [64, 4]); pm = load("p_pm", [64, 64])
    cosT = load("p_cos", [64, ntok]); sinT = load("p_sin", [64, ntok])
    cosc = load("p_cosc", [64, NB]); sinc = load("p_sinc", [64, NB])
    ones = k.sb([64, 64], F32, "ones64")
    k.op("pool", lambda g: g.memset(ones[:], 1.0), [], [ones])
    xin = [k.sb([64, W], F32, f"xin{i}") for i in range(3)]
    sq = k.sb([64, W], F32, "sq"); rs = k.sb([64, W], F32, "rs"); xn = [k.sb([64, W], F32, f"xn{i}") for i in range(2)]
    t1 = [k.sb([64, W], F32, f"t1{i}") for i in range(2)]
    ob = [k.sb([64, W], F32, f"ob{i}") for i in range(3)]
    pst = [k.ps([128, W], F32, f"pst{i}") for i in range(2)]
    prt = [k.ps([128, W], F32, f"prt{i}") for i in range(2)]
    cnt = [0]

    def norm_rope(src_t, src_ap, n, gcol, scale_eps, ones_val_scale, cos_ap, sin_ap, dst_t, dst_ap):
        i = cnt[0]; cnt[0] += 1
        ps = pst[i % 2]; pr = prt[i % 2]; x = xn[i % 2]; tt = t1[i % 2]
        k.op("act", lambda g: g.activation(out=sq[:, :n], in_=src_ap, func=AF.Square), [src_t], [sq])
        k.op("pe", lambda g: g.matmul(ps[0:64, :n], lhsT=ones[:], rhs=sq[:, :n], start=True, stop=True), [ones, sq], [ps])
        k.op("dve", lambda g: g.tensor_scalar(out=rs[:, :n], in0=ps[0:64, :n], scalar1=ones_val_scale, scalar2=scale_eps,
                                              op0=ALU.mult, op1=ALU.add), [ps], [rs])
        k.op("act", lambda g: g.activation(out=rs[:, :n], in_=rs[:, :n], func=AF.Sqrt), [rs], [rs])
        k.op("dve", lambda g: g.reciprocal(out=rs[:, :n], in_=rs[:, :n]), [rs], [rs])
        k.op("dve", lambda g: g.scalar_tensor_tensor(out=x[:, :n], in0=src_ap, scalar=gains[:, gcol:gcol + 1], in1=rs[:, :n],
                                                     op0=ALU.mult, op1=ALU.mult), [src_t, gains, rs], [x])
        k.op("pe", lambda g: g.matmul(pr[0:64, :n], lhsT=pm[:], rhs=x[:, :n], start=True, stop=True), [pm, x], [pr])
        k.op("dve", lambda g: g.tensor_tensor(out=tt[:, :n], in0=pr[0:64, :n], in1=sin_ap, op=ALU.mult), [pr, sinT, sinc], [tt])
        k.op("pool", lambda g: g.tensor_tensor(out=dst_ap, in0=x[:, :n], in1=cos_ap, op=ALU.mult), [x, cosT, cosc], [dst_t])
        k.op("pool", lambda g: g.tensor_tensor(out=dst_ap, in0=dst_ap, in1=tt[:, :n], op=ALU.add), [dst_t, tt], [dst_t])

    n_it = 0
    for name, oname, nh, gcol, isq in (("p_q", "o_q", 8, 0, True), ("p_ks", "o_ks", 2, 2, False), ("p_kw", "o_kw", 2, 3, False)):
        for h in range(nh):
            for s in range(ntok // W):
                xi = xin[n_it % 3]; oo = ob[n_it % 3]; n_it += 1
                k.dma("sp", xi[:], d[name][h, :, s * W:(s + 1) * W], t_out=xi)
                if isq:
                    a, bb = 1.0, 64.0 * EPS
                else:
                    a, bb = 1.0 / 64.0, EPS
                norm_rope(xi, xi[:], W, gcol, bb, a, cosT[:, s * W:(s + 1) * W], sinT[:, s * W:(s + 1) * W], oo, oo[:])
                k.dma("sp", o[oname][h, :, s * W:(s + 1) * W], oo[:], t_in=oo, final=True)

    peT = load("p_peT", [64, 32])
    w2k = load("p_w2k", [128, 2, 64]); w2v = load("p_w2v", [128, 2, 64])
    w1 = k.sb([64, 32, 256], F32, "w1")
    raw = [k.sb([64, ntok + 16], F32, f"raw{i}") for i in range(2)]
    gel = [k.sb([128, NB], F32, f"gel{i}") for i in range(2)]
    ga = k.sb([128, NB], F32, "ga"); gb = k.sb([128, NB], F32, "gb")
    bias = k.sb([128, 2], F32, "bias")
    kcn = k.sb([64, NB], F32, "kcn"); kco = k.sb([64, NB], F32, "kco")
    vco = k.sb([NB, 2, 64], F32, "vco")
    ph = [k.ps([128, 512], F32, f"ph{i}") for i in range(2)]
    pb = k.ps([128, 512], F32, "pb")
    po = k.ps([128, 512], F32, "po")
    for which, (rname, wname, w2) in enumerate((("p_kc", "p_w1k", w2k), ("p_vc", "p_w1v", w2v))):
        k.dma("sp", w1[:], d[wname], t_out=w1)
        for ft in range(2):
            for l in range(32):
                k.op("pe", lambda g, l=l, ft=ft: g.matmul(pb[:, ft:ft + 1], lhsT=w1[:, l, ft * 128:(ft + 1) * 128],
                                                        rhs=peT[:, l:l + 1], start=(l == 0), stop=(l == 31)), [w1, peT], [pb])
        k.op("dve", lambda g: g.tensor_copy(out=bias[:], in_=pb[:, 0:2]), [pb], [bias])
        for hk in range(2):
            r = raw[hk]
            k.dma("sp", r[:], d[rname][hk], t_out=r)
            for ft in range(2):
                p = ph[ft]
                for l in range(32):
                    k.op("pe", lambda g, l=l, ft=ft, p=p, r=r: g.matmul(
                        p[:, :NB], lhsT=w1[:, l, ft * 128:(ft + 1) * 128], rhs=r[:, l:l + 16 * (NB - 1) + 1:16],
                        start=(l == 0), stop=(l == 31)), [w1, r], [p])
                k.op("act", lambda g, p=p, ft=ft: g.activation(out=ga[:], in_=p[:, :NB], func=AF.Identity,
                                                               bias=bias[:, ft:ft + 1], scale=1.0), [p, bias], [ga])
                k.op("pool", lambda g: g.tensor_tensor(out=gb[:], in0=ga[:], in1=ga[:], op=ALU.mult), [ga], [gb])
                k.op("pool", lambda g: g.tensor_scalar(out=gb[:], in0=gb[:], scalar1=0.044715, scalar2=1.0, op0=ALU.mult,
                                                       op1=ALU.add), [gb], [gb])
                k.op("pool", lambda g: g.tensor_tensor(out=gb[:], in0=gb[:], in1=ga[:], op=ALU.mult), [gb, ga], [gb])
                k.op("act", lambda g: g.activation(out=gb[:], in_=gb[:], func=AF.Tanh, scale=math.sqrt(2.0 / math.pi)),
                     [gb], [gb])
                k.op("pool", lambda g: g.tensor_scalar(out=gb[:], in0=gb[:], scalar1=1.0, scalar2=0.5, op0=ALU.add,
                                                       op1=ALU.mult), [gb], [gb])
                k.op("pool", lambda g, ft=ft: g.tensor_tensor(out=gel[ft][:], in0=gb[:], in1=ga[:], op=ALU.mult),
                     [gb, ga], [gel[ft]])
            if which == 0:
                for ft in range(2):
                    k.op("pe", lambda g, ft=ft: g.matmul(po[0:64, :NB], lhsT=w2[:, ft, :], rhs=gel[ft][:],
                                                        start=(ft == 0), stop=(ft == 1)), [w2, gel[ft]], [po])
                k.op("dve", lambda g: g.tensor_copy(out=kcn[:], in_=po[0:64, :NB]), [po], [kcn])
                norm_rope(kcn, kcn[:], NB, 1, EPS, 1.0 / 64.0, cosc[:], sinc[:], kco, kco[:])
                k.dma("sp", o["o_kc"][hk], kco[:], t_in=kco, final=True)
            else:
                for ft in range(2):
                    k.op("pe", lambda g, ft=ft: g.matmul(po[0:NB, 0:64], lhsT=gel[ft][:], rhs=w2[:, ft, :],
                                                        start=(ft == 0), stop=(ft == 1)), [w2, gel[ft]], [po])
                k.op("dve", lambda g, hk=hk: g.tensor_copy(out=vco[:, hk, :], in_=po[0:NB, 0:64]), [po], [vco])
    k.dma("sp", o["o_vc"], vco[:], t_in=vco, final=True)
    k.finish()
    return nc


def prep_inputs(projT, inp, l, c, ntok=TOK):
    r = lambda a: np.ascontiguousarray(a, dtype=np.float32)
    t0 = c * ntok
    L = projT.shape[1]

    def halo(rows):
        a = np.zeros((rows.shape[0], ntok + 16), np.float32)
        n = min(ntok + 16, L - t0)
        a[:, :n] = rows[:, t0:t0 + n]
        return a.reshape(2, 64, ntok + 16)

    pos = np.arange(t0, t0 + ntok, dtype=np.float32)
    cos, sin = rope_tables(pos)
    posc = (np.arange(t0 // 16, t0 // 16 + ntok // 16) * 16 + 16).astype(np.float32)
    cosc, sinc = rope_tables(posc)
    f = lambda nm: np.asarray(inp[nm][l], np.float32)
    gains = np.stack([f("nsa_q_norm"), f("nsa_kc_norm"), f("nsa_ks_norm"), f("nsa_kw_norm")], axis=1)
    return {
        "p_q": r(projT[0:512, t0:t0 + ntok].reshape(8, 64, ntok)),
        "p_ks": r(projT[768:896, t0:t0 + ntok].reshape(2, 64, ntok)),
        "p_kw": r(projT[1024:1152, t0:t0 + ntok].reshape(2, 64, ntok)),
        "p_kc": halo(projT[512:640]), "p_vc": halo(projT[640:768]),
        "p_gains": r(gains), "p_cos": cos, "p_sin": sin, "p_cosc": cosc, "p_sinc": sinc, "p_pm": rope_perm(),
        "p_w1k": r(f("cmp_k_w1").transpose(1, 0, 2)), "p_w1v": r(f("cmp_v_w1").transpose(1, 0, 2)),
        "p_w2k": r(f("cmp_k_w2").reshape(2, 128, 64).transpose(1, 0, 2)),
        "p_w2v": r(f("cmp_v_w2").reshape(2, 128, 64).transpose(1, 0, 2)),
        "p_peT": r(f("cmp_pe").T),
    }


NSLOT = 16
MASKV = -30000.0
BIGNEG = -1.0e30


def build_nsa(nslots=NSLOT, L=SEQ):
    nc = bass.Bass("TRN2", target_bir_lowering=False)
    d = {}

    def inp(name, shape):
        d[name] = nc.dram_tensor(name, list(shape), F32, kind="ExternalInput").ap()

    NT = L // 128
    inp("n_ksT", [2, 64, L]); inp("n_vs", [L, 2, 65]); inp("n_kcT", [2, 64, 1024]); inp("n_vc", [1024, 2, 65])
    inp("n_qT", [NSLOT, 64, 8, 128]); inp("n_kw", [NSLOT, 2, 64, 640]); inp("n_vw", [NSLOT, 640, 2, 65])
    inp("n_gates", [NSLOT, 128, 24])
    inp("m_diag", [8, 128, 512]); inp("m_win", [NSLOT, 5, 128, 512]); inp("m_cmpT", [NSLOT, 2, 128, 512])
    inp("m_cmpqn", [NSLOT, 128, 1024]); inp("r_tab", [NSLOT, 128, 2, 256])
    inp("c_ident", [128, 128]); inp("c_expand", [128, 64 * 128])
    out_d = nc.dram_tensor("n_out", [NSLOT, 128, 512], F32, kind="ExternalOutput").ap()
    k = KB(nc)
    emit_nsa(k, d, out_d, nslots, L)
    k.finish()
    return nc


def emit_nsa(k, d, out_d, nslots, L):
    NT = L // 128
    ksT = k.sb([64, 2, L], BF16, "ksT")
    for hk in range(2):
        for c4 in range(4):
            sl = slice(c4 * (L // 4), (c4 + 1) * (L // 4))
            k.dma("pool", ksT[:, hk, sl], d["n_ksT"][hk][:, sl], t_out=ksT)
    vs = k.sb([128, NT, 2, 65], BF16, "vs")
    vsv = d["n_vs"].rearrange("(t p) h e -> p t h e", p=128)
    for c4 in range(8):
        sl = slice(c4 * (NT // 8), (c4 + 1) * (NT // 8))
        k.dma("pool", vs[:, sl], vsv[:, sl], t_out=vs)
    kcT = k.sb([64, 2, 1024], BF16, "kcT")
    for hk in range(2):
        k.dma("pool", kcT[:, hk, :], d["n_kcT"][hk], t_out=kcT)
    vc = k.sb([128, 8, 2, 65], BF16, "vc")
    k.dma("pool", vc[:], d["n_vc"].rearrange("(t p) h e -> p t h e", p=128), t_out=vc)
    mdiag = k.sb([128, 8, 512], BF16, "mdiag")
    k.dma("pool", mdiag[:], d["m_diag"].rearrange("r p f -> p r f"), t_out=mdiag)
    expand = k.sb([128, 64 * 128], BF16, "expand")
    k.dma("pool", expand[:], d["c_expand"], t_out=expand)
    ident = k.sb([128, 128], F32, "ident")
    k.dma("sp", ident[:], d["c_ident"], t_out=ident)

    two = lambda shape, dt, nm: [k.sb(shape, dt, f"{nm}{i}") for i in range(2)]
    qT = two([64, 8, 128], BF16, "qT"); kw = two([64, 2, 640], BF16, "kw"); vw = two([128, 5, 2, 65], BF16, "vw")
    mwin = two([128, 5, 512], BF16, "mwin"); mcT = two([128, 2, 512], BF16, "mcT"); mqn = two([128, 1024], BF16, "mqn")
    rtab = two([128, 2, 256], F32, "rtab"); gates = two([128, 24], F32, "gates")
    ya = two([128, 512], F32, "ya")
    negT4 = two([128, 2, 4, 128], BF16, "negT4")
    E = [k.sb([128, 512], BF16, f"E{i}") for i in range(3)]
    Ecmp = [k.sb([128, 1024], F32, f"Ecmp{i}") for i in range(2)]
    Em = k.sb([128, 1024], F32, "Em")
    pcs = k.sb([128, 1032], F32, "pcs")
    imp = k.sb([128, 256], F32, "imp"); ieff = k.sb([128, 256], F32, "ieff")
    zap1 = k.sb([128, 256], F32, "zap1"); zap2 = k.sb([128, 256], F32, "zap2"); negm = k.sb([128, 256], F32, "negm")
    mx8 = k.sb([128, 8], F32, "mx8")
    sm = k.sb([128, 16], F32, "sm")
    cf = k.sb([128, 8], F32, "cf")
    pS = [k.ps([128, 512], F32, f"pS{i}") for i in range(2)]
    pO = [k.ps([128, 512], F32, f"pO{i}") for i in range(4)]
    pC = [k.ps([128, 512], F32, f"pC{i}") for i in range(2)]
    pT = pC[0]
    cnt = {"s": 0, "e": 0, "o": 0, "ec": 0}

    def slot_loads(i):
        b = i % 2
        k.dma("pool", qT[b][:], d["n_qT"][i], t_out=qT[b])
        k.dma("pool", kw[b][:], d["n_kw"][i].rearrange("h d n -> d h n"), t_out=kw[b])
        k.dma("pool", vw[b][:], d["n_vw"][i].rearrange("(t p) h e -> p t h e", p=128), t_out=vw[b])
        k.dma("pool", mwin[b][:], d["m_win"][i].rearrange("w p f -> p w f"), t_out=mwin[b])
        k.dma("pool", mcT[b][:], d["m_cmpT"][i].rearrange("w p f -> p w f"), t_out=mcT[b])
        k.dma("pool", mqn[b][:], d["m_cmpqn"][i], t_out=mqn[b])
        k.dma("sp", rtab[b][:], d["r_tab"][i], t_out=rtab[b])
        k.dma("sp", gates[b][:], d["n_gates"][i], t_out=gates[b])
        k.op("act", lambda g: g.activation(out=gates[b][:], in_=gates[b][:], func=AF.Sigmoid), [gates[b]], [gates[b]])

    def stage1(i, hk):
        b = i % 2
        x = (2 * i + hk) % 2
        NV = 64 * i + 64
        NJ = 16 * i + 16
        k.op("pool", lambda g: g.memset(pcs[:], 0.0), [], [pcs])
        k.op("pool", lambda g: g.memset(imp[:], 0.0), [], [imp])
        for gq in range(4):
            h = hk * 4 + gq
            ec = Ecmp[cnt["ec"] % 2]; cnt["ec"] += 1
            for c0 in range(0, NV, 512):
                cw = min(512, NV - c0)
                p = pC[(c0 // 512) % 2]
                k.op("pe", lambda g, p=p, c0=c0, cw=cw, h=h: g.matmul(p[:, :cw], lhsT=qT[b][:, h, :], rhs=kcT[:, hk, c0:c0 + cw],
                                                                   start=True, stop=True), [qT[b], kcT], [p])
                k.op("act", lambda g, p=p, c0=c0, cw=cw, ec=ec: g.activation(out=ec[:, c0:c0 + cw], in_=p[:, :cw], func=AF.Exp),
                     [p], [ec])
            k.op("dve", lambda g, ec=ec, gq=gq: g.scalar_tensor_tensor(
                out=Em[:, :NV], in0=ec[:, :NV], scalar=1.0, in1=mqn[b][:, :NV], op0=ALU.mult, op1=ALU.mult,
                accum_out=sm[:, gq:gq + 1]), [ec, mqn[b]], [Em, sm])
            k.op("dve", lambda g, gq=gq: g.tensor_scalar(out=sm[:, 4 + gq:5 + gq], in0=sm[:, gq:gq + 1], scalar1=1e-30,
                                                         scalar2=None, op0=ALU.max), [sm], [sm])
            k.op("dve", lambda g, gq=gq: g.reciprocal(out=sm[:, 4 + gq:5 + gq], in_=sm[:, 4 + gq:5 + gq]), [sm], [sm])
            if gq == 0:
                k.op("dve", lambda g, gq=gq: g.tensor_scalar(out=pcs[:, 1:1 + NV], in0=Em[:, :NV], scalar1=sm[:, 4 + gq:5 + gq],
                                                             scalar2=None, op0=ALU.mult), [Em, sm], [pcs])
            else:
                k.op("dve", lambda g, gq=gq: g.scalar_tensor_tensor(out=pcs[:, 1:1 + NV], in0=Em[:, :NV],
                                                                    scalar=sm[:, 4 + gq:5 + gq], in1=pcs[:, 1:1 + NV],
                                                                    op0=ALU.mult, op1=ALU.add), [Em, sm, pcs], [pcs])
        vw_ = lambda r: pcs[:, r:r + 4 * (NJ - 1) + 1:4]
        k.op("dve", lambda g: g.tensor_tensor(out=imp[:, :NJ], in0=vw_(0), in1=vw_(1), op=ALU.add), [pcs], [imp])
        for r in (2, 3, 4):
            k.op("dve", lambda g, r=r: g.tensor_tensor(out=imp[:, :NJ], in0=imp[:, :NJ], in1=vw_(r), op=ALU.add), [pcs, imp], [imp])
        k.op("dve", lambda g: g.tensor_tensor(out=ieff[:], in0=imp[:], in1=rtab[b][:, 0, :], op=ALU.add), [imp, rtab[b]], [ieff])
        k.op("dve", lambda g: g.tensor_tensor(out=ieff[:], in0=ieff[:], in1=rtab[b][:, 1, :], op=ALU.max), [ieff, rtab[b]], [ieff])
        k.op("dve", lambda g: g.max(out=mx8[:], in_=ieff[:]), [ieff], [mx8])
        k.op("dve", lambda g: g.match_replace(out=zap1[:], in_to_replace=mx8[:], in_values=ieff[:], imm_value=BIGNEG),
             [mx8, ieff], [zap1])
        k.op("dve", lambda g: g.max(out=mx8[:], in_=zap1[:]), [zap1], [mx8])
        k.op("dve", lambda g: g.match_replace(out=zap2[:], in_to_replace=mx8[:], in_values=zap1[:], imm_value=BIGNEG),
             [mx8, zap1], [zap2])
        k.op("dve", lambda g: g.tensor_tensor(out=negm[:], in0=ieff[:], in1=zap2[:], op=ALU.subtract), [ieff, zap2], [negm])
        k.op("dve", lambda g: g.tensor_scalar(out=negm[:], in0=negm[:], scalar1=1.0, scalar2=None, op0=ALU.min), [negm], [negm])
        k.op("dve", lambda g: g.tensor_scalar(out=negm[:], in0=negm[:], scalar1=-MASKV, scalar2=MASKV, op0=ALU.mult,
                                              op1=ALU.add), [negm], [negm])
        for half in range(2):
            k.op("pe", lambda g, half=half: g.matmul(pT[:, half * 128:(half + 1) * 128], lhsT=negm[:, half * 128:(half + 1) * 128],
                                                    rhs=ident[:], start=True, stop=True), [negm, ident], [pT])
        n4 = negT4[x]
        for half in range(2):
            for gq in range(4):
                e = "act" if (gq % 2 == 0) else "dve"
                k.copy(e, n4, n4[:, half, gq, :], pT, pT[:, half * 128:(half + 1) * 128])
        return x

    def attend(tiles, q4, q4_t, po):
        nt = len(tiles)
        es = []

        def pv(idx):
            tl = tiles[idx]; e = es[idx]
            for gq in range(4):
                k.op("pe", lambda g, e=e, gq=gq, tl=tl: g.matmul(po[gq][:, 0:65], lhsT=e[:, gq * 128:(gq + 1) * 128], rhs=tl["V"],
                                                              start=(idx == 0), stop=(idx == nt - 1)), [e, tl["Vt"]], [po[gq]])

        for idx, tl in enumerate(tiles):
            p = pS[cnt["s"] % 2]; cnt["s"] += 1
            e = E[cnt["e"] % 3]; cnt["e"] += 1
            es.append(e)
            add = tl.get("add")
            k.op("pe", lambda g, p=p, tl=tl: g.matmul(p[:], lhsT=tl["K"], rhs=q4, start=True, stop=(tl.get("add") is None)),
                 [tl["Kt"], q4_t], [p])
            if add is not None:
                k.op("pe", lambda g, p=p, add=add: g.matmul(p[:], lhsT=add[0], rhs=add[1], start=False, stop=True),
                     [expand, add[2]], [p])
            k.op("act", lambda g, p=p, e=e: g.activation(out=e[:], in_=p[:], func=AF.Exp), [p], [e])
            mul = tl.get("mul")
            if mul is not None:
                k.op("pool", lambda g, e=e, mul=mul: g.tensor_tensor(out=e[:], in0=e[:], in1=mul[0], op=ALU.mult), [e, mul[1]], [e])
            if idx >= 1:
                pv(idx - 1)
        pv(nt - 1)

    def combine(i, hk, br, po):
        b = i % 2
        y = ya[b]
        for gq in range(4):
            h = hk * 4 + gq
            k.op("dve", lambda g, gq=gq: g.tensor_scalar(out=cf[:, 0:1], in0=po[gq][:, 64:65], scalar1=1e-30, scalar2=None,
                                                         op0=ALU.max), [po[gq]], [cf])
            k.op("dve", lambda g: g.reciprocal(out=cf[:, 0:1], in_=cf[:, 0:1]), [cf], [cf])
            k.op("dve", lambda g, h=h: g.tensor_tensor(out=cf[:, 1:2], in0=cf[:, 0:1], in1=gates[b][:, h * 3 + br:h * 3 + br + 1],
                                                       op=ALU.mult), [cf, gates[b]], [cf])
            if br == 0:
                k.op("dve", lambda g, gq=gq, h=h: g.tensor_scalar(out=y[:, h * 64:(h + 1) * 64], in0=po[gq][:, 0:64],
                                                                  scalar1=cf[:, 1:2], scalar2=None, op0=ALU.mult), [po[gq], cf], [y])
            else:
                k.op("dve", lambda g, gq=gq, h=h: g.scalar_tensor_tensor(out=y[:, h * 64:(h + 1) * 64], in0=po[gq][:, 0:64],
                                                                         scalar=cf[:, 1:2], in1=y[:, h * 64:(h + 1) * 64],
                                                                         op0=ALU.mult, op1=ALU.add), [po[gq], cf, y], [y])

    def stage2(i, hk, x):
        b = i % 2
        q4 = qT[b][:, hk * 4:(hk + 1) * 4, :]
        ntc = i // 2 + 1
        tiles = []
        for nt_ in range(ntc):
            tl = {"K": kcT[:, hk, nt_ * 128:(nt_ + 1) * 128], "Kt": kcT, "V": vc[:, nt_, hk, :], "Vt": vc}
            m = nt_ - (ntc - 2)
            if m >= 0:
                tl["mul"] = (mcT[b][:, m, :], mcT[b])
            tiles.append(tl)
        po = pO
        attend(tiles, q4, qT[b], po)
        combine(i, hk, 0, po)
        tiles = []
        for w in range(5):
            tl = {"K": kw[b][:, hk, w * 128:(w + 1) * 128], "Kt": kw[b], "V": vw[b][:, w, hk, :], "Vt": vw[b]}
            if i == 0 or w in (0, 4):
                tl["mul"] = (mwin[b][:, w, :], mwin[b])
            tiles.append(tl)
        po = pO
        attend(tiles, q4, qT[b], po)
        combine(i, hk, 2, po)
        tiles = []
        for kt in range(8 * i + 8):
            tl = {"K": ksT[:, hk, kt * 128:(kt + 1) * 128], "Kt": ksT, "V": vs[:, kt, hk, :], "Vt": vs,
                  "add": (expand[:, (kt % 64) * 128:(kt % 64 + 1) * 128], negT4[x][:, kt // 64, :, :], negT4[x])}
            if kt >= 8 * i:
                tl["mul"] = (mdiag[:, kt - 8 * i, :], mdiag)
            tiles.append(tl)
        po = pO
        attend(tiles, q4, qT[b], po)
        combine(i, hk, 1, po)
        if hk == 1:
            k.dma("sp", out_d[i], ya[b][:], t_in=ya[b], final=True)

    units = [(i, hk) for i in range(nslots) for hk in range(2)]
    slot_loads(0)
    xs = {}
    xs[units[0]] = stage1(*units[0])
    for ui, (i, hk) in enumerate(units):
        if ui + 1 < len(units):
            ni, nhk = units[ui + 1]
            if nhk == 0:
                slot_loads(ni)
            xs[(ni, nhk)] = stage1(ni, nhk)
        stage2(i, hk, xs[(i, hk)])


def nsa_consts():
    ident = np.eye(128, dtype=np.float32)
    ex = np.zeros((128, 64, 128), np.float32)
    for kt in range(64):
        ex[2 * kt, kt, 0:64] = 1.0
        ex[2 * kt + 1, kt, 64:128] = 1.0
    return ident, ex.reshape(128, 64 * 128)


def nsa_masks(c, L=SEQ):
    q = np.arange(128)
    key = np.arange(128)
    m_diag = np.zeros((8, 128, 512), np.float32)
    for r in range(8):
        if r < c:
            m_diag[r] = 1.0
        elif r == c:
            m_diag[r] = np.tile((key[:, None] <= q[None, :]).astype(np.float32), (1, 4))
    m_win = np.zeros((NSLOT, 5, 128, 512), np.float32)
    m_cmpT = np.zeros((NSLOT, 2, 128, 512), np.float32)
    m_cmpqn = np.zeros((NSLOT, 128, 1024), np.float32)
    r_tab = np.zeros((NSLOT, 128, 2, 256), np.float32)
    n_all = np.arange(1024)
    j = np.arange(256)
    for i in range(NSLOT):
        qb = 8 * i + c
        s = 128 * qb
        t = s + q
        for w in range(5):
            kpos = s - 512 + 128 * w + key
            ok = (kpos[:, None] <= t[None, :]) & (kpos[:, None] > t[None, :] - 512) & (kpos[:, None] >= 0)
            m_win[i, w] = np.tile(ok.astype(np.float32), (1, 4))
        ntc = i // 2 + 1
        for m in range(2):
            nt_ = ntc - 2 + m
            if nt_ < 0:
                continue
            n = 128 * nt_ + key
            ok = (16 * n[:, None] + 31 <= t[None, :]) & (n[:, None] <= 1022)
            m_cmpT[i, m] = np.tile(ok.astype(np.float32), (1, 4))
        m_cmpqn[i] = ((16 * n_all[None, :] + 31 <= t[:, None]) & (n_all[None, :] <= 1022)).astype(np.float32)
        cur = t // 64
        r_tab[i, :, 0, :] = np.where(j[None, :] <= cur[:, None], 0.0, BIGNEG)
        force = np.full((128, 256), 2 * BIGNEG, np.float32)
        force[:, 0] = 8.0
        force[q, cur] = 16.0
        prev = cur - 1
        okp = prev >= 0
        force[q[okp], prev[okp]] = 32.0
        r_tab[i, :, 1, :] = force
    return {"m_diag": m_diag, "m_win": m_win, "m_cmpT": m_cmpT, "m_cmpqn": m_cmpqn, "r_tab": r_tab}


def nsa_inputs(projT, prep, c, L=SEQ):
    r = lambda a: np.ascontiguousarray(a, dtype=np.float32)
    ones = lambda shp: np.ones(shp, np.float32)
    vs = projT[896:1024, :L].T.reshape(L, 2, 64)
    vs_aug = np.concatenate([vs, ones((L, 2, 1))], axis=2)
    kcT = np.zeros((2, 64, 1024), np.float32); kcT[:, :, :1023] = prep["kc"][:, :, :1023]
    vc_aug = np.zeros((1024, 2, 65), np.float32); vc_aug[:1023, :, :64] = prep["vc"][:1023]; vc_aug[:1023, :, 64] = 1.0
    kw_pad = np.concatenate([np.zeros((2, 64, 512), np.float32), prep["kw"]], axis=2)
    vw = projT[1152:1280, :L].T.reshape(L, 2, 64)
    vw_aug = np.concatenate([vw, ones((L, 2, 1))], axis=2)
    vw_pad = np.concatenate([np.zeros((512, 2, 65), np.float32), vw_aug], axis=0)
    gl = projT[1280:1304, :L].T
    qT = np.zeros((NSLOT, 64, 8, 128), np.float32); kws = np.zeros((NSLOT, 2, 64, 640), np.float32)
    vws = np.zeros((NSLOT, 640, 2, 65), np.float32); gts = np.zeros((NSLOT, 128, 24), np.float32)
    for i in range(NSLOT):
        qb = 8 * i + c
        s = 128 * qb
        qT[i] = prep["q"][:, :, s:s + 128].transpose(1, 0, 2)
        kws[i] = kw_pad[:, :, s:s + 640]
        vws[i] = vw_pad[s:s + 640]
        gts[i] = gl[s:s + 128]
    ident, ex = nsa_consts()
    dd = {"n_ksT": r(prep["ks"]), "n_vs": r(vs_aug), "n_kcT": kcT, "n_vc": vc_aug, "n_qT": qT, "n_kw": kws, "n_vw": vws,
          "n_gates": gts, "c_ident": ident, "c_expand": ex}
    dd.update(nsa_masks(c, L))
    return dd


_PROGS = {}


def _prog(name, builder):
    if name not in _PROGS:
        _PROGS[name] = builder()
    return _PROGS[name]


def _run(nc, maps):
    res = run_bass_kernel_spmd(nc, maps, core_ids=list(range(NCORES)))
    return res.results


def kernel(**inputs):
    inp = {k_: np.asarray(v) for k_, v in inputs.items()}
    L = SEQ
    xT = np.ascontiguousarray(inp["x"][0].T.astype(np.float32))
    for l in range(DEPTH):
        ncA = _prog("A", build_stage_A)
        gainA = lay128(inp["attn_norm"][l])
        wA = np.asarray(inp["w_in"][l], np.float32)
        res = _run(ncA, [{"xT": np.ascontiguousarray(xT[:, c * TOK:(c + 1) * TOK]), "gain": gainA, "w": wA}
                         for c in range(NCORES)])
        projT = np.concatenate([r["projT"] for r in res], axis=1)
        del res
        ncP = _prog("P", build_prep)
        res = _run(ncP, [prep_inputs(projT, inp, l, c) for c in range(NCORES)])
        prep = {
            "q": np.concatenate([r["o_q"] for r in res], axis=2),
            "ks": np.concatenate([r["o_ks"] for r in res], axis=2),
            "kw": np.concatenate([r["o_kw"] for r in res], axis=2),
            "kc": np.concatenate([r["o_kc"] for r in res], axis=2),
            "vc": np.concatenate([r["o_vc"] for r in res], axis=0),
        }
        del res
        ncN = _prog("N", build_nsa)
        res = _run(ncN, [nsa_inputs(projT, prep, c) for c in range(NCORES)])
        ya = np.zeros((L, 512), np.float32)
        for c in range(NCORES):
            o = res[c]["n_out"]
            for i in range(NSLOT):
                qb = 8 * i + c
                ya[qb * 128:(qb + 1) * 128] = o[i]
        del res, prep
        ncGS = _prog("GS", build_gs)
        maps = []
        for c in range(NCORES):
            m = gdn_inputs(projT, inp, l, c)
            m.update(s5_inputs(projT, inp, l, c))
            maps.append(m)
        res = _run(ncGS, maps)
        ycT = np.concatenate([np.ascontiguousarray(r["g_out"].T) for r in res], axis=0)
        ybT = np.concatenate([r["s_out"] for r in res], axis=0)
        del res, maps, projT
        ncC = _prog("C", build_stage_C)

        def halo(a, c):
            out = np.zeros((a.shape[0], TOK + 2), np.float32)
            lo = c * TOK - 2
            if lo < 0:
                out[:, 2:] = a[:, 0:TOK]
            else:
                out[:] = a[:, lo:lo + TOK + 2]
            return out

        yaT = np.ascontiguousarray(ya.T)
        maps = [stage_C_inputs(inp, l, halo(xT, c), halo(yaT, c), halo(ybT, c), halo(ycT, c)) for c in range(NCORES)]
        res = _run(ncC, maps)
        xT = np.concatenate([r["xoT"] for r in res], axis=1)
        del res, maps
    return np.ascontiguousarray(xT.T)[None].astype(np.float32)


def build_gs(L=SEQ):
    nc = bass.Bass("TRN2", target_bir_lowering=False)
    din = {}

    def inp(name, shape):
        din[name] = nc.dram_tensor(name, list(shape), F32, kind="ExternalInput").ap()
        return din[name]

    qT_d = inp("g_qT", [128, L]); kT_d = inp("g_kT", [128, L]); vT_d = inp("g_vT", [128, L])
    z_d = inp("g_z", [L, 128]); a_d = inp("g_a33", [33, L]); b_d = inp("g_b33", [33, L])
    cw_d = inp("g_cw", [128, 3, 4]); prm_d = inp("g_prm33", [33, 2]); gain_d = inp("g_gain", [64, 128])
    cd = {n: inp("gc_" + n, v.shape) for n, v in gdn_consts().items()}
    W = S5_W
    inp("s_uT", [64, L]); inp("s_lamre", [128, 2]); inp("s_lamim", [128, 2]); inp("s_logdt", [128, 2])
    inp("s_bre", [2, 64, 128]); inp("s_bim", [2, 64, 128]); inp("s_cre", [2, 128, 64]); inp("s_cim", [2, 128, 64])
    inp("s_d", [64, 1]); inp("s_trow", [128, W + 1])
    gout = nc.dram_tensor("g_out", [L, 128], F32, kind="ExternalOutput").ap()
    sout = nc.dram_tensor("s_out", [64, L], F32, kind="ExternalOutput").ap()
    k = KB(nc)
    s5_step = emit_s5(k, L, din, sout, stepper=True, npp=1, npy=1)
    emit_gdn(k, L, qT_d, kT_d, vT_d, z_d, a_d, b_d, cw_d, prm_d, gain_d, cd, gout, merged=True, extra=s5_step)
    k.finish()
    return nc
```

```python
from contextlib import ExitStack
import numpy as np
import concourse.bass as bass
import concourse.mybir as mybir
from concourse.bass_utils import run_bass_kernel_spmd

F32 = mybir.dt.float32
BF16 = mybir.dt.bfloat16
AF = mybir.ActivationFunctionType
ALU = mybir.AluOpType
AX = mybir.AxisListType

NCORES = 8
D_MODEL = 2048
SEQ = 16384
DEPTH = 2
IN_COLS = 5928
D_FF = 5504
EPS = 1e-6
TOK = SEQ // NCORES


class T:
    __slots__ = ("h", "name", "w", "r", "din", "dout", "sin", "sout", "psum")

    def __init__(self, h, name, psum=False):
        self.psum = psum
        self.h = h
        self.name = name
        self.w = None
        self.r = {}
        self.din = 0
        self.dout = 0
        self.sin = None
        self.sout = None

    def __getitem__(self, idx):
        return self.h[idx]


class KB:
    ENGS = ("pe", "act", "dve", "pool", "sp")

    def __init__(self, nc):
        self.nc = nc
        self.es = ExitStack()
        self.eng = {"pe": nc.tensor, "act": nc.scalar, "dve": nc.vector,
                    "pool": nc.gpsimd, "sp": nc.sync}
        self.sem = {k: self.es.enter_context(nc.semaphore("s_" + k)) for k in self.ENGS}
        self.cnt = {k: 0 for k in self.ENGS}
        self.known = {k: {} for k in self.ENGS}
        self.nsem = len(self.ENGS)
        self.ntile = 0
        self.out_waits = []
        self.rr = 0
        self.rec = None

    def sb(self, shape, dtype=F32, name=None):
        self.ntile += 1
        name = name or "t"
        h = self.es.enter_context(self.nc.sbuf_tensor(f"{name}_{self.ntile}", list(shape), dtype))
        return T(h, name)

    def ps(self, shape, dtype=F32, name=None):
        self.ntile += 1
        name = name or "p"
        h = self.es.enter_context(self.nc.psum_tensor(f"{name}_{self.ntile}", list(shape), dtype))
        return T(h, name, psum=True)

    def newsem(self, name):
        self.nsem += 1
        assert self.nsem < 145, "too many semaphores"
        return self.es.enter_context(self.nc.semaphore(f"{name}_{self.nsem}"))

    def _collect(self, e, reads, writes):
        needs = {}

        def need(sem, val):
            key = id(sem)
            if needs.get(key, (None, 0))[1] < val:
                needs[key] = (sem, val)

        for t in reads:
            if t.w is not None:
                need(self.sem[t.w[0]], t.w[1])
            if t.din:
                need(t.sin, t.din)
            if t.psum:
                for kk, c in t.r.items():
                    if kk != e:
                        need(self.sem[kk], c)
        for t in writes:
            if t.w is not None and not (t.w[0] == e and e == "pe"):
                need(self.sem[t.w[0]], t.w[1])
            for kk, c in t.r.items():
                need(self.sem[kk], c)
            if t.din:
                need(t.sin, t.din)
            if t.dout:
                need(t.sout, t.dout)
        kn = self.known[e]
        for key, (sem, val) in needs.items():
            if kn.get(key, 0) < val:
                self.eng[e].wait_ge(sem, val)
                kn[key] = val

    def op(self, e, fn, reads=(), writes=()):
        if self.rec is not None:
            self.rec.append((e, fn, list(reads), list(writes)))
            return None
        reads = [getattr(t, "base", t) for t in reads]
        writes = [getattr(t, "base", t) for t in writes]
        self._collect(e, reads, writes)
        ins = fn(self.eng[e])
        ins.then_inc(self.sem[e], 1)
        self.cnt[e] += 1
        c = self.cnt[e]
        self.known[e][id(self.sem[e])] = max(self.known[e].get(id(self.sem[e]), 0), 0)
        for t in writes:
            t.w = (e, c)
            t.r = {}
        for t in reads:
            if t not in writes:
                t.r[e] = c
        return ins

    def dma(self, q, out, in_, t_out=None, t_in=None, final=False, **kw):
        if self.rec is not None:
            self.rec.append(("dma", q, out, in_, dict(t_out=t_out, t_in=t_in, final=final, **kw)))
            return None
        reads = [t_in] if t_in is not None else []
        writes = [t_out] if t_out is not None else []
        self._collect(q, reads, writes)
        ins = self.eng[q].dma_start(out=out, in_=in_, **kw)
        if t_out is not None:
            if t_out.sin is None:
                t_out.sin = self.newsem("di")
            ins.then_inc(t_out.sin, 16)
            t_out.din += 16
            t_out.w = None
            t_out.r = {}
        if t_in is not None and t_out is None:
            if t_in.sout is None:
                t_in.sout = self.newsem("do")
            ins.then_inc(t_in.sout, 16)
            t_in.dout += 16
            if final and t_in not in self.out_waits:
                self.out_waits.append(t_in)
        return ins

    def finish(self):
        sp = self.eng["sp"]
        for t in self.out_waits:
            sp.wait_ge(t.sout, t.dout)
        for kk in self.ENGS:
            if kk != "sp" and self.cnt[kk]:
                sp.wait_ge(self.sem[kk], self.cnt[kk])
        self.es.close()

    def record(self, fn, *a, **kw):
        assert self.rec is None
        self.rec = []
        try:
            r = fn(*a, **kw)
        finally:
            lst, self.rec = self.rec, None
        return r, lst

    def emit_interleaved(self, lists):
        n = max(len(l) for l in lists)
        for j in range(n):
            for l in lists:
                if j < len(l):
                    it = l[j]
                    if it[0] == "dma":
                        self.dma(it[1], it[2], it[3], **it[4])
                    else:
                        self.op(*it)

    def evac_eng(self):
        self.rr += 1
        return ("act", "dve")[self.rr % 2]

    def copy(self, e, out_t, out_ap, in_t, in_ap):
        if e == "act":
            return self.op("act", lambda g: g.copy(out=out_ap, in_=in_ap), [in_t], [out_t])
        return self.op(e, lambda g: g.tensor_copy(out=out_ap, in_=in_ap), [in_t], [out_t])


def col_chunks(n0, n1):
    out = []
    c = n0
    while c < n1:
        w = min(128, n1 - c)
        out.append((c, w))
        c += w
    return out


def rms_stats(k, ones, src, nchunk, W, ps, sq_tiles, rstd, eps=EPS):
    for c in range(nchunk):
        sq = sq_tiles[c % len(sq_tiles)]
        k.op("act", lambda g, sq=sq, c=c: g.activation(out=sq[:, :W], in_=src[:, c, :W], func=AF.Square),
             [src], [sq])
        k.op("pe", lambda g, sq=sq, c=c: g.matmul(ps[:, :W], lhsT=ones[:], rhs=sq[:, :W],
                                               start=(c == 0), stop=(c == nchunk - 1)), [ones, sq], [ps])
    k.op("dve", lambda g: g.tensor_scalar(out=rstd[:, :W], in0=ps[:, :W], scalar1=eps, scalar2=None,
                                          op0=ALU.add), [ps], [rstd])
    k.op("act", lambda g: g.activation(out=rstd[:, :W], in_=rstd[:, :W], func=AF.Sqrt), [rstd], [rstd])
    k.op("dve", lambda g: g.reciprocal(out=rstd[:, :W], in_=rstd[:, :W]), [rstd], [rstd])


def build_stage_A(ntok=TOK):
    nc = bass.Bass("TRN2", target_bir_lowering=False)
    C = D_MODEL // 128
    TW = 512
    ntile = ntok // TW
    xT = nc.dram_tensor("xT", [D_MODEL, ntok], F32, kind="ExternalInput").ap()
    gain_d = nc.dram_tensor("gain", [128, C], F32, kind="ExternalInput").ap()
    w_d = nc.dram_tensor("w", [D_MODEL, IN_COLS], F32, kind="ExternalInput").ap()
    out_d = nc.dram_tensor("projT", [IN_COLS, ntok], F32, kind="ExternalOutput").ap()
    k = KB(nc)
    ones = k.sb([128, 128], F32, "ones")
    k.op("pool", lambda g: g.memset(ones[:], 1.0 / D_MODEL), [], [ones])
    gain = k.sb([128, C], F32, "gain")
    k.dma("sp", gain[:], gain_d, t_out=gain)
    hT = [k.sb([128, C, TW], BF16, f"hT{i}") for i in range(ntile)]
    xs = [k.sb([128, C, TW], F32, f"xs{i}") for i in range(2)]
    sqs = [k.sb([128, TW], F32, f"sq{i}") for i in range(2)]
    rstd = k.sb([128, TW], F32, "rstd")
    pstat = k.ps([128, TW], F32, "pstat")
    pacc = [k.ps([128, TW], F32, f"pacc{i}") for i in range(6)]
    xv = xT.rearrange("(c p) t -> p c t", p=128)
    wv = w_d.rearrange("(c p) n -> p c n", p=128)
    wb = [k.sb([128, C, 512], BF16, f"wb{i}") for i in range(2)]
    ost = [k.sb([128, TW], F32, f"ost{i}") for i in range(4)]
    groups = [(g0, min(512, IN_COLS - g0)) for g0 in range(0, IN_COLS, 512)]
    k.dma("pool", wb[0][:, :, :groups[0][1]], wv[:, :, 0:groups[0][1]], t_out=wb[0])
    for i in range(ntile):
        xt = xs[i % 2]
        k.dma("sp", xt[:], xv[:, :, i * TW:(i + 1) * TW], t_out=xt)
        rms_stats(k, ones, xt, C, TW, pstat, sqs, rstd)
        for c in range(C):
            k.op("dve", lambda g, c=c: g.scalar_tensor_tensor(
                out=hT[i][:, c, :], in0=xt[:, c, :], scalar=gain[:, c:c + 1], in1=rstd[:],
                op0=ALU.mult, op1=ALU.mult), [xt, gain, rstd], [hT[i]])
    n = 0
    for gi, (g0, gw) in enumerate(groups):
        w = wb[gi % 2]
        if gi + 1 < len(groups):
            g1, gw1 = groups[gi + 1]
            k.dma("pool", wb[(gi + 1) % 2][:, :, :gw1], wv[:, :, g1:g1 + gw1], t_out=wb[(gi + 1) % 2])
        for (c0, cw) in col_chunks(g0, g0 + gw):
            off = c0 - g0
            for i in range(ntile):
                p = pacc[n % len(pacc)]
                o = ost[n % len(ost)]
                n += 1
                for c in range(C):
                    k.op("pe", lambda g, c=c, p=p: g.matmul(p[:cw, :], lhsT=w[:, c, off:off + cw], rhs=hT[i][:, c, :],
                                                        start=(c == 0), stop=(c == C - 1)), [w, hT[i]], [p])
                k.copy(k.evac_eng(), o, o[:cw, :], p, p[:cw, :])
                k.dma("sp", out_d[c0:c0 + cw, i * TW:(i + 1) * TW], o[:cw, :], t_in=o, final=True)
    k.finish()
    return nc


def build_stage_C(ntok=TOK):
    nc = bass.Bass("TRN2", target_bir_lowering=False)
    C = D_MODEL // 128
    HALO = 2
    NT = ntok + HALO
    TW = 512
    NJ = D_FF // 128
    xT = nc.dram_tensor("xT", [D_MODEL, NT], F32, kind="ExternalInput").ap()
    yaT = nc.dram_tensor("yaT", [512, NT], F32, kind="ExternalInput").ap()
    ybT = nc.dram_tensor("ybT", [512, NT], F32, kind="ExternalInput").ap()
    ycT = nc.dram_tensor("ycT", [1024, NT], F32, kind="ExternalInput").ap()
    g_nsa = nc.dram_tensor("g_nsa", [128, 4], F32, kind="ExternalInput").ap()
    g_s5 = nc.dram_tensor("g_s5", [128, 4], F32, kind="ExternalInput").ap()
    g_ffn = nc.dram_tensor("g_ffn", [128, C], F32, kind="ExternalInput").ap()
    cw_d = nc.dram_tensor("ffn_conv", [128, 3, 2 * NJ], F32, kind="ExternalInput").ap()
    cb_d = nc.dram_tensor("ffn_conv_b", [128, 2 * NJ], F32, kind="ExternalInput").ap()
    wglu_d = nc.dram_tensor("w_glu", [512, 512], F32, kind="ExternalInput").ap()
    wout_d = nc.dram_tensor("w_out", [D_MODEL, D_MODEL], F32, kind="ExternalInput").ap()
    wfi_d = nc.dram_tensor("ffn_w_in", [D_MODEL, 2 * D_FF], F32, kind="ExternalInput").ap()
    wfo_d = nc.dram_tensor("ffn_w_out", [D_FF, D_MODEL], F32, kind="ExternalInput").ap()
    out_d = nc.dram_tensor("xoT", [D_MODEL, ntok], F32, kind="ExternalOutput").ap()

    k = KB(nc)
    ones_d = k.sb([128, 128], F32, "ones_d")
    k.op("pool", lambda g: g.memset(ones_d[:], 1.0 / D_MODEL), [], [ones_d])
    ones_4 = k.sb([128, 128], F32, "ones_4")
    k.op("pool", lambda g: g.memset(ones_4[:], 1.0 / 512.0), [], [ones_4])
    gn = k.sb([128, 4], F32, "gn"); k.dma("sp", gn[:], g_nsa, t_out=gn)
    gs = k.sb([128, 4], F32, "gs"); k.dma("sp", gs[:], g_s5, t_out=gs)
    gf = k.sb([128, C], F32, "gf"); k.dma("sp", gf[:], g_ffn, t_out=gf)
    cw = k.sb([128, 3, 2 * NJ], F32, "cw"); k.dma("sp", cw[:], cw_d, t_out=cw)
    cb = k.sb([128, 2 * NJ], F32, "cb"); k.dma("sp", cb[:], cb_d, t_out=cb)
    wglu = k.sb([128, 4, 512], BF16, "wglu")
    k.dma("pool", wglu[:], wglu_d.rearrange("(c p) n -> p c n", p=128), t_out=wglu)

    xt = k.sb([128, C, TW], F32, "xt")
    ain = k.sb([128, C, TW], BF16, "ain")
    act = k.sb([128, NJ, TW], BF16, "act")
    ya = k.sb([128, 4, TW], F32, "ya")
    yb = k.sb([128, 4, TW], F32, "yb")
    ybb = k.sb([128, 4, TW], BF16, "ybb")
    y2 = ya
    sqs = [k.sb([128, TW], F32, f"sq{i}") for i in range(2)]
    rstd = k.sb([128, TW], F32, "rstd")
    sig = k.sb([128, TW], F32, "sig")
    tails = k.sb([128, 2 * NJ, 2], F32, "tails")
    k.op("pool", lambda g: g.memset(tails[:], 0.0), [], [tails])
    ext = [k.sb([128, TW + 2], F32, f"ext{i}") for i in range(4)]
    gt = [k.sb([128, TW], F32, f"gt{i}") for i in range(2)]
    ut = [k.sb([128, TW], F32, f"ut{i}") for i in range(2)]
    ost = [k.sb([128, TW], F32, f"ost{i}") for i in range(2)]
    wb = [k.sb([128, C, 512], BF16, f"wb{i}") for i in range(3)]
    pstat = k.ps([128, TW], F32, "pstat")
    pacc = [k.ps([128, TW], F32, f"pacc{i}") for i in range(7)]
    state = {"wn": 0, "pn": 0}

    def next_w():
        w = wb[state["wn"] % len(wb)]
        state["wn"] += 1
        return w

    def next_p():
        p = pacc[state["pn"] % len(pacc)]
        state["pn"] += 1
        return p

    woutv = wout_d.rearrange("(c p) n -> p c n", p=128)
    wfiv = wfi_d.rearrange("(c p) n -> p c n", p=128)
    wfov = wfo_d.rearrange("(c p) n -> p c n", p=128)

    tiles = [(0, HALO)] + [(HALO + i * TW, TW) for i in range(ntok // TW)]
    for ti, (t0, W) in enumerate(tiles):
        halo = (ti == 0)
        k.dma("sp", xt[:, :, :W], xT.rearrange("(c p) t -> p c t", p=128)[:, :, t0:t0 + W], t_out=xt)
        k.dma("sp", ya[:, :, :W], yaT.rearrange("(c p) t -> p c t", p=128)[:, :, t0:t0 + W], t_out=ya)
        k.dma("sp", yb[:, :, :W], ybT.rearrange("(c p) t -> p c t", p=128)[:, :, t0:t0 + W], t_out=yb)
        rms_stats(k, ones_4, ya, 4, W, pstat, sqs, rstd)
        for c in range(4):
            k.op("dve", lambda g, c=c: g.scalar_tensor_tensor(
                out=ain[:, c, :W], in0=ya[:, c, :W], scalar=gn[:, c:c + 1], in1=rstd[:, :W],
                op0=ALU.mult, op1=ALU.mult), [ya, gn, rstd], [ain])
        k.op("pool", lambda g: g.tensor_copy(out=ybb[:, :, :W], in_=yb[:, :, :W]), [yb], [ybb])
        for m in range(4):
            p = next_p()
            for c in range(4):
                k.op("pe", lambda g, c=c, p=p, m=m: g.matmul(p[:, :W], lhsT=wglu[:, c, m * 128:(m + 1) * 128],
                                                         rhs=ybb[:, c, :W], start=(c == 0), stop=(c == 3)),
                     [wglu, ybb], [p])
            k.op("act", lambda g, p=p: g.activation(out=sig[:, :W], in_=p[:, :W], func=AF.Sigmoid), [p], [sig])
            k.op("dve", lambda g, m=m: g.tensor_tensor(out=y2[:, m, :W], in0=yb[:, m, :W], in1=sig[:, :W],
                                                   op=ALU.mult), [yb, sig], [y2])
        rms_stats(k, ones_4, y2, 4, W, pstat, sqs, rstd)
        for c in range(4):
            k.op("dve", lambda g, c=c: g.scalar_tensor_tensor(
                out=ain[:, 4 + c, :W], in0=y2[:, c, :W], scalar=gs[:, c:c + 1], in1=rstd[:, :W],
                op0=ALU.mult, op1=ALU.mult), [y2, gs, rstd], [ain])
        k.dma("pool", ain[:, 8:16, :W], ycT.rearrange("(c p) t -> p c t", p=128)[:, :, t0:t0 + W], t_out=ain)
        for mg in range(4):
            w = next_w()
            k.dma("pool", w[:], woutv[:, :, mg * 512:(mg + 1) * 512], t_out=w)
            for mm in range(4):
                m = mg * 4 + mm
                p = next_p()
                for c in range(C):
                    k.op("pe", lambda g, c=c, p=p, mm=mm, w=w: g.matmul(
                        p[:, :W], lhsT=w[:, c, mm * 128:(mm + 1) * 128], rhs=ain[:, c, :W],
                        start=(c == 0), stop=(c == C - 1)), [w, ain], [p])
                k.op("dve", lambda g, m=m, p=p: g.tensor_tensor(out=xt[:, m, :W], in0=xt[:, m, :W], in1=p[:, :W],
                                                            op=ALU.add), [xt, p], [xt])
        rms_stats(k, ones_d, xt, C, W, pstat, sqs, rstd)
        for c in range(C):
            k.op("dve", lambda g, c=c: g.scalar_tensor_tensor(
                out=ain[:, c, :W], in0=xt[:, c, :W], scalar=gf[:, c:c + 1], in1=rstd[:, :W],
                op0=ALU.mult, op1=ALU.mult), [xt, gf, rstd], [ain])
        ngrp = (NJ + 3) // 4
        en = 0
        for jg in range(ngrp):
            j0 = jg * 4
            nj = min(4, NJ - j0)
            wg = next_w()
            k.dma("pool", wg[:, :, :nj * 128], wfiv[:, :, j0 * 128:(j0 + nj) * 128], t_out=wg)
            wu = next_w()
            k.dma("pool", wu[:, :, :nj * 128], wfiv[:, :, D_FF + j0 * 128:D_FF + (j0 + nj) * 128], t_out=wu)
            for jj in range(nj):
                j = j0 + jj
                res = []
                for which, w in ((0, wg), (1, wu)):
                    idx = which * NJ + j
                    p = next_p()
                    for c in range(C):
                        k.op("pe", lambda g, c=c, p=p, jj=jj, w=w: g.matmul(
                            p[:, :W], lhsT=w[:, c, jj * 128:(jj + 1) * 128], rhs=ain[:, c, :W],
                            start=(c == 0), stop=(c == C - 1)), [w, ain], [p])
                    e = ext[en % len(ext)]
                    en += 1
                    k.op("pool", lambda g, e=e, idx=idx: g.tensor_copy(out=e[:, 0:2], in_=tails[:, idx, :]),
                         [tails], [e])
                    k.op("act", lambda g, e=e, p=p: g.copy(out=e[:, 2:2 + W], in_=p[:, :W]), [p], [e])
                    k.op("pool", lambda g, e=e, idx=idx: g.tensor_copy(out=tails[:, idx, :], in_=e[:, W:W + 2]),
                         [e], [tails])
                    if halo:
                        continue
                    dst = (gt if which == 0 else ut)[j % 2]
                    k.op("dve", lambda g, e=e, idx=idx, dst=dst: g.tensor_scalar(
                        out=dst[:, :W], in0=e[:, 2:2 + W], scalar1=cw[:, 2, idx:idx + 1], scalar2=cb[:, idx:idx + 1],
                        op0=ALU.mult, op1=ALU.add), [e, cw, cb], [dst])
                    k.op("dve", lambda g, e=e, idx=idx, dst=dst: g.scalar_tensor_tensor(
                        out=dst[:, :W], in0=e[:, 1:1 + W], scalar=cw[:, 1, idx:idx + 1], in1=dst[:, :W],
                        op0=ALU.mult, op1=ALU.add), [e, cw, dst], [dst])
                    k.op("dve", lambda g, e=e, idx=idx, dst=dst: g.scalar_tensor_tensor(
                        out=dst[:, :W], in0=e[:, 0:W], scalar=cw[:, 0, idx:idx + 1], in1=dst[:, :W],
                        op0=ALU.mult, op1=ALU.add), [e, cw, dst], [dst])
                    res.append(dst)
                if halo:
                    continue
                gtile, utile = res
                k.op("act", lambda g, gtile=gtile: g.activation(out=gtile[:, :W], in_=gtile[:, :W], func=AF.Silu),
                     [gtile], [gtile])
                k.op("pool", lambda g, gtile=gtile, utile=utile, j=j: g.tensor_tensor(
                    out=act[:, j, :W], in0=gtile[:, :W], in1=utile[:, :W], op=ALU.mult), [gtile, utile], [act])
        if halo:
            continue
        jgroups = [(0, 16), (16, 16), (32, NJ - 32)]
        for mg in range(4):
            ps4 = [next_p() for _ in range(4)]
            for gi, (ja, jn) in enumerate(jgroups):
                w = next_w()
                k.dma("pool", w[:, :jn, :], wfov[:, ja:ja + jn, mg * 512:(mg + 1) * 512], t_out=w)
                for mm in range(4):
                    p = ps4[mm]
                    for jj in range(jn):
                        j = ja + jj
                        k.op("pe", lambda g, p=p, jj=jj, j=j, mm=mm, w=w: g.matmul(
                            p[:, :W], lhsT=w[:, jj, mm * 128:(mm + 1) * 128], rhs=act[:, j, :W],
                            start=(j == 0), stop=(j == NJ - 1)), [w, act], [p])
            for mm in range(4):
                m = mg * 4 + mm
                o = ost[m % 2]
                k.op("dve", lambda g, m=m, o=o, p=ps4[mm]: g.tensor_tensor(
                    out=o[:, :W], in0=xt[:, m, :W], in1=p[:, :W], op=ALU.add), [xt, ps4[mm]], [o])
                k.dma("sp", out_d[m * 128:(m + 1) * 128, t0 - HALO:t0 - HALO + W], o[:, :W], t_in=o, final=True)
    k.finish()
    return nc


def lay128(v):
    v = np.asarray(v, np.float32)
    return np.ascontiguousarray(v.reshape(-1, 128).T)


def stage_C_inputs(inp, l, xT, yaT, ybT, ycT):
    NJ = D_FF // 128
    return {
        "xT": xT, "yaT": yaT, "ybT": ybT, "ycT": ycT,
        "g_nsa": lay128(inp["nsa_out_norm"][l]), "g_s5": lay128(inp["s5_out_norm"][l]),
        "g_ffn": lay128(inp["ffn_norm"][l]),
        "ffn_conv": np.ascontiguousarray(np.asarray(inp["ffn_conv"][l], np.float32).reshape(3, 2 * NJ, 128).transpose(2, 0, 1)),
        "ffn_conv_b": lay128(inp["ffn_conv_b"][l]),
        "w_glu": np.asarray(inp["s5_w_glu"][l], np.float32), "w_out": np.asarray(inp["w_out"][l], np.float32),
        "ffn_w_in": np.asarray(inp["ffn_w_in"][l], np.float32), "ffn_w_out": np.asarray(inp["ffn_w_out"][l], np.float32),
    }


GDN_C = 64
GDN_DEBUG = 3
GDN_IL = 4
GDN_PAR_STOP = 0
NEG = -1.0e6


class View:
    __slots__ = ("base", "ap")

    def __init__(self, base, ap):
        self.base = base
        self.ap = ap

    def __getitem__(self, idx):
        return self.ap[idx]


class Slots:
    def __init__(self, k, nbanks, width, name):
        banks = [k.ps([128, 512], F32, f"{name}{b}") for b in range(nbanks)]
        self.slots = []
        for j in range(512 // width):
            for bank in banks:
                self.slots.append(View(bank, bank.h[:, j * width:(j + 1) * width]))
        self.n = {}

    def get(self, part=0, nparts=1):
        sub = self.slots[part::nparts]
        c = self.n.get((part, nparts), 0)
        self.n[(part, nparts)] = c + 1
        return sub[c % len(sub)]


def gdn_consts():
    c = {}
    c["ident"] = np.eye(128, dtype=np.float32)
    ii = np.arange(64)
    c["maskc"] = np.where(ii[:, None] >= ii[None, :], 0.0, NEG).astype(np.float32)
    c["maskcT"] = np.ascontiguousarray(c["maskc"].T)
    c["strict01"] = (ii[:, None] > ii[None, :]).astype(np.float32)
    cm = np.zeros((33, 4), np.float32)
    cm[0, 0] = 1.0; cm[32, 1] = 1.0; cm[32, 2] = -1.0; cm[0, 3] = 1.0
    c["cm"] = cm
    sel = np.zeros((33, 2), np.float32); sel[0, 0] = 1.0; sel[32, 1] = 1.0
    c["sel"] = sel
    bsel = np.zeros((33, 128), np.float32); bsel[0, :] = 1.0
    c["bsel"] = bsel
    rm = np.ones((33, 512), np.float32); rm[:, ::64] = 0.0
    c["resetmask"] = rm
    return c


def build_gdn(L=SEQ):
    nc = bass.Bass("TRN2", target_bir_lowering=False)
    ST = 512
    NST = L // ST
    CH = GDN_C
    NCH = ST // CH
    din = {}

    def inp(name, shape):
        din[name] = nc.dram_tensor(name, list(shape), F32, kind="ExternalInput").ap()
        return din[name]

    qT_d = inp("g_qT", [128, L]); kT_d = inp("g_kT", [128, L]); vT_d = inp("g_vT", [128, L])
    z_d = inp("g_z", [L, 128]); a_d = inp("g_a33", [33, L]); b_d = inp("g_b33", [33, L])
    cw_d = inp("g_cw", [128, 3, 4]); prm_d = inp("g_prm33", [33, 2]); gain_d = inp("g_gain", [64, 128])
    cd = {n: inp("gc_" + n, v.shape) for n, v in gdn_consts().items()}
    out_d = nc.dram_tensor("g_out", [L, 128], F32, kind="ExternalOutput").ap()
    k = KB(nc)
    emit_gdn(k, L, qT_d, kT_d, vT_d, z_d, a_d, b_d, cw_d, prm_d, gain_d, cd, out_d)
    k.finish()
    return nc


def emit_gdn(k, L, qT_d, kT_d, vT_d, z_d, a_d, b_d, cw_d, prm_d, gain_d, cd, out_d, merged=False, extra=None):
    ST = 512
    NST = L // ST
    CH = GDN_C
    NCH = ST // CH

    def const(name, shape):
        t = k.sb(shape, F32, "c_" + name)
        k.dma("sp", t[:], cd[name], t_out=t)
        return t

    ident = const("ident", [128, 128]); maskc = const("maskc", [64, 64]); maskcT = const("maskcT", [64, 64])
    strict01 = const("strict01", [64, 64]); cm = const("cm", [33, 4]); sel = const("sel", [33, 2])
    bsel = const("bsel", [33, 128]); rmask = const("resetmask", [33, 512])
    cw = k.sb([128, 3, 4], F32, "cw"); k.dma("sp", cw[:], cw_d, t_out=cw)
    prm = k.sb([33, 2], F32, "prm"); k.dma("sp", prm[:], prm_d, t_out=prm)
    gain = k.sb([64, 128], F32, "gain"); k.dma("sp", gain[:], gain_d, t_out=gain)
    ones = k.sb([128, 128], F32, "ones")
    k.op("pool", lambda g: g.memset(ones[:], 1.0), [], [ones])
    nA = k.sb([33, 1], F32, "nA")
    k.op("act", lambda g: g.activation(out=nA[:], in_=prm[:, 0:1], func=AF.Exp), [prm], [nA])
    k.op("dve", lambda g: g.tensor_scalar(out=nA[:], in0=nA[:], scalar1=-1.0, scalar2=None, op0=ALU.mult), [nA], [nA])
    S = k.sb([128, 128], F32, "S")
    k.op("pool", lambda g: g.memset(S[:], 0.0), [], [S])

    ext = [[k.sb([128, ST + 3], F32, f"ext{w}{i}") for i in range(2)] for w in range(3)]
    for w in range(3):
        k.op("pool", lambda g, w=w: g.memset(ext[w][1][:, ST:ST + 3], 0.0), [], [ext[w][1]])
    cv = [[k.sb([128, ST], F32, f"cv{w}{i}") for i in range(2)] for w in range(3)]
    qn = [k.sb([128, ST], F32, f"qn{i}") for i in range(2)]
    kn = [k.sb([128, ST], F32, f"kn{i}") for i in range(2)]
    sq = k.sb([128, ST], F32, "sq")
    rs = k.sb([128, ST], F32, "rs")
    a33 = [k.sb([33, ST], F32, f"a33{i}") for i in range(2)]
    b33 = [k.sb([33, ST], F32, f"b33{i}") for i in range(2)]
    gc33 = k.sb([33, ST], F32, "gc33")
    U33 = [k.sb([33, ST], F32, f"U33{i}") for i in range(2)]
    V33 = [k.sb([33, ST], F32, f"V33{i}") for i in range(2)]
    RC = [k.sb([33, ST], F32, f"RC{i}") for i in range(2)]
    GB = [k.sb([128, ST], F32, f"GB{i}") for i in range(2)]
    EG = [k.sb([128, ST], F32, f"EG{i}") for i in range(2)]
    zt = [k.sb([64, NCH, 128], F32, f"zt{i}") for i in range(2)]
    yo = [k.sb([64, NCH, 128], F32, f"yo{i}") for i in range(2)]
    pbig = k.ps([128, 512], F32, "pbig")
    pstat = pbig if merged else k.ps([128, 512], F32, "pstat")
    small = Slots(k, 2, 64, "ps")
    wide = Slots(k, 3 if merged else 4, 128, "pw")
    NB = 2 * GDN_IL
    mk = lambda shape, nm: [k.sb(shape, F32, f"{nm}{i}") for i in range(NB)]
    Dm = mk([64, 64], "Dm"); DTm = mk([64, 64], "DTm"); Ds = mk([64, 64], "Ds")
    Xa = mk([64, 64], "Xa"); Xb = mk([64, 64], "Xb"); Ya = mk([64, 64], "Ya"); Yb = mk([64, 64], "Yb")
    Pm = mk([64, 64], "Pm")
    cols = mk([64, 8], "cols")
    kb = mk([64, 128], "kb"); kd = mk([64, 128], "kd"); vb = mk([64, 128], "vb")
    val = mk([64, 128], "val"); kcumT = mk([128, 64], "kcumT"); qkT = mk([64, 64], "qkT")
    qgT = mk([128, 64], "qgT"); vnew = mk([64, 128], "vnew"); junk = [k.sb([64, 128], F32, "junk")] * NB
    ycur = mk([64, 128], "ycur")

    qv = [qT_d, kT_d, vT_d]
    zv = z_d.rearrange("(n c) d -> c n d", c=CH)
    ov = out_d.rearrange("(n c) d -> c n d", c=CH)
    pending = []
    gi = [0]

    def supertile(s):
        b = s % 2
        t0 = s * ST
        for w in range(3):
            e = ext[w][b]
            k.dma("sp", e[:, 3:3 + ST], qv[w][:, t0:t0 + ST], t_out=e)
        k.dma("sp", a33[b][:], a_d[:, t0:t0 + ST], t_out=a33[b])
        k.dma("sp", b33[b][:], b_d[:, t0:t0 + ST], t_out=b33[b])
        k.dma("sp", zt[b][:], zv[:, s * NCH:(s + 1) * NCH, :], t_out=zt[b])
        for w in range(3):
            e = ext[w][b]; eo = ext[w][1 - b]; o = cv[w][b]
            k.op("pool", lambda g, e=e, eo=eo: g.tensor_copy(out=e[:, 0:3], in_=eo[:, ST:ST + 3]), [eo], [e])
            k.op("dve", lambda g, e=e, o=o, w=w: g.tensor_scalar(out=o[:], in0=e[:, 3:3 + ST], scalar1=cw[:, w, 3:4],
                                                              scalar2=None, op0=ALU.mult), [e, cw], [o])
            for j in range(3):
                k.op("dve", lambda g, e=e, o=o, w=w, j=j: g.scalar_tensor_tensor(
                    out=o[:], in0=e[:, j:j + ST], scalar=cw[:, w, j:j + 1], in1=o[:], op0=ALU.mult, op1=ALU.add),
                    [e, cw, o], [o])
            k.op("act", lambda g, o=o: g.activation(out=o[:], in_=o[:], func=AF.Silu), [o], [o])
        for w, dst, scl in ((0, qn[b], 128.0 ** -0.5), (1, kn[b], 1.0)):
            src = cv[w][b]
            k.op("act", lambda g, src=src: g.activation(out=sq[:], in_=src[:], func=AF.Square), [src], [sq])
            k.op("pe", lambda g: g.matmul(pstat[:], lhsT=ones[:], rhs=sq[:], start=True, stop=True), [ones, sq], [pstat])
            k.op("dve", lambda g: g.tensor_scalar(out=rs[:], in0=pstat[:], scalar1=EPS, scalar2=None, op0=ALU.add),
                 [pstat], [rs])
            k.op("act", lambda g: g.activation(out=rs[:], in_=rs[:], func=AF.Sqrt), [rs], [rs])
            k.op("dve", lambda g: g.reciprocal(out=rs[:], in_=rs[:]), [rs], [rs])
            k.op("dve", lambda g, src=src, dst=dst, scl=scl: g.scalar_tensor_tensor(
                out=dst[:], in0=src[:], scalar=scl, in1=rs[:], op0=ALU.mult, op1=ALU.mult), [src, rs], [dst])
        A = a33[b]; B = b33[b]
        k.op("act", lambda g: g.activation(out=A[:], in_=A[:], func=AF.Exp, bias=prm[:, 1:2], scale=1.0), [A, prm], [A])
        k.op("dve", lambda g: g.tensor_scalar(out=A[:], in0=A[:], scalar1=1.0, scalar2=None, op0=ALU.add), [A], [A])
        k.op("act", lambda g: g.activation(out=A[:], in_=A[:], func=AF.Ln), [A], [A])
        k.op("dve", lambda g: g.tensor_scalar(out=A[:], in0=A[:], scalar1=nA[:, 0:1], scalar2=None, op0=ALU.mult),
             [A, nA], [A])
        k.op("dve", lambda g: g.tensor_tensor_scan(out=gc33[:], data0=rmask[:], data1=A[:], initial=0.0,
                                                   op0=ALU.mult, op1=ALU.add), [rmask, A], [gc33])
        k.op("act", lambda g: g.activation(out=B[:], in_=B[:], func=AF.Sigmoid), [B], [B])
        k.op("dve", lambda g: g.tensor_scalar(out=U33[b][:], in0=gc33[:], scalar1=cm[:, 0:1], scalar2=cm[:, 1:2],
                                              op0=ALU.mult, op1=ALU.add), [gc33, cm], [U33[b]])
        k.op("dve", lambda g: g.tensor_scalar(out=V33[b][:], in0=gc33[:], scalar1=cm[:, 2:3], scalar2=cm[:, 3:4],
                                              op0=ALU.mult, op1=ALU.add), [gc33, cm], [V33[b]])
        k.op("dve", lambda g: g.tensor_scalar(out=RC[b][:], in0=gc33[:], scalar1=cm[:, 0:1], scalar2=None,
                                              op0=ALU.mult), [gc33, cm], [RC[b]])
        k.op("dve", lambda g: g.scalar_tensor_tensor(out=RC[b][:], in0=B[:], scalar=cm[:, 1:2], in1=RC[b][:],
                                                     op0=ALU.mult, op1=ALU.add), [B, cm, RC[b]], [RC[b]])
        k.op("pe", lambda g: g.matmul(pbig[:], lhsT=bsel[:], rhs=gc33[:], start=True, stop=True), [bsel, gc33], [pbig])
        k.op("dve", lambda g: g.tensor_copy(out=GB[b][:], in_=pbig[:]), [pbig], [GB[b]])
        k.op("act", lambda g: g.activation(out=EG[b][:], in_=pbig[:], func=AF.Exp), [pbig], [EG[b]])
        k.op("act", lambda g: g.activation(out=zt[b][:], in_=zt[b][:], func=AF.Silu), [zt[b]], [zt[b]])

    def par(s, ci, u=0):
        b = s % 2
        i = gi[0] % NB
        gi[0] += 1
        c0 = ci * CH
        cs = slice(c0, c0 + CH)
        kT = kn[b]; qT = qn[b]; vT = cv[2][b]
        U = U33[b]; V = V33[b]
        pc = small.get(u, GDN_IL)
        k.op("pe", lambda g: g.matmul(pc[0:64, 0:2], lhsT=RC[b][:, cs], rhs=sel[:], start=True, stop=True),
             [RC[b], sel], [pc])
        cl = cols[i]
        k.op("dve", lambda g: g.tensor_copy(out=cl[:, 0:2], in_=pc[0:64, 0:2]), [pc], [cl])
        k.op("act", lambda g: g.activation(out=cl[:, 2:3], in_=cl[:, 0:1], func=AF.Exp), [cl], [cl])
        k.op("dve", lambda g: g.tensor_tensor(out=cl[:, 3:4], in0=cl[:, 2:3], in1=cl[:, 1:2], op=ALU.mult), [cl], [cl])
        k.op("dve", lambda g: g.tensor_scalar(out=cl[:, 4:5], in0=cl[:, 1:2], scalar1=-1.0, scalar2=None, op0=ALU.mult),
             [cl], [cl])
        last = c0 + CH - 1
        k.op("act", lambda g: g.activation(out=cl[:, 5:6], in_=cl[:, 0:1], func=AF.Exp, scale=-1.0,
                                           bias=GB[b][0:64, last:last + 1]), [cl, GB[b]], [cl])
        if GDN_PAR_STOP == 1:
            return None
        pd = small.get(u, GDN_IL); pdT = small.get(u, GDN_IL)
        k.op("pe", lambda g: g.matmul(pd[0:64, :], lhsT=U[:, cs], rhs=V[:, cs], start=True, stop=True), [U, V], [pd])
        k.op("pe", lambda g: g.matmul(pdT[0:64, :], lhsT=V[:, cs], rhs=U[:, cs], start=True, stop=True), [U, V], [pdT])
        k.op("dve", lambda g: g.tensor_tensor(out=Dm[i][:], in0=pd[0:64, :], in1=maskc[:], op=ALU.add), [pd, maskc], [Dm[i]])
        k.op("act", lambda g: g.activation(out=Dm[i][:], in_=Dm[i][:], func=AF.Exp), [Dm[i]], [Dm[i]])
        k.op("dve", lambda g: g.tensor_tensor(out=DTm[i][:], in0=pdT[0:64, :], in1=maskcT[:], op=ALU.add),
             [pdT, maskcT], [DTm[i]])
        k.op("act", lambda g: g.activation(out=DTm[i][:], in_=DTm[i][:], func=AF.Exp), [DTm[i]], [DTm[i]])
        k.op("pool", lambda g: g.tensor_tensor(out=Ds[i][:], in0=Dm[i][:], in1=strict01[:], op=ALU.mult),
             [Dm[i], strict01], [Ds[i]])
        if GDN_PAR_STOP == 2:
            return None
        pkk = small.get(u, GDN_IL)
        k.op("pe", lambda g: g.matmul(pkk[0:64, :], lhsT=kT[:, cs], rhs=kT[:, cs], start=True, stop=True), [kT], [pkk])
        Y = Ya[i]; X = Xa[i]; Y2 = Yb[i]; X2 = Xb[i]
        k.op("dve", lambda g, Y=Y: g.scalar_tensor_tensor(out=Y[:], in0=pkk[0:64, :], scalar=cl[:, 4:5], in1=Ds[i][:],
                                                     op0=ALU.mult, op1=ALU.mult), [pkk, cl, Ds[i]], [Y])
        if GDN_PAR_STOP == 3:
            return None
        px = small.get(u, GDN_IL)
        k.op("pe", lambda g, Y=Y: g.matmul(px[0:64, :], lhsT=Y[:], rhs=ident[0:64, 0:64], start=True, stop=True), [Y, ident], [px])
        P = Pm[i]
        if GDN_PAR_STOP == 41:
            return None
        k.op("act", lambda g, X=X: g.copy(out=X[:], in_=px[0:64, :]), [px], [X])
        if GDN_PAR_STOP == 42:
            return None
        k.op("dve", lambda g, X=X: g.tensor_tensor(out=P[:], in0=X[:], in1=ident[0:64, 0:64], op=ALU.add),
             [X, ident], [P])
        if GDN_PAR_STOP == 4:
            return None
        for lvl in range(1, 6):
            lastl = (lvl == 5)
            py = small.get(u, GDN_IL)
            k.op("pe", lambda g, X=X, Y=Y, py=py: g.matmul(py[0:64, :], lhsT=X[:], rhs=Y[:], start=True, stop=True),
                 [X, Y], [py])
            if not lastl:
                pxx = small.get(u, GDN_IL)
                k.op("pe", lambda g, X=X, Y=Y, pxx=pxx: g.matmul(pxx[0:64, :], lhsT=Y[:], rhs=X[:], start=True, stop=True),
                     [X, Y], [pxx])
            k.op("act", lambda g, Y2=Y2, py=py: g.copy(out=Y2[:], in_=py[0:64, :]), [py], [Y2])
            if not lastl:
                k.op("dve", lambda g, X2=X2, pxx=pxx: g.tensor_copy(out=X2[:], in_=pxx[0:64, :]), [pxx], [X2])
            pp = small.get(u, GDN_IL)
            k.op("pe", lambda g, Y2=Y2, P=P, pp=pp: g.matmul(pp[0:64, :], lhsT=Y2[:], rhs=P[:], start=True, stop=True),
                 [Y2, P], [pp])
            k.op("dve", lambda g, P=P, pp=pp: g.tensor_tensor(out=P[:], in0=P[:], in1=pp[0:64, :], op=ALU.add),
                 [P, pp], [P])
            X, X2 = X2, X
            Y, Y2 = Y2, Y
        if GDN_PAR_STOP == 5:
            return None
        pk = wide.get(u, GDN_IL + 1)
        k.op("pe", lambda g: g.matmul(pk[0:64, :], lhsT=kT[:, cs], rhs=ident[:], start=True, stop=True), [kT, ident], [pk])
        k.op("dve", lambda g: g.tensor_scalar(out=kb[i][:], in0=pk[0:64, :], scalar1=cl[:, 3:4], scalar2=None,
                                              op0=ALU.mult), [pk, cl], [kb[i]])
        k.op("dve", lambda g: g.tensor_scalar(out=kd[i][:], in0=pk[0:64, :], scalar1=cl[:, 5:6], scalar2=None,
                                              op0=ALU.mult), [pk, cl], [kd[i]])
        pvv = wide.get(u, GDN_IL + 1)
        k.op("pe", lambda g: g.matmul(pvv[0:64, :], lhsT=vT[:, cs], rhs=ident[:], start=True, stop=True), [vT, ident], [pvv])
        k.op("dve", lambda g: g.tensor_scalar(out=vb[i][:], in0=pvv[0:64, :], scalar1=cl[:, 1:2], scalar2=None,
                                              op0=ALU.mult), [pvv, cl], [vb[i]])
        if GDN_PAR_STOP == 6:
            return None
        pval = wide.get(u, GDN_IL + 1)
        k.op("pe", lambda g: g.matmul(pval[0:64, :], lhsT=P[:], rhs=vb[i][:], start=True, stop=True), [P, vb[i]], [pval])
        k.op("act", lambda g: g.copy(out=val[i][:], in_=pval[0:64, :]), [pval], [val[i]])
        pkc = small.get(u, GDN_IL)
        k.op("pe", lambda g: g.matmul(pkc[:, :], lhsT=kb[i][:], rhs=P[:], start=True, stop=True), [kb[i], P], [pkc])
        k.op("act", lambda g: g.copy(out=kcumT[i][:], in_=pkc[:, :]), [pkc], [kcumT[i]])
        pqk = small.get(u, GDN_IL)
        k.op("pe", lambda g: g.matmul(pqk[0:64, :], lhsT=kT[:, cs], rhs=qT[:, cs], start=True, stop=True), [kT, qT], [pqk])
        k.op("dve", lambda g: g.tensor_tensor(out=qkT[i][:], in0=pqk[0:64, :], in1=DTm[i][:], op=ALU.mult),
             [pqk, DTm[i]], [qkT[i]])
        k.op("pool", lambda g: g.tensor_tensor(out=qgT[i][:], in0=qT[:, cs], in1=EG[b][:, cs], op=ALU.mult),
             [qT, EG[b]], [qgT[i]])
        return dict(i=i, b=b, ci=ci, s=s, last=last)

    def seq(d):
        i = d["i"]; b = d["b"]; ci = d["ci"]; last = d["last"]
        pv = wide.get(GDN_IL, GDN_IL + 1)
        k.op("pe", lambda g: g.matmul(pv[0:64, :], lhsT=kcumT[i][:], rhs=S[:], start=True, stop=True), [kcumT[i], S], [pv])
        k.op("dve", lambda g: g.tensor_tensor(out=vnew[i][:], in0=val[i][:], in1=pv[0:64, :], op=ALU.subtract),
             [val[i], pv], [vnew[i]])
        po = wide.get(GDN_IL, GDN_IL + 1)
        k.op("pe", lambda g: g.matmul(po[0:64, :], lhsT=qgT[i][:], rhs=S[:], start=True, stop=False), [qgT[i], S], [po])
        k.op("pe", lambda g: g.matmul(po[0:64, :], lhsT=qkT[i][:], rhs=vnew[i][:], start=False, stop=True),
             [qkT[i], vnew[i]], [po])
        pS = wide.get(GDN_IL, GDN_IL + 1)
        k.op("pe", lambda g: g.matmul(pS[:, :], lhsT=kd[i][:], rhs=vnew[i][:], start=True, stop=True), [kd[i], vnew[i]], [pS])
        k.op("dve", lambda g: g.scalar_tensor_tensor(out=S[:], in0=S[:], scalar=EG[b][:, last:last + 1], in1=pS[:, :],
                                                     op0=ALU.mult, op1=ALU.add), [S, EG[b], pS], [S])
        cl = cols[i]
        k.op("act", lambda g: g.activation(out=junk[i][:], in_=po[0:64, :], func=AF.Square, accum_out=cl[:, 6:7]),
             [po], [junk[i], cl])
        k.op("dve", lambda g: g.tensor_scalar(out=cl[:, 6:7], in0=cl[:, 6:7], scalar1=1.0 / 128.0, scalar2=EPS,
                                              op0=ALU.mult, op1=ALU.add), [cl], [cl])
        k.op("act", lambda g: g.activation(out=cl[:, 6:7], in_=cl[:, 6:7], func=AF.Sqrt), [cl], [cl])
        k.op("dve", lambda g: g.reciprocal(out=cl[:, 6:7], in_=cl[:, 6:7]), [cl], [cl])
        k.op("dve", lambda g: g.scalar_tensor_tensor(out=ycur[i][:], in0=po[0:64, :], scalar=cl[:, 6:7], in1=gain[:],
                                                     op0=ALU.mult, op1=ALU.mult), [po, cl, gain], [ycur[i]])
        k.op("pool", lambda g: g.tensor_tensor(out=yo[b][:, ci, :], in0=ycur[i][:], in1=zt[b][:, ci, :], op=ALU.mult),
             [ycur[i], zt[b]], [yo[b]])
        if ci == NCH - 1:
            s = d["s"]
            k.dma("sp", ov[:, s * NCH:(s + 1) * NCH, :], yo[b][:], t_in=yo[b], final=True)

    for s in range(NST):
        supertile(s)
        if GDN_DEBUG < 2:
            continue
        xl = []
        if extra is not None:
            xr = k.record(extra, s)[1]
            ngrp = NCH // GDN_IL
            per = (len(xr) + ngrp - 1) // ngrp
            xl = [xr[gg * per:(gg + 1) * per] for gg in range(ngrp)]
        for ci in range(0, NCH, GDN_IL):
            recs = [k.record(par, s, ci + u, u) for u in range(GDN_IL)]
            chains = [r[1] for r in recs]
            if xl:
                chains.append(xl[ci // GDN_IL])
            k.emit_interleaved(chains)
            if GDN_DEBUG < 3:
                continue
            for dd in pending:
                seq(dd)
            pending[:] = [r[0] for r in recs]
    if GDN_DEBUG >= 3:
        for dd in pending:
            seq(dd)


def gdn_inputs(projT, inp, l, h, L=SEQ):
    r = lambda a: np.ascontiguousarray(a, dtype=np.float32)
    q0, k0, v0, z0 = 1816, 2840, 3864, 4888
    a = projT[5912 + h, :L]
    b = projT[5920 + h, :L]
    a33 = np.zeros((33, L), np.float32); a33[0] = a; a33[32] = a
    b33 = np.zeros((33, L), np.float32); b33[0] = b; b33[32] = b
    conv = np.asarray(inp["gdn_conv"][l], np.float32)
    cw = np.stack([conv[:, w * 1024 + h * 128: w * 1024 + (h + 1) * 128].T for w in range(3)], axis=1)
    prm = np.zeros((33, 2), np.float32)
    prm[:, 0] = inp["gdn_a_log"][l][h]; prm[:, 1] = inp["gdn_dt_bias"][l][h]
    d = {
        "g_qT": r(projT[q0 + h * 128:q0 + (h + 1) * 128, :L]), "g_kT": r(projT[k0 + h * 128:k0 + (h + 1) * 128, :L]),
        "g_vT": r(projT[v0 + h * 128:v0 + (h + 1) * 128, :L]), "g_z": r(projT[z0 + h * 128:z0 + (h + 1) * 128, :L].T),
        "g_a33": a33, "g_b33": b33, "g_cw": r(cw), "g_prm33": prm,
        "g_gain": r(np.tile(np.asarray(inp["gdn_norm"][l], np.float32)[None, :], (64, 1))),
    }
    for n, v in gdn_consts().items():
        d["gc_" + n] = v
    return d


import math
I32 = mybir.dt.int32
S5_W = 512
PI_SAFE = 3.141592


def build_s5(L=SEQ):
    nc = bass.Bass("TRN2", target_bir_lowering=False)
    W = S5_W
    d = {}

    def inp(name, shape):
        d[name] = nc.dram_tensor(name, list(shape), F32, kind="ExternalInput").ap()
        return d[name]

    inp("s_uT", [64, L]); inp("s_lamre", [128, 2]); inp("s_lamim", [128, 2]); inp("s_logdt", [128, 2])
    inp("s_bre", [2, 64, 128]); inp("s_bim", [2, 64, 128]); inp("s_cre", [2, 128, 64]); inp("s_cim", [2, 128, 64])
    inp("s_d", [64, 1]); inp("s_trow", [128, W + 1])
    out_d = nc.dram_tensor("s_out", [64, L], F32, kind="ExternalOutput").ap()
    k = KB(nc)
    emit_s5(k, L, d, out_d)
    k.finish()
    return nc


def emit_s5(k, L, d, out_d, stepper=False, npp=4, npy=2):
    W = S5_W
    NST = L // W
    TWO_PI = 2.0 * math.pi

    def load(name, shape):
        t = k.sb(shape, F32, name)
        k.dma("sp", t[:], d[name], t_out=t)
        return t

    lamre = load("s_lamre", [128, 2]); lamim = load("s_lamim", [128, 2]); logdt = load("s_logdt", [128, 2])
    trow = load("s_trow", [128, W + 1]); dsk = load("s_d", [64, 1])
    Bre = []; Bim = []; Cre = []; Cim = []
    for j in range(2):
        for lst, nm, shp in ((Bre, "s_bre", [64, 128]), (Bim, "s_bim", [64, 128]), (Cre, "s_cre", [128, 64]),
                             (Cim, "s_cim", [128, 64])):
            t = k.sb(shp, F32, f"{nm}{j}")
            k.dma("sp", t[:], d[nm][j], t_out=t)
            lst.append(t)
    ki = k.sb([128, W + 1], I32, "ki")
    kf = k.sb([128, W + 1], F32, "kf")
    ang = k.sb([128, W + 1], F32, "ang")
    sc = k.sb([128, 16], F32, "sc")
    Ct = []; St = []; nSt = []; Trt = []; Tit = []; rr = []; eW = []

    def reduce_inplace(t, n):
        k.op("dve", lambda g: g.tensor_scalar(out=ki[:, :n], in0=t[:, :n], scalar1=1.0 / TWO_PI, scalar2=None,
                                              op0=ALU.mult), [t], [ki])
        k.op("dve", lambda g: g.tensor_copy(out=kf[:, :n], in_=ki[:, :n]), [ki], [kf])
        k.op("dve", lambda g: g.scalar_tensor_tensor(out=t[:, :n], in0=kf[:, :n], scalar=-TWO_PI, in1=t[:, :n],
                                                     op0=ALU.mult, op1=ALU.add), [kf, t], [t])
        k.op("dve", lambda g: g.tensor_scalar(out=t[:, :n], in0=t[:, :n], scalar1=-PI_SAFE, scalar2=PI_SAFE,
                                              op0=ALU.max, op1=ALU.min), [t], [t])

    for j in range(2):
        C = k.sb([128, W + 1], F32, f"C{j}"); S = k.sb([128, W + 1], F32, f"S{j}"); nS = k.sb([128, W + 1], F32, f"nS{j}")
        Tr = k.sb([128, W], F32, f"Tr{j}"); Ti = k.sb([128, W], F32, f"Ti{j}")
        pr = k.sb([128, 16], F32, f"pr{j}")
        k.op("act", lambda g: g.activation(out=pr[:, 0:1], in_=logdt[:, j:j + 1], func=AF.Exp), [logdt], [pr])
        k.op("dve", lambda g: g.tensor_tensor(out=pr[:, 1:2], in0=lamim[:, j:j + 1], in1=pr[:, 0:1], op=ALU.mult),
             [lamim, pr], [pr])
        k.op("dve", lambda g: g.tensor_tensor(out=pr[:, 2:3], in0=lamre[:, j:j + 1], in1=pr[:, 0:1], op=ALU.mult),
             [lamre, pr], [pr])
        k.op("act", lambda g: g.activation(out=pr[:, 2:3], in_=pr[:, 2:3], func=AF.Exp), [pr], [pr])
        k.op("dve", lambda g: g.tensor_copy(out=ang[:, 0:1], in_=pr[:, 1:2]), [pr], [ang])
        reduce_inplace(ang, 1)
        k.op("dve", lambda g: g.tensor_copy(out=pr[:, 3:4], in_=ang[:, 0:1]), [ang], [pr])
        k.op("dve", lambda g: g.tensor_scalar(out=ang[:], in0=trow[:], scalar1=pr[:, 3:4], scalar2=None, op0=ALU.mult),
             [trow, pr], [ang])
        reduce_inplace(ang, W + 1)
        k.op("act", lambda g: g.activation(out=S[:], in_=ang[:], func=AF.Sin), [ang], [S])
        k.op("dve", lambda g: g.tensor_scalar(out=ang[:], in0=ang[:], scalar1=0.5 * math.pi, scalar2=None, op0=ALU.add),
             [ang], [ang])
        reduce_inplace(ang, W + 1)
        k.op("act", lambda g: g.activation(out=C[:], in_=ang[:], func=AF.Sin), [ang], [C])
        k.op("pool", lambda g: g.tensor_scalar(out=nS[:], in0=S[:], scalar1=-1.0, scalar2=None, op0=ALU.mult), [S], [nS])
        k.op("dve", lambda g: g.tensor_tensor(out=pr[:, 4:5], in0=pr[:, 2:3], in1=C[:, 1:2], op=ALU.mult), [pr, C], [pr])
        k.op("dve", lambda g: g.tensor_scalar(out=pr[:, 4:5], in0=pr[:, 4:5], scalar1=-1.0, scalar2=None, op0=ALU.add),
             [pr], [pr])
        k.op("dve", lambda g: g.tensor_tensor(out=pr[:, 5:6], in0=pr[:, 2:3], in1=S[:, 1:2], op=ALU.mult), [pr, S], [pr])
        k.op("dve", lambda g: g.tensor_tensor(out=pr[:, 6:7], in0=lamre[:, j:j + 1], in1=lamre[:, j:j + 1], op=ALU.mult),
             [lamre], [pr])
        k.op("dve", lambda g: g.scalar_tensor_tensor(out=pr[:, 6:7], in0=lamim[:, j:j + 1], scalar=lamim[:, j:j + 1],
                                                     in1=pr[:, 6:7], op0=ALU.mult, op1=ALU.add), [lamim, pr], [pr])
        k.op("dve", lambda g: g.reciprocal(out=pr[:, 6:7], in_=pr[:, 6:7]), [pr], [pr])
        k.op("dve", lambda g: g.tensor_tensor(out=pr[:, 7:8], in0=pr[:, 4:5], in1=lamre[:, j:j + 1], op=ALU.mult),
             [pr, lamre], [pr])
        k.op("dve", lambda g: g.scalar_tensor_tensor(out=pr[:, 7:8], in0=pr[:, 5:6], scalar=lamim[:, j:j + 1],
                                                     in1=pr[:, 7:8], op0=ALU.mult, op1=ALU.add), [pr, lamim], [pr])
        k.op("dve", lambda g: g.tensor_tensor(out=pr[:, 7:8], in0=pr[:, 7:8], in1=pr[:, 6:7], op=ALU.mult), [pr], [pr])
        k.op("dve", lambda g: g.tensor_tensor(out=pr[:, 9:10], in0=pr[:, 4:5], in1=lamim[:, j:j + 1], op=ALU.mult),
             [pr, lamim], [pr])
        k.op("dve", lambda g: g.scalar_tensor_tensor(out=pr[:, 8:9], in0=pr[:, 5:6], scalar=lamre[:, j:j + 1],
                                                     in1=pr[:, 9:10], op0=ALU.mult, op1=ALU.subtract), [pr, lamre], [pr])
        k.op("dve", lambda g: g.tensor_tensor(out=pr[:, 8:9], in0=pr[:, 8:9], in1=pr[:, 6:7], op=ALU.mult), [pr], [pr])
        k.op("dve", lambda g: g.tensor_scalar(out=Tr[:], in0=C[:, 0:W], scalar1=pr[:, 7:8], scalar2=None, op0=ALU.mult),
             [C, pr], [Tr])
        k.op("dve", lambda g: g.scalar_tensor_tensor(out=Tr[:], in0=S[:, 0:W], scalar=pr[:, 8:9], in1=Tr[:],
                                                     op0=ALU.mult, op1=ALU.add), [S, pr, Tr], [Tr])
        k.op("dve", lambda g: g.tensor_scalar(out=Ti[:], in0=C[:, 0:W], scalar1=pr[:, 8:9], scalar2=None, op0=ALU.mult),
             [C, pr], [Ti])
        k.op("dve", lambda g: g.scalar_tensor_tensor(out=Ti[:], in0=nS[:, 0:W], scalar=pr[:, 7:8], in1=Ti[:],
                                                     op0=ALU.mult, op1=ALU.add), [nS, pr, Ti], [Ti])
        Ct.append(C); St.append(S); nSt.append(nS); Trt.append(Tr); Tit.append(Ti); rr.append(pr)

    ub = [k.sb([64, W], F32, f"ub{i}") for i in range(2)]
    pP = [k.ps([128, W], F32, f"pP{i}") for i in range(npp)]
    pY = [k.ps([128, W], F32, f"pY{i}") for i in range(npy)]
    mkt = lambda nm, n=2: [k.sb([128, W], F32, f"{nm}{i}") for i in range(n)]
    prs = mkt("prs", 1) * 2; pis = mkt("pis", 1) * 2; m1 = mkt("m1", 1) * 2; m2 = mkt("m2", 1) * 2; m3 = mkt("m3", 1) * 2; m4 = mkt("m4", 1) * 2
    cr = mkt("cr", 1) * 2; ci = mkt("ci", 1) * 2
    vr = [mkt("vr0", 1) * 2, mkt("vr1", 1) * 2]; vi = [mkt("vi0", 1) * 2, mkt("vi1", 1) * 2]
    xr = mkt("xr", 1) * 2; xi = mkt("xi", 1) * 2
    init = [k.sb([128, 4], F32, f"init{j}") for j in range(2)]
    yb = [k.sb([64, W], F32, f"yb{i}") for i in range(2)]
    g1 = k.sb([64, W], F32, "g1"); g2 = k.sb([64, W], F32, "g2")
    pn = [0]

    def step(s):
        u = ub[s % 2]
        k.dma("sp", u[:], d["s_uT"][:, s * W:(s + 1) * W], t_out=u)
        py = pY[s % len(pY)]
        def tile_body(j):
            b = (2 * s + j) % 2
            Pr = pP[pn[0] % len(pP)]; Pi = pP[(pn[0] + 1) % len(pP)]; pn[0] += 2
            k.op("pe", lambda g, Pr=Pr: g.matmul(Pr[:], lhsT=Bre[j][:], rhs=u[:], start=True, stop=True), [Bre[j], u], [Pr])
            k.op("act", lambda g, Pr=Pr: g.copy(out=prs[b][:], in_=Pr[:]), [Pr], [prs[b]])
            k.op("pe", lambda g, Pi=Pi: g.matmul(Pi[:], lhsT=Bim[j][:], rhs=u[:], start=True, stop=True), [Bim[j], u], [Pi])
            k.op("act", lambda g, Pi=Pi: g.copy(out=pis[b][:], in_=Pi[:]), [Pi], [pis[b]])
            Tr = Trt[j]; Ti = Tit[j]; C = Ct[j]; nS = nSt[j]; pr = rr[j]
            k.op("pool", lambda g: g.tensor_tensor(out=m1[b][:], in0=Tr[:], in1=prs[b][:], op=ALU.mult), [Tr, prs[b]], [m1[b]])
            k.op("pool", lambda g: g.tensor_tensor(out=m2[b][:], in0=Ti[:], in1=pis[b][:], op=ALU.mult), [Ti, pis[b]], [m2[b]])
            k.op("pool", lambda g: g.tensor_tensor(out=cr[b][:], in0=m1[b][:], in1=m2[b][:], op=ALU.subtract),
                 [m1[b], m2[b]], [cr[b]])
            k.op("dve", lambda g: g.tensor_tensor(out=m3[b][:], in0=Tr[:], in1=pis[b][:], op=ALU.mult), [Tr, pis[b]], [m3[b]])
            k.op("dve", lambda g: g.tensor_tensor(out=m4[b][:], in0=Ti[:], in1=prs[b][:], op=ALU.mult), [Ti, prs[b]], [m4[b]])
            k.op("dve", lambda g: g.tensor_tensor(out=ci[b][:], in0=m3[b][:], in1=m4[b][:], op=ALU.add),
                 [m3[b], m4[b]], [ci[b]])
            VR = vr[j][s % 2]; VI = vi[j][s % 2]
            rb = pr[:, 2:3].to_broadcast([128, W])
            if s == 0:
                ir = 0.0; ii_ = 0.0; extra = []
            else:
                ir = init[j][:, 0:1]; ii_ = init[j][:, 1:2]; extra = [init[j]]
            k.op("dve", lambda g, ir=ir: g.tensor_tensor_scan(out=VR[:], data0=rb, data1=cr[b][:], initial=ir,
                                                           op0=ALU.mult, op1=ALU.add), [pr, cr[b]] + extra, [VR])
            k.op("dve", lambda g, ii_=ii_: g.tensor_tensor_scan(out=VI[:], data0=rb, data1=ci[b][:], initial=ii_,
                                                             op0=ALU.mult, op1=ALU.add), [pr, ci[b]] + extra, [VI])
            it = init[j]
            k.op("dve", lambda g: g.tensor_tensor(out=it[:, 2:3], in0=VI[:, W - 1:W], in1=St[j][:, W:W + 1], op=ALU.mult),
                 [VI, St[j]], [it])
            k.op("dve", lambda g: g.scalar_tensor_tensor(out=it[:, 0:1], in0=VR[:, W - 1:W], scalar=C[:, W:W + 1],
                                                         in1=it[:, 2:3], op0=ALU.mult, op1=ALU.subtract), [VR, C, it], [it])
            k.op("dve", lambda g: g.tensor_tensor(out=it[:, 3:4], in0=VI[:, W - 1:W], in1=C[:, W:W + 1], op=ALU.mult),
                 [VI, C], [it])
            k.op("dve", lambda g: g.scalar_tensor_tensor(out=it[:, 1:2], in0=VR[:, W - 1:W], scalar=St[j][:, W:W + 1],
                                                         in1=it[:, 3:4], op0=ALU.mult, op1=ALU.add), [VR, St[j], it], [it])
            k.op("pool", lambda g: g.tensor_tensor(out=m1[b][:], in0=VR[:], in1=C[:, 0:W], op=ALU.mult), [VR, C], [m1[b]])
            k.op("pool", lambda g: g.tensor_tensor(out=m2[b][:], in0=VI[:], in1=nS[:, 0:W], op=ALU.mult), [VI, nS], [m2[b]])
            k.op("pool", lambda g: g.tensor_tensor(out=xr[b][:], in0=m1[b][:], in1=m2[b][:], op=ALU.add),
                 [m1[b], m2[b]], [xr[b]])
            k.op("dve", lambda g: g.tensor_tensor(out=m3[b][:], in0=VR[:], in1=nS[:, 0:W], op=ALU.mult), [VR, nS], [m3[b]])
            k.op("dve", lambda g: g.tensor_tensor(out=m4[b][:], in0=VI[:], in1=C[:, 0:W], op=ALU.mult), [VI, C], [m4[b]])
            k.op("dve", lambda g: g.tensor_tensor(out=xi[b][:], in0=m3[b][:], in1=m4[b][:], op=ALU.subtract),
                 [m3[b], m4[b]], [xi[b]])
            k.op("pe", lambda g: g.matmul(py[0:64, :], lhsT=Cre[j][:], rhs=xr[b][:], start=(j == 0), stop=False),
                 [Cre[j], xr[b]], [py])
            k.op("pe", lambda g: g.matmul(py[0:64, :], lhsT=Cim[j][:], rhs=xi[b][:], start=False, stop=(j == 1)),
                 [Cim[j], xi[b]], [py])
        for j in range(2):
            tile_body(j)
        y = yb[s % 2]
        k.op("dve", lambda g: g.scalar_tensor_tensor(out=g1[:], in0=u[:], scalar=dsk[:, 0:1], in1=py[0:64, :],
                                                     op0=ALU.mult, op1=ALU.add), [u, dsk, py], [g1])
        k.op("pool", lambda g: g.tensor_tensor(out=g2[:], in0=g1[:], in1=g1[:], op=ALU.mult), [g1], [g2])
        k.op("pool", lambda g: g.tensor_scalar(out=g2[:], in0=g2[:], scalar1=0.044715, scalar2=1.0, op0=ALU.mult, op1=ALU.add),
             [g2], [g2])
        k.op("pool", lambda g: g.tensor_tensor(out=g2[:], in0=g2[:], in1=g1[:], op=ALU.mult), [g2, g1], [g2])
        k.op("act", lambda g: g.activation(out=g2[:], in_=g2[:], func=AF.Tanh, scale=math.sqrt(2.0 / math.pi)), [g2], [g2])
        k.op("pool", lambda g: g.tensor_scalar(out=g2[:], in0=g2[:], scalar1=1.0, scalar2=0.5, op0=ALU.add, op1=ALU.mult),
             [g2], [g2])
        k.op("pool", lambda g: g.tensor_tensor(out=y[:], in0=g2[:], in1=g1[:], op=ALU.mult), [g2, g1], [y])
        k.dma("sp", out_d[:, s * W:(s + 1) * W], y[:], t_in=y, final=True)

    if stepper:
        return step
    for s in range(NST):
        step(s)


def s5_inputs(projT, inp, l, core, L=SEQ):
    r = lambda a: np.ascontiguousarray(a, dtype=np.float32)
    g0 = core * 4
    f = lambda nm: np.asarray(inp[nm][l], np.float32)
    lamre = f("s5_lam_re")[g0:g0 + 4].reshape(2, 128).T
    lamim = f("s5_lam_im")[g0:g0 + 4].reshape(2, 128).T
    logdt = np.repeat(f("s5_log_dt")[g0:g0 + 4], 64).reshape(2, 128).T
    bre = np.zeros((2, 64, 128), np.float32); bim = np.zeros((2, 64, 128), np.float32)
    cre = np.zeros((2, 128, 64), np.float32); cim = np.zeros((2, 128, 64), np.float32)
    for j in range(2):
        for gl in range(2):
            g = g0 + 2 * j + gl
            ch = slice((2 * j + gl) * 16, (2 * j + gl + 1) * 16)
            st = slice(gl * 64, (gl + 1) * 64)
            bre[j, ch, st] = f("s5_b_re")[g].T
            bim[j, ch, st] = f("s5_b_im")[g].T
            cre[j, st, ch] = f("s5_c_re")[g].T
            cim[j, st, ch] = f("s5_c_im")[g].T
    return {
        "s_uT": r(projT[1304 + core * 64:1304 + (core + 1) * 64, :L]),
        "s_lamre": r(lamre), "s_lamim": r(lamim), "s_logdt": r(logdt),
        "s_bre": bre, "s_bim": bim, "s_cre": cre, "s_cim": cim,
        "s_d": r(f("s5_d")[core * 64:(core + 1) * 64].reshape(64, 1)),
        "s_trow": r(np.tile(np.arange(S5_W + 1, dtype=np.float32)[None], (128, 1))),
    }


ROPE_THETA = 500000.0


def rope_tables(pos):
    half = 8
    inv = (ROPE_THETA ** (-np.arange(half, dtype=np.float32) / half)).astype(np.float32)
    ang = pos.astype(np.float32)[None, :] * inv[:, None]
    c = np.ones((64, len(pos)), np.float32); s = np.zeros((64, len(pos)), np.float32)
    c[0:8] = np.cos(ang); c[8:16] = np.cos(ang)
    s[0:8] = np.sin(ang); s[8:16] = np.sin(ang)
    return c, s


def rope_perm():
    pm = np.zeros((64, 64), np.float32)
    for dd in range(8):
        pm[dd + 8, dd] = -1.0
        pm[dd, dd + 8] = 1.0
    return pm


def build_prep(ntok=TOK):
    nc = bass.Bass("TRN2", target_bir_lowering=False)
    d = {}

    def inp(name, shape):
        d[name] = nc.dram_tensor(name, list(shape), F32, kind="ExternalInput").ap()

    NB = ntok // 16
    inp("p_q", [8, 64, ntok]); inp("p_ks", [2, 64, ntok]); inp("p_kw", [2, 64, ntok])
    inp("p_kc", [2, 64, ntok + 16]); inp("p_vc", [2, 64, ntok + 16])
    inp("p_gains", [64, 4]); inp("p_cos", [64, ntok]); inp("p_sin", [64, ntok])
    inp("p_cosc", [64, NB]); inp("p_sinc", [64, NB]); inp("p_pm", [64, 64])
    inp("p_w1k", [64, 32, 256]); inp("p_w1v", [64, 32, 256]); inp("p_w2k", [128, 2, 64]); inp("p_w2v", [128, 2, 64])
    inp("p_peT", [64, 32])
    o = {}
    for name, shape in (("o_q", [8, 64, ntok]), ("o_ks", [2, 64, ntok]), ("o_kw", [2, 64, ntok]),
                        ("o_kc", [2, 64, NB]), ("o_vc", [NB, 2, 64])):
        o[name] = nc.dram_tensor(name, shape, F32, kind="ExternalOutput").ap()
    k = KB(nc)
    W = 512

    def load(name, shape):
        t = k.sb(shape, F32, name)
        k.dma("sp", t[:], d[name], t_out=t)
        return t

    gains = load("p_gains", [64, 4]); pm = load("p_pm", [64, 64])
    cosT = load("p_cos", [64, ntok]); sinT = load("p_sin", [64, ntok])
    cosc = load("p_cosc", [64, NB]); sinc = load("p_sinc", [64, NB])
    ones = k.sb([64, 64], F32, "ones64")
    k.op("pool", lambda g: g.memset(ones[:], 1.0), [], [ones])
    xin = [k.sb([64, W], F32, f"xin{i}") for i in range(3)]
    sq = k.sb([64, W], F32, "sq"); rs = k.sb([64, W], F32, "rs"); xn = [k.sb([64, W], F32, f"xn{i}") for i in range(2)]
    t1 = [k.sb([64, W], F32, f"t1{i}") for i in range(2)]
    ob = [k.sb([64, W], F32, f"ob{i}") for i in range(3)]
    pst = [k.ps([128, W], F32, f"pst{i}") for i in range(2)]
    prt = [k.ps([128, W], F32, f"prt{i}") for i in range(2)]
    cnt = [0]

    def norm_rope(src_t, src_ap, n, gcol, scale_eps, ones_val_scale, cos_ap, sin_ap, dst_t, dst_ap):
        i = cnt[0]; cnt[0] += 1
        ps = pst[i % 2]; pr = prt[i % 2]; x = xn[i % 2]; tt = t1[i % 2]
        k.op("act", lambda g: g.activation(out=sq[:, :n], in_=src_ap, func=AF.Square), [src_t], [sq])
        k.op("pe", lambda g: g.matmul(ps[0:64, :n], lhsT=ones[:], rhs=sq[:, :n], start=True, stop=True), [ones, sq], [ps])
        k.op("dve", lambda g: g.tensor_scalar(out=rs[:, :n], in0=ps[0:64, :n], scalar1=ones_val_scale, scalar2=scale_eps,
                                              op0=ALU.mult, op1=ALU.add), [ps], [rs])
        k.op("act", lambda g: g.activation(out=rs[:, :n], in_=rs[:, :n], func=AF.Sqrt), [rs], [rs])
        k.op("dve", lambda g: g.reciprocal(out=rs[:, :n], in_=rs[:, :n]), [rs], [rs])
        k.op("dve", lambda g: g.scalar_tensor_tensor(out=x[:, :n], in0=src_ap, scalar=gains[:, gcol:gcol + 1], in1=rs[:, :n],
                                                     op0=ALU.mult, op1=ALU.mult), [src_t, gains, rs], [x])
        k.op("pe", lambda g: g.matmul(pr[0:64, :n], lhsT=pm[:], rhs=x[:, :n], start=True, stop=True), [pm, x], [pr])
        k.op("dve", lambda g: g.tensor_tensor(out=tt[:, :n], in0=pr[0:64, :n], in1=sin_ap, op=ALU.mult), [pr, sinT, sinc], [tt])
        k.op("pool", lambda g: g.tensor_tensor(out=dst_ap, in0=x[:, :n], in1=cos_ap, op=ALU.mult), [x, cosT, cosc], [dst_t])
        k.op("pool", lambda g: g.tensor_tensor(out=dst_ap, in0=dst_ap, in1=tt[:, :n], op=ALU.add), [dst_t, tt], [dst_t])

    n_it = 0
    for name, oname, nh, gcol, isq in (("p_q", "o_q", 8, 0, True), ("p_ks", "o_ks", 2, 2, False), ("p_kw", "o_kw", 2, 3, False)):
        for h in range(nh):
            for s in range(ntok // W):
                xi = xin[n_it % 3]; oo = ob[n_it % 3]; n_it += 1
                k.dma("sp", xi[:], d[name][h, :, s * W:(s + 1) * W], t_out=xi)
                if isq:
                    a, bb = 1.0, 64.0 * EPS
                else:
                    a, bb = 1.0 / 64.0, EPS
                norm_rope(xi, xi[:], W, gcol, bb, a, cosT[:, s * W:(s + 1) * W], sinT[:, s * W:(s + 1) * W], oo, oo[:])
                k.dma("sp", o[oname][h, :, s * W:(s + 1) * W], oo[:], t_in=oo, final=True)

    peT = load("p_peT", [64, 32])
    w2k = load("p_w2k", [128, 2, 64]); w2v = load("p_w2v", [128, 2, 64])
    w1 = k.sb([64, 32, 256], F32, "w1")
    raw = [k.sb([64, ntok + 16], F32, f"raw{i}") for i in range(2)]
    gel = [k.sb([128, NB], F32, f"gel{i}") for i in range(2)]
    ga = k.sb([128, NB], F32, "ga"); gb = k.sb([128, NB], F32, "gb")
    bias = k.sb([128, 2], F32, "bias")
    kcn = k.sb([64, NB], F32, "kcn"); kco = k.sb([64, NB], F32, "kco")
    vco = k.sb([NB, 2, 64], F32, "vco")
    ph = [k.ps([128, 512], F32, f"ph{i}") for i in range(2)]
    pb = k.ps([128, 512], F32, "pb")
    po = k.ps([128, 512], F32, "po")
    for which, (rname, wname, w2) in enumerate((("p_kc", "p_w1k", w2k), ("p_vc", "p_w1v", w2v))):
        k.dma("sp", w1[:], d[wname], t_out=w1)
        for ft in range(2):
            for l in range(32):
                k.op("pe", lambda g, l=l, ft=ft: g.matmul(pb[:, ft:ft + 1], lhsT=w1[:, l, ft * 128:(ft + 1) * 128],
                                                        rhs=peT[:, l:l + 1], start=(l == 0), stop=(l == 31)), [w1, peT], [pb])
        k.op("dve", lambda g: g.tensor_copy(out=bias[:], in_=pb[:, 0:2]), [pb], [bias])
        for hk in range(2):
            r = raw[hk]
            k.dma("sp", r[:], d[rname][hk], t_out=r)
            for ft in range(2):
                p = ph[ft]
                for l in range(32):
                    k.op("pe", lambda g, l=l, ft=ft, p=p, r=r: g.matmul(
                        p[:, :NB], lhsT=w1[:, l, ft * 128:(ft + 1) * 128], rhs=r[:, l:l + 16 * (NB - 1) + 1:16],
                        start=(l == 0), stop=(l == 31)), [w1, r], [p])
                k.op("act", lambda g, p=p, ft=ft: g.activation(out=ga[:], in_=p[:, :NB], func=AF.Identity,
                                                               bias=bias[:, ft:ft + 1], scale=1.0), [p, bias], [ga])
                k.op("pool", lambda g: g.tensor_tensor(out=gb[:], in0=ga[:], in1=ga[:], op=ALU.mult), [ga], [gb])
                k.op("pool", lambda g: g.tensor_scalar(out=gb[:], in0=gb[:], scalar1=0.044715, scalar2=1.0, op0=ALU.mult,
                                                       op1=ALU.add), [gb], [gb])
                k.op("pool", lambda g: g.tensor_tensor(out=gb[:], in0=gb[:], in1=ga[:], op=ALU.mult), [gb, ga], [gb])
                k.op("act", lambda g: g.activation(out=gb[:], in_=gb[:], func=AF.Tanh, scale=math.sqrt(2.0 / math.pi)),
                     [gb], [gb])
                k.op("pool", lambda g: g.tensor_scalar(out=gb[:], in0=gb[:], scalar1=1.0, scalar2=0.5, op0=ALU.add,
                                                       op1=ALU.mult), [gb], [gb])
                k.op("pool", lambda g, ft=ft: g.tensor_tensor(out=gel[ft][:], in0=gb[:], in1=ga[:], op=ALU.mult),
                     [gb, ga], [gel[ft]])
            if which == 0:
                for ft in range(2):
                    k.op("pe", lambda g, ft=ft: g.matmul(po[0:64, :NB], lhsT=w2[:, ft, :], rhs=gel[ft][:],
                                                        start=(ft == 0), stop=(ft == 1)), [w2, gel[ft]], [po])
                k.op("dve", lambda g: g.tensor_copy(out=kcn[:], in_=po[0:64, :NB]), [po], [kcn])
                norm_rope(kcn, kcn[:], NB, 1, EPS, 1.0 / 64.0, cosc[:], sinc[:], kco, kco[:])
                k.dma("sp", o["o_kc"][hk], kco[:], t_in=kco, final=True)
            else:
                for ft in range(2):
                    k.op("pe", lambda g, ft=ft: g.matmul(po[0:NB, 0:64], lhsT=gel[ft][:], rhs=w2[:, ft, :],
                                                        start=(ft == 0), stop=(ft == 1)), [w2, gel[ft]], [po])
                k.op("dve", lambda g, hk=hk: g.tensor_copy(out=vco[:, hk, :], in_=po[0:NB, 0:64]), [po], [vco])
    k.dma("sp", o["o_vc"], vco[:], t_in=vco, final=True)
    k.finish()
    return nc


def prep_inputs(projT, inp, l, c, ntok=TOK):
    r = lambda a: np.ascontiguousarray(a, dtype=np.float32)
    t0 = c * ntok
    L = projT.shape[1]

    def halo(rows):
        a = np.zeros((rows.shape[0], ntok + 16), np.float32)
        n = min(ntok + 16, L - t0)
        a[:, :n] = rows[:, t0:t0 + n]
        return a.reshape(2, 64, ntok + 16)

    pos = np.arange(t0, t0 + ntok, dtype=np.float32)
    cos, sin = rope_tables(pos)
    posc = (np.arange(t0 // 16, t0 // 16 + ntok // 16) * 16 + 16).astype(np.float32)
    cosc, sinc = rope_tables(posc)
    f = lambda nm: np.asarray(inp[nm][l], np.float32)
    gains = np.stack([f("nsa_q_norm"), f("nsa_kc_norm"), f("nsa_ks_norm"), f("nsa_kw_norm")], axis=1)
    return {
        "p_q": r(projT[0:512, t0:t0 + ntok].reshape(8, 64, ntok)),
        "p_ks": r(projT[768:896, t0:t0 + ntok].reshape(2, 64, ntok)),
        "p_kw": r(projT[1024:1152, t0:t0 + ntok].reshape(2, 64, ntok)),
        "p_kc": halo(projT[512:640]), "p_vc": halo(projT[640:768]),
        "p_gains": r(gains), "p_cos": cos, "p_sin": sin, "p_cosc": cosc, "p_sinc": sinc, "p_pm": rope_perm(),
        "p_w1k": r(f("cmp_k_w1").transpose(1, 0, 2)), "p_w1v": r(f("cmp_v_w1").transpose(1, 0, 2)),
        "p_w2k": r(f("cmp_k_w2").reshape(2, 128, 64).transpose(1, 0, 2)),
        "p_w2v": r(f("cmp_v_w2").reshape(2, 128, 64).transpose(1, 0, 2)),
        "p_peT": r(f("cmp_pe").T),
    }


NSLOT = 16
MASKV = -30000.0
BIGNEG = -1.0e30


def build_nsa(nslots=NSLOT, L=SEQ):
    nc = bass.Bass("TRN2", target_bir_lowering=False)
    d = {}

    def inp(name, shape):
        d[name] = nc.dram_tensor(name, list(shape), F32, kind="ExternalInput").ap()

    NT = L // 128
    inp("n_ksT", [2, 64, L]); inp("n_vs", [L, 2, 65]); inp("n_kcT", [2, 64, 1024]); inp("n_vc", [1024, 2, 65])
    inp("n_qT", [NSLOT, 64, 8, 128]); inp("n_kw", [NSLOT, 2, 64, 640]); inp("n_vw", [NSLOT, 640, 2, 65])
    inp("n_gates", [NSLOT, 128, 24])
    inp("m_diag", [8, 128, 512]); inp("m_win", [NSLOT, 5, 128, 512]); inp("m_cmpT", [NSLOT, 2, 128, 512])
    inp("m_cmpqn", [NSLOT, 128, 1024]); inp("r_tab", [NSLOT, 128, 2, 256])
    inp("c_ident", [128, 128]); inp("c_expand", [128, 64 * 128])
    out_d = nc.dram_tensor("n_out", [NSLOT, 128, 512], F32, kind="ExternalOutput").ap()
    k = KB(nc)
    emit_nsa(k, d, out_d, nslots, L)
    k.finish()
    return nc


def emit_nsa(k, d, out_d, nslots, L):
    NT = L // 128
    ksT = k.sb([64, 2, L], BF16, "ksT")
    for hk in range(2):
        for c4 in range(4):
            sl = slice(c4 * (L // 4), (c4 + 1) * (L // 4))
            k.dma("pool", ksT[:, hk, sl], d["n_ksT"][hk][:, sl], t_out=ksT)
    vs = k.sb([128, NT, 2, 65], BF16, "vs")
    vsv = d["n_vs"].rearrange("(t p) h e -> p t h e", p=128)
    for c4 in range(8):
        sl = slice(c4 * (NT // 8), (c4 + 1) * (NT // 8))
        k.dma("pool", vs[:, sl], vsv[:, sl], t_out=vs)
    kcT = k.sb([64, 2, 1024], BF16, "kcT")
    for hk in range(2):
        k.dma("pool", kcT[:, hk, :], d["n_kcT"][hk], t_out=kcT)
    vc = k.sb([128, 8, 2, 65], BF16, "vc")
    k.dma("pool", vc[:], d["n_vc"].rearrange("(t p) h e -> p t h e", p=128), t_out=vc)
    mdiag = k.sb([128, 8, 512], BF16, "mdiag")
    k.dma("pool", mdiag[:], d["m_diag"].rearrange("r p f -> p r f"), t_out=mdiag)
    expand = k.sb([128, 64 * 128], BF16, "expand")
    k.dma("pool", expand[:], d["c_expand"], t_out=expand)
    ident = k.sb([128, 128], F32, "ident")
    k.dma("sp", ident[:], d["c_ident"], t_out=ident)

    two = lambda shape, dt, nm: [k.sb(shape, dt, f"{nm}{i}") for i in range(2)]
    qT = two([64, 8, 128], BF16, "qT"); kw = two([64, 2, 640], BF16, "kw"); vw = two([128, 5, 2, 65], BF16, "vw")
    mwin = two([128, 5, 512], BF16, "mwin"); mcT = two([128, 2, 512], BF16, "mcT"); mqn = two([128, 1024], BF16, "mqn")
    rtab = two([128, 2, 256], F32, "rtab"); gates = two([128, 24], F32, "gates")
    ya = two([128, 512], F32, "ya")
    negT4 = two([128, 2, 4, 128], BF16, "negT4")
    E = [k.sb([128, 512], BF16, f"E{i}") for i in range(3)]
    Ecmp = [k.sb([128, 1024], F32, f"Ecmp{i}") for i in range(2)]
    Em = k.sb([128, 1024], F32, "Em")
    pcs = k.sb([128, 1032], F32, "pcs")
    imp = k.sb([128, 256], F32, "imp"); ieff = k.sb([128, 256], F32, "ieff")
    zap1 = k.sb([128, 256], F32, "zap1"); zap2 = k.sb([128, 256], F32, "zap2"); negm = k.sb([128, 256], F32, "negm")
    mx8 = k.sb([128, 8], F32, "mx8")
    sm = k.sb([128, 16], F32, "sm")
    cf = k.sb([128, 8], F32, "cf")
    pS = [k.ps([128, 512], F32, f"pS{i}") for i in range(2)]
    pO = [k.ps([128, 512], F32, f"pO{i}") for i in range(4)]
    pC = [k.ps([128, 512], F32, f"pC{i}") for i in range(2)]
    pT = pC[0]
    cnt = {"s": 0, "e": 0, "o": 0, "ec": 0}

    def slot_loads(i):
        b = i % 2
        k.dma("pool", qT[b][:], d["n_qT"][i], t_out=qT[b])
        k.dma("pool", kw[b][:], d["n_kw"][i].rearrange("h d n -> d h n"), t_out=kw[b])
        k.dma("pool", vw[b][:], d["n_vw"][i].rearrange("(t p) h e -> p t h e", p=128), t_out=vw[b])
        k.dma("pool", mwin[b][:], d["m_win"][i].rearrange("w p f -> p w f"), t_out=mwin[b])
        k.dma("pool", mcT[b][:], d["m_cmpT"][i].rearrange("w p f -> p w f"), t_out=mcT[b])
        k.dma("pool", mqn[b][:], d["m_cmpqn"][i], t_out=mqn[b])
        k.dma("sp", rtab[b][:], d["r_tab"][i], t_out=rtab[b])
        k.dma("sp", gates[b][:], d["n_gates"][i], t_out=gates[b])
        k.op("act", lambda g: g.activation(out=gates[b][:], in_=gates[b][:], func=AF.Sigmoid), [gates[b]], [gates[b]])

    def stage1(i, hk):
        b = i % 2
        x = (2 * i + hk) % 2
        NV = 64 * i + 64
        NJ = 16 * i + 16
        k.op("pool", lambda g: g.memset(pcs[:], 0.0), [], [pcs])
        k.op("pool", lambda g: g.memset(imp[:], 0.0), [], [imp])
        for gq in range(4):
            h = hk * 4 + gq
            ec = Ecmp[cnt["ec"] % 2]; cnt["ec"] += 1
            for c0 in range(0, NV, 512):
                cw = min(512, NV - c0)
                p = pC[(c0 // 512) % 2]
                k.op("pe", lambda g, p=p, c0=c0, cw=cw, h=h: g.matmul(p[:, :cw], lhsT=qT[b][:, h, :], rhs=kcT[:, hk, c0:c0 + cw],
                                                                   start=True, stop=True), [qT[b], kcT], [p])
                k.op("act", lambda g, p=p, c0=c0, cw=cw, ec=ec: g.activation(out=ec[:, c0:c0 + cw], in_=p[:, :cw], func=AF.Exp),
                     [p], [ec])
            k.op("dve", lambda g, ec=ec, gq=gq: g.scalar_tensor_tensor(
                out=Em[:, :NV], in0=ec[:, :NV], scalar=1.0, in1=mqn[b][:, :NV], op0=ALU.mult, op1=ALU.mult,
                accum_out=sm[:, gq:gq + 1]), [ec, mqn[b]], [Em, sm])
            k.op("dve", lambda g, gq=gq: g.tensor_scalar(out=sm[:, 4 + gq:5 + gq], in0=sm[:, gq:gq + 1], scalar1=1e-30,
                                                         scalar2=None, op0=ALU.max), [sm], [sm])
            k.op("dve", lambda g, gq=gq: g.reciprocal(out=sm[:, 4 + gq:5 + gq], in_=sm[:, 4 + gq:5 + gq]), [sm], [sm])
            if gq == 0:
                k.op("dve", lambda g, gq=gq: g.tensor_scalar(out=pcs[:, 1:1 + NV], in0=Em[:, :NV], scalar1=sm[:, 4 + gq:5 + gq],
                                                             scalar2=None, op0=ALU.mult), [Em, sm], [pcs])
            else:
                k.op("dve", lambda g, gq=gq: g.scalar_tensor_tensor(out=pcs[:, 1:1 + NV], in0=Em[:, :NV],
                                                                    scalar=sm[:, 4 + gq:5 + gq], in1=pcs[:, 1:1 + NV],
                                                                    op0=ALU.mult, op1=ALU.add), [Em, sm, pcs], [pcs])
        vw_ = lambda r: pcs[:, r:r + 4 * (NJ - 1) + 1:4]
        k.op("dve", lambda g: g.tensor_tensor(out=imp[:, :NJ], in0=vw_(0), in1=vw_(1), op=ALU.add), [pcs], [imp])
        for r in (2, 3, 4):
            k.op("dve", lambda g, r=r: g.tensor_tensor(out=imp[:, :NJ], in0=imp[:, :NJ], in1=vw_(r), op=ALU.add), [pcs, imp], [imp])
        k.op("dve", lambda g: g.tensor_tensor(out=ieff[:], in0=imp[:], in1=rtab[b][:, 0, :], op=ALU.add), [imp, rtab[b]], [ieff])
        k.op("dve", lambda g: g.tensor_tensor(out=ieff[:], in0=ieff[:], in1=rtab[b][:, 1, :], op=ALU.max), [ieff, rtab[b]], [ieff])
        k.op("dve", lambda g: g.max(out=mx8[:], in_=ieff[:]), [ieff], [mx8])
        k.op("dve", lambda g: g.match_replace(out=zap1[:], in_to_replace=mx8[:], in_values=ieff[:], imm_value=BIGNEG),
             [mx8, ieff], [zap1])
        k.op("dve", lambda g: g.max(out=mx8[:], in_=zap1[:]), [zap1], [mx8])
        k.op("dve", lambda g: g.match_replace(out=zap2[:], in_to_replace=mx8[:], in_values=zap1[:], imm_value=BIGNEG),
             [mx8, zap1], [zap2])
        k.op("dve", lambda g: g.tensor_tensor(out=negm[:], in0=ieff[:], in1=zap2[:], op=ALU.subtract), [ieff, zap2], [negm])
        k.op("dve", lambda g: g.tensor_scalar(out=negm[:], in0=negm[:], scalar1=1.0, scalar2=None, op0=ALU.min), [negm], [negm])
        k.op("dve", lambda g: g.tensor_scalar(out=negm[:], in0=negm[:], scalar1=-MASKV, scalar2=MASKV, op0=ALU.mult,
                                              op1=ALU.add), [negm], [negm])
        for half in range(2):
            k.op("pe", lambda g, half=half: g.matmul(pT[:, half * 128:(half + 1) * 128], lhsT=negm[:, half * 128:(half + 1) * 128],
                                                    rhs=ident[:], start=True, stop=True), [negm, ident], [pT])
        n4 = negT4[x]
        for half in range(2):
            for gq in range(4):
                e = "act" if (gq % 2 == 0) else "dve"
                k.copy(e, n4, n4[:, half, gq, :], pT, pT[:, half * 128:(half + 1) * 128])
        return x

    def attend(tiles, q4, q4_t, po):
        nt = len(tiles)
        es = []

        def pv(idx):
            tl = tiles[idx]; e = es[idx]
            for gq in range(4):
                k.op("pe", lambda g, e=e, gq=gq, tl=tl: g.matmul(po[gq][:, 0:65], lhsT=e[:, gq * 128:(gq + 1) * 128], rhs=tl["V"],
                                                              start=(idx == 0), stop=(idx == nt - 1)), [e, tl["Vt"]], [po[gq]])

        for idx, tl in enumerate(tiles):
            p = pS[cnt["s"] % 2]; cnt["s"] += 1
            e = E[cnt["e"] % 3]; cnt["e"] += 1
            es.append(e)
            add = tl.get("add")
            k.op("pe", lambda g, p=p, tl=tl: g.matmul(p[:], lhsT=tl["K"], rhs=q4, start=True, stop=(tl.get("add") is None)),
                 [tl["Kt"], q4_t], [p])
            if add is not None:
                k.op("pe", lambda g, p=p, add=add: g.matmul(p[:], lhsT=add[0], rhs=add[1], start=False, stop=True),
                     [expand, add[2]], [p])
            k.op("act", lambda g, p=p, e=e: g.activation(out=e[:], in_=p[:], func=AF.Exp), [p], [e])
            mul = tl.get("mul")
            if mul is not None:
                k.op("pool", lambda g, e=e, mul=mul: g.tensor_tensor(out=e[:], in0=e[:], in1=mul[0], op=ALU.mult), [e, mul[1]], [e])
            if idx >= 1:
                pv(idx - 1)
        pv(nt - 1)

    def combine(i, hk, br, po):
        b = i % 2
        y = ya[b]
        for gq in range(4):
            h = hk * 4 + gq
            k.op("dve", lambda g, gq=gq: g.tensor_scalar(out=cf[:, 0:1], in0=po[gq][:, 64:65], scalar1=1e-30, scalar2=None,
                                                         op0=ALU.max), [po[gq]], [cf])
            k.op("dve", lambda g: g.reciprocal(out=cf[:, 0:1], in_=cf[:, 0:1]), [cf], [cf])
            k.op("dve", lambda g, h=h: g.tensor_tensor(out=cf[:, 1:2], in0=cf[:, 0:1], in1=gates[b][:, h * 3 + br:h * 3 + br + 1],
                                                       op=ALU.mult), [cf, gates[b]], [cf])
            if br == 0:
                k.op("dve", lambda g, gq=gq, h=h: g.tensor_scalar(out=y[:, h * 64:(h + 1) * 64], in0=po[gq][:, 0:64],
                                                                  scalar1=cf[:, 1:2], scalar2=None, op0=ALU.mult), [po[gq], cf], [y])
            else:
                k.op("dve", lambda g, gq=gq, h=h: g.scalar_tensor_tensor(out=y[:, h * 64:(h + 1) * 64], in0=po[gq][:, 0:64],
                                                                         scalar=cf[:, 1:2], in1=y[:, h * 64:(h + 1) * 64],
                                                                         op0=ALU.mult, op1=ALU.add), [po[gq], cf, y], [y])

    def stage2(i, hk, x):
        b = i % 2
        q4 = qT[b][:, hk * 4:(hk + 1) * 4, :]
        ntc = i // 2 + 1
        tiles = []
        for nt_ in range(ntc):
            tl = {"K": kcT[:, hk, nt_ * 128:(nt_ + 1) * 128], "Kt": kcT, "V": vc[:, nt_, hk, :], "Vt": vc}
            m = nt_ - (ntc - 2)
            if m >= 0:
                tl["mul"] = (mcT[b][:, m, :], mcT[b])
            tiles.append(tl)
        po = pO
        attend(tiles, q4, qT[b], po)
        combine(i, hk, 0, po)
        tiles = []
        for w in range(5):
            tl = {"K": kw[b][:, hk, w * 128:(w + 1) * 128], "Kt": kw[b], "V": vw[b][:, w, hk, :], "Vt": vw[b]}
            if i == 0 or w in (0, 4):
                tl["mul"] = (mwin[b][:, w, :], mwin[b])
            tiles.append(tl)
        po = pO
        attend(tiles, q4, qT[b], po)
        combine(i, hk, 2, po)
        tiles = []
        for kt in range(8 * i + 8):
            tl = {"K": ksT[:, hk, kt * 128:(kt + 1) * 128], "Kt": ksT, "V": vs[:, kt, hk, :], "Vt": vs,
                  "add": (expand[:, (kt % 64) * 128:(kt % 64 + 1) * 128], negT4[x][:, kt // 64, :, :], negT4[x])}
            if kt >= 8 * i:
                tl["mul"] = (mdiag[:, kt - 8 * i, :], mdiag)
            tiles.append(tl)
        po = pO
        attend(tiles, q4, qT[b], po)
        combine(i, hk, 1, po)
        if hk == 1:
            k.dma("sp", out_d[i], ya[b][:], t_in=ya[b], final=True)

    units = [(i, hk) for i in range(nslots) for hk in range(2)]
    slot_loads(0)
    xs = {}
    xs[units[0]] = stage1(*units[0])
    for ui, (i, hk) in enumerate(units):
        if ui + 1 < len(units):
            ni, nhk = units[ui + 1]
            if nhk == 0:
                slot_loads(ni)
            xs[(ni, nhk)] = stage1(ni, nhk)
        stage2(i, hk, xs[(i, hk)])


def nsa_consts():
    ident = np.eye(128, dtype=np.float32)
    ex = np.zeros((128, 64, 128), np.float32)
    for kt in range(64):
        ex[2 * kt, kt, 0:64] = 1.0
        ex[2 * kt + 1, kt, 64:128] = 1.0
    return ident, ex.reshape(128, 64 * 128)


def nsa_masks(c, L=SEQ):
    q = np.arange(128)
    key = np.arange(128)
    m_diag = np.zeros((8, 128, 512), np.float32)
    for r in range(8):
        if r < c:
            m_diag[r] = 1.0
        elif r == c:
            m_diag[r] = np.tile((key[:, None] <= q[None, :]).astype(np.float32), (1, 4))
    m_win = np.zeros((NSLOT, 5, 128, 512), np.float32)
    m_cmpT = np.zeros((NSLOT, 2, 128, 512), np.float32)
    m_cmpqn = np.zeros((NSLOT, 128, 1024), np.float32)
    r_tab = np.zeros((NSLOT, 128, 2, 256), np.float32)
    n_all = np.arange(1024)
    j = np.arange(256)
    for i in range(NSLOT):
        qb = 8 * i + c
        s = 128 * qb
        t = s + q
        for w in range(5):
            kpos = s - 512 + 128 * w + key
            ok = (kpos[:, None] <= t[None, :]) & (kpos[:, None] > t[None, :] - 512) & (kpos[:, None] >= 0)
            m_win[i, w] = np.tile(ok.astype(np.float32), (1, 4))
        ntc = i // 2 + 1
        for m in range(2):
            nt_ = ntc - 2 + m
            if nt_ < 0:
                continue
            n = 128 * nt_ + key
            ok = (16 * n[:, None] + 31 <= t[None, :]) & (n[:, None] <= 1022)
            m_cmpT[i, m] = np.tile(ok.astype(np.float32), (1, 4))
        m_cmpqn[i] = ((16 * n_all[None, :] + 31 <= t[:, None]) & (n_all[None, :] <= 1022)).astype(np.float32)
        cur = t // 64
        r_tab[i, :, 0, :] = np.where(j[None, :] <= cur[:, None], 0.0, BIGNEG)
        force = np.full((128, 256), 2 * BIGNEG, np.float32)
        force[:, 0] = 8.0
        force[q, cur] = 16.0
        prev = cur - 1
        okp = prev >= 0
        force[q[okp], prev[okp]] = 32.0
        r_tab[i, :, 1, :] = force
    return {"m_diag": m_diag, "m_win": m_win, "m_cmpT": m_cmpT, "m_cmpqn": m_cmpqn, "r_tab": r_tab}


def nsa_inputs(projT, prep, c, L=SEQ):
    r = lambda a: np.ascontiguousarray(a, dtype=np.float32)
    ones = lambda shp: np.ones(shp, np.float32)
    vs = projT[896:1024, :L].T.reshape(L, 2, 64)
    vs_aug = np.concatenate([vs, ones((L, 2, 1))], axis=2)
    kcT = np.zeros((2, 64, 1024), np.float32); kcT[:, :, :1023] = prep["kc"][:, :, :1023]
    vc_aug = np.zeros((1024, 2, 65), np.float32); vc_aug[:1023, :, :64] = prep["vc"][:1023]; vc_aug[:1023, :, 64] = 1.0
    kw_pad = np.concatenate([np.zeros((2, 64, 512), np.float32), prep["kw"]], axis=2)
    vw = projT[1152:1280, :L].T.reshape(L, 2, 64)
    vw_aug = np.concatenate([vw, ones((L, 2, 1))], axis=2)
    vw_pad = np.concatenate([np.zeros((512, 2, 65), np.float32), vw_aug], axis=0)
    gl = projT[1280:1304, :L].T
    qT = np.zeros((NSLOT, 64, 8, 128), np.float32); kws = np.zeros((NSLOT, 2, 64, 640), np.float32)
    vws = np.zeros((NSLOT, 640, 2, 65), np.float32); gts = np.zeros((NSLOT, 128, 24), np.float32)
    for i in range(NSLOT):
        qb = 8 * i + c
        s = 128 * qb
        qT[i] = prep["q"][:, :, s:s + 128].transpose(1, 0, 2)
        kws[i] = kw_pad[:, :, s:s + 640]
        vws[i] = vw_pad[s:s + 640]
        gts[i] = gl[s:s + 128]
    ident, ex = nsa_consts()
    dd = {"n_ksT": r(prep["ks"]), "n_vs": r(vs_aug), "n_kcT": kcT, "n_vc": vc_aug, "n_qT": qT, "n_kw": kws, "n_vw": vws,
          "n_gates": gts, "c_ident": ident, "c_expand": ex}
    dd.update(nsa_masks(c, L))
    return dd


_PROGS = {}


def _prog(name, builder):
    if name not in _PROGS:
        _PROGS[name] = builder()
    return _PROGS[name]


def _run(nc, maps):
    res = run_bass_kernel_spmd(nc, maps, core_ids=list(range(NCORES)))
    return res.results


def kernel(**inputs):
    inp = {k_: np.asarray(v) for k_, v in inputs.items()}
    L = SEQ
    xT = np.ascontiguousarray(inp["x"][0].T.astype(np.float32))
    for l in range(DEPTH):
        ncA = _prog("A", build_stage_A)
        gainA = lay128(inp["attn_norm"][l])
        wA = np.asarray(inp["w_in"][l], np.float32)
        res = _run(ncA, [{"xT": np.ascontiguousarray(xT[:, c * TOK:(c + 1) * TOK]), "gain": gainA, "w": wA}
                         for c in range(NCORES)])
        projT = np.concatenate([r["projT"] for r in res], axis=1)
        del res
        ncP = _prog("P", build_prep)
        res = _run(ncP, [prep_inputs(projT, inp, l, c) for c in range(NCORES)])
        prep = {
            "q": np.concatenate([r["o_q"] for r in res], axis=2),
            "ks": np.concatenate([r["o_ks"] for r in res], axis=2),
            "kw": np.concatenate([r["o_kw"] for r in res], axis=2),
            "kc": np.concatenate([r["o_kc"] for r in res], axis=2),
            "vc": np.concatenate([r["o_vc"] for r in res], axis=0),
        }
        del res
        ncN = _prog("N", build_nsa)
        res = _run(ncN, [nsa_inputs(projT, prep, c) for c in range(NCORES)])
        ya = np.zeros((L, 512), np.float32)
        for c in range(NCORES):
            o = res[c]["n_out"]
            for i in range(NSLOT):
                qb = 8 * i + c
                ya[qb * 128:(qb + 1) * 128] = o[i]
        del res, prep
        ncGS = _prog("GS", build_gs)
        maps = []
        for c in range(NCORES):
            m = gdn_inputs(projT, inp, l, c)
            m.update(s5_inputs(projT, inp, l, c))
            maps.append(m)
        res = _run(ncGS, maps)
        ycT = np.concatenate([np.ascontiguousarray(r["g_out"].T) for r in res], axis=0)
        ybT = np.concatenate([r["s_out"] for r in res], axis=0)
        del res, maps, projT
        ncC = _prog("C", build_stage_C)

        def halo(a, c):
            out = np.zeros((a.shape[0], TOK + 2), np.float32)
            lo = c * TOK - 2
            if lo < 0:
                out[:, 2:] = a[:, 0:TOK]
            else:
                out[:] = a[:, lo:lo + TOK + 2]
            return out

        yaT = np.ascontiguousarray(ya.T)
        maps = [stage_C_inputs(inp, l, halo(xT, c), halo(yaT, c), halo(ybT, c), halo(ycT, c)) for c in range(NCORES)]
        res = _run(ncC, maps)
        xT = np.concatenate([r["xoT"] for r in res], axis=1)
        del res, maps
    return np.ascontiguousarray(xT.T)[None].astype(np.float32)


def build_gs(L=SEQ):
    nc = bass.Bass("TRN2", target_bir_lowering=False)
    din = {}

    def inp(name, shape):
        din[name] = nc.dram_tensor(name, list(shape), F32, kind="ExternalInput").ap()
        return din[name]

    qT_d = inp("g_qT", [128, L]); kT_d = inp("g_kT", [128, L]); vT_d = inp("g_vT", [128, L])
    z_d = inp("g_z", [L, 128]); a_d = inp("g_a33", [33, L]); b_d = inp("g_b33", [33, L])
    cw_d = inp("g_cw", [128, 3, 4]); prm_d = inp("g_prm33", [33, 2]); gain_d = inp("g_gain", [64, 128])
    cd = {n: inp("gc_" + n, v.shape) for n, v in gdn_consts().items()}
    W = S5_W
    inp("s_uT", [64, L]); inp("s_lamre", [128, 2]); inp("s_lamim", [128, 2]); inp("s_logdt", [128, 2])
    inp("s_bre", [2, 64, 128]); inp("s_bim", [2, 64, 128]); inp("s_cre", [2, 128, 64]); inp("s_cim", [2, 128, 64])
    inp("s_d", [64, 1]); inp("s_trow", [128, W + 1])
    gout = nc.dram_tensor("g_out", [L, 128], F32, kind="ExternalOutput").ap()
    sout = nc.dram_tensor("s_out", [64, L], F32, kind="ExternalOutput").ap()
    k = KB(nc)
    s5_step = emit_s5(k, L, din, sout, stepper=True, npp=1, npy=1)
    emit_gdn(k, L, qT_d, kT_d, vT_d, z_d, a_d, b_d, cw_d, prm_d, gain_d, cd, gout, merged=True, extra=s5_step)
    k.finish()
    return nc
```
